# Optimizing a Trainium2 kernel written in Bass

```python
import math
import jax
import jax.numpy as jnp
from jax import lax
import numpy as np

D_MODEL = 1024
BATCH = 8
SEQ = 4096
DEPTH = 1

MEM_LEN = 256
D_MIX = D_MODEL
D_S5 = D_MIX // 2
S5_GROUP = 16
S5_GROUPS = D_S5 // S5_GROUP
S5_STATE = 64
D_LRU = D_MIX - D_S5
LRU_HEADS = 8
LRU_HEAD_DIM = D_LRU // LRU_HEADS
LRU_CONV = 4
LRU_C = 8.0
D_FF = ((8 * D_MODEL // 3 + 127) // 128) * 128
FFN_CONV = 3
XA_HEADS = 4
XA_HEAD_DIM = D_MODEL // XA_HEADS
EPS = 1e-6

kernel_name = "hybrid_s5_rglru_xattn_convffn"


def rms_norm(x, g):
    xf = x.astype(jnp.float32)
    y = xf * lax.rsqrt(jnp.mean(xf * xf, axis=-1, keepdims=True) + EPS)
    return (y * g.astype(jnp.float32)).astype(x.dtype)


def causal_dwconv(x, w, b):
    k = w.shape[0]
    y = lax.conv_general_dilated(
        x, w[:, None, :].astype(x.dtype), window_strides=(1,),
        padding=[(k - 1, 0)], dimension_numbers=("NWC", "WIO", "NWC"),
        feature_group_count=x.shape[-1])
    return y + b.astype(x.dtype)


def _linear_scan_combine(left, right):
    a_l, b_l = left
    a_r, b_r = right
    return a_l * a_r, a_r * b_l + b_r


def s5_mixer(u, lam_re, lam_im, log_dt, b_re, b_im, c_re, c_im, d_skip, w_glu, b_glu):
    f32 = jnp.float32
    bsz, seq, _ = u.shape
    ug = u.astype(f32).reshape(bsz, seq, S5_GROUPS, S5_GROUP)
    lam = lax.complex(lam_re.astype(f32), lam_im.astype(f32))
    dt = jnp.exp(log_dt.astype(f32))[:, None]
    lam_bar = jnp.exp(lam * dt)
    b_mat = lax.complex(b_re.astype(f32), b_im.astype(f32))
    b_bar = ((lam_bar - 1.0) / lam)[..., None] * b_mat
    bu = jnp.einsum("gph,bsgh->bsgp", b_bar, ug.astype(jnp.complex64))
    a = jnp.broadcast_to(lam_bar, bu.shape)
    _, state = lax.associative_scan(_linear_scan_combine, (a, bu), axis=1)
    c_mat = lax.complex(c_re.astype(f32), c_im.astype(f32))
    y = jnp.real(jnp.einsum("ghp,bsgp->bsgh", c_mat, state)) + d_skip.astype(f32) * ug
    y = jax.nn.gelu(y.reshape(bsz, seq, D_S5))
    gate = jax.nn.sigmoid(y @ w_glu.astype(f32) + b_glu.astype(f32))
    return (y * gate).astype(u.dtype)


def rglru_mixer(xb, gb, conv_w, conv_b, w_a, b_a, w_x, b_x, lam):
    f32 = jnp.float32
    bsz, seq, _ = xb.shape
    xc = causal_dwconv(xb, conv_w, conv_b).astype(f32).reshape(bsz, seq, LRU_HEADS, LRU_HEAD_DIM)
    r = jax.nn.sigmoid(jnp.einsum("bshi,hij->bshj", xc, w_a.astype(f32)) + b_a.astype(f32))
    i = jax.nn.sigmoid(jnp.einsum("bshi,hij->bshj", xc, w_x.astype(f32)) + b_x.astype(f32))
    log_a = -LRU_C * r * jax.nn.softplus(-lam.astype(f32))
    a = jnp.exp(log_a)
    bx = jnp.sqrt(-jnp.expm1(2.0 * log_a)) * (i * xc)
    _, h = lax.associative_scan(_linear_scan_combine, (a, bx), axis=1)
    y = h.reshape(bsz, seq, D_LRU) * jax.nn.gelu(gb.astype(f32))
    return y.astype(xb.dtype)


def memory_cross_attention(h, mem_n, w_q, w_k, w_v, w_o):
    bsz, seq, _ = h.shape
    m = mem_n.shape[1]
    q = (h @ w_q).reshape(bsz, seq, XA_HEADS, XA_HEAD_DIM)
    k = (mem_n @ w_k).reshape(bsz, m, XA_HEADS, XA_HEAD_DIM)
    v = (mem_n @ w_v).reshape(bsz, m, XA_HEADS, XA_HEAD_DIM)
    scores = jnp.einsum("bshk,bmhk->bhsm", q.astype(jnp.float32), k.astype(jnp.float32)) * (XA_HEAD_DIM ** -0.5)
    p = jax.nn.softmax(scores, axis=-1).astype(v.dtype)
    o = jnp.einsum("bhsm,bmhk->bshk", p, v).reshape(bsz, seq, D_MODEL)
    return o @ w_o


def conv_ffn(h, w_up, conv_w, conv_b, w_down):
    z = causal_dwconv(h @ w_up, conv_w, conv_b)
    val, gate = z[..., :D_FF], z[..., D_FF:]
    return (jax.nn.gelu(gate) * val) @ w_down


def setup_inputs(seed: int = 0) -> dict:
    key = jax.random.key(seed)
    ks = iter(jax.random.split(key, 48))
    f32 = jnp.float32
    L = DEPTH

    def nrm(shape, scale):
        return scale * jax.random.normal(next(ks), shape, f32)

    def gain(shape):
        return 1.0 + nrm(shape, 0.02)

    x = nrm((BATCH, SEQ, D_MODEL), 1.0)
    mem = nrm((BATCH, MEM_LEN, D_MODEL), 1.0)
    mem_norm_g = gain((D_MODEL,))
    ln_mix_g = gain((L, D_MODEL))
    w_in = nrm((L, D_MODEL, D_S5 + 2 * D_LRU), D_MODEL ** -0.5)
    w_out = nrm((L, D_MIX, D_MODEL), D_MIX ** -0.5)
    n = jnp.arange(S5_STATE, dtype=f32)
    s5_lam_re = -0.5 + nrm((L, S5_GROUPS, S5_STATE), 0.01)
    s5_lam_im = jnp.pi * n + nrm((L, S5_GROUPS, S5_STATE), 0.01)
    s5_log_dt = jax.random.uniform(next(ks), (L, S5_GROUPS), f32, math.log(1e-3), math.log(1e-1))
    s5_b_re = nrm((L, S5_GROUPS, S5_STATE, S5_GROUP), (2 * S5_GROUP) ** -0.5)
    s5_b_im = nrm((L, S5_GROUPS, S5_STATE, S5_GROUP), (2 * S5_GROUP) ** -0.5)
    s5_c_re = nrm((L, S5_GROUPS, S5_GROUP, S5_STATE), S5_STATE ** -0.5)
    s5_c_im = nrm((L, S5_GROUPS, S5_GROUP, S5_STATE), S5_STATE ** -0.5)
    s5_d = nrm((L, S5_GROUPS, S5_GROUP), 1.0)
    s5_w_glu = nrm((L, D_S5, D_S5), D_S5 ** -0.5)
    s5_b_glu = nrm((L, D_S5), 0.01)
    lru_conv_w = nrm((L, LRU_CONV, D_LRU), LRU_CONV ** -0.5)
    lru_conv_b = nrm((L, D_LRU), 0.01)
    lru_w_a = nrm((L, LRU_HEADS, LRU_HEAD_DIM, LRU_HEAD_DIM), LRU_HEAD_DIM ** -0.5)
    lru_b_a = nrm((L, LRU_HEADS, LRU_HEAD_DIM), 0.01)
    lru_w_x = nrm((L, LRU_HEADS, LRU_HEAD_DIM, LRU_HEAD_DIM), LRU_HEAD_DIM ** -0.5)
    lru_b_x = nrm((L, LRU_HEADS, LRU_HEAD_DIM), 0.01)
    a_c = jax.random.uniform(next(ks), (L, LRU_HEADS, LRU_HEAD_DIM), f32, 0.9, 0.999)
    p = a_c ** (1.0 / LRU_C)
    lru_lam = jnp.log(p) - jnp.log1p(-p)
    ln_xa_g = gain((L, D_MODEL))
    xa_w_q = nrm((L, D_MODEL, D_MODEL), D_MODEL ** -0.5)
    xa_w_k = nrm((L, D_MODEL, D_MODEL), D_MODEL ** -0.5)
    xa_w_v = nrm((L, D_MODEL, D_MODEL), D_MODEL ** -0.5)
    xa_w_o = nrm((L, D_MODEL, D_MODEL), D_MODEL ** -0.5)
    ln_ffn_g = gain((L, D_MODEL))
    ffn_w_up = nrm((L, D_MODEL, 2 * D_FF), D_MODEL ** -0.5)
    ffn_conv_w = nrm((L, FFN_CONV, 2 * D_FF), FFN_CONV ** -0.5)
    ffn_conv_b = nrm((L, 2 * D_FF), 0.01)
    ffn_w_down = nrm((L, D_FF, D_MODEL), D_FF ** -0.5)
    final_norm_g = gain((D_MODEL,))
    return {"x": x, "mem": mem, "mem_norm_g": mem_norm_g, "ln_mix_g": ln_mix_g,
            "w_in": w_in, "w_out": w_out,
            "s5_lam_re": s5_lam_re, "s5_lam_im": s5_lam_im, "s5_log_dt": s5_log_dt,
            "s5_b_re": s5_b_re, "s5_b_im": s5_b_im, "s5_c_re": s5_c_re, "s5_c_im": s5_c_im,
            "s5_d": s5_d, "s5_w_glu": s5_w_glu, "s5_b_glu": s5_b_glu,
            "lru_conv_w": lru_conv_w, "lru_conv_b": lru_conv_b, "lru_w_a": lru_w_a,
            "lru_b_a": lru_b_a, "lru_w_x": lru_w_x, "lru_b_x": lru_b_x, "lru_lam": lru_lam,
            "ln_xa_g": ln_xa_g, "xa_w_q": xa_w_q, "xa_w_k": xa_w_k, "xa_w_v": xa_w_v,
            "xa_w_o": xa_w_o, "ln_ffn_g": ln_ffn_g, "ffn_w_up": ffn_w_up,
            "ffn_conv_w": ffn_conv_w, "ffn_conv_b": ffn_conv_b, "ffn_w_down": ffn_w_down,
            "final_norm_g": final_norm_g}


def reference(x, mem, mem_norm_g, ln_mix_g, w_in, w_out,
              s5_lam_re, s5_lam_im, s5_log_dt, s5_b_re, s5_b_im, s5_c_re, s5_c_im,
              s5_d, s5_w_glu, s5_b_glu,
              lru_conv_w, lru_conv_b, lru_w_a, lru_b_a, lru_w_x, lru_b_x, lru_lam,
              ln_xa_g, xa_w_q, xa_w_k, xa_w_v, xa_w_o,
              ln_ffn_g, ffn_w_up, ffn_conv_w, ffn_conv_b, ffn_w_down,
              final_norm_g):
    mem_n = rms_norm(mem, mem_norm_g)
    for l in range(DEPTH):
        h = rms_norm(x, ln_mix_g[l])
        z = h @ w_in[l]
        u_s5 = z[..., :D_S5]
        x_lru = z[..., D_S5:D_S5 + D_LRU]
        g_lru = z[..., D_S5 + D_LRU:]
        y_s5 = s5_mixer(u_s5, s5_lam_re[l], s5_lam_im[l], s5_log_dt[l], s5_b_re[l], s5_b_im[l],
                        s5_c_re[l], s5_c_im[l], s5_d[l], s5_w_glu[l], s5_b_glu[l])
        y_lru = rglru_mixer(x_lru, g_lru, lru_conv_w[l], lru_conv_b[l], lru_w_a[l], lru_b_a[l],
                            lru_w_x[l], lru_b_x[l], lru_lam[l])
        x = x + jnp.concatenate([y_s5, y_lru], axis=-1) @ w_out[l]
        x = x + memory_cross_attention(rms_norm(x, ln_xa_g[l]), mem_n,
                                       xa_w_q[l], xa_w_k[l], xa_w_v[l], xa_w_o[l])
        x = x + conv_ffn(rms_norm(x, ln_ffn_g[l]), ffn_w_up[l], ffn_conv_w[l], ffn_conv_b[l], ffn_w_down[l])
    return rms_norm(x, final_norm_g)
```

```python
import math
from contextlib import ExitStack
import numpy as np
import ml_dtypes
import concourse.bass as bass
import concourse.mybir as mybir
from concourse.bass_utils import run_bass_kernel_spmd

F32 = mybir.dt.float32
BF16 = mybir.dt.bfloat16
ALU = mybir.AluOpType
AF = mybir.ActivationFunctionType

SEQ = 4096
TB = 512
D = 1024
NS = 5
PI = math.pi
import os
FFN_VARIANT = int(os.environ.get('FFN_VARIANT', '1'))

SL_WIN = 0
SL_S5W = 3
SL_S5V = 7
SL_KG = 11
SL_WOUT = 12
SL_WQ = 14
SL_WO = 16
SL_UP = 18
SL_DN = 29
SL_WK = 35
SL_WV = 37
NSLAB = 39
HOST_SLABS = [0, 1, 2, 11, 12, 13, 14, 15, 16, 17] + list(range(18, 35)) + [35, 36, 37, 38]

C_G1, C_G2, C_G3, C_GMEM = 0, 8, 16, 24
C_BGLU = 32
C_LCW = 36
C_LCB = 52
C_BA = 56
C_BX = 60
C_LAM = 64
C_FCW = 68
C_FCB = 200
C_DS5 = 244
NCOL = 248


class Prog:
    ENG = ("pe", "act", "dve", "pool", "sp")

    def __init__(self, nc, es):
        self.nc = nc
        self.es = es
        self.q = {e: [] for e in self.ENG}
        self.cnt = {}
        self.sems = {}
        self.seen = {e: {} for e in self.ENG}
        self.lastw = {}
        self.readers = {}

    def sem(self, key):
        if key not in self.sems:
            self.sems[key] = self.es.enter_context(self.nc.semaphore("s_" + key))
            self.cnt[key] = 0
        return self.sems[key]

    def op(self, eng, fn, r=(), w=(), dsem=None):
        deps = {}

        def add(tok):
            if tok is None:
                return
            k, v = tok
            if deps.get(k, 0) < v:
                deps[k] = v
        for b in r:
            add(self.lastw.get(b))
        for b in w:
            add(self.lastw.get(b))
            for k, v in self.readers.get(b, {}).items():
                add((k, v))
        waits = []
        for k, v in deps.items():
            if eng == "pe" and k == "pe":
                continue
            if self.seen[eng].get(k, 0) >= v:
                continue
            self.seen[eng][k] = v
            waits.append((k, v))
        if dsem is None:
            key, inc = eng, 1
        else:
            key, inc = dsem, 16
        self.sem(key)
        self.cnt[key] += inc
        tok = (key, self.cnt[key])
        self.q[eng].append((waits, fn, key, inc))
        for b in r:
            self.readers.setdefault(b, {})[key] = tok[1]
        for b in w:
            self.lastw[b] = tok
            self.readers[b] = {}
        return tok

    def barrier(self, skip=None):
        for e in self.ENG:
            waits = []
            for k, v in self.cnt.items():
                if skip is not None and k.startswith(skip):
                    continue
                if v > 0 and self.seen[e].get(k, 0) < v:
                    self.seen[e][k] = v
                    waits.append((k, v))
            if waits:
                self.q[e].append((waits, None, None, 0))

    def emit(self, block):
        sems = self.sems

        def run(eng, lst):
            for waits, fn, key, inc in lst:
                for k, v in waits:
                    eng.wait_ge(sems[k], v)
                if fn is not None:
                    ins = fn(eng)
                    ins.then_inc(sems[key], inc)

        @block.tensor
        def _(e):
            run(e, self.q["pe"])

        @block.scalar
        def _(e):
            run(e, self.q["act"])

        @block.vector
        def _(e):
            run(e, self.q["dve"])

        @block.gpsimd
        def _(e):
            run(e, self.q["pool"])

        @block.sync
        def _(e):
            run(e, self.q["sp"])


def build(nblk=8, stage=6, dbg=None, pro_only=False):
    nc = bass.Bass("TRN2", target_bir_lowering=False)
    ntok = nblk * TB

    def din(name, shape, dt=F32):
        return nc.dram_tensor(name, list(shape), dt, kind="ExternalInput").ap()
    x_d = din("x", [ntok, D])
    mem_d = din("mem", [256, D])
    w32_d = din("w32", [len(HOST_SLABS), 128, 4096])
    colp_d = din("colp", [128, NCOL])
    gfin_d = din("gfin", [128, D])
    s5par_d = din("s5par", [128, 3, 16])
    bcc_d = din("bcc", [128, 4, 16, 16])
    mask8_d = din("mask8", [128, 2, 8])
    lruw_d = din("lruw", [128, 8, 128])
    ident_d = din("ident", [128, 128], BF16)
    identf_d = din("identf", [128, 128])
    out_d = nc.dram_tensor("out", [ntok, D], F32, kind="ExternalOutput").ap()
    wscr = nc.dram_tensor("wscr", [NSLAB, 128, 4096], BF16, kind="Internal").ap()
    dbg_d = None
    if dbg is not None:
        dbg_d = nc.dram_tensor("dbg", [16, 128, 4096], F32, kind="ExternalOutput").ap()

    es = ExitStack()
    P = Prog(nc, es)

    def sb(name, shape, dt=F32, stack=es):
        return stack.enter_context(nc.sbuf_tensor("sb_" + name, list(shape), dt))

    colp = sb("colp", [128, NCOL])
    gfin = sb("gfin", [128, D])
    ident = sb("ident", [128, 128], BF16)
    onesb = sb("onesb", [128, 128], BF16)
    lruw = sb("lruw", [128, 8, 128], BF16)
    cneg = sb("cneg", [128, 4])
    epsc = sb("epsc", [128, 1])
    hpic = sb("hpic", [128, 1])
    onec = sb("onec", [128, 1])
    hbias = sb("hbias", [128, 12])
    cnegh = sb("cnegh", [128, 4])
    cosT = sb("cosT", [128, 16, 128])
    sinT = sb("sinT", [128, 16, 128])
    Rtab = sb("Rtab", [128, 16, 128])
    R4 = sb("R4", [128, 16])
    E4r = sb("E4r", [128, 16])
    E4i = sb("E4i", [128, 16])
    Kfm = sb("Kfm", [128, 8, 256], BF16)
    Vtok = sb("Vtok", [128, 2, D], BF16)
    slots = [sb(f"slot{i}", [128, 4096], BF16) for i in range(NS)]
    xres = [sb("xres0", [128, 4, D]), None]
    hT = sb("hT", [128, 4, D], BF16)
    hfm = sb("hfm", [128, 8, TB], BF16)
    ss = sb("ss", [128, 8])
    rstd = sb("rstd", [128, 8])
    junk = sb("junk", [128, D], BF16)
    s5carry_r = sb("s5cr", [128, 16])
    s5carry_i = sb("s5ci", [128, 16])
    lrucarry = sb("lrucarry", [128, 4])
    psb = [es.enter_context(nc.psum_tensor(f"pp{i}", [128, 512], F32)) for i in range(6)]
    psT = [es.enter_context(nc.psum_tensor(f"ppT{i}", [128, 1024], BF16)) for i in range(2)]

    cp = lambda c, n=1: colp[:, c:c + n]

    slab_state = {"n": 0}

    def load_slab(idx, eng="sp"):
        i = slab_state["n"] % NS
        slab_state["n"] += 1
        name = f"slot{i}"
        P.op(eng, lambda e, i=i, idx=idx: e.dma_start(out=slots[i][:, :], in_=wscr[idx]),
             r=[f"scr{idx}", f"scr{idx}g"], w=[name], dsem=f"d_slot{i}")
        return slots[i], name

    def mm_group(out_ap, pairs, r, w):
        n = len(pairs)

        def fn(e):
            ins = None
            for j, (l, rr) in enumerate(pairs):
                ins = e.matmul(out_ap, l, rr, start=(j == 0), stop=(j == n - 1))
            return ins
        P.op("pe", fn, r=r, w=w)

    def V(fn, r, w):
        P.op("dve", fn, r=r, w=w)

    def A(fn, r, w):
        P.op("act", fn, r=r, w=w)

    def G(fn, r, w):
        P.op("pool", fn, r=r, w=w)

    pst = ExitStack()
    par = sb("par", [128, 3, 16], stack=pst)
    identf = sb("identf", [128, 128], stack=pst)
    bcc = sb("bcc", [128, 4, 16, 16], stack=pst)
    mask8 = sb("mask8", [128, 2, 8], stack=pst)
    cs = sb("cs", [128, 4, 16, 16], stack=pst)
    t1 = sb("t1", [128, 16, 128], stack=pst)
    lruw32 = t1[:, 0:8, :]
    t2 = sb("t2", [128, 16, 128], stack=pst)
    NB = sb("NB", [128, 4, 2, 16, 128], BF16, stack=pst)
    Vm = sb("Vm", [128, 2, 4, 8, 128], BF16, stack=pst)
    Wst = sb("Wst", [128, 1, 8, 4, 128], BF16, stack=pst)
    Kst = hT[:, 2:4, :].rearrange("p a (b c) -> p (a b) c", c=128).rearrange("p (k t) c -> p k t c", k=4)
    sm = xres[0][:, 2:4, :].rearrange("p a (b c) -> p (a b) c", c=16)
    memx = xres[0]
    memfm = hfm

    def ld(dst, src, name, eng="sp"):
        P.op(eng, lambda e: e.dma_start(out=dst, in_=src), w=[name], dsem="d_" + name)
    ld(colp[:, :], colp_d, "colp")
    ld(gfin[:, :], gfin_d, "gfin")
    ld(ident[:, :], ident_d, "ident")
    ld(identf[:, :], identf_d, "identf")
    ld(par[:, :, :], s5par_d, "par")
    ld(bcc[:, :, :, :], bcc_d, "bcc")
    ld(mask8[:, :, :], mask8_d, "mask8")
    ld(lruw32, lruw_d, "t1")
    P.op("sp", lambda e: e.dma_start(out=memx[:, 0:2, :], in_=mem_d.rearrange("(a p) d -> p a d", p=128)),
         w=["x0a0", "x0a1"], dsem="d_x0")

    G(lambda e: e.memset(onesb[:, :], 1.0), [], ["onesb"])
    G(lambda e: e.memset(epsc[:, :], 1e-6), [], ["epsc"])
    G(lambda e: e.memset(hpic[:, :], PI / 2), [], ["hpic"])
    G(lambda e: e.memset(onec[:, :], 1.0), [], ["onec"])
    G(lambda e: e.memset(lrucarry[:, :], 0.0), [], ["lrucarry"])
    G(lambda e: e.memset(s5carry_r[:, :], 0.0), [], ["s5c"])
    G(lambda e: e.memset(s5carry_i[:, :], 0.0), [], ["s5c"])
    V(lambda e: e.tensor_copy(lruw[:, :, :], lruw32), ["t1"], ["lruw"])

    A(lambda e: e.activation(cneg[:, :], cp(C_LAM, 4), AF.Exp, scale=-1.0), ["colp"], ["cneg"])
    A(lambda e: e.activation(cneg[:, :], cneg[:, :], AF.Ln, bias=onec[:, :]), ["cneg", "onec"], ["cneg"])
    V(lambda e: e.tensor_scalar_mul(cneg[:, :], cneg[:, :], -8.0), ["cneg"], ["cneg"])
    V(lambda e: e.tensor_scalar_mul(cnegh[:, :], cneg[:, :], 0.5), ["cneg"], ["cnegh"])
    V(lambda e: e.tensor_scalar_mul(hbias[:, 0:4], cp(C_BA, 4), 0.5), ["colp"], ["hbias"])
    V(lambda e: e.tensor_scalar_mul(hbias[:, 4:8], cp(C_BX, 4), 0.5), ["colp"], ["hbias"])
    V(lambda e: e.tensor_scalar_mul(hbias[:, 8:12], cp(C_BGLU, 4), 0.5), ["colp"], ["hbias"])

    smn = {"i": 0}

    def S(name=None):
        i = smn["i"]
        smn["i"] += 1
        return sm[:, i, :], f"sm{i}"

    def vtt(o, a, b, op):
        (oa, on), (aa, an), (ba, bn) = o, a, b
        V(lambda e: e.tensor_tensor(oa, aa, ba, op), [an, bn], [on])

    def cmul(a_r, a_i, b_r, b_i):
        o_r, o_i, u1, u2 = S(), S(), S(), S()
        vtt(u1, a_r, b_r, ALU.mult)
        vtt(u2, a_i, b_i, ALU.mult)
        vtt(o_r, u1, u2, ALU.subtract)
        vtt(u1, a_r, b_i, ALU.mult)
        vtt(u2, a_i, b_r, ALU.mult)
        vtt(o_i, u1, u2, ALU.add)
        return o_r, o_i

    lre = (par[:, 0, :], "par")
    lim = (par[:, 1, :], "par")
    ldt = (par[:, 2, :], "par")
    dt_ = S()
    A(lambda e: e.activation(dt_[0], ldt[0], AF.Exp), ["par"], [dt_[1]])
    zr, zi = S(), S()
    vtt(zr, lre, dt_, ALU.mult)
    vtt(zi, lim, dt_, ALU.mult)
    mag = S()
    A(lambda e: e.activation(mag[0], zr[0], AF.Exp), [zr[1]], [mag[1]])

    sn0, cs0 = S(), S()
    A(lambda e: e.activation(sn0[0], zi[0], AF.Sin, scale=1.0 / 16), [zi[1]], [sn0[1]])
    A(lambda e: e.activation(cs0[0], zi[0], AF.Sin, scale=1.0 / 16, bias=hpic[:, :]), [zi[1], "hpic"], [cs0[1]])
    sn1, cs1 = sn0, cs0
    for _ in range(4):
        cs1, sn1 = cmul(cs1, sn1, cs1, sn1)
    L = [None] * 5
    L1r, L1i = S(), S()
    vtt(L1r, mag, cs1, ALU.mult)
    vtt(L1i, mag, sn1, ALU.mult)
    L[1] = (L1r, L1i)
    L[2] = cmul(L1r, L1i, L1r, L1i)
    L[3] = cmul(L[2][0], L[2][1], L1r, L1i)
    L[4] = cmul(L[2][0], L[2][1], L[2][0], L[2][1])
    one_, zero_ = S(), S()
    V(lambda e: e.memset(one_[0], 1.0), [], [one_[1]])
    V(lambda e: e.memset(zero_[0], 0.0), [], [zero_[1]])
    L[0] = (one_, zero_)
    am1 = S()
    V(lambda e: e.tensor_scalar_add(am1[0], L1r[0], -1.0), [L1r[1]], [am1[1]])
    nli = S()
    V(lambda e: e.tensor_scalar_mul(nli[0], lim[0], -1.0), ["par"], [nli[1]])
    num_r, num_i = cmul(am1, L1i, lre, nli)
    den, u3 = S(), S()
    vtt(den, lre, lre, ALU.mult)
    vtt(u3, lim, lim, ALU.mult)
    vtt(den, den, u3, ALU.add)
    kr, ki = S(), S()
    V(lambda e: e.reciprocal(den[0], den[0]), [den[1]], [den[1]])
    vtt(kr, num_r, den, ALU.mult)
    vtt(ki, num_i, den, ALU.mult)
    M = [cmul(L[3 - j][0], L[3 - j][1], kr, ki) for j in range(4)]
    r2 = S()
    vtt(r2, L[4][0], L[4][0], ALU.mult)
    vtt(u3, L[4][1], L[4][1], ALU.mult)
    vtt(r2, r2, u3, ALU.add)
    A(lambda e: e.activation(R4[:, :], r2[0], AF.Sqrt), [r2[1]], ["R4"])
    ir4 = S()
    V(lambda e: e.reciprocal(ir4[0], R4[:, :]), ["R4"], [ir4[1]])
    V(lambda e: e.tensor_tensor(E4r[:, :], L[4][0][0], ir4[0], ALU.mult), [L[4][0][1], ir4[1]], ["E4"])
    V(lambda e: e.tensor_tensor(E4i[:, :], L[4][1][0], ir4[0], ALU.mult), [L[4][1][1], ir4[1]], ["E4"])
    V(lambda e: e.memset(cosT[:, :, 0:1], 1.0), [], ["tab"])
    V(lambda e: e.memset(sinT[:, :, 0:1], 0.0), [], ["tab"])
    wr, wi = (E4r[:, :], "E4"), (E4i[:, :], "E4")
    n = 1
    while n < 128:
        wrb = wr[0].unsqueeze(2).to_broadcast([128, 16, n])
        wib = wi[0].unsqueeze(2).to_broadcast([128, 16, n])
        c0, s0 = cosT[:, :, 0:n], sinT[:, :, 0:n]
        c1, s1 = cosT[:, :, n:2 * n], sinT[:, :, n:2 * n]
        ta, tb_ = t1[:, :, 0:n], t2[:, :, 0:n]
        V(lambda e, c0=c0, wrb=wrb, ta=ta: e.tensor_tensor(ta, c0, wrb, ALU.mult), ["tab", wr[1]], ["t1"])
        V(lambda e, s0=s0, wib=wib, tb_=tb_: e.tensor_tensor(tb_, s0, wib, ALU.mult), ["tab", wi[1]], ["t2"])
        V(lambda e, c1=c1, ta=ta, tb_=tb_: e.tensor_tensor(c1, ta, tb_, ALU.subtract), ["t1", "t2"], ["tab"])
        V(lambda e, c0=c0, wib=wib, ta=ta: e.tensor_tensor(ta, c0, wib, ALU.mult), ["tab", wi[1]], ["t1"])
        V(lambda e, s0=s0, wrb=wrb, tb_=tb_: e.tensor_tensor(tb_, s0, wrb, ALU.mult), ["tab", wr[1]], ["t2"])
        V(lambda e, s1=s1, ta=ta, tb_=tb_: e.tensor_tensor(s1, ta, tb_, ALU.add), ["t1", "t2"], ["tab"])
        if n < 64:
            wr, wi = cmul(wr, wi, wr, wi)
        n *= 2
    V(lambda e: e.tensor_copy(Rtab[:, :, :], R4[:, :].unsqueeze(2).to_broadcast([128, 16, 128])), ["R4"], ["Rtab"])
    V(lambda e: e.memset(Rtab[:, :, 0:1], 0.0), ["Rtab"], ["Rtab"])

    def norm_stats(src, srcname, nsub):
        for a in range(nsub):
            A(lambda e, a=a: e.activation(junk[:, :], src[:, a, :], AF.Square, accum_out=ss[:, a:a + 1]),
              [f"{srcname}{a}"], ["junk", f"ss{a}"])
            A(lambda e, a=a: e.activation(rstd[:, a:a + 1], ss[:, a:a + 1], AF.Sqrt, scale=1.0 / D, bias=epsc[:, :]),
              [f"ss{a}", "epsc"], [f"rstd{a}"])
            V(lambda e, a=a: e.reciprocal(rstd[:, a:a + 1], rstd[:, a:a + 1]), [f"rstd{a}"], [f"rstd{a}"])
            if a % 2 == 0:
                A(lambda e, a=a: e.activation(hT[:, a, :], src[:, a, :], AF.Copy, scale=rstd[:, a:a + 1]),
                  [f"{srcname}{a}", f"rstd{a}"], [f"hT{a}"])
            else:
                V(lambda e, a=a: e.tensor_scalar_mul(hT[:, a, :], src[:, a, :], rstd[:, a:a + 1]),
                  [f"{srcname}{a}", f"rstd{a}"], [f"hT{a}"])

    def norm_transposes(nsub, col_g, dst_fm, dstname, ncols_tok):
        for kc in range(8):
            bi = kc % 2
            bank = psT[bi]

            def fn(e, kc=kc, bank=bank):
                ins = None
                for a in range(nsub):
                    ins = e.transpose(bank[:, a * 128:(a + 1) * 128], hT[:, a, kc * 128:(kc + 1) * 128], ident[:, :])
                return ins
            P.op("pe", fn, r=[f"hT{a}" for a in range(nsub)] + ["ident"], w=[f"psT{bi}"])
            if kc % 2 == 0:
                A(lambda e, kc=kc, bank=bank: e.activation(dst_fm[:, kc, 0:ncols_tok], bank[:, 0:ncols_tok], AF.Copy, scale=cp(col_g + kc)),
                  [f"psT{bi}", "colp"], [f"{dstname}{kc}"])
            else:
                V(lambda e, kc=kc, bank=bank: e.tensor_scalar_mul(dst_fm[:, kc, 0:ncols_tok], bank[:, 0:ncols_tok], cp(col_g + kc)),
                  [f"psT{bi}", "colp"], [f"{dstname}{kc}"])


    def rmsnorm_to_hT(src, srcname, nsub, col_g, dst_fm, dstname, ncols_tok):
        norm_stats(src, srcname, nsub)
        norm_transposes(nsub, col_g, dst_fm, dstname, ncols_tok)

    rmsnorm_to_hT(memx, "x0a", 2, C_GMEM, memfm, "hfm", 256)
    for half in range(2):
        P.op("pool", lambda e, half=half: e.dma_start(out=slots[half][:, :], in_=w32_d[HOST_SLABS.index(SL_WK + half)]),
             w=[f"slot{half}"], dsem=f"d_slot{half}")
        for mt in range(4):
            bi = 2 + mt % 2
            pairs = [(slots[half][:, kc * 512 + mt * 128: kc * 512 + (mt + 1) * 128], memfm[:, kc, 0:256]) for kc in range(8)]
            mm_group(psb[bi][:, 0:256], pairs, [f"slot{half}"] + [f"hfm{kc}" for kc in range(8)], [f"ps{bi}"])
            A(lambda e, half=half, mt=mt, bi=bi: e.copy(Kfm[:, half * 4 + mt, :], psb[bi][:, 0:256]), [f"ps{bi}"], ["Kfm"])
    for half in range(2):
        P.op("pool", lambda e, half=half: e.dma_start(out=slots[2 + half][:, :], in_=w32_d[HOST_SLABS.index(SL_WV + half)]),
             w=[f"slot{2 + half}"], dsem=f"d_slot{2 + half}")
        for mc in range(2):
            bi = 4 + mc
            pairs = [(memfm[:, kc, mc * 128:(mc + 1) * 128], slots[2 + half][:, kc * 512:(kc + 1) * 512]) for kc in range(8)]
            mm_group(psb[bi][:, :], pairs, [f"slot{2 + half}"] + [f"hfm{kc}" for kc in range(8)], [f"ps{bi}"])
            V(lambda e, half=half, mc=mc, bi=bi: e.tensor_copy(Vtok[:, mc, half * 512:(half + 1) * 512], psb[bi][:, :]),
              [f"ps{bi}"], ["Vtok"])

    bre, bim, cre, cim = (bcc[:, i_, :, :] for i_ in range(4))
    c1, c2, c3, c4 = (cs[:, i_, :, :] for i_ in range(4))
    mpos = mask8[:, 0, :].unsqueeze(1).unsqueeze(3)
    mneg = mask8[:, 1, :].unsqueeze(1).unsqueeze(3)

    def bc16(ap):
        return ap.unsqueeze(2).to_broadcast([128, 16, 16])
    for j in range(4):
        mr, mi = bc16(M[j][0][0]), bc16(M[j][1][0])
        mrn, min_ = M[j][0][1], M[j][1][1]
        V(lambda e, mr=mr: e.tensor_tensor(c1, bre, mr, ALU.mult), ["bcc", mrn], ["c1"])
        V(lambda e, mi=mi: e.tensor_tensor(c2, bim, mi, ALU.mult), ["bcc", min_], ["c2"])
        V(lambda e, mi=mi: e.tensor_tensor(c3, bre, mi, ALU.mult), ["bcc", min_], ["c3"])
        V(lambda e, mr=mr: e.tensor_tensor(c4, bim, mr, ALU.mult), ["bcc", mrn], ["c4"])
        V(lambda e: e.tensor_tensor(c1, c1, c2, ALU.subtract), ["c1", "c2"], ["c1"])
        V(lambda e: e.tensor_tensor(c3, c3, c4, ALU.add), ["c3", "c4"], ["c3"])
        for ri, cc, cn in ((0, c1, "c1"), (1, c3, "c3")):
            o = NB[:, j, ri, :, :].rearrange("p a (g h) -> p a g h", g=8)
            V(lambda e, o=o, cc=cc: e.tensor_tensor(o, cc.unsqueeze(2).to_broadcast([128, 16, 8, 16]),
                                                    mpos.to_broadcast([128, 16, 8, 16]), ALU.mult), [cn, "mask8"], [f"NB{j}"])
    tix = {"i": 0}

    def w_section(k):
        for sg in range(8):
            s_, ri = sg % 4, sg // 4
            bank = psT[tix["i"] % 2]
            bname = f"psT{tix['i'] % 2}"
            tix["i"] += 1

            def fn(e, s_=s_, ri=ri, bank=bank):
                ins = None
                for j in range(4):
                    ins = e.transpose(bank[:, j * 128:(j + 1) * 128], NB[:, j, ri, k * 4 + s_, :], ident[:, :])
                return ins
            P.op("pe", fn, r=[f"NB{j}" for j in range(4)] + ["ident"], w=[bname])
            dst = Wst[:, 0, sg, :, :]
            src = bank[:, 0:512].rearrange("p (j c) -> p j c", j=4)
            A(lambda e, dst=dst, src=src: e.copy(dst, src), [bname], ["Wst"])
        P.op("sp", lambda e: e.dma_start(out=wscr[SL_S5W + k], in_=Wst[:, 0, :, :, :].rearrange("p a b c -> p (a b c)")),
             r=["Wst"], w=[f"scr{SL_S5W + k}"], dsem=f"d_w{k}")
    def gen_V(m):
        lr, li = bc16(L[m][0][0]), bc16(L[m][1][0])
        lrn, lin = L[m][0][1], L[m][1][1]
        vb = Vm[:, m % 2, :, :, :]
        on = f"Vm{m % 2}"
        V(lambda e: e.tensor_tensor(c1, cre, lr, ALU.mult), ["bcc", lrn], ["c1"])
        V(lambda e: e.tensor_tensor(c2, cim, li, ALU.mult), ["bcc", lin], ["c2"])
        V(lambda e: e.tensor_tensor(c3, cre, li, ALU.mult), ["bcc", lin], ["c3"])
        V(lambda e: e.tensor_tensor(c4, cim, lr, ALU.mult), ["bcc", lrn], ["c4"])
        V(lambda e: e.tensor_tensor(c1, c1, c2, ALU.subtract), ["c1", "c2"], ["c1"])
        V(lambda e: e.tensor_tensor(c3, c3, c4, ALU.add), ["c3", "c4"], ["c3"])
        for k in range(4):
            for half, cc, cn, mk in ((0, c1, "c1", mpos), (1, c3, "c3", mneg)):
                o = vb[:, k, half * 4:(half + 1) * 4, :].rearrange("p s (g h) -> p s g h", g=8)
                V(lambda e, o=o, cc=cc, mk=mk, k=k: e.tensor_tensor(o, cc[:, k * 4:(k + 1) * 4, :].unsqueeze(2).to_broadcast([128, 4, 8, 16]),
                                                                   mk.to_broadcast([128, 4, 8, 16]), ALU.mult), [cn, "mask8"], [on])
        if m >= 1:
            for k in range(4):
                dst = wscr[SL_S5V + k].rearrange("p (s m c) -> p s m c", s=8, m=4)[:, :, m - 1, :]
                P.op("sp", lambda e, k=k, dst=dst: e.dma_start(out=dst, in_=vb[:, k, :, :]),
                     r=[on], w=[f"scr{SL_S5V + k}"], dsem=f"d_v{k}_{m}")
        if m <= 3:
            for k in range(4):
                pairs = [(NB[:, 3, sg // 4, k * 4 + sg % 4, :], vb[:, k, sg, :]) for sg in range(8)]
                mm_group(psb[k][:, m * 128:(m + 1) * 128], pairs, ["NB3", on], [f"ps{k}"])
    for m in range(5):
        gen_V(m)
        if m < 4:
            w_section(m)
    for k in range(4):
        V(lambda e, k=k: e.scalar_tensor_tensor(Kst[:, k, 0, :], identf[:, :], cp(C_DS5 + k), psb[k][:, 0:128],
                                                op0=ALU.mult, op1=ALU.add), [f"ps{k}", "identf", "colp"], ["Kst"])
        V(lambda e, k=k: e.tensor_copy(Kst[:, k, 1:4, :], psb[k][:, 128:512].rearrange("p (j c) -> p j c", j=3)),
          [f"ps{k}"], ["Kst"])
    P.op("sp", lambda e: e.dma_start(out=wscr[SL_KG][:, 0:2048], in_=hT[:, 2:4, :].rearrange("p a b -> p (a b)")),
         r=["Kst"], w=[f"scr{SL_KG}"], dsem="d_kst")

    for hi, sl in enumerate(HOST_SLABS):
        if sl >= SL_WK:
            continue
        if sl == SL_KG:
            P.op("pool", lambda e, hi=hi, sl=sl: e.dma_start(out=wscr[sl][:, 2048:4096], in_=w32_d[hi][:, 2048:4096]),
                 w=[f"scr{sl}g"], dsem=f"d_cast{hi}")
        else:
            P.op("pool", lambda e, hi=hi, sl=sl: e.dma_start(out=wscr[sl], in_=w32_d[hi]),
                 w=[f"scr{sl}"], dsem=f"d_cast{hi}")

    if dbg == "pro":
        pass
    P.barrier(skip="d_cast")
    pst.close()

    xres[1] = sb("xres1", [128, 4, D])
    xlh = sb("xlh", [128, 4, 3 + TB])
    ffnhalo = sb("ffnhalo", [128, 2, 44, 2])
    G(lambda e: e.memset(xlh[:, :, 0:3], 0.0), [], [f"xl{i}" for i in range(4)])
    G(lambda e: e.memset(ffnhalo[:, :, :, :], 0.0), [], ["ffnhalo0", "ffnhalo1"])
    gl = sb("gl", [128, 4, TB], BF16)
    zq = sb("zq", [128, 4, 512])
    S_sb = sb("S_sb", [128, 4, 8, 130], BF16)
    ymix = sb("ymix", [128, 8, TB], BF16)
    ltb = sb("lt", [128, 6, TB + 4])
    lt = ltb[:, :, 0:TB]
    xcb = sb("xcb", [128, 2, TB], BF16)
    hlb = sb("hl", [128, 2, TB + 4])
    hl = hlb[:, :, 0:TB]
    gated = sb("gated", [128, 22, TB], BF16)
    q_sb = ymix
    pT = gated[:, 0:8, :].rearrange("p (h m) t -> p h m t", h=4)
    o_sb = gated[:, 8:16, :]
    u_sb = gated[:, 16:20, :]
    ygelu = gated[:, 0:4, :]
    rden = hl
    zbuf = ltb[:, 0:2, 0:TB + 2]
    cv = lt[:, 2:4, :]
    cg = lt[:, 4:6, :]
    gg = cg
    qinit = sb("qinit", [128, 3, 16])
    hc = sb("hc", [128, 44, 2])
    hc2 = sb("hc2", [128, 44])
    G(lambda e: e.memset(S_sb[:, :, :, :], 0.0), [], ["S_sb0", "S_sb1", "S_sb2", "S_sb3"])

    rr = {"ps": 0, "ps6": 0}

    def nextbank():
        b = rr["ps"] % 4
        rr["ps"] += 1
        return b

    def nextbank6():
        b = rr["ps6"] % 6
        rr["ps6"] += 1
        return b

    hnames = [f"hfm{kc}" for kc in range(8)]

    def add_resid(xb, a, nh, bi):
        xs = xres[xb][:, a, nh * 512:(nh + 1) * 512]
        V(lambda e: e.tensor_tensor(xs, xs, psb[bi][:, :], ALU.add), [f"ps{bi}", f"x{xb}a{a}"], [f"x{xb}a{a}"])

    def proj_tok(lhs_tile, lhs_names, slab_ids, xb, bankfn):
        for nh in range(2):
            st, sn = load_slab(slab_ids[nh])
            for a in range(4):
                bi = bankfn()
                pairs = [(lhs_tile[:, kc, a * 128:(a + 1) * 128], st[:, kc * 512:(kc + 1) * 512]) for kc in range(8)]
                mm_group(psb[bi][:, :], pairs, [sn] + lhs_names, [f"ps{bi}"])
                add_resid(xb, a, nh, bi)

    def fm_proj(st, sn, mt, evac, bankfn=None):
        bi = (bankfn or nextbank)()
        pairs = [(st[:, kc * 512 + mt * 128: kc * 512 + (mt + 1) * 128], hfm[:, kc, :]) for kc in range(8)]
        mm_group(psb[bi][:, :], pairs, [sn] + hnames, [f"ps{bi}"])
        evac(bi)

    u4 = u_sb[:, :, :].rearrange("p k (c j) -> p k c j", j=4)

    def v4(ap):
        return ap.rearrange("p (s c) -> p s c", s=4)

    def s5_qinit():
        q0, q1, q2 = qinit[:, 0, :], qinit[:, 1, :], qinit[:, 2, :]
        cr, ci = s5carry_r[:, :], s5carry_i[:, :]
        er, ei, r4 = E4r[:, :], E4i[:, :], R4[:, :]
        V(lambda e: e.tensor_tensor(q0, cr, er, ALU.mult), ["s5c", "E4"], ["qi0"])
        V(lambda e: e.tensor_tensor(q2, ci, ei, ALU.mult), ["s5c", "E4"], ["qi2"])
        V(lambda e: e.tensor_tensor(q1, cr, ei, ALU.mult), ["s5c", "E4"], ["qi1"])
        V(lambda e: e.tensor_tensor(q0, q0, q2, ALU.subtract), ["qi0", "qi2"], ["qi0"])
        V(lambda e: e.tensor_tensor(q2, ci, er, ALU.mult), ["s5c", "E4", "qi0"], ["qi2"])
        V(lambda e: e.tensor_tensor(q0, q0, r4, ALU.mult), ["qi0", "R4"], ["qi0"])
        V(lambda e: e.tensor_tensor(q1, q1, q2, ALU.add), ["qi1", "qi2"], ["qi1"])
        V(lambda e: e.tensor_tensor(q1, q1, r4, ALU.mult), ["qi1", "R4"], ["qi1"])

    def s5_B_pe(k):
        st, sn = load_slab(SL_S5W + k)
        for half, bi in ((0, 4), (1, 5)):
            for s_ in range(4):
                sg = half * 4 + s_
                pairs = [(st[:, (sg * 4 + j) * 128:(sg * 4 + j + 1) * 128], u4[:, k, :, j]) for j in range(4)]
                mm_group(psb[bi][:, s_ * 128:(s_ + 1) * 128], pairs, [sn, f"gated{16 + k}"], [f"ps{bi}"])

    def s5_B_steps(k):
        cT = cosT[:, k * 4:(k + 1) * 4, :]
        sT = sinT[:, k * 4:(k + 1) * 4, :]
        Xr = v4(psb[4][:, :])
        Xi = v4(psb[5][:, :])
        Zr, Zi, Qr, Qi = (zq[:, i, :] for i in range(4))
        tmp = lt[:, 0, :]
        ks = slice(k * 4, (k + 1) * 4)
        q0, q1 = qinit[:, 0, ks], qinit[:, 1, ks]
        cr, ci = s5carry_r[:, ks], s5carry_i[:, ks]
        Rk = Rtab[:, k * 4:(k + 1) * 4, :].rearrange("p s c -> p (s c)")
        Sre = S_sb[:, k, 0:4, 1:129]
        Sim = S_sb[:, k, 4:8, 1:129]
        sname = f"S_sb{k}"
        ta, tb_ = lt[:, 2, :], lt[:, 3, :]
        TT = lambda o, a, b, op, r, w: (lambda: V(lambda e: e.tensor_tensor(o, a, b, op), r, w))
        return [
            TT(v4(Zr), Xr, cT, ALU.mult, ["ps4", "tab"], ["zq0"]),
            TT(v4(tmp), Xi, sT, ALU.mult, ["ps5", "tab"], ["lt0"]),
            TT(Zr, Zr, tmp, ALU.add, ["zq0", "lt0"], ["zq0"]),
            TT(v4(Zi), Xi, cT, ALU.mult, ["ps5", "tab"], ["zq1"]),
            TT(v4(tmp), Xr, sT, ALU.mult, ["ps4", "tab"], ["lt0"]),
            TT(Zi, Zi, tmp, ALU.subtract, ["zq1", "lt0"], ["zq1"]),
            TT(v4(Zr)[:, :, 0], v4(Zr)[:, :, 0], q0, ALU.add, ["zq0", "qi0"], ["zq0"]),
            TT(v4(Zi)[:, :, 0], v4(Zi)[:, :, 0], q1, ALU.add, ["zq1", "qi1"], ["zq1"]),
            lambda: V(lambda e: e.tensor_tensor_scan(Qr, Rk, Zr, 0.0, op0=ALU.mult, op1=ALU.add), ["zq0", "Rtab"], ["zq2"]),
            lambda: V(lambda e: e.tensor_tensor_scan(Qi, Rk, Zi, 0.0, op0=ALU.mult, op1=ALU.add), ["zq1", "Rtab"], ["zq3"]),
            lambda: V(lambda e: e.tensor_copy(S_sb[:, k, :, 0:1], S_sb[:, k, :, 128:129]), [sname], [sname]),
            TT(v4(ta), v4(Qr), cT, ALU.mult, ["zq2", "tab"], ["lt2"]),
            TT(v4(tb_), v4(Qi), sT, ALU.mult, ["zq3", "tab"], ["lt3"]),
            TT(Sre, v4(ta), v4(tb_), ALU.subtract, ["lt2", "lt3"], [sname]),
            TT(cr, v4(ta)[:, :, 127], v4(tb_)[:, :, 127], ALU.subtract, ["lt2", "lt3"], ["s5c"]),
            TT(v4(ta), v4(Qr), sT, ALU.mult, ["zq2", "tab"], ["lt2"]),
            TT(v4(tb_), v4(Qi), cT, ALU.mult, ["zq3", "tab"], ["lt3"]),
            TT(Sim, v4(ta), v4(tb_), ALU.add, ["lt2", "lt3"], [sname]),
            TT(ci, v4(ta)[:, :, 127], v4(tb_)[:, :, 127], ALU.add, ["lt2", "lt3"], ["s5c"]),
        ]

    def lru_steps(mt):
        xl = xlh[:, mt, :]
        xn = f"xl{mt}"
        (xc, xcn), (ra, ran), (i_, in_) = [(lt[:, r, :], f"lt{r}") for r in (4, 5, 1)]
        m_, mn = xc, xcn
        xb_ = xcb[:, 0, :]
        xbn = "xcb0"
        hb = hl[:, mt % 2, :]
        hn = f"hl{mt % 2}"
        bk = {}
        st = []
        st.append(lambda: A(lambda e: e.activation(xc, xl[:, 3:3 + TB], AF.Identity, scale=cp(C_LCW + 3 * 4 + mt), bias=cp(C_LCB + mt)),
                            [xn, "colp"], [xcn]))

        def tap(t_):
            return lambda: V(lambda e: e.scalar_tensor_tensor(xc, xl[:, t_:t_ + TB], cp(C_LCW + t_ * 4 + mt), xc, op0=ALU.mult, op1=ALU.add),
                             [xn, xcn], [xcn])
        for t_ in range(3):
            st.append(tap(t_))
        st.append(lambda: G(lambda e: e.tensor_copy(xl[:, 0:3], xl[:, TB:TB + 3]), [xn], [xn]))
        st.append(lambda: A(lambda e: e.copy(xb_, xc), [xcn], [xbn]))

        def gates():
            bk["b1"], bk["b2"] = nextbank(), nextbank()
            mm_group(psb[bk["b1"]][:, :], [(lruw[:, mt, :], xb_)], ["lruw", xbn], [f"ps{bk['b1']}"])
            mm_group(psb[bk["b2"]][:, :], [(lruw[:, 4 + mt, :], xb_)], ["lruw", xbn], [f"ps{bk['b2']}"])
        st.append(gates)
        st.append(lambda: A(lambda e: e.activation(ra, psb[bk["b1"]][:, :], AF.Tanh, scale=0.5, bias=hbias[:, mt:mt + 1]),
                            [f"ps{bk['b1']}", "hbias"], [ran]))
        st.append(lambda: A(lambda e: e.activation(i_, psb[bk["b2"]][:, :], AF.Tanh, scale=0.5, bias=hbias[:, 4 + mt:5 + mt]),
                            [f"ps{bk['b2']}", "hbias"], [in_]))
        st.append(lambda: V(lambda e: e.scalar_tensor_tensor(i_, i_, 1.0, xc, op0=ALU.add, op1=ALU.mult), [in_, xcn], [in_]))
        st.append(lambda: A(lambda e: e.activation(ra, ra, AF.Exp, scale=cnegh[:, mt:mt + 1], bias=cnegh[:, mt:mt + 1]), [ran, "cnegh"], [ran]))
        st.append(lambda: A(lambda e: e.activation(m_, ra, AF.Square), [ran, in_], [mn]))
        st.append(lambda: A(lambda e: e.activation(m_, m_, AF.Sqrt, scale=-1.0, bias=onec[:, :]), [mn, "onec"], [mn]))
        st.append(lambda: V(lambda e: e.scalar_tensor_tensor(i_, i_, 0.5, m_, op0=ALU.mult, op1=ALU.mult), [in_, mn], [in_]))
        st.append(lambda: V(lambda e: e.tensor_tensor_scan(hb, ra, i_, lrucarry[:, mt:mt + 1], op0=ALU.mult, op1=ALU.add),
                            [ran, in_, "lrucarry"], [hn]))
        st.append(lambda: V(lambda e: e.tensor_copy(lrucarry[:, mt:mt + 1], hb[:, TB - 1:TB]), [hn], ["lrucarry"]))
        prod = lambda: V(lambda e: e.tensor_tensor(ymix[:, 4 + mt, :], hb, gl[:, mt, :], ALU.mult), [hn, f"gl{mt}"], [f"ymix{4 + mt}"])
        return st, prod

    def zip_steps(a, b):
        for i in range(max(len(a), len(b))):
            if i < len(a):
                a[i]()
            if i < len(b):
                b[i]()

    def s5_D(k, stK, snK):
        st, sn = load_slab(SL_S5V + k)
        bi = nextbank()
        yv = psb[bi][:, :].rearrange("p (c i) -> p c i", i=4)
        for i in range(4):
            pairs = [(st[:, (sg * 4 + i) * 128:(sg * 4 + i + 1) * 128], S_sb[:, k, sg, 0:128]) for sg in range(8)]
            pairs += [(stK[:, (k * 4 + (i - j)) * 128:(k * 4 + (i - j) + 1) * 128], u4[:, k, :, j]) for j in range(i + 1)]
            mm_group(yv[:, :, i], pairs, [sn, snK, f"S_sb{k}", f"gated{16 + k}"], [f"ps{bi}"])
        A(lambda e: e.activation(ygelu[:, k, :], psb[bi][:, :], AF.Gelu), [f"ps{bi}"], [f"gated{k}"])

    def glu_tile(mt, stK, snK):
        bi = nextbank()
        pairs = [(stK[:, 2048 + kc * 512 + mt * 128: 2048 + kc * 512 + (mt + 1) * 128], ygelu[:, kc, :]) for kc in range(4)]
        mm_group(psb[bi][:, :], pairs, [snK] + [f"gated{k}" for k in range(4)], [f"ps{bi}"])
        gt = lt[:, 2 + mt % 2, :]
        gn = f"lt{2 + mt % 2}"
        A(lambda e: e.activation(gt, psb[bi][:, :], AF.Sigmoid, bias=cp(C_BGLU + mt)), [f"ps{bi}", "colp"], [gn])
        V(lambda e: e.tensor_tensor(ymix[:, mt, :], ygelu[:, mt, :], gt, ALU.mult), [f"gated{mt}", gn], [f"ymix{mt}"])

    def attn_head(hd):
        def sc(mc):
            bi = nextbank6()
            pairs = [(Kfm[:, hd * 2 + c2, mc * 128:(mc + 1) * 128], q_sb[:, hd * 2 + c2, :]) for c2 in range(2)]
            mm_group(psb[bi][:, :], pairs, ["Kfm", f"ymix{hd * 2}", f"ymix{hd * 2 + 1}"], [f"ps{bi}"])
            A(lambda e: e.activation(pT[:, hd, mc, :], psb[bi][:, :], AF.Exp), [f"ps{bi}"], [f"gated{2 * hd}", f"gated{2 * hd + 1}"])
        sc(0)
        sc(1)

    def attn_tail(hd):
        bd = nextbank6()
        mm_group(psb[bd][:, :], [(onesb[:, :], pT[:, hd, mc, :]) for mc in range(2)], ["onesb", f"gated{2 * hd}", f"gated{2 * hd + 1}"], [f"ps{bd}"])
        rd = rden[:, hd % 2, :]
        rn = f"hl{hd % 2}"
        V(lambda e: e.reciprocal(rd, psb[bd][:, :]), [f"ps{bd}"], [rn])

        def pv(j):
            bi = nextbank6()
            pairs = [(Vtok[:, mc, hd * 256 + j * 128: hd * 256 + (j + 1) * 128], pT[:, hd, mc, :]) for mc in range(2)]
            mm_group(psb[bi][:, :], pairs, ["Vtok", f"gated{2 * hd}", f"gated{2 * hd + 1}"], [f"ps{bi}"])
            V(lambda e: e.tensor_tensor(o_sb[:, hd * 2 + j, :], psb[bi][:, :], rd, ALU.mult), [f"ps{bi}", rn], [f"gated{8 + hd * 2 + j}"])
        pv(0)
        pv(1)

    def ffn_tile_mm(st, sn, sa, tt, isg, tb):
        vt = 2 * sa + tt
        ch = vt + 22 * isg
        col = isg * 256 + tt * 128
        bi = nextbank6()
        pairs = [(st[:, kc * 512 + col: kc * 512 + col + 128], hfm[:, kc, :]) for kc in range(8)]
        mm_group(psb[bi][:, :], pairs, [sn] + hnames, [f"ps{bi}"])
        ps = psb[bi]
        pn = f"ps{bi}"
        dst = (cv if isg == 0 else cg)[:, tt, :]
        dn = f"lt{2 + 2 * isg + tt}"
        hold = ffnhalo[:, tb % 2, ch, :]
        hnew = ffnhalo[:, (tb + 1) % 2, ch, :]
        wcol = lambda t_: cp(C_FCW + t_ * 44 + ch)
        A(lambda e: e.activation(dst, ps[:, :], AF.Identity, scale=wcol(2), bias=cp(C_FCB + ch)), [pn, "colp"], [dn])
        A(lambda e: e.copy(hnew, ps[:, TB - 2:TB]), [pn], [f"ffnhalo{(tb + 1) % 2}"])
        return ps, pn, dst, dn, wcol, hold, f"ffnhalo{tb % 2}"

    def ffn_tap(t_, ps, pn, dst, dn, wcol, hold, hn):
        sh = 2 - t_
        V(lambda e: e.scalar_tensor_tensor(dst[:, sh:TB], ps[:, 0:TB - sh], wcol(t_), dst[:, sh:TB], op0=ALU.mult, op1=ALU.add),
          [pn, dn, "colp"], [dn])

    def ffn_halo_prep(tb):
        hold = ffnhalo[:, tb % 2, :, :]
        hn = f"ffnhalo{tb % 2}"
        W0, W1 = cp(C_FCW, 44), cp(C_FCW + 44, 44)
        V(lambda e: e.tensor_tensor(hc[:, :, 1], hold[:, :, 1], W0, ALU.mult), [hn, "colp"], ["hc"])
        V(lambda e: e.tensor_tensor(hc[:, :, 0], hold[:, :, 0], W0, ALU.mult), [hn, "colp"], ["hc"])
        V(lambda e: e.tensor_tensor(hc2[:, :], hold[:, :, 1], W1, ALU.mult), [hn, "colp"], ["hc2"])
        V(lambda e: e.tensor_tensor(hc[:, :, 0], hc[:, :, 0], hc2[:, :], ALU.add), ["hc", "hc2"], ["hc"])

    def ffn_halo_add(ch, dst, dn):
        G(lambda e: e.tensor_tensor(dst[:, 0:2], dst[:, 0:2], hc[:, ch, :], ALU.add), ["hc", dn], [dn])

    def ffn_pair(st, sn, sa, tt, tb, prev_tail):
        vt = 2 * sa + tt
        tiles = [ffn_tile_mm(st, sn, sa, tt, 0, tb), ffn_tile_mm(st, sn, sa, tt, 1, tb)]
        if prev_tail is not None:
            prev_tail()
        for t_ in (1, 0):
            for tl in tiles:
                ffn_tap(t_, *tl)
        for isg, tl in enumerate(tiles):
            ffn_halo_add(vt + 22 * isg, tl[2], tl[3])

        def tail():
            A(lambda e: e.activation(cg[:, tt, :], cg[:, tt, :], AF.Gelu), [f"lt{4 + tt}"], [f"lt{4 + tt}"])
            G(lambda e: e.tensor_tensor(gated[:, vt, :], cg[:, tt, :], cv[:, tt, :], ALU.mult), [f"lt{4 + tt}", f"lt{2 + tt}"], [f"gated{vt}"])
        return tail

    def down_group(xb, nh, sg3, kc0, nk, a):
        bi = a
        st, sn = down_group.cur

        def fn(e):
            ins = None
            for kk in range(nk):
                ins = e.matmul(psb[bi][:, :], gated[:, kc0 + kk, a * 128:(a + 1) * 128], st[:, kk * 512:(kk + 1) * 512],
                               start=(sg3 == 0 and kk == 0), stop=(sg3 == 2 and kk == nk - 1))
            return ins
        P.op("pe", fn, r=[sn] + [f"gated{kc0 + kk}" for kk in range(nk)], w=[f"ps{bi}"])

    def final_norm(xb, a):
        xa = xres[xb][:, a, :]
        sa_, ra_ = ss[:, 4 + a:5 + a], rstd[:, 4 + a:5 + a]
        xn = f"x{xb}a{a}"
        A(lambda e: e.activation(junk[:, :], xa, AF.Square, accum_out=sa_), [xn], ["junk", f"ssf{a}"])
        A(lambda e: e.activation(ra_, sa_, AF.Sqrt, scale=1.0 / D, bias=epsc[:, :]), [f"ssf{a}", "epsc"], [f"rstdf{a}"])
        V(lambda e: e.reciprocal(ra_, ra_), [f"rstdf{a}"], [f"rstdf{a}"])
        V(lambda e: e.scalar_tensor_tensor(xa, xa, ra_, gfin[:, :], op0=ALU.mult, op1=ALU.mult), [xn, f"rstdf{a}", "gfin"], [xn])

    x_t = x_d.rearrange("(b a p) d -> b p a d", a=4, p=128)
    out_t = out_d.rearrange("(b a p) d -> b p a d", a=4, p=128)

    def load_x(tb):
        xb = tb % 2
        P.op("sp", lambda e: e.dma_start(out=xres[xb][:, :, :], in_=x_t[tb]),
             w=[f"x{xb}a{a}" for a in range(4)], dsem=f"d_x{xb}")

    def store_out(tb):
        xb = tb % 2
        P.op("sp", lambda e: e.dma_start(out=out_t[tb], in_=xres[xb][:, :, :]),
             r=[f"x{xb}a{a}" for a in range(4)], dsem=f"d_o{xb}")

    def win_evac(grp, mt):
        def ev(bi):
            if grp == 0:
                A(lambda e: e.copy(u_sb[:, mt, :], psb[bi][:, :]), [f"ps{bi}"], [f"gated{16 + mt}"])
            elif grp == 1:
                A(lambda e: e.copy(xlh[:, mt, 3:3 + TB], psb[bi][:, :]), [f"ps{bi}"], [f"xl{mt}"])
            else:
                A(lambda e: e.activation(gl[:, mt, :], psb[bi][:, :], AF.Gelu), [f"ps{bi}"], [f"gl{mt}"])
        return ev

    def q_evac(half, mt):
        def ev(bi):
            A(lambda e: e.activation(q_sb[:, half * 4 + mt, :], psb[bi][:, :], AF.Copy, scale=0.0625),
              [f"ps{bi}"], [f"ymix{half * 4 + mt}"])
        return ev

    def finish_block(tb):
        if stage >= 6:
            for a in range(4):
                final_norm(tb % 2, a)

    def block_body(tb):
        xb = tb % 2
        if tb == 0:
            norm_stats(xres[xb], f"x{xb}a", 4)
        norm_transposes(4, C_G1, hfm, "hfm", TB)
        st0, sn0 = load_slab(SL_WIN)
        for mt in range(4):
            fm_proj(st0, sn0, mt, win_evac(0, mt))
        if tb > 0:
            finish_block(tb - 1)
        s5_qinit()
        wslabs = {}

        def win_tiles(grp, mts):
            if grp not in wslabs:
                wslabs[grp] = load_slab(SL_WIN + grp)
            st, sn = wslabs[grp]
            for mt in mts:
                fm_proj(st, sn, mt, win_evac(grp, mt))
        s5_B_pe(0)
        zip_steps(s5_B_steps(0), [])
        win_tiles(1, [0, 1])
        prods = {}
        for k in range(1, 4):
            s5_B_pe(k)
            lst, prods[k - 1] = lru_steps(k - 1)
            zip_steps(s5_B_steps(k), lst)
            if k == 1:
                win_tiles(1, [2, 3])
            elif k == 2:
                win_tiles(2, [0, 1])
                prods[0]()
                prods[1]()
            else:
                win_tiles(2, [2, 3])
        if stage < 2:
            return
        stK, snK = load_slab(SL_KG)
        lst, prods[3] = lru_steps(3)
        cuts = [0, 6, 11, 15, len(lst)]
        for k in range(4):
            s5_D(k, stK, snK)
            for f in lst[cuts[k]:cuts[k + 1]]:
                f()
        prods[2]()
        prods[3]()
        for mt in range(4):
            glu_tile(mt, stK, snK)
        if tb > 0:
            store_out(tb - 1)
        if tb + 1 < nblk:
            load_x(tb + 1)
        if stage < 3:
            return
        proj_tok(ymix, [f"ymix{i}" for i in range(8)], [SL_WOUT, SL_WOUT + 1], xb, nextbank6)
        if stage < 4:
            return
        rmsnorm_to_hT(xres[xb], f"x{xb}a", 4, C_G2, hfm, "hfm", TB)
        for half in range(2):
            st, sn = load_slab(SL_WQ + half)
            for mt in range(4):
                fm_proj(st, sn, mt, q_evac(half, mt), nextbank6)
        for hd in range(4):
            attn_head(hd)
            attn_tail(hd)
        proj_tok(o_sb, [f"gated{8 + i}" for i in range(8)], [SL_WO, SL_WO + 1], xb, nextbank6)
        if stage < 5:
            return
        rmsnorm_to_hT(xres[xb], f"x{xb}a", 4, C_G3, hfm, "hfm", TB)
        ffn_halo_prep(tb)
        ptail = None
        for sa in range(11):
            st, sn = load_slab(SL_UP + sa)
            for tt in range(2):
                ptail = ffn_pair(st, sn, sa, tt, tb, ptail)
        ptail()
        if tb + 1 < nblk and stage >= 6:
            norm_stats(xres[1 - xb], f"x{1 - xb}a", 4)
        for nh in range(2):
            kc0 = 0
            for sg3 in range(3):
                nk = 8 if sg3 < 2 else 6
                down_group.cur = load_slab(SL_DN + nh * 3 + sg3)
                for a in range(4):
                    down_group(xb, nh, sg3, kc0, nk, a)
                kc0 += nk
            for a in range(4):
                add_resid(xb, a, nh, a)

    load_x(0)
    for tb in range(nblk):
        block_body(tb)
    finish_block(nblk - 1)
    store_out(nblk - 1)

    if dbg is not None:
        P.barrier()
        for i_, (ap_, a_, b_) in enumerate(dbg(locals())):
            P.op("pool", lambda e, i_=i_, ap_=ap_, a_=a_, b_=b_: e.dma_start(
                out=dbg_d[i_][:, 0:a_ * b_].rearrange("p (a b) -> p a b", a=a_), in_=ap_), dsem=f"d_dbg{i_}")

    P.barrier()
    with nc.Block() as block:
        P.emit(block)
    es.close()
    return nc


def _slab_kc(w, c0, ncols=512, kc0=0, nkc=8):
    out = np.zeros((128, 8, ncols), np.float32)
    K = w.shape[0]
    for kc in range(nkc):
        r0 = (kc0 + kc) * 128
        if r0 >= K:
            break
        out[:, kc, :] = w[r0:r0 + 128, c0:c0 + ncols]
    return out.reshape(128, 8 * ncols)


def host_layout(inp):
    f = lambda k: np.asarray(inp[k], np.float32)
    w_in, w_out = f("w_in")[0], f("w_out")[0]
    wq, wk, wv, wo = f("xa_w_q")[0], f("xa_w_k")[0], f("xa_w_v")[0], f("xa_w_o")[0]
    wup, wdn, glu = f("ffn_w_up")[0], f("ffn_w_down")[0], f("s5_w_glu")[0]
    slabs = {}
    for g in range(3):
        slabs[SL_WIN + g] = _slab_kc(w_in, g * 512)
    kg = np.zeros((128, 4096), np.float32)
    kg[:, 2048:] = _slab_kc(glu, 0, 512, 0, 4)[:, :2048]
    slabs[SL_KG] = kg
    for h in range(2):
        slabs[SL_WOUT + h] = _slab_kc(w_out, h * 512)
        slabs[SL_WQ + h] = _slab_kc(wq, h * 512)
        slabs[SL_WO + h] = _slab_kc(wo, h * 512)
        slabs[SL_WK + h] = _slab_kc(wk, h * 512)
        slabs[SL_WV + h] = _slab_kc(wv, h * 512)
    for sa in range(11):
        cols = np.concatenate([np.arange(sa * 256, sa * 256 + 256), 2816 + np.arange(sa * 256, sa * 256 + 256)])
        slabs[SL_UP + sa] = _slab_kc(wup[:, cols], 0)
    for nh in range(2):
        for g3 in range(3):
            slabs[SL_DN + nh * 3 + g3] = _slab_kc(wdn, nh * 512, 512, g3 * 8, 8 if g3 < 2 else 6)
    w32 = np.stack([slabs[s] for s in HOST_SLABS]).astype(np.float32)

    colp = np.zeros((128, NCOL), np.float32)

    def putcols(c0, vec):
        v = np.asarray(vec, np.float32).reshape(-1, 128).T
        colp[:, c0:c0 + v.shape[1]] = v
    putcols(C_G1, f("ln_mix_g")[0])
    putcols(C_G2, f("ln_xa_g")[0])
    putcols(C_G3, f("ln_ffn_g")[0])
    putcols(C_GMEM, f("mem_norm_g"))
    putcols(C_BGLU, f("s5_b_glu")[0])
    lcw = f("lru_conv_w")[0]
    for t in range(4):
        putcols(C_LCW + t * 4, lcw[t])
    putcols(C_LCB, f("lru_conv_b")[0])
    putcols(C_BA, f("lru_b_a")[0].reshape(-1))
    putcols(C_BX, f("lru_b_x")[0].reshape(-1))
    putcols(C_LAM, f("lru_lam")[0].reshape(-1))
    fcw = f("ffn_conv_w")[0]
    for t in range(3):
        putcols(C_FCW + t * 44, fcw[t])
    putcols(C_FCB, f("ffn_conv_b")[0])
    putcols(C_DS5, f("s5_d")[0].reshape(-1))
    gfin = np.ascontiguousarray(np.broadcast_to(f("final_norm_g")[None, :], (128, D)))

    def gq(arr):
        a = arr.reshape(4, 8, 4, 16)
        return np.ascontiguousarray(a.transpose(1, 3, 0, 2).reshape(128, 16))
    lre, lim = f("s5_lam_re")[0], f("s5_lam_im")[0]
    ldt = np.broadcast_to(f("s5_log_dt")[0][:, None], (32, 64))
    s5par = np.stack([gq(lre), gq(lim), gq(ldt)], axis=1).astype(np.float32)

    def cb(b):
        a = b.reshape(4, 8, 4, 16, 16)
        return np.ascontiguousarray(a.transpose(1, 3, 0, 2, 4).reshape(128, 16, 16))
    bcc = np.stack([cb(f("s5_b_re")[0]), cb(f("s5_b_im")[0]),
                    cb(np.ascontiguousarray(f("s5_c_re")[0].transpose(0, 2, 1))),
                    cb(np.ascontiguousarray(f("s5_c_im")[0].transpose(0, 2, 1)))], axis=1).astype(np.float32)
    mask8 = np.zeros((128, 2, 8), np.float32)
    for p_ in range(128):
        mask8[p_, 0, p_ // 16] = 1.0
        mask8[p_, 1, p_ // 16] = -1.0
    lruw = np.zeros((128, 8, 128), np.float32)
    wa, wx = f("lru_w_a")[0], f("lru_w_x")[0]
    for mt in range(4):
        for hh in range(2):
            lruw[hh * 64:(hh + 1) * 64, mt, hh * 64:(hh + 1) * 64] = wa[2 * mt + hh]
            lruw[hh * 64:(hh + 1) * 64, 4 + mt, hh * 64:(hh + 1) * 64] = wx[2 * mt + hh]
    shared = {"w32": w32, "colp": colp, "gfin": gfin, "s5par": s5par, "bcc": bcc, "mask8": mask8, "lruw": lruw,
              "ident": np.eye(128, dtype=np.float32).astype(ml_dtypes.bfloat16),
              "identf": np.eye(128, dtype=np.float32)}
    return shared


def kernel(**inputs):
    shared = host_layout(inputs)
    x = np.asarray(inputs["x"], np.float32)
    mem = np.asarray(inputs["mem"], np.float32)
    nc = build(SEQ // TB)
    in_maps = []
    for c in range(8):
        m = dict(shared)
        m["x"] = np.ascontiguousarray(x[c])
        m["mem"] = np.ascontiguousarray(mem[c])
        in_maps.append(m)
    res = run_bass_kernel_spmd(nc, in_maps, core_ids=list(range(8)))
    return np.stack([np.asarray(r["out"], np.float32) for r in res.results], axis=0)
```

```python
import math
from contextlib import ExitStack
import numpy as np
import ml_dtypes
import concourse.bass as bass
import concourse.mybir as mybir
from concourse.bass_utils import run_bass_kernel_spmd

F32 = mybir.dt.float32
BF16 = mybir.dt.bfloat16
ALU = mybir.AluOpType
AF = mybir.ActivationFunctionType

SEQ = 4096
TB = 512
D = 1024
NS = 5
PI = math.pi

SL_WIN = 0
SL_S5W = 3
SL_S5V = 7
SL_KG = 11
SL_WOUT = 12
SL_WQ = 14
SL_WO = 16
SL_UP = 18
SL_DN = 29
SL_WK = 35
SL_WV = 37
NSLAB = 39
HOST_SLABS = [0, 1, 2, 11, 12, 13, 14, 15, 16, 17] + list(range(18, 35)) + [35, 36, 37, 38]

C_G1, C_G2, C_G3, C_GMEM = 0, 8, 16, 24
C_BGLU = 32
C_LCW = 36
C_LCB = 52
C_BA = 56
C_BX = 60
C_LAM = 64
C_FCW = 68
C_FCB = 200
C_DS5 = 244
NCOL = 248


class Prog:
    ENG = ("pe", "act", "dve", "pool", "sp")

    def __init__(self, nc, es):
        self.nc = nc
        self.es = es
        self.q = {e: [] for e in self.ENG}
        self.cnt = {}
        self.sems = {}
        self.seen = {e: {} for e in self.ENG}
        self.lastw = {}
        self.readers = {}

    def sem(self, key):
        if key not in self.sems:
            self.sems[key] = self.es.enter_context(self.nc.semaphore("s_" + key))
            self.cnt[key] = 0
        return self.sems[key]

    def op(self, eng, fn, r=(), w=(), dsem=None):
        deps = {}

        def add(tok):
            if tok is None:
                return
            k, v = tok
            if deps.get(k, 0) < v:
                deps[k] = v
        for b in r:
            add(self.lastw.get(b))
        for b in w:
            add(self.lastw.get(b))
            for k, v in self.readers.get(b, {}).items():
                add((k, v))
        waits = []
        for k, v in deps.items():
            if eng == "pe" and k == "pe":
                continue
            if self.seen[eng].get(k, 0) >= v:
                continue
            self.seen[eng][k] = v
            waits.append((k, v))
        if dsem is None:
            key, inc = eng, 1
        else:
            key, inc = dsem, 16
        self.sem(key)
        self.cnt[key] += inc
        tok = (key, self.cnt[key])
        self.q[eng].append((waits, fn, key, inc))
        for b in r:
            self.readers.setdefault(b, {})[key] = tok[1]
        for b in w:
            self.lastw[b] = tok
            self.readers[b] = {}
        return tok

    def barrier(self, skip=None):
        for e in self.ENG:
            waits = []
            for k, v in self.cnt.items():
                if skip is not None and k.startswith(skip):
                    continue
                if v > 0 and self.seen[e].get(k, 0) < v:
                    self.seen[e][k] = v
                    waits.append((k, v))
            if waits:
                self.q[e].append((waits, None, None, 0))

    def emit(self, block):
        sems = self.sems

        def run(eng, lst):
            for waits, fn, key, inc in lst:
                for k, v in waits:
                    eng.wait_ge(sems[k], v)
                if fn is not None:
                    ins = fn(eng)
                    ins.then_inc(sems[key], inc)

        @block.tensor
        def _(e):
            run(e, self.q["pe"])

        @block.scalar
        def _(e):
            run(e, self.q["act"])

        @block.vector
        def _(e):
            run(e, self.q["dve"])

        @block.gpsimd
        def _(e):
            run(e, self.q["pool"])

        @block.sync
        def _(e):
            run(e, self.q["sp"])


def build(nblk=8, stage=6, dbg=None, pro_only=False):
    nc = bass.Bass("TRN2", target_bir_lowering=False)
    ntok = nblk * TB

    def din(name, shape, dt=F32):
        return nc.dram_tensor(name, list(shape), dt, kind="ExternalInput").ap()
    x_d = din("x", [ntok, D])
    mem_d = din("mem", [256, D])
    w32_d = din("w32", [len(HOST_SLABS), 128, 4096])
    colp_d = din("colp", [128, NCOL])
    gfin_d = din("gfin", [128, D])
    s5par_d = din("s5par", [128, 3, 16])
    bcc_d = din("bcc", [128, 4, 16, 16])
    mask8_d = din("mask8", [128, 2, 8])
    lruw_d = din("lruw", [128, 8, 128])
    ident_d = din("ident", [128, 128], BF16)
    identf_d = din("identf", [128, 128])
    out_d = nc.dram_tensor("out", [ntok, D], F32, kind="ExternalOutput").ap()
    wscr = nc.dram_tensor("wscr", [NSLAB, 128, 4096], BF16, kind="Internal").ap()
    dbg_d = None
    if dbg is not None:
        dbg_d = nc.dram_tensor("dbg", [16, 128, 4096], F32, kind="ExternalOutput").ap()

    es = ExitStack()
    P = Prog(nc, es)

    def sb(name, shape, dt=F32, stack=es):
        return stack.enter_context(nc.sbuf_tensor("sb_" + name, list(shape), dt))

    colp = sb("colp", [128, NCOL])
    gfin = sb("gfin", [128, D])
    ident = sb("ident", [128, 128], BF16)
    onesb = sb("onesb", [128, 128], BF16)
    lruw = sb("lruw", [128, 8, 128], BF16)
    cneg = sb("cneg", [128, 4])
    epsc = sb("epsc", [128, 1])
    hpic = sb("hpic", [128, 1])
    onec = sb("onec", [128, 1])
    hbias = sb("hbias", [128, 12])
    cnegh = sb("cnegh", [128, 4])
    cosT = sb("cosT", [128, 16, 128])
    sinT = sb("sinT", [128, 16, 128])
    Rtab = sb("Rtab", [128, 16, 128])
    R4 = sb("R4", [128, 16])
    E4r = sb("E4r", [128, 16])
    E4i = sb("E4i", [128, 16])
    Kfm = sb("Kfm", [128, 8, 256], BF16)
    Vtok = sb("Vtok", [128, 2, D], BF16)
    slots = [sb(f"slot{i}", [128, 4096], BF16) for i in range(NS)]
    xres = [sb("xres0", [128, 4, D]), None]
    hT = sb("hT", [128, 4, D], BF16)
    hfm = sb("hfm", [128, 8, TB], BF16)
    ss = sb("ss", [128, 8])
    rstd = sb("rstd", [128, 8])
    junk = sb("junk", [128, D], BF16)
    s5carry_r = sb("s5cr", [128, 16])
    s5carry_i = sb("s5ci", [128, 16])
    lrucarry = sb("lrucarry", [128, 4])
    psb = [es.enter_context(nc.psum_tensor(f"pp{i}", [128, 512], F32)) for i in range(6)]
    psT = [es.enter_context(nc.psum_tensor(f"ppT{i}", [128, 1024], BF16)) for i in range(2)]

    cp = lambda c, n=1: colp[:, c:c + n]

    slab_state = {"n": 0}

    def load_slab(idx, eng="sp"):
        i = slab_state["n"] % NS
        slab_state["n"] += 1
        name = f"slot{i}"
        P.op(eng, lambda e, i=i, idx=idx: e.dma_start(out=slots[i][:, :], in_=wscr[idx]),
             r=[f"scr{idx}", f"scr{idx}g"], w=[name], dsem=f"d_slot{i}")
        return slots[i], name

    def mm_group(out_ap, pairs, r, w):
        n = len(pairs)

        def fn(e):
            ins = None
            for j, (l, rr) in enumerate(pairs):
                ins = e.matmul(out_ap, l, rr, start=(j == 0), stop=(j == n - 1))
            return ins
        P.op("pe", fn, r=r, w=w)

    def V(fn, r, w):
        P.op("dve", fn, r=r, w=w)

    def A(fn, r, w):
        P.op("act", fn, r=r, w=w)

    def G(fn, r, w):
        P.op("pool", fn, r=r, w=w)

    pst = ExitStack()
    par = sb("par", [128, 3, 16], stack=pst)
    identf = sb("identf", [128, 128], stack=pst)
    bcc = sb("bcc", [128, 4, 16, 16], stack=pst)
    mask8 = sb("mask8", [128, 2, 8], stack=pst)
    cs = sb("cs", [128, 4, 16, 16], stack=pst)
    t1 = sb("t1", [128, 16, 128], stack=pst)
    lruw32 = t1[:, 0:8, :]
    t2 = sb("t2", [128, 16, 128], stack=pst)
    NB = sb("NB", [128, 4, 2, 16, 128], BF16, stack=pst)
    Vm = sb("Vm", [128, 2, 4, 8, 128], BF16, stack=pst)
    Wst = sb("Wst", [128, 1, 8, 4, 128], BF16, stack=pst)
    Kst = hT[:, 2:4, :].rearrange("p a (b c) -> p (a b) c", c=128).rearrange("p (k t) c -> p k t c", k=4)
    sm = xres[0][:, 2:4, :].rearrange("p a (b c) -> p (a b) c", c=16)
    memx = xres[0]
    memfm = hfm

    def ld(dst, src, name, eng="sp"):
        P.op(eng, lambda e: e.dma_start(out=dst, in_=src), w=[name], dsem="d_" + name)
    ld(colp[:, :], colp_d, "colp")
    ld(gfin[:, :], gfin_d, "gfin")
    ld(ident[:, :], ident_d, "ident")
    ld(identf[:, :], identf_d, "identf")
    ld(par[:, :, :], s5par_d, "par")
    ld(bcc[:, :, :, :], bcc_d, "bcc")
    ld(mask8[:, :, :], mask8_d, "mask8")
    ld(lruw32, lruw_d, "t1")
    P.op("sp", lambda e: e.dma_start(out=memx[:, 0:2, :], in_=mem_d.rearrange("(a p) d -> p a d", p=128)),
         w=["x0a0", "x0a1"], dsem="d_x0")

    G(lambda e: e.memset(onesb[:, :], 1.0), [], ["onesb"])
    G(lambda e: e.memset(epsc[:, :], 1e-6), [], ["epsc"])
    G(lambda e: e.memset(hpic[:, :], PI / 2), [], ["hpic"])
    G(lambda e: e.memset(onec[:, :], 1.0), [], ["onec"])
    G(lambda e: e.memset(lrucarry[:, :], 0.0), [], ["lrucarry"])
    G(lambda e: e.memset(s5carry_r[:, :], 0.0), [], ["s5c"])
    G(lambda e: e.memset(s5carry_i[:, :], 0.0), [], ["s5c"])
    V(lambda e: e.tensor_copy(lruw[:, :, :], lruw32), ["t1"], ["lruw"])

    A(lambda e: e.activation(cneg[:, :], cp(C_LAM, 4), AF.Exp, scale=-1.0), ["colp"], ["cneg"])
    A(lambda e: e.activation(cneg[:, :], cneg[:, :], AF.Ln, bias=onec[:, :]), ["cneg", "onec"], ["cneg"])
    V(lambda e: e.tensor_scalar_mul(cneg[:, :], cneg[:, :], -8.0), ["cneg"], ["cneg"])
    V(lambda e: e.tensor_scalar_mul(cnegh[:, :], cneg[:, :], 0.5), ["cneg"], ["cnegh"])
    V(lambda e: e.tensor_scalar_mul(hbias[:, 0:4], cp(C_BA, 4), 0.5), ["colp"], ["hbias"])
    V(lambda e: e.tensor_scalar_mul(hbias[:, 4:8], cp(C_BX, 4), 0.5), ["colp"], ["hbias"])
    V(lambda e: e.tensor_scalar_mul(hbias[:, 8:12], cp(C_BGLU, 4), 0.5), ["colp"], ["hbias"])

    smn = {"i": 0}

    def S(name=None):
        i = smn["i"]
        smn["i"] += 1
        return sm[:, i, :], f"sm{i}"

    def vtt(o, a, b, op):
        (oa, on), (aa, an), (ba, bn) = o, a, b
        V(lambda e: e.tensor_tensor(oa, aa, ba, op), [an, bn], [on])

    def cmul(a_r, a_i, b_r, b_i):
        o_r, o_i, u1, u2 = S(), S(), S(), S()
        vtt(u1, a_r, b_r, ALU.mult)
        vtt(u2, a_i, b_i, ALU.mult)
        vtt(o_r, u1, u2, ALU.subtract)
        vtt(u1, a_r, b_i, ALU.mult)
        vtt(u2, a_i, b_r, ALU.mult)
        vtt(o_i, u1, u2, ALU.add)
        return o_r, o_i

    lre = (par[:, 0, :], "par")
    lim = (par[:, 1, :], "par")
    ldt = (par[:, 2, :], "par")
    dt_ = S()
    A(lambda e: e.activation(dt_[0], ldt[0], AF.Exp), ["par"], [dt_[1]])
    zr, zi = S(), S()
    vtt(zr, lre, dt_, ALU.mult)
    vtt(zi, lim, dt_, ALU.mult)
    mag = S()
    A(lambda e: e.activation(mag[0], zr[0], AF.Exp), [zr[1]], [mag[1]])

    sn0, cs0 = S(), S()
    A(lambda e: e.activation(sn0[0], zi[0], AF.Sin, scale=1.0 / 16), [zi[1]], [sn0[1]])
    A(lambda e: e.activation(cs0[0], zi[0], AF.Sin, scale=1.0 / 16, bias=hpic[:, :]), [zi[1], "hpic"], [cs0[1]])
    sn1, cs1 = sn0, cs0
    for _ in range(4):
        cs1, sn1 = cmul(cs1, sn1, cs1, sn1)
    L = [None] * 5
    L1r, L1i = S(), S()
    vtt(L1r, mag, cs1, ALU.mult)
    vtt(L1i, mag, sn1, ALU.mult)
    L[1] = (L1r, L1i)
    L[2] = cmul(L1r, L1i, L1r, L1i)
    L[3] = cmul(L[2][0], L[2][1], L1r, L1i)
    L[4] = cmul(L[2][0], L[2][1], L[2][0], L[2][1])
    one_, zero_ = S(), S()
    V(lambda e: e.memset(one_[0], 1.0), [], [one_[1]])
    V(lambda e: e.memset(zero_[0], 0.0), [], [zero_[1]])
    L[0] = (one_, zero_)
    am1 = S()
    V(lambda e: e.tensor_scalar_add(am1[0], L1r[0], -1.0), [L1r[1]], [am1[1]])
    nli = S()
    V(lambda e: e.tensor_scalar_mul(nli[0], lim[0], -1.0), ["par"], [nli[1]])
    num_r, num_i = cmul(am1, L1i, lre, nli)
    den, u3 = S(), S()
    vtt(den, lre, lre, ALU.mult)
    vtt(u3, lim, lim, ALU.mult)
    vtt(den, den, u3, ALU.add)
    kr, ki = S(), S()
    V(lambda e: e.reciprocal(den[0], den[0]), [den[1]], [den[1]])
    vtt(kr, num_r, den, ALU.mult)
    vtt(ki, num_i, den, ALU.mult)
    M = [cmul(L[3 - j][0], L[3 - j][1], kr, ki) for j in range(4)]
    r2 = S()
    vtt(r2, L[4][0], L[4][0], ALU.mult)
    vtt(u3, L[4][1], L[4][1], ALU.mult)
    vtt(r2, r2, u3, ALU.add)
    A(lambda e: e.activation(R4[:, :], r2[0], AF.Sqrt), [r2[1]], ["R4"])
    ir4 = S()
    V(lambda e: e.reciprocal(ir4[0], R4[:, :]), ["R4"], [ir4[1]])
    V(lambda e: e.tensor_tensor(E4r[:, :], L[4][0][0], ir4[0], ALU.mult), [L[4][0][1], ir4[1]], ["E4"])
    V(lambda e: e.tensor_tensor(E4i[:, :], L[4][1][0], ir4[0], ALU.mult), [L[4][1][1], ir4[1]], ["E4"])
    V(lambda e: e.memset(cosT[:, :, 0:1], 1.0), [], ["tab"])
    V(lambda e: e.memset(sinT[:, :, 0:1], 0.0), [], ["tab"])
    wr, wi = (E4r[:, :], "E4"), (E4i[:, :], "E4")
    n = 1
    while n < 128:
        wrb = wr[0].unsqueeze(2).to_broadcast([128, 16, n])
        wib = wi[0].unsqueeze(2).to_broadcast([128, 16, n])
        c0, s0 = cosT[:, :, 0:n], sinT[:, :, 0:n]
        c1, s1 = cosT[:, :, n:2 * n], sinT[:, :, n:2 * n]
        ta, tb_ = t1[:, :, 0:n], t2[:, :, 0:n]
        V(lambda e, c0=c0, wrb=wrb, ta=ta: e.tensor_tensor(ta, c0, wrb, ALU.mult), ["tab", wr[1]], ["t1"])
        V(lambda e, s0=s0, wib=wib, tb_=tb_: e.tensor_tensor(tb_, s0, wib, ALU.mult), ["tab", wi[1]], ["t2"])
        V(lambda e, c1=c1, ta=ta, tb_=tb_: e.tensor_tensor(c1, ta, tb_, ALU.subtract), ["t1", "t2"], ["tab"])
        V(lambda e, c0=c0, wib=wib, ta=ta: e.tensor_tensor(ta, c0, wib, ALU.mult), ["tab", wi[1]], ["t1"])
        V(lambda e, s0=s0, wrb=wrb, tb_=tb_: e.tensor_tensor(tb_, s0, wrb, ALU.mult), ["tab", wr[1]], ["t2"])
        V(lambda e, s1=s1, ta=ta, tb_=tb_: e.tensor_tensor(s1, ta, tb_, ALU.add), ["t1", "t2"], ["tab"])
        if n < 64:
            wr, wi = cmul(wr, wi, wr, wi)
        n *= 2
    V(lambda e: e.tensor_copy(Rtab[:, :, :], R4[:, :].unsqueeze(2).to_broadcast([128, 16, 128])), ["R4"], ["Rtab"])
    V(lambda e: e.memset(Rtab[:, :, 0:1], 0.0), ["Rtab"], ["Rtab"])

    def norm_stats(src, srcname, nsub):
        for a in range(nsub):
            A(lambda e, a=a: e.activation(junk[:, :], src[:, a, :], AF.Square, accum_out=ss[:, a:a + 1]),
              [f"{srcname}{a}"], ["junk", f"ss{a}"])
            A(lambda e, a=a: e.activation(rstd[:, a:a + 1], ss[:, a:a + 1], AF.Sqrt, scale=1.0 / D, bias=epsc[:, :]),
              [f"ss{a}", "epsc"], [f"rstd{a}"])
            V(lambda e, a=a: e.reciprocal(rstd[:, a:a + 1], rstd[:, a:a + 1]), [f"rstd{a}"], [f"rstd{a}"])
            if a % 2 == 0:
                A(lambda e, a=a: e.activation(hT[:, a, :], src[:, a, :], AF.Copy, scale=rstd[:, a:a + 1]),
                  [f"{srcname}{a}", f"rstd{a}"], [f"hT{a}"])
            else:
                V(lambda e, a=a: e.tensor_scalar_mul(hT[:, a, :], src[:, a, :], rstd[:, a:a + 1]),
                  [f"{srcname}{a}", f"rstd{a}"], [f"hT{a}"])

    def norm_transposes(nsub, col_g, dst_fm, dstname, ncols_tok):
        for kc in range(8):
            bi = kc % 2
            bank = psT[bi]

            def fn(e, kc=kc, bank=bank):
                ins = None
                for a in range(nsub):
                    ins = e.transpose(bank[:, a * 128:(a + 1) * 128], hT[:, a, kc * 128:(kc + 1) * 128], ident[:, :])
                return ins
            P.op("pe", fn, r=[f"hT{a}" for a in range(nsub)] + ["ident"], w=[f"psT{bi}"])
            if kc % 2 == 0:
                A(lambda e, kc=kc, bank=bank: e.activation(dst_fm[:, kc, 0:ncols_tok], bank[:, 0:ncols_tok], AF.Copy, scale=cp(col_g + kc)),
                  [f"psT{bi}", "colp"], [f"{dstname}{kc}"])
            else:
                V(lambda e, kc=kc, bank=bank: e.tensor_scalar_mul(dst_fm[:, kc, 0:ncols_tok], bank[:, 0:ncols_tok], cp(col_g + kc)),
                  [f"psT{bi}", "colp"], [f"{dstname}{kc}"])


    def rmsnorm_to_hT(src, srcname, nsub, col_g, dst_fm, dstname, ncols_tok):
        norm_stats(src, srcname, nsub)
        norm_transposes(nsub, col_g, dst_fm, dstname, ncols_tok)

    rmsnorm_to_hT(memx, "x0a", 2, C_GMEM, memfm, "hfm", 256)
    for half in range(2):
        P.op("pool", lambda e, half=half: e.dma_start(out=slots[half][:, :], in_=w32_d[HOST_SLABS.index(SL_WK + half)]),
             w=[f"slot{half}"], dsem=f"d_slot{half}")
        for mt in range(4):
            bi = 2 + mt % 2
            pairs = [(slots[half][:, kc * 512 + mt * 128: kc * 512 + (mt + 1) * 128], memfm[:, kc, 0:256]) for kc in range(8)]
            mm_group(psb[bi][:, 0:256], pairs, [f"slot{half}"] + [f"hfm{kc}" for kc in range(8)], [f"ps{bi}"])
            A(lambda e, half=half, mt=mt, bi=bi: e.copy(Kfm[:, half * 4 + mt, :], psb[bi][:, 0:256]), [f"ps{bi}"], ["Kfm"])
    for half in range(2):
        P.op("pool", lambda e, half=half: e.dma_start(out=slots[2 + half][:, :], in_=w32_d[HOST_SLABS.index(SL_WV + half)]),
             w=[f"slot{2 + half}"], dsem=f"d_slot{2 + half}")
        for mc in range(2):
            bi = 4 + mc
            pairs = [(memfm[:, kc, mc * 128:(mc + 1) * 128], slots[2 + half][:, kc * 512:(kc + 1) * 512]) for kc in range(8)]
            mm_group(psb[bi][:, :], pairs, [f"slot{2 + half}"] + [f"hfm{kc}" for kc in range(8)], [f"ps{bi}"])
            V(lambda e, half=half, mc=mc, bi=bi: e.tensor_copy(Vtok[:, mc, half * 512:(half + 1) * 512], psb[bi][:, :]),
              [f"ps{bi}"], ["Vtok"])

    bre, bim, cre, cim = (bcc[:, i_, :, :] for i_ in range(4))
    c1, c2, c3, c4 = (cs[:, i_, :, :] for i_ in range(4))
    mpos = mask8[:, 0, :].unsqueeze(1).unsqueeze(3)
    mneg = mask8[:, 1, :].unsqueeze(1).unsqueeze(3)

    def bc16(ap):
        return ap.unsqueeze(2).to_broadcast([128, 16, 16])
    for j in range(4):
        mr, mi = bc16(M[j][0][0]), bc16(M[j][1][0])
        mrn, min_ = M[j][0][1], M[j][1][1]
        V(lambda e, mr=mr: e.tensor_tensor(c1, bre, mr, ALU.mult), ["bcc", mrn], ["c1"])
        V(lambda e, mi=mi: e.tensor_tensor(c2, bim, mi, ALU.mult), ["bcc", min_], ["c2"])
        V(lambda e, mi=mi: e.tensor_tensor(c3, bre, mi, ALU.mult), ["bcc", min_], ["c3"])
        V(lambda e, mr=mr: e.tensor_tensor(c4, bim, mr, ALU.mult), ["bcc", mrn], ["c4"])
        V(lambda e: e.tensor_tensor(c1, c1, c2, ALU.subtract), ["c1", "c2"], ["c1"])
        V(lambda e: e.tensor_tensor(c3, c3, c4, ALU.add), ["c3", "c4"], ["c3"])
        for ri, cc, cn in ((0, c1, "c1"), (1, c3, "c3")):
            o = NB[:, j, ri, :, :].rearrange("p a (g h) -> p a g h", g=8)
            V(lambda e, o=o, cc=cc: e.tensor_tensor(o, cc.unsqueeze(2).to_broadcast([128, 16, 8, 16]),
                                                    mpos.to_broadcast([128, 16, 8, 16]), ALU.mult), [cn, "mask8"], [f"NB{j}"])
    tix = {"i": 0}

    def w_section(k):
        for sg in range(8):
            s_, ri = sg % 4, sg // 4
            bank = psT[tix["i"] % 2]
            bname = f"psT{tix['i'] % 2}"
            tix["i"] += 1

            def fn(e, s_=s_, ri=ri, bank=bank):
                ins = None
                for j in range(4):
                    ins = e.transpose(bank[:, j * 128:(j + 1) * 128], NB[:, j, ri, k * 4 + s_, :], ident[:, :])
                return ins
            P.op("pe", fn, r=[f"NB{j}" for j in range(4)] + ["ident"], w=[bname])
            dst = Wst[:, 0, sg, :, :]
            src = bank[:, 0:512].rearrange("p (j c) -> p j c", j=4)
            A(lambda e, dst=dst, src=src: e.copy(dst, src), [bname], ["Wst"])
        P.op("sp", lambda e: e.dma_start(out=wscr[SL_S5W + k], in_=Wst[:, 0, :, :, :].rearrange("p a b c -> p (a b c)")),
             r=["Wst"], w=[f"scr{SL_S5W + k}"], dsem=f"d_w{k}")
    def gen_V(m):
        lr, li = bc16(L[m][0][0]), bc16(L[m][1][0])
        lrn, lin = L[m][0][1], L[m][1][1]
        vb = Vm[:, m % 2, :, :, :]
        on = f"Vm{m % 2}"
        V(lambda e: e.tensor_tensor(c1, cre, lr, ALU.mult), ["bcc", lrn], ["c1"])
        V(lambda e: e.tensor_tensor(c2, cim, li, ALU.mult), ["bcc", lin], ["c2"])
        V(lambda e: e.tensor_tensor(c3, cre, li, ALU.mult), ["bcc", lin], ["c3"])
        V(lambda e: e.tensor_tensor(c4, cim, lr, ALU.mult), ["bcc", lrn], ["c4"])
        V(lambda e: e.tensor_tensor(c1, c1, c2, ALU.subtract), ["c1", "c2"], ["c1"])
        V(lambda e: e.tensor_tensor(c3, c3, c4, ALU.add), ["c3", "c4"], ["c3"])
        for k in range(4):
            for half, cc, cn, mk in ((0, c1, "c1", mpos), (1, c3, "c3", mneg)):
                o = vb[:, k, half * 4:(half + 1) * 4, :].rearrange("p s (g h) -> p s g h", g=8)
                V(lambda e, o=o, cc=cc, mk=mk, k=k: e.tensor_tensor(o, cc[:, k * 4:(k + 1) * 4, :].unsqueeze(2).to_broadcast([128, 4, 8, 16]),
                                                                   mk.to_broadcast([128, 4, 8, 16]), ALU.mult), [cn, "mask8"], [on])
        if m >= 1:
            for k in range(4):
                dst = wscr[SL_S5V + k].rearrange("p (s m c) -> p s m c", s=8, m=4)[:, :, m - 1, :]
                P.op("sp", lambda e, k=k, dst=dst: e.dma_start(out=dst, in_=vb[:, k, :, :]),
                     r=[on], w=[f"scr{SL_S5V + k}"], dsem=f"d_v{k}_{m}")
        if m <= 3:
            for k in range(4):
                pairs = [(NB[:, 3, sg // 4, k * 4 + sg % 4, :], vb[:, k, sg, :]) for sg in range(8)]
                mm_group(psb[k][:, m * 128:(m + 1) * 128], pairs, ["NB3", on], [f"ps{k}"])
    for m in range(5):
        gen_V(m)
        if m < 4:
            w_section(m)
    for k in range(4):
        V(lambda e, k=k: e.scalar_tensor_tensor(Kst[:, k, 0, :], identf[:, :], cp(C_DS5 + k), psb[k][:, 0:128],
                                                op0=ALU.mult, op1=ALU.add), [f"ps{k}", "identf", "colp"], ["Kst"])
        V(lambda e, k=k: e.tensor_copy(Kst[:, k, 1:4, :], psb[k][:, 128:512].rearrange("p (j c) -> p j c", j=3)),
          [f"ps{k}"], ["Kst"])
    P.op("sp", lambda e: e.dma_start(out=wscr[SL_KG][:, 0:2048], in_=hT[:, 2:4, :].rearrange("p a b -> p (a b)")),
         r=["Kst"], w=[f"scr{SL_KG}"], dsem="d_kst")

    for hi, sl in enumerate(HOST_SLABS):
        if sl >= SL_WK:
            continue
        if sl == SL_KG:
            P.op("pool", lambda e, hi=hi, sl=sl: e.dma_start(out=wscr[sl][:, 2048:4096], in_=w32_d[hi][:, 2048:4096]),
                 w=[f"scr{sl}g"], dsem=f"d_cast{hi}")
        else:
            P.op("pool", lambda e, hi=hi, sl=sl: e.dma_start(out=wscr[sl], in_=w32_d[hi]),
                 w=[f"scr{sl}"], dsem=f"d_cast{hi}")

    if dbg == "pro":
        pass
    P.barrier(skip="d_cast")
    pst.close()

    xres[1] = sb("xres1", [128, 4, D])
    xlh = sb("xlh", [128, 4, 3 + TB])
    ffnhalo = sb("ffnhalo", [128, 2, 44, 2])
    G(lambda e: e.memset(xlh[:, :, 0:3], 0.0), [], [f"xl{i}" for i in range(4)])
    G(lambda e: e.memset(ffnhalo[:, :, :, :], 0.0), [], ["ffnhalo0", "ffnhalo1"])
    gl = sb("gl", [128, 4, TB], BF16)
    zq = sb("zq", [128, 4, 512])
    S_sb = sb("S_sb", [128, 4, 8, 130], BF16)
    ymix = sb("ymix", [128, 8, TB], BF16)
    ltb = sb("lt", [128, 6, TB + 4])
    lt = ltb[:, :, 0:TB]
    xcb = sb("xcb", [128, 2, TB], BF16)
    hlb = sb("hl", [128, 2, TB + 4])
    hl = hlb[:, :, 0:TB]
    gated = sb("gated", [128, 22, TB], BF16)
    q_sb = ymix
    pT = gated[:, 0:8, :].rearrange("p (h m) t -> p h m t", h=4)
    o_sb = gated[:, 8:16, :]
    u_sb = gated[:, 16:20, :]
    ygelu = gated[:, 0:4, :]
    rden = hl
    zbuf = ltb[:, 0:2, 0:TB + 2]
    cv = lt[:, 2:4, :]
    cg = lt[:, 4:6, :]
    gg = cg
    qinit = sb("qinit", [128, 3, 16])
    hc = sb("hc", [128, 44, 2])
    hc2 = sb("hc2", [128, 44])
    G(lambda e: e.memset(S_sb[:, :, :, :], 0.0), [], ["S_sb0", "S_sb1", "S_sb2", "S_sb3"])

    rr = {"ps": 0, "ps6": 0}

    def nextbank():
        b = rr["ps"] % 4
        rr["ps"] += 1
        return b

    def nextbank6():
        b = rr["ps6"] % 6
        rr["ps6"] += 1
        return b

    hnames = [f"hfm{kc}" for kc in range(8)]

    def add_resid(xb, a, nh, bi):
        xs = xres[xb][:, a, nh * 512:(nh + 1) * 512]
        V(lambda e: e.tensor_tensor(xs, xs, psb[bi][:, :], ALU.add), [f"ps{bi}", f"x{xb}a{a}"], [f"x{xb}a{a}"])

    def proj_tok(lhs_tile, lhs_names, slab_ids, xb, bankfn):
        for nh in range(2):
            st, sn = load_slab(slab_ids[nh])
            for a in range(4):
                bi = bankfn()
                pairs = [(lhs_tile[:, kc, a * 128:(a + 1) * 128], st[:, kc * 512:(kc + 1) * 512]) for kc in range(8)]
                mm_group(psb[bi][:, :], pairs, [sn] + lhs_names, [f"ps{bi}"])
                add_resid(xb, a, nh, bi)

    def fm_proj(st, sn, mt, evac, bankfn=None):
        bi = (bankfn or nextbank)()
        pairs = [(st[:, kc * 512 + mt * 128: kc * 512 + (mt + 1) * 128], hfm[:, kc, :]) for kc in range(8)]
        mm_group(psb[bi][:, :], pairs, [sn] + hnames, [f"ps{bi}"])
        evac(bi)

    u4 = u_sb[:, :, :].rearrange("p k (c j) -> p k c j", j=4)

    def v4(ap):
        return ap.rearrange("p (s c) -> p s c", s=4)

    def s5_qinit():
        q0, q1, q2 = qinit[:, 0, :], qinit[:, 1, :], qinit[:, 2, :]
        cr, ci = s5carry_r[:, :], s5carry_i[:, :]
        er, ei, r4 = E4r[:, :], E4i[:, :], R4[:, :]
        V(lambda e: e.tensor_tensor(q0, cr, er, ALU.mult), ["s5c", "E4"], ["qi0"])
        V(lambda e: e.tensor_tensor(q2, ci, ei, ALU.mult), ["s5c", "E4"], ["qi2"])
        V(lambda e: e.tensor_tensor(q1, cr, ei, ALU.mult), ["s5c", "E4"], ["qi1"])
        V(lambda e: e.tensor_tensor(q0, q0, q2, ALU.subtract), ["qi0", "qi2"], ["qi0"])
        V(lambda e: e.tensor_tensor(q2, ci, er, ALU.mult), ["s5c", "E4", "qi0"], ["qi2"])
        V(lambda e: e.tensor_tensor(q0, q0, r4, ALU.mult), ["qi0", "R4"], ["qi0"])
        V(lambda e: e.tensor_tensor(q1, q1, q2, ALU.add), ["qi1", "qi2"], ["qi1"])
        V(lambda e: e.tensor_tensor(q1, q1, r4, ALU.mult), ["qi1", "R4"], ["qi1"])

    def s5_B_pe(k):
        st, sn = load_slab(SL_S5W + k)
        for half, bi in ((0, 4), (1, 5)):
            for s_ in range(4):
                sg = half * 4 + s_
                pairs = [(st[:, (sg * 4 + j) * 128:(sg * 4 + j + 1) * 128], u4[:, k, :, j]) for j in range(4)]
                mm_group(psb[bi][:, s_ * 128:(s_ + 1) * 128], pairs, [sn, f"gated{16 + k}"], [f"ps{bi}"])

    def s5_B_steps(k):
        cT = cosT[:, k * 4:(k + 1) * 4, :]
        sT = sinT[:, k * 4:(k + 1) * 4, :]
        Xr = v4(psb[4][:, :])
        Xi = v4(psb[5][:, :])
        Zr, Zi, Qr, Qi = (zq[:, i, :] for i in range(4))
        tmp = lt[:, 0, :]
        ks = slice(k * 4, (k + 1) * 4)
        q0, q1 = qinit[:, 0, ks], qinit[:, 1, ks]
        cr, ci = s5carry_r[:, ks], s5carry_i[:, ks]
        Rk = Rtab[:, k * 4:(k + 1) * 4, :].rearrange("p s c -> p (s c)")
        Sre = S_sb[:, k, 0:4, 1:129]
        Sim = S_sb[:, k, 4:8, 1:129]
        sname = f"S_sb{k}"
        ta, tb_ = lt[:, 2, :], lt[:, 3, :]
        TT = lambda o, a, b, op, r, w: (lambda: V(lambda e: e.tensor_tensor(o, a, b, op), r, w))
        return [
            TT(v4(Zr), Xr, cT, ALU.mult, ["ps4", "tab"], ["zq0"]),
            TT(v4(tmp), Xi, sT, ALU.mult, ["ps5", "tab"], ["lt0"]),
            TT(Zr, Zr, tmp, ALU.add, ["zq0", "lt0"], ["zq0"]),
            TT(v4(Zi), Xi, cT, ALU.mult, ["ps5", "tab"], ["zq1"]),
            TT(v4(tmp), Xr, sT, ALU.mult, ["ps4", "tab"], ["lt0"]),
            TT(Zi, Zi, tmp, ALU.subtract, ["zq1", "lt0"], ["zq1"]),
            TT(v4(Zr)[:, :, 0], v4(Zr)[:, :, 0], q0, ALU.add, ["zq0", "qi0"], ["zq0"]),
            TT(v4(Zi)[:, :, 0], v4(Zi)[:, :, 0], q1, ALU.add, ["zq1", "qi1"], ["zq1"]),
            lambda: V(lambda e: e.tensor_tensor_scan(Qr, Rk, Zr, 0.0, op0=ALU.mult, op1=ALU.add), ["zq0", "Rtab"], ["zq2"]),
            lambda: V(lambda e: e.tensor_tensor_scan(Qi, Rk, Zi, 0.0, op0=ALU.mult, op1=ALU.add), ["zq1", "Rtab"], ["zq3"]),
            lambda: V(lambda e: e.tensor_copy(S_sb[:, k, :, 0:1], S_sb[:, k, :, 128:129]), [sname], [sname]),
            TT(v4(ta), v4(Qr), cT, ALU.mult, ["zq2", "tab"], ["lt2"]),
            TT(v4(tb_), v4(Qi), sT, ALU.mult, ["zq3", "tab"], ["lt3"]),
            TT(Sre, v4(ta), v4(tb_), ALU.subtract, ["lt2", "lt3"], [sname]),
            TT(cr, v4(ta)[:, :, 127], v4(tb_)[:, :, 127], ALU.subtract, ["lt2", "lt3"], ["s5c"]),
            TT(v4(ta), v4(Qr), sT, ALU.mult, ["zq2", "tab"], ["lt2"]),
            TT(v4(tb_), v4(Qi), cT, ALU.mult, ["zq3", "tab"], ["lt3"]),
            TT(Sim, v4(ta), v4(tb_), ALU.add, ["lt2", "lt3"], [sname]),
            TT(ci, v4(ta)[:, :, 127], v4(tb_)[:, :, 127], ALU.add, ["lt2", "lt3"], ["s5c"]),
        ]

    def lru_steps(mt):
        xl = xlh[:, mt, :]
        xn = f"xl{mt}"
        (xc, xcn), (ra, ran), (i_, in_) = [(lt[:, r, :], f"lt{r}") for r in (4, 5, 1)]
        m_, mn = xc, xcn
        xb_ = xcb[:, 0, :]
        xbn = "xcb0"
        hb = hl[:, mt % 2, :]
        hn = f"hl{mt % 2}"
        bk = {}
        st = []
        st.append(lambda: A(lambda e: e.activation(xc, xl[:, 3:3 + TB], AF.Identity, scale=cp(C_LCW + 3 * 4 + mt), bias=cp(C_LCB + mt)),
                            [xn, "colp"], [xcn]))

        def tap(t_):
            return lambda: V(lambda e: e.scalar_tensor_tensor(xc, xl[:, t_:t_ + TB], cp(C_LCW + t_ * 4 + mt), xc, op0=ALU.mult, op1=ALU.add),
                             [xn, xcn], [xcn])
        for t_ in range(3):
            st.append(tap(t_))
        st.append(lambda: G(lambda e: e.tensor_copy(xl[:, 0:3], xl[:, TB:TB + 3]), [xn], [xn]))
        st.append(lambda: A(lambda e: e.copy(xb_, xc), [xcn], [xbn]))

        def gates():
            bk["b1"], bk["b2"] = nextbank(), nextbank()
            mm_group(psb[bk["b1"]][:, :], [(lruw[:, mt, :], xb_)], ["lruw", xbn], [f"ps{bk['b1']}"])
            mm_group(psb[bk["b2"]][:, :], [(lruw[:, 4 + mt, :], xb_)], ["lruw", xbn], [f"ps{bk['b2']}"])
        st.append(gates)
        st.append(lambda: A(lambda e: e.activation(ra, psb[bk["b1"]][:, :], AF.Tanh, scale=0.5, bias=hbias[:, mt:mt + 1]),
                            [f"ps{bk['b1']}", "hbias"], [ran]))
        st.append(lambda: A(lambda e: e.activation(i_, psb[bk["b2"]][:, :], AF.Tanh, scale=0.5, bias=hbias[:, 4 + mt:5 + mt]),
                            [f"ps{bk['b2']}", "hbias"], [in_]))
        st.append(lambda: V(lambda e: e.scalar_tensor_tensor(i_, i_, 1.0, xc, op0=ALU.add, op1=ALU.mult), [in_, xcn], [in_]))
        st.append(lambda: A(lambda e: e.activation(ra, ra, AF.Exp, scale=cnegh[:, mt:mt + 1], bias=cnegh[:, mt:mt + 1]), [ran, "cnegh"], [ran]))
        st.append(lambda: A(lambda e: e.activation(m_, ra, AF.Square), [ran, in_], [mn]))
        st.append(lambda: A(lambda e: e.activation(m_, m_, AF.Sqrt, scale=-1.0, bias=onec[:, :]), [mn, "onec"], [mn]))
        st.append(lambda: V(lambda e: e.scalar_tensor_tensor(i_, i_, 0.5, m_, op0=ALU.mult, op1=ALU.mult), [in_, mn], [in_]))
        st.append(lambda: V(lambda e: e.tensor_tensor_scan(hb, ra, i_, lrucarry[:, mt:mt + 1], op0=ALU.mult, op1=ALU.add),
                            [ran, in_, "lrucarry"], [hn]))
        st.append(lambda: V(lambda e: e.tensor_copy(lrucarry[:, mt:mt + 1], hb[:, TB - 1:TB]), [hn], ["lrucarry"]))
        prod = lambda: V(lambda e: e.tensor_tensor(ymix[:, 4 + mt, :], hb, gl[:, mt, :], ALU.mult), [hn, f"gl{mt}"], [f"ymix{4 + mt}"])
        return st, prod

    def zip_steps(a, b):
        for i in range(max(len(a), len(b))):
            if i < len(a):
                a[i]()
            if i < len(b):
                b[i]()

    def s5_D(k, stK, snK):
        st, sn = load_slab(SL_S5V + k)
        bi = nextbank()
        yv = psb[bi][:, :].rearrange("p (c i) -> p c i", i=4)
        for i in range(4):
            pairs = [(st[:, (sg * 4 + i) * 128:(sg * 4 + i + 1) * 128], S_sb[:, k, sg, 0:128]) for sg in range(8)]
            pairs += [(stK[:, (k * 4 + (i - j)) * 128:(k * 4 + (i - j) + 1) * 128], u4[:, k, :, j]) for j in range(i + 1)]
            mm_group(yv[:, :, i], pairs, [sn, snK, f"S_sb{k}", f"gated{16 + k}"], [f"ps{bi}"])
        A(lambda e: e.activation(ygelu[:, k, :], psb[bi][:, :], AF.Gelu), [f"ps{bi}"], [f"gated{k}"])

    def glu_tile(mt, stK, snK):
        bi = nextbank()
        pairs = [(stK[:, 2048 + kc * 512 + mt * 128: 2048 + kc * 512 + (mt + 1) * 128], ygelu[:, kc, :]) for kc in range(4)]
        mm_group(psb[bi][:, :], pairs, [snK] + [f"gated{k}" for k in range(4)], [f"ps{bi}"])
        gt = lt[:, 2 + mt % 2, :]
        gn = f"lt{2 + mt % 2}"
        A(lambda e: e.activation(gt, psb[bi][:, :], AF.Sigmoid, bias=cp(C_BGLU + mt)), [f"ps{bi}", "colp"], [gn])
        V(lambda e: e.tensor_tensor(ymix[:, mt, :], ygelu[:, mt, :], gt, ALU.mult), [f"gated{mt}", gn], [f"ymix{mt}"])

    def attn_head(hd):
        def sc(mc):
            bi = nextbank6()
            pairs = [(Kfm[:, hd * 2 + c2, mc * 128:(mc + 1) * 128], q_sb[:, hd * 2 + c2, :]) for c2 in range(2)]
            mm_group(psb[bi][:, :], pairs, ["Kfm", f"ymix{hd * 2}", f"ymix{hd * 2 + 1}"], [f"ps{bi}"])
            A(lambda e: e.activation(pT[:, hd, mc, :], psb[bi][:, :], AF.Exp), [f"ps{bi}"], [f"gated{2 * hd}", f"gated{2 * hd + 1}"])
        sc(0)
        sc(1)

    def attn_tail(hd):
        bd = nextbank6()
        mm_group(psb[bd][:, :], [(onesb[:, :], pT[:, hd, mc, :]) for mc in range(2)], ["onesb", f"gated{2 * hd}", f"gated{2 * hd + 1}"], [f"ps{bd}"])
        rd = rden[:, hd % 2, :]
        rn = f"hl{hd % 2}"
        V(lambda e: e.reciprocal(rd, psb[bd][:, :]), [f"ps{bd}"], [rn])

        def pv(j):
            bi = nextbank6()
            pairs = [(Vtok[:, mc, hd * 256 + j * 128: hd * 256 + (j + 1) * 128], pT[:, hd, mc, :]) for mc in range(2)]
            mm_group(psb[bi][:, :], pairs, ["Vtok", f"gated{2 * hd}", f"gated{2 * hd + 1}"], [f"ps{bi}"])
            V(lambda e: e.tensor_tensor(o_sb[:, hd * 2 + j, :], psb[bi][:, :], rd, ALU.mult), [f"ps{bi}", rn], [f"gated{8 + hd * 2 + j}"])
        pv(0)
        pv(1)

    def ffn_tile_mm(st, sn, sa, tt, isg, tb):
        vt = 2 * sa + tt
        ch = vt + 22 * isg
        col = isg * 256 + tt * 128
        bi = nextbank6()
        pairs = [(st[:, kc * 512 + col: kc * 512 + col + 128], hfm[:, kc, :]) for kc in range(8)]
        mm_group(psb[bi][:, :], pairs, [sn] + hnames, [f"ps{bi}"])
        ps = psb[bi]
        pn = f"ps{bi}"
        dst = (cv if isg == 0 else cg)[:, tt, :]
        dn = f"lt{2 + 2 * isg + tt}"
        hold = ffnhalo[:, tb % 2, ch, :]
        hnew = ffnhalo[:, (tb + 1) % 2, ch, :]
        wcol = lambda t_: cp(C_FCW + t_ * 44 + ch)
        A(lambda e: e.activation(dst, ps[:, :], AF.Identity, scale=wcol(2), bias=cp(C_FCB + ch)), [pn, "colp"], [dn])
        A(lambda e: e.copy(hnew, ps[:, TB - 2:TB]), [pn], [f"ffnhalo{(tb + 1) % 2}"])
        return ps, pn, dst, dn, wcol, hold, f"ffnhalo{tb % 2}"

    def ffn_tap(t_, ps, pn, dst, dn, wcol, hold, hn):
        sh = 2 - t_
        V(lambda e: e.scalar_tensor_tensor(dst[:, sh:TB], ps[:, 0:TB - sh], wcol(t_), dst[:, sh:TB], op0=ALU.mult, op1=ALU.add),
          [pn, dn, "colp"], [dn])

    def ffn_halo_prep(tb):
        hold = ffnhalo[:, tb % 2, :, :]
        hn = f"ffnhalo{tb % 2}"
        W0, W1 = cp(C_FCW, 44), cp(C_FCW + 44, 44)
        V(lambda e: e.tensor_tensor(hc[:, :, 1], hold[:, :, 1], W0, ALU.mult), [hn, "colp"], ["hc"])
        V(lambda e: e.tensor_tensor(hc[:, :, 0], hold[:, :, 0], W0, ALU.mult), [hn, "colp"], ["hc"])
        V(lambda e: e.tensor_tensor(hc2[:, :], hold[:, :, 1], W1, ALU.mult), [hn, "colp"], ["hc2"])
        V(lambda e: e.tensor_tensor(hc[:, :, 0], hc[:, :, 0], hc2[:, :], ALU.add), ["hc", "hc2"], ["hc"])

    def ffn_halo_add(ch, dst, dn):
        G(lambda e: e.tensor_tensor(dst[:, 0:2], dst[:, 0:2], hc[:, ch, :], ALU.add), ["hc", dn], [dn])

    def ffn_pair(st, sn, sa, tt, tb, prev_tail):
        vt = 2 * sa + tt
        tiles = [ffn_tile_mm(st, sn, sa, tt, 0, tb), ffn_tile_mm(st, sn, sa, tt, 1, tb)]
        if prev_tail is not None:
            prev_tail()
        for t_ in (1, 0):
            for tl in tiles:
                ffn_tap(t_, *tl)
        for isg, tl in enumerate(tiles):
            ffn_halo_add(vt + 22 * isg, tl[2], tl[3])

        def tail():
            A(lambda e: e.activation(cg[:, tt, :], cg[:, tt, :], AF.Gelu), [f"lt{4 + tt}"], [f"lt{4 + tt}"])
            G(lambda e: e.tensor_tensor(gated[:, vt, :], cg[:, tt, :], cv[:, tt, :], ALU.mult), [f"lt{4 + tt}", f"lt{2 + tt}"], [f"gated{vt}"])
        return tail

    def down_group(xb, nh, sg3, kc0, nk, a, bi):
        st, sn = down_group.cur

        def fn(e):
            ins = None
            for kk in range(nk):
                ins = e.matmul(psb[bi][:, :], gated[:, kc0 + kk, a * 128:(a + 1) * 128], st[:, kk * 512:(kk + 1) * 512],
                               start=(sg3 == 0 and kk == 0), stop=(sg3 == 2 and kk == nk - 1))
            return ins
        P.op("pe", fn, r=[sn] + [f"gated{kc0 + kk}" for kk in range(nk)], w=[f"ps{bi}"])

    def final_norm(xb, a):
        xa = xres[xb][:, a, :]
        sa_, ra_ = ss[:, 4 + a:5 + a], rstd[:, 4 + a:5 + a]
        xn = f"x{xb}a{a}"
        A(lambda e: e.activation(junk[:, :], xa, AF.Square, accum_out=sa_), [xn], ["junk", f"ssf{a}"])
        A(lambda e: e.activation(ra_, sa_, AF.Sqrt, scale=1.0 / D, bias=epsc[:, :]), [f"ssf{a}", "epsc"], [f"rstdf{a}"])
        V(lambda e: e.reciprocal(ra_, ra_), [f"rstdf{a}"], [f"rstdf{a}"])
        V(lambda e: e.scalar_tensor_tensor(xa, xa, ra_, gfin[:, :], op0=ALU.mult, op1=ALU.mult), [xn, f"rstdf{a}", "gfin"], [xn])

    x_t = x_d.rearrange("(b a p) d -> b p a d", a=4, p=128)
    out_t = out_d.rearrange("(b a p) d -> b p a d", a=4, p=128)

    def load_x(tb):
        xb = tb % 2
        P.op("sp", lambda e: e.dma_start(out=xres[xb][:, :, :], in_=x_t[tb]),
             w=[f"x{xb}a{a}" for a in range(4)], dsem=f"d_x{xb}")

    def store_out(tb):
        xb = tb % 2
        P.op("sp", lambda e: e.dma_start(out=out_t[tb], in_=xres[xb][:, :, :]),
             r=[f"x{xb}a{a}" for a in range(4)], dsem=f"d_o{xb}")

    def win_evac(grp, mt):
        def ev(bi):
            if grp == 0:
                A(lambda e: e.copy(u_sb[:, mt, :], psb[bi][:, :]), [f"ps{bi}"], [f"gated{16 + mt}"])
            elif grp == 1:
                A(lambda e: e.copy(xlh[:, mt, 3:3 + TB], psb[bi][:, :]), [f"ps{bi}"], [f"xl{mt}"])
            else:
                A(lambda e: e.activation(gl[:, mt, :], psb[bi][:, :], AF.Gelu), [f"ps{bi}"], [f"gl{mt}"])
        return ev

    def q_evac(half, mt):
        def ev(bi):
            A(lambda e: e.activation(q_sb[:, half * 4 + mt, :], psb[bi][:, :], AF.Copy, scale=0.0625),
              [f"ps{bi}"], [f"ymix{half * 4 + mt}"])
        return ev

    def finish_block(tb):
        if stage >= 6:
            for a in range(4):
                final_norm(tb % 2, a)

    def block_body(tb):
        xb = tb % 2
        if tb == 0:
            norm_stats(xres[xb], f"x{xb}a", 4)
        norm_transposes(4, C_G1, hfm, "hfm", TB)
        st0, sn0 = load_slab(SL_WIN)
        for mt in range(4):
            fm_proj(st0, sn0, mt, win_evac(0, mt))
        if tb > 0:
            finish_block(tb - 1)
        s5_qinit()
        wslabs = {}

        def win_tiles(grp, mts):
            if grp not in wslabs:
                wslabs[grp] = load_slab(SL_WIN + grp)
            st, sn = wslabs[grp]
            for mt in mts:
                fm_proj(st, sn, mt, win_evac(grp, mt))
        s5_B_pe(0)
        zip_steps(s5_B_steps(0), [])
        win_tiles(1, [0, 1])
        prods = {}
        for k in range(1, 4):
            s5_B_pe(k)
            lst, prods[k - 1] = lru_steps(k - 1)
            zip_steps(s5_B_steps(k), lst)
            if k == 1:
                win_tiles(1, [2, 3])
            elif k == 2:
                win_tiles(2, [0, 1])
                prods[0]()
                prods[1]()
            else:
                win_tiles(2, [2, 3])
        if stage < 2:
            return
        stK, snK = load_slab(SL_KG)
        lst, prods[3] = lru_steps(3)
        cuts = [0, 6, 11, 15, len(lst)]
        for k in range(4):
            s5_D(k, stK, snK)
            for f in lst[cuts[k]:cuts[k + 1]]:
                f()
        prods[2]()
        prods[3]()
        for mt in range(4):
            glu_tile(mt, stK, snK)
        if tb > 0:
            store_out(tb - 1)
        if tb + 1 < nblk:
            load_x(tb + 1)
        if stage < 3:
            return
        proj_tok(ymix, [f"ymix{i}" for i in range(8)], [SL_WOUT, SL_WOUT + 1], xb, nextbank6)
        if stage < 4:
            return
        rmsnorm_to_hT(xres[xb], f"x{xb}a", 4, C_G2, hfm, "hfm", TB)
        for half in range(2):
            st, sn = load_slab(SL_WQ + half)
            for mt in range(4):
                fm_proj(st, sn, mt, q_evac(half, mt), nextbank6)
        for hd in range(4):
            attn_head(hd)
            attn_tail(hd)
        proj_tok(o_sb, [f"gated{8 + i}" for i in range(8)], [SL_WO, SL_WO + 1], xb, nextbank6)
        if stage < 5:
            return
        rmsnorm_to_hT(xres[xb], f"x{xb}a", 4, C_G3, hfm, "hfm", TB)
        ffn_halo_prep(tb)
        ptail = None
        for sa in range(11):
            st, sn = load_slab(SL_UP + sa)
            for tt in range(2):
                ptail = ffn_pair(st, sn, sa, tt, tb, ptail)
        ptail()
        if tb + 1 < nblk and stage >= 6:
            norm_stats(xres[1 - xb], f"x{1 - xb}a", 4)
        for nh in range(2):
            kc0 = 0
            dbanks = [nextbank6() for _ in range(4)]
            for sg3 in range(3):
                nk = 8 if sg3 < 2 else 6
                down_group.cur = load_slab(SL_DN + nh * 3 + sg3)
                for a in range(4):
                    down_group(xb, nh, sg3, kc0, nk, a, dbanks[a])
                kc0 += nk
            for a in range(4):
                add_resid(xb, a, nh, dbanks[a])

    load_x(0)
    for tb in range(nblk):
        block_body(tb)
    finish_block(nblk - 1)
    store_out(nblk - 1)

    if dbg is not None:
        P.barrier()
        for i_, (ap_, a_, b_) in enumerate(dbg(locals())):
            P.op("pool", lambda e, i_=i_, ap_=ap_, a_=a_, b_=b_: e.dma_start(
                out=dbg_d[i_][:, 0:a_ * b_].rearrange("p (a b) -> p a b", a=a_), in_=ap_), dsem=f"d_dbg{i_}")

    P.barrier()
    with nc.Block() as block:
        P.emit(block)
    es.close()
    return nc


def _slab_kc(w, c0, ncols=512, kc0=0, nkc=8):
    out = np.zeros((128, 8, ncols), np.float32)
    K = w.shape[0]
    for kc in range(nkc):
        r0 = (kc0 + kc) * 128
        if r0 >= K:
            break
        out[:, kc, :] = w[r0:r0 + 128, c0:c0 + ncols]
    return out.reshape(128, 8 * ncols)


def host_layout(inp):
    f = lambda k: np.asarray(inp[k], np.float32)
    w_in, w_out = f("w_in")[0], f("w_out")[0]
    wq, wk, wv, wo = f("xa_w_q")[0], f("xa_w_k")[0], f("xa_w_v")[0], f("xa_w_o")[0]
    wup, wdn, glu = f("ffn_w_up")[0], f("ffn_w_down")[0], f("s5_w_glu")[0]
    slabs = {}
    for g in range(3):
        slabs[SL_WIN + g] = _slab_kc(w_in, g * 512)
    kg = np.zeros((128, 4096), np.float32)
    kg[:, 2048:] = _slab_kc(glu, 0, 512, 0, 4)[:, :2048]
    slabs[SL_KG] = kg
    for h in range(2):
        slabs[SL_WOUT + h] = _slab_kc(w_out, h * 512)
        slabs[SL_WQ + h] = _slab_kc(wq, h * 512)
        slabs[SL_WO + h] = _slab_kc(wo, h * 512)
        slabs[SL_WK + h] = _slab_kc(wk, h * 512)
        slabs[SL_WV + h] = _slab_kc(wv, h * 512)
    for sa in range(11):
        cols = np.concatenate([np.arange(sa * 256, sa * 256 + 256), 2816 + np.arange(sa * 256, sa * 256 + 256)])
        slabs[SL_UP + sa] = _slab_kc(wup[:, cols], 0)
    for nh in range(2):
        for g3 in range(3):
            slabs[SL_DN + nh * 3 + g3] = _slab_kc(wdn, nh * 512, 512, g3 * 8, 8 if g3 < 2 else 6)
    w32 = np.stack([slabs[s] for s in HOST_SLABS]).astype(np.float32)

    colp = np.zeros((128, NCOL), np.float32)

    def putcols(c0, vec):
        v = np.asarray(vec, np.float32).reshape(-1, 128).T
        colp[:, c0:c0 + v.shape[1]] = v
    putcols(C_G1, f("ln_mix_g")[0])
    putcols(C_G2, f("ln_xa_g")[0])
    putcols(C_G3, f("ln_ffn_g")[0])
    putcols(C_GMEM, f("mem_norm_g"))
    putcols(C_BGLU, f("s5_b_glu")[0])
    lcw = f("lru_conv_w")[0]
    for t in range(4):
        putcols(C_LCW + t * 4, lcw[t])
    putcols(C_LCB, f("lru_conv_b")[0])
    putcols(C_BA, f("lru_b_a")[0].reshape(-1))
    putcols(C_BX, f("lru_b_x")[0].reshape(-1))
    putcols(C_LAM, f("lru_lam")[0].reshape(-1))
    fcw = f("ffn_conv_w")[0]
    for t in range(3):
        putcols(C_FCW + t * 44, fcw[t])
    putcols(C_FCB, f("ffn_conv_b")[0])
    putcols(C_DS5, f("s5_d")[0].reshape(-1))
    gfin = np.ascontiguousarray(np.broadcast_to(f("final_norm_g")[None, :], (128, D)))

    def gq(arr):
        a = arr.reshape(4, 8, 4, 16)
        return np.ascontiguousarray(a.transpose(1, 3, 0, 2).reshape(128, 16))
    lre, lim = f("s5_lam_re")[0], f("s5_lam_im")[0]
    ldt = np.broadcast_to(f("s5_log_dt")[0][:, None], (32, 64))
    s5par = np.stack([gq(lre), gq(lim), gq(ldt)], axis=1).astype(np.float32)

    def cb(b):
        a = b.reshape(4, 8, 4, 16, 16)
        return np.ascontiguousarray(a.transpose(1, 3, 0, 2, 4).reshape(128, 16, 16))
    bcc = np.stack([cb(f("s5_b_re")[0]), cb(f("s5_b_im")[0]),
                    cb(np.ascontiguousarray(f("s5_c_re")[0].transpose(0, 2, 1))),
                    cb(np.ascontiguousarray(f("s5_c_im")[0].transpose(0, 2, 1)))], axis=1).astype(np.float32)
    mask8 = np.zeros((128, 2, 8), np.float32)
    for p_ in range(128):
        mask8[p_, 0, p_ // 16] = 1.0
        mask8[p_, 1, p_ // 16] = -1.0
    lruw = np.zeros((128, 8, 128), np.float32)
    wa, wx = f("lru_w_a")[0], f("lru_w_x")[0]
    for mt in range(4):
        for hh in range(2):
            lruw[hh * 64:(hh + 1) * 64, mt, hh * 64:(hh + 1) * 64] = wa[2 * mt + hh]
            lruw[hh * 64:(hh + 1) * 64, 4 + mt, hh * 64:(hh + 1) * 64] = wx[2 * mt + hh]
    shared = {"w32": w32, "colp": colp, "gfin": gfin, "s5par": s5par, "bcc": bcc, "mask8": mask8, "lruw": lruw,
              "ident": np.eye(128, dtype=np.float32).astype(ml_dtypes.bfloat16),
              "identf": np.eye(128, dtype=np.float32)}
    return shared


def kernel(**inputs):
    shared = host_layout(inputs)
    x = np.asarray(inputs["x"], np.float32)
    mem = np.asarray(inputs["mem"], np.float32)
    nc = build(SEQ // TB)
    in_maps = []
    for c in range(8):
        m = dict(shared)
        m["x"] = np.ascontiguousarray(x[c])
        m["mem"] = np.ascontiguousarray(mem[c])
        in_maps.append(m)
    res = run_bass_kernel_spmd(nc, in_maps, core_ids=list(range(8)))
    return np.stack([np.asarray(r["out"], np.float32) for r in res.results], axis=0)
```

```python
import math
from contextlib import ExitStack
import numpy as np
import ml_dtypes
import concourse.bass as bass
import concourse.mybir as mybir
from concourse.bass_utils import run_bass_kernel_spmd

F32 = mybir.dt.float32
BF16 = mybir.dt.bfloat16
ALU = mybir.AluOpType
AF = mybir.ActivationFunctionType

SEQ = 4096
TB = 512
D = 1024
NS = 5
PI = math.pi

SL_WIN = 0
SL_S5W = 3
SL_S5V = 7
SL_KG = 11
SL_WOUT = 12
SL_WQ = 14
SL_WO = 16
SL_UP = 18
SL_DN = 29
SL_WK = 35
SL_WV = 37
NSLAB = 39
HOST_SLABS = [0, 1, 2, 11, 12, 13, 14, 15, 16, 17] + list(range(18, 35)) + [35, 36, 37, 38]

C_G1, C_G2, C_G3, C_GMEM = 0, 8, 16, 24
C_BGLU = 32
C_LCW = 36
C_LCB = 52
C_BA = 56
C_BX = 60
C_LAM = 64
C_FCW = 68
C_FCB = 200
C_DS5 = 244
NCOL = 248


class Prog:
    ENG = ("pe", "act", "dve", "pool", "sp")

    def __init__(self, nc, es):
        self.nc = nc
        self.es = es
        self.q = {e: [] for e in self.ENG}
        self.cnt = {}
        self.sems = {}
        self.seen = {e: {} for e in self.ENG}
        self.lastw = {}
        self.readers = {}

    def sem(self, key):
        if key not in self.sems:
            self.sems[key] = self.es.enter_context(self.nc.semaphore("s_" + key))
            self.cnt[key] = 0
        return self.sems[key]

    def op(self, eng, fn, r=(), w=(), dsem=None):
        deps = {}

        def add(tok):
            if tok is None:
                return
            k, v = tok
            if deps.get(k, 0) < v:
                deps[k] = v
        for b in r:
            add(self.lastw.get(b))
        for b in w:
            add(self.lastw.get(b))
            for k, v in self.readers.get(b, {}).items():
                add((k, v))
        waits = []
        for k, v in deps.items():
            if eng == "pe" and k == "pe":
                continue
            if self.seen[eng].get(k, 0) >= v:
                continue
            self.seen[eng][k] = v
            waits.append((k, v))
        if dsem is None:
            key, inc = eng, 1
        else:
            key, inc = dsem, 16
        self.sem(key)
        self.cnt[key] += inc
        tok = (key, self.cnt[key])
        self.q[eng].append((waits, fn, key, inc))
        for b in r:
            self.readers.setdefault(b, {})[key] = tok[1]
        for b in w:
            self.lastw[b] = tok
            self.readers[b] = {}
        return tok

    def barrier(self, skip=None):
        for e in self.ENG:
            waits = []
            for k, v in self.cnt.items():
                if skip is not None and k.startswith(skip):
                    continue
                if v > 0 and self.seen[e].get(k, 0) < v:
                    self.seen[e][k] = v
                    waits.append((k, v))
            if waits:
                self.q[e].append((waits, None, None, 0))

    def emit(self, block):
        sems = self.sems

        def run(eng, lst):
            for waits, fn, key, inc in lst:
                for k, v in waits:
                    eng.wait_ge(sems[k], v)
                if fn is not None:
                    ins = fn(eng)
                    ins.then_inc(sems[key], inc)

        @block.tensor
        def _(e):
            run(e, self.q["pe"])

        @block.scalar
        def _(e):
            run(e, self.q["act"])

        @block.vector
        def _(e):
            run(e, self.q["dve"])

        @block.gpsimd
        def _(e):
            run(e, self.q["pool"])

        @block.sync
        def _(e):
            run(e, self.q["sp"])


def build(nblk=8, stage=6, dbg=None, pro_only=False):
    nc = bass.Bass("TRN2", target_bir_lowering=False)
    ntok = nblk * TB

    def din(name, shape, dt=F32):
        return nc.dram_tensor(name, list(shape), dt, kind="ExternalInput").ap()
    x_d = din("x", [ntok, D])
    mem_d = din("mem", [256, D])
    w32_d = din("w32", [len(HOST_SLABS), 128, 4096])
    colp_d = din("colp", [128, NCOL])
    gfin_d = din("gfin", [128, D])
    s5par_d = din("s5par", [128, 3, 16])
    bcc_d = din("bcc", [128, 4, 16, 16])
    mask8_d = din("mask8", [128, 2, 8])
    lruw_d = din("lruw", [128, 8, 128])
    ident_d = din("ident", [128, 128], BF16)
    identf_d = din("identf", [128, 128])
    out_d = nc.dram_tensor("out", [ntok, D], F32, kind="ExternalOutput").ap()
    wscr = nc.dram_tensor("wscr", [NSLAB, 128, 4096], BF16, kind="Internal").ap()
    dbg_d = None
    if dbg is not None:
        dbg_d = nc.dram_tensor("dbg", [16, 128, 4096], F32, kind="ExternalOutput").ap()

    es = ExitStack()
    P = Prog(nc, es)

    def sb(name, shape, dt=F32, stack=es):
        return stack.enter_context(nc.sbuf_tensor("sb_" + name, list(shape), dt))

    colp = sb("colp", [128, NCOL])
    gfin = sb("gfin", [128, D])
    ident = sb("ident", [128, 128], BF16)
    onesb = sb("onesb", [128, 128], BF16)
    lruw = sb("lruw", [128, 8, 128], BF16)
    cneg = sb("cneg", [128, 4])
    epsc = sb("epsc", [128, 1])
    hpic = sb("hpic", [128, 1])
    onec = sb("onec", [128, 1])
    hbias = sb("hbias", [128, 12])
    cnegh = sb("cnegh", [128, 4])
    cosT = sb("cosT", [128, 16, 128])
    sinT = sb("sinT", [128, 16, 128])
    Rtab = sb("Rtab", [128, 16, 128])
    R4 = sb("R4", [128, 16])
    E4r = sb("E4r", [128, 16])
    E4i = sb("E4i", [128, 16])
    Kfm = sb("Kfm", [128, 8, 256], BF16)
    Vtok = sb("Vtok", [128, 2, D], BF16)
    slots = [sb(f"slot{i}", [128, 4096], BF16) for i in range(NS)]
    xres = [sb("xres0", [128, 4, D]), None]
    hT = sb("hT", [128, 4, D], BF16)
    hfm = sb("hfm", [128, 8, TB], BF16)
    ss = sb("ss", [128, 8])
    rstd = sb("rstd", [128, 8])
    junk = sb("junk", [128, D], BF16)
    s5carry_r = sb("s5cr", [128, 16])
    s5carry_i = sb("s5ci", [128, 16])
    lrucarry = sb("lrucarry", [128, 4])
    psb = [es.enter_context(nc.psum_tensor(f"pp{i}", [128, 512], F32)) for i in range(6)]
    psT = [es.enter_context(nc.psum_tensor(f"ppT{i}", [128, 1024], BF16)) for i in range(2)]

    cp = lambda c, n=1: colp[:, c:c + n]

    slab_state = {"n": 0}

    def load_slab(idx, eng="sp"):
        i = slab_state["n"] % NS
        slab_state["n"] += 1
        name = f"slot{i}"
        P.op(eng, lambda e, i=i, idx=idx: e.dma_start(out=slots[i][:, :], in_=wscr[idx]),
             r=[f"scr{idx}", f"scr{idx}g"], w=[name], dsem=f"d_slot{i}")
        return slots[i], name

    def mm_group(out_ap, pairs, r, w):
        n = len(pairs)

        def fn(e):
            ins = None
            for j, (l, rr) in enumerate(pairs):
                ins = e.matmul(out_ap, l, rr, start=(j == 0), stop=(j == n - 1))
            return ins
        P.op("pe", fn, r=r, w=w)

    def V(fn, r, w):
        P.op("dve", fn, r=r, w=w)

    def A(fn, r, w):
        P.op("act", fn, r=r, w=w)

    def G(fn, r, w):
        P.op("pool", fn, r=r, w=w)

    pst = ExitStack()
    par = sb("par", [128, 3, 16], stack=pst)
    identf = sb("identf", [128, 128], stack=pst)
    bcc = sb("bcc", [128, 4, 16, 16], stack=pst)
    mask8 = sb("mask8", [128, 2, 8], stack=pst)
    cs = sb("cs", [128, 4, 16, 16], stack=pst)
    t1 = sb("t1", [128, 16, 128], stack=pst)
    lruw32 = t1[:, 0:8, :]
    t2 = sb("t2", [128, 16, 128], stack=pst)
    NB = sb("NB", [128, 4, 2, 16, 128], BF16, stack=pst)
    Vm = sb("Vm", [128, 2, 4, 8, 128], BF16, stack=pst)
    Wst = sb("Wst", [128, 1, 8, 4, 128], BF16, stack=pst)
    Kst = hT[:, 2:4, :].rearrange("p a (b c) -> p (a b) c", c=128).rearrange("p (k t) c -> p k t c", k=4)
    sm = xres[0][:, 2:4, :].rearrange("p a (b c) -> p (a b) c", c=16)
    memx = xres[0]
    memfm = hfm

    def ld(dst, src, name, eng="sp"):
        P.op(eng, lambda e: e.dma_start(out=dst, in_=src), w=[name], dsem="d_" + name)
    ld(colp[:, :], colp_d, "colp")
    ld(gfin[:, :], gfin_d, "gfin")
    ld(ident[:, :], ident_d, "ident")
    ld(identf[:, :], identf_d, "identf")
    ld(par[:, :, :], s5par_d, "par")
    ld(bcc[:, :, :, :], bcc_d, "bcc")
    ld(mask8[:, :, :], mask8_d, "mask8")
    ld(lruw32, lruw_d, "t1")
    P.op("sp", lambda e: e.dma_start(out=memx[:, 0:2, :], in_=mem_d.rearrange("(a p) d -> p a d", p=128)),
         w=["x0a0", "x0a1"], dsem="d_x0")

    G(lambda e: e.memset(onesb[:, :], 1.0), [], ["onesb"])
    G(lambda e: e.memset(epsc[:, :], 1e-6), [], ["epsc"])
    G(lambda e: e.memset(hpic[:, :], PI / 2), [], ["hpic"])
    G(lambda e: e.memset(onec[:, :], 1.0), [], ["onec"])
    G(lambda e: e.memset(lrucarry[:, :], 0.0), [], ["lrucarry"])
    G(lambda e: e.memset(s5carry_r[:, :], 0.0), [], ["s5c"])
    G(lambda e: e.memset(s5carry_i[:, :], 0.0), [], ["s5c"])
    V(lambda e: e.tensor_copy(lruw[:, :, :], lruw32), ["t1"], ["lruw"])

    A(lambda e: e.activation(cneg[:, :], cp(C_LAM, 4), AF.Exp, scale=-1.0), ["colp"], ["cneg"])
    A(lambda e: e.activation(cneg[:, :], cneg[:, :], AF.Ln, bias=onec[:, :]), ["cneg", "onec"], ["cneg"])
    V(lambda e: e.tensor_scalar_mul(cneg[:, :], cneg[:, :], -8.0), ["cneg"], ["cneg"])
    V(lambda e: e.tensor_scalar_mul(cnegh[:, :], cneg[:, :], 0.5), ["cneg"], ["cnegh"])
    V(lambda e: e.tensor_scalar_mul(hbias[:, 0:4], cp(C_BA, 4), 0.5), ["colp"], ["hbias"])
    V(lambda e: e.tensor_scalar_mul(hbias[:, 4:8], cp(C_BX, 4), 0.5), ["colp"], ["hbias"])
    V(lambda e: e.tensor_scalar_mul(hbias[:, 8:12], cp(C_BGLU, 4), 0.5), ["colp"], ["hbias"])

    smn = {"i": 0}

    def S(name=None):
        i = smn["i"]
        smn["i"] += 1
        return sm[:, i, :], f"sm{i}"

    def vtt(o, a, b, op):
        (oa, on), (aa, an), (ba, bn) = o, a, b
        V(lambda e: e.tensor_tensor(oa, aa, ba, op), [an, bn], [on])

    def cmul(a_r, a_i, b_r, b_i):
        o_r, o_i, u1, u2 = S(), S(), S(), S()
        vtt(u1, a_r, b_r, ALU.mult)
        vtt(u2, a_i, b_i, ALU.mult)
        vtt(o_r, u1, u2, ALU.subtract)
        vtt(u1, a_r, b_i, ALU.mult)
        vtt(u2, a_i, b_r, ALU.mult)
        vtt(o_i, u1, u2, ALU.add)
        return o_r, o_i

    lre = (par[:, 0, :], "par")
    lim = (par[:, 1, :], "par")
    ldt = (par[:, 2, :], "par")
    dt_ = S()
    A(lambda e: e.activation(dt_[0], ldt[0], AF.Exp), ["par"], [dt_[1]])
    zr, zi = S(), S()
    vtt(zr, lre, dt_, ALU.mult)
    vtt(zi, lim, dt_, ALU.mult)
    mag = S()
    A(lambda e: e.activation(mag[0], zr[0], AF.Exp), [zr[1]], [mag[1]])

    sn0, cs0 = S(), S()
    A(lambda e: e.activation(sn0[0], zi[0], AF.Sin, scale=1.0 / 16), [zi[1]], [sn0[1]])
    A(lambda e: e.activation(cs0[0], zi[0], AF.Sin, scale=1.0 / 16, bias=hpic[:, :]), [zi[1], "hpic"], [cs0[1]])
    sn1, cs1 = sn0, cs0
    for _ in range(4):
        cs1, sn1 = cmul(cs1, sn1, cs1, sn1)
    L = [None] * 5
    L1r, L1i = S(), S()
    vtt(L1r, mag, cs1, ALU.mult)
    vtt(L1i, mag, sn1, ALU.mult)
    L[1] = (L1r, L1i)
    L[2] = cmul(L1r, L1i, L1r, L1i)
    L[3] = cmul(L[2][0], L[2][1], L1r, L1i)
    L[4] = cmul(L[2][0], L[2][1], L[2][0], L[2][1])
    one_, zero_ = S(), S()
    V(lambda e: e.memset(one_[0], 1.0), [], [one_[1]])
    V(lambda e: e.memset(zero_[0], 0.0), [], [zero_[1]])
    L[0] = (one_, zero_)
    am1 = S()
    V(lambda e: e.tensor_scalar_add(am1[0], L1r[0], -1.0), [L1r[1]], [am1[1]])
    nli = S()
    V(lambda e: e.tensor_scalar_mul(nli[0], lim[0], -1.0), ["par"], [nli[1]])
    num_r, num_i = cmul(am1, L1i, lre, nli)
    den, u3 = S(), S()
    vtt(den, lre, lre, ALU.mult)
    vtt(u3, lim, lim, ALU.mult)
    vtt(den, den, u3, ALU.add)
    kr, ki = S(), S()
    V(lambda e: e.reciprocal(den[0], den[0]), [den[1]], [den[1]])
    vtt(kr, num_r, den, ALU.mult)
    vtt(ki, num_i, den, ALU.mult)
    M = [cmul(L[3 - j][0], L[3 - j][1], kr, ki) for j in range(4)]
    r2 = S()
    vtt(r2, L[4][0], L[4][0], ALU.mult)
    vtt(u3, L[4][1], L[4][1], ALU.mult)
    vtt(r2, r2, u3, ALU.add)
    A(lambda e: e.activation(R4[:, :], r2[0], AF.Sqrt), [r2[1]], ["R4"])
    ir4 = S()
    V(lambda e: e.reciprocal(ir4[0], R4[:, :]), ["R4"], [ir4[1]])
    V(lambda e: e.tensor_tensor(E4r[:, :], L[4][0][0], ir4[0], ALU.mult), [L[4][0][1], ir4[1]], ["E4"])
    V(lambda e: e.tensor_tensor(E4i[:, :], L[4][1][0], ir4[0], ALU.mult), [L[4][1][1], ir4[1]], ["E4"])
    V(lambda e: e.memset(cosT[:, :, 0:1], 1.0), [], ["tab"])
    V(lambda e: e.memset(sinT[:, :, 0:1], 0.0), [], ["tab"])
    wr, wi = (E4r[:, :], "E4"), (E4i[:, :], "E4")
    n = 1
    while n < 128:
        wrb = wr[0].unsqueeze(2).to_broadcast([128, 16, n])
        wib = wi[0].unsqueeze(2).to_broadcast([128, 16, n])
        c0, s0 = cosT[:, :, 0:n], sinT[:, :, 0:n]
        c1, s1 = cosT[:, :, n:2 * n], sinT[:, :, n:2 * n]
        ta, tb_ = t1[:, :, 0:n], t2[:, :, 0:n]
        V(lambda e, c0=c0, wrb=wrb, ta=ta: e.tensor_tensor(ta, c0, wrb, ALU.mult), ["tab", wr[1]], ["t1"])
        V(lambda e, s0=s0, wib=wib, tb_=tb_: e.tensor_tensor(tb_, s0, wib, ALU.mult), ["tab", wi[1]], ["t2"])
        V(lambda e, c1=c1, ta=ta, tb_=tb_: e.tensor_tensor(c1, ta, tb_, ALU.subtract), ["t1", "t2"], ["tab"])
        V(lambda e, c0=c0, wib=wib, ta=ta: e.tensor_tensor(ta, c0, wib, ALU.mult), ["tab", wi[1]], ["t1"])
        V(lambda e, s0=s0, wrb=wrb, tb_=tb_: e.tensor_tensor(tb_, s0, wrb, ALU.mult), ["tab", wr[1]], ["t2"])
        V(lambda e, s1=s1, ta=ta, tb_=tb_: e.tensor_tensor(s1, ta, tb_, ALU.add), ["t1", "t2"], ["tab"])
        if n < 64:
            wr, wi = cmul(wr, wi, wr, wi)
        n *= 2
    V(lambda e: e.tensor_copy(Rtab[:, :, :], R4[:, :].unsqueeze(2).to_broadcast([128, 16, 128])), ["R4"], ["Rtab"])
    V(lambda e: e.memset(Rtab[:, :, 0:1], 0.0), ["Rtab"], ["Rtab"])

    def norm_stats(src, srcname, nsub):
        for a in range(nsub):
            if a % 2 == 0:
                A(lambda e, a=a: e.activation(junk[:, :], src[:, a, :], AF.Square, accum_out=ss[:, a:a + 1]),
                  [f"{srcname}{a}"], ["junk", f"ss{a}"])
            else:
                V(lambda e, a=a: e.scalar_tensor_tensor(hT[:, a, :], src[:, a, :], 1.0, src[:, a, :], op0=ALU.mult, op1=ALU.mult,
                                                        accum_out=ss[:, a:a + 1]), [f"{srcname}{a}"], [f"hT{a}", f"ss{a}"])
            A(lambda e, a=a: e.activation(rstd[:, a:a + 1], ss[:, a:a + 1], AF.Sqrt, scale=1.0 / D, bias=epsc[:, :]),
              [f"ss{a}", "epsc"], [f"rstd{a}"])
            V(lambda e, a=a: e.reciprocal(rstd[:, a:a + 1], rstd[:, a:a + 1]), [f"rstd{a}"], [f"rstd{a}"])
            if a % 2 == 0:
                A(lambda e, a=a: e.activation(hT[:, a, :], src[:, a, :], AF.Copy, scale=rstd[:, a:a + 1]),
                  [f"{srcname}{a}", f"rstd{a}"], [f"hT{a}"])
            else:
                V(lambda e, a=a: e.tensor_scalar_mul(hT[:, a, :], src[:, a, :], rstd[:, a:a + 1]),
                  [f"{srcname}{a}", f"rstd{a}"], [f"hT{a}"])

    def norm_transposes(nsub, col_g, dst_fm, dstname, ncols_tok):
        for kc in range(8):
            bi = kc % 2
            bank = psT[bi]

            def fn(e, kc=kc, bank=bank):
                ins = None
                for a in range(nsub):
                    ins = e.transpose(bank[:, a * 128:(a + 1) * 128], hT[:, a, kc * 128:(kc + 1) * 128], ident[:, :])
                return ins
            P.op("pe", fn, r=[f"hT{a}" for a in range(nsub)] + ["ident"], w=[f"psT{bi}"])
            if kc % 2 == 0:
                A(lambda e, kc=kc, bank=bank: e.activation(dst_fm[:, kc, 0:ncols_tok], bank[:, 0:ncols_tok], AF.Copy, scale=cp(col_g + kc)),
                  [f"psT{bi}", "colp"], [f"{dstname}{kc}"])
            else:
                V(lambda e, kc=kc, bank=bank: e.tensor_scalar_mul(dst_fm[:, kc, 0:ncols_tok], bank[:, 0:ncols_tok], cp(col_g + kc)),
                  [f"psT{bi}", "colp"], [f"{dstname}{kc}"])


    def rmsnorm_to_hT(src, srcname, nsub, col_g, dst_fm, dstname, ncols_tok):
        norm_stats(src, srcname, nsub)
        norm_transposes(nsub, col_g, dst_fm, dstname, ncols_tok)

    rmsnorm_to_hT(memx, "x0a", 2, C_GMEM, memfm, "hfm", 256)
    for half in range(2):
        P.op("pool", lambda e, half=half: e.dma_start(out=slots[half][:, :], in_=w32_d[HOST_SLABS.index(SL_WK + half)]),
             w=[f"slot{half}"], dsem=f"d_slot{half}")
        for mt in range(4):
            bi = 2 + mt % 2
            pairs = [(slots[half][:, kc * 512 + mt * 128: kc * 512 + (mt + 1) * 128], memfm[:, kc, 0:256]) for kc in range(8)]
            mm_group(psb[bi][:, 0:256], pairs, [f"slot{half}"] + [f"hfm{kc}" for kc in range(8)], [f"ps{bi}"])
            A(lambda e, half=half, mt=mt, bi=bi: e.copy(Kfm[:, half * 4 + mt, :], psb[bi][:, 0:256]), [f"ps{bi}"], ["Kfm"])
    for half in range(2):
        P.op("pool", lambda e, half=half: e.dma_start(out=slots[2 + half][:, :], in_=w32_d[HOST_SLABS.index(SL_WV + half)]),
             w=[f"slot{2 + half}"], dsem=f"d_slot{2 + half}")
        for mc in range(2):
            bi = 4 + mc
            pairs = [(memfm[:, kc, mc * 128:(mc + 1) * 128], slots[2 + half][:, kc * 512:(kc + 1) * 512]) for kc in range(8)]
            mm_group(psb[bi][:, :], pairs, [f"slot{2 + half}"] + [f"hfm{kc}" for kc in range(8)], [f"ps{bi}"])
            V(lambda e, half=half, mc=mc, bi=bi: e.tensor_copy(Vtok[:, mc, half * 512:(half + 1) * 512], psb[bi][:, :]),
              [f"ps{bi}"], ["Vtok"])

    bre, bim, cre, cim = (bcc[:, i_, :, :] for i_ in range(4))
    c1, c2, c3, c4 = (cs[:, i_, :, :] for i_ in range(4))
    mpos = mask8[:, 0, :].unsqueeze(1).unsqueeze(3)
    mneg = mask8[:, 1, :].unsqueeze(1).unsqueeze(3)

    def bc16(ap):
        return ap.unsqueeze(2).to_broadcast([128, 16, 16])
    for j in range(4):
        mr, mi = bc16(M[j][0][0]), bc16(M[j][1][0])
        mrn, min_ = M[j][0][1], M[j][1][1]
        V(lambda e, mr=mr: e.tensor_tensor(c1, bre, mr, ALU.mult), ["bcc", mrn], ["c1"])
        V(lambda e, mi=mi: e.tensor_tensor(c2, bim, mi, ALU.mult), ["bcc", min_], ["c2"])
        V(lambda e, mi=mi: e.tensor_tensor(c3, bre, mi, ALU.mult), ["bcc", min_], ["c3"])
        V(lambda e, mr=mr: e.tensor_tensor(c4, bim, mr, ALU.mult), ["bcc", mrn], ["c4"])
        V(lambda e: e.tensor_tensor(c1, c1, c2, ALU.subtract), ["c1", "c2"], ["c1"])
        V(lambda e: e.tensor_tensor(c3, c3, c4, ALU.add), ["c3", "c4"], ["c3"])
        for ri, cc, cn in ((0, c1, "c1"), (1, c3, "c3")):
            o = NB[:, j, ri, :, :].rearrange("p a (g h) -> p a g h", g=8)
            V(lambda e, o=o, cc=cc: e.tensor_tensor(o, cc.unsqueeze(2).to_broadcast([128, 16, 8, 16]),
                                                    mpos.to_broadcast([128, 16, 8, 16]), ALU.mult), [cn, "mask8"], [f"NB{j}"])
    tix = {"i": 0}

    def w_section(k):
        for sg in range(8):
            s_, ri = sg % 4, sg // 4
            bank = psT[tix["i"] % 2]
            bname = f"psT{tix['i'] % 2}"
            tix["i"] += 1

            def fn(e, s_=s_, ri=ri, bank=bank):
                ins = None
                for j in range(4):
                    ins = e.transpose(bank[:, j * 128:(j + 1) * 128], NB[:, j, ri, k * 4 + s_, :], ident[:, :])
                return ins
            P.op("pe", fn, r=[f"NB{j}" for j in range(4)] + ["ident"], w=[bname])
            dst = Wst[:, 0, sg, :, :]
            src = bank[:, 0:512].rearrange("p (j c) -> p j c", j=4)
            A(lambda e, dst=dst, src=src: e.copy(dst, src), [bname], ["Wst"])
        P.op("sp", lambda e: e.dma_start(out=wscr[SL_S5W + k], in_=Wst[:, 0, :, :, :].rearrange("p a b c -> p (a b c)")),
             r=["Wst"], w=[f"scr{SL_S5W + k}"], dsem=f"d_w{k}")
    def gen_V(m):
        lr, li = bc16(L[m][0][0]), bc16(L[m][1][0])
        lrn, lin = L[m][0][1], L[m][1][1]
        vb = Vm[:, m % 2, :, :, :]
        on = f"Vm{m % 2}"
        V(lambda e: e.tensor_tensor(c1, cre, lr, ALU.mult), ["bcc", lrn], ["c1"])
        V(lambda e: e.tensor_tensor(c2, cim, li, ALU.mult), ["bcc", lin], ["c2"])
        V(lambda e: e.tensor_tensor(c3, cre, li, ALU.mult), ["bcc", lin], ["c3"])
        V(lambda e: e.tensor_tensor(c4, cim, lr, ALU.mult), ["bcc", lrn], ["c4"])
        V(lambda e: e.tensor_tensor(c1, c1, c2, ALU.subtract), ["c1", "c2"], ["c1"])
        V(lambda e: e.tensor_tensor(c3, c3, c4, ALU.add), ["c3", "c4"], ["c3"])
        for k in range(4):
            for half, cc, cn, mk in ((0, c1, "c1", mpos), (1, c3, "c3", mneg)):
                o = vb[:, k, half * 4:(half + 1) * 4, :].rearrange("p s (g h) -> p s g h", g=8)
                V(lambda e, o=o, cc=cc, mk=mk, k=k: e.tensor_tensor(o, cc[:, k * 4:(k + 1) * 4, :].unsqueeze(2).to_broadcast([128, 4, 8, 16]),
                                                                   mk.to_broadcast([128, 4, 8, 16]), ALU.mult), [cn, "mask8"], [on])
        if m >= 1:
            for k in range(4):
                dst = wscr[SL_S5V + k].rearrange("p (s m c) -> p s m c", s=8, m=4)[:, :, m - 1, :]
                P.op("sp", lambda e, k=k, dst=dst: e.dma_start(out=dst, in_=vb[:, k, :, :]),
                     r=[on], w=[f"scr{SL_S5V + k}"], dsem=f"d_v{k}_{m}")
        if m <= 3:
            for k in range(4):
                pairs = [(NB[:, 3, sg // 4, k * 4 + sg % 4, :], vb[:, k, sg, :]) for sg in range(8)]
                mm_group(psb[k][:, m * 128:(m + 1) * 128], pairs, ["NB3", on], [f"ps{k}"])
    for m in range(5):
        gen_V(m)
        if m < 4:
            w_section(m)
    for k in range(4):
        V(lambda e, k=k: e.scalar_tensor_tensor(Kst[:, k, 0, :], identf[:, :], cp(C_DS5 + k), psb[k][:, 0:128],
                                                op0=ALU.mult, op1=ALU.add), [f"ps{k}", "identf", "colp"], ["Kst"])
        V(lambda e, k=k: e.tensor_copy(Kst[:, k, 1:4, :], psb[k][:, 128:512].rearrange("p (j c) -> p j c", j=3)),
          [f"ps{k}"], ["Kst"])
    P.op("sp", lambda e: e.dma_start(out=wscr[SL_KG][:, 0:2048], in_=hT[:, 2:4, :].rearrange("p a b -> p (a b)")),
         r=["Kst"], w=[f"scr{SL_KG}"], dsem="d_kst")

    for hi, sl in enumerate(HOST_SLABS):
        if sl >= SL_WK:
            continue
        if sl == SL_KG:
            P.op("pool", lambda e, hi=hi, sl=sl: e.dma_start(out=wscr[sl][:, 2048:4096], in_=w32_d[hi][:, 2048:4096]),
                 w=[f"scr{sl}g"], dsem=f"d_cast{hi}")
        else:
            P.op("pool", lambda e, hi=hi, sl=sl: e.dma_start(out=wscr[sl], in_=w32_d[hi]),
                 w=[f"scr{sl}"], dsem=f"d_cast{hi}")

    if dbg == "pro":
        pass
    P.barrier(skip="d_cast")
    pst.close()

    xres[1] = sb("xres1", [128, 4, D])
    xlh = sb("xlh", [128, 4, 3 + TB])
    ffnhalo = sb("ffnhalo", [128, 2, 44, 2])
    G(lambda e: e.memset(xlh[:, :, 0:3], 0.0), [], [f"xl{i}" for i in range(4)])
    G(lambda e: e.memset(ffnhalo[:, :, :, :], 0.0), [], ["ffnhalo0", "ffnhalo1"])
    gl = sb("gl", [128, 4, TB], BF16)
    zq = sb("zq", [128, 4, 512])
    S_sb = sb("S_sb", [128, 4, 8, 130], BF16)
    ymix = sb("ymix", [128, 8, TB], BF16)
    ltb = sb("lt", [128, 6, TB + 4])
    lt = ltb[:, :, 0:TB]
    xcb = sb("xcb", [128, 2, TB], BF16)
    hlb = sb("hl", [128, 2, TB + 4])
    hl = hlb[:, :, 0:TB]
    gated = sb("gated", [128, 22, TB], BF16)
    q_sb = ymix
    pT = gated[:, 0:8, :].rearrange("p (h m) t -> p h m t", h=4)
    o_sb = gated[:, 8:16, :]
    u_sb = gated[:, 16:20, :]
    ygelu = gated[:, 0:4, :]
    rden = hl
    zbuf = ltb[:, 0:2, 0:TB + 2]
    cv = lt[:, 2:4, :]
    cg = lt[:, 4:6, :]
    gg = cg
    qinit = sb("qinit", [128, 3, 16])
    hc = sb("hc", [128, 44, 2])
    hc2 = sb("hc2", [128, 44])
    G(lambda e: e.memset(S_sb[:, :, :, :], 0.0), [], ["S_sb0", "S_sb1", "S_sb2", "S_sb3"])

    rr = {"ps": 0, "ps6": 0}

    def nextbank():
        b = rr["ps"] % 4
        rr["ps"] += 1
        return b

    def nextbank6():
        b = rr["ps6"] % 6
        rr["ps6"] += 1
        return b

    hnames = [f"hfm{kc}" for kc in range(8)]

    def add_resid(xb, a, nh, bi):
        xs = xres[xb][:, a, nh * 512:(nh + 1) * 512]
        V(lambda e: e.tensor_tensor(xs, xs, psb[bi][:, :], ALU.add), [f"ps{bi}", f"x{xb}a{a}"], [f"x{xb}a{a}"])

    def proj_tok(lhs_tile, lhs_names, slab_ids, xb, bankfn):
        for nh in range(2):
            st, sn = load_slab(slab_ids[nh])
            for a in range(4):
                bi = bankfn()
                pairs = [(lhs_tile[:, kc, a * 128:(a + 1) * 128], st[:, kc * 512:(kc + 1) * 512]) for kc in range(8)]
                mm_group(psb[bi][:, :], pairs, [sn] + lhs_names, [f"ps{bi}"])
                add_resid(xb, a, nh, bi)

    def fm_proj(st, sn, mt, evac, bankfn=None):
        bi = (bankfn or nextbank)()
        pairs = [(st[:, kc * 512 + mt * 128: kc * 512 + (mt + 1) * 128], hfm[:, kc, :]) for kc in range(8)]
        mm_group(psb[bi][:, :], pairs, [sn] + hnames, [f"ps{bi}"])
        evac(bi)

    u4 = u_sb[:, :, :].rearrange("p k (c j) -> p k c j", j=4)

    def v4(ap):
        return ap.rearrange("p (s c) -> p s c", s=4)

    def s5_qinit():
        q0, q1, q2 = qinit[:, 0, :], qinit[:, 1, :], qinit[:, 2, :]
        cr, ci = s5carry_r[:, :], s5carry_i[:, :]
        er, ei, r4 = E4r[:, :], E4i[:, :], R4[:, :]
        V(lambda e: e.tensor_tensor(q0, cr, er, ALU.mult), ["s5c", "E4"], ["qi0"])
        V(lambda e: e.tensor_tensor(q2, ci, ei, ALU.mult), ["s5c", "E4"], ["qi2"])
        V(lambda e: e.tensor_tensor(q1, cr, ei, ALU.mult), ["s5c", "E4"], ["qi1"])
        V(lambda e: e.tensor_tensor(q0, q0, q2, ALU.subtract), ["qi0", "qi2"], ["qi0"])
        V(lambda e: e.tensor_tensor(q2, ci, er, ALU.mult), ["s5c", "E4", "qi0"], ["qi2"])
        V(lambda e: e.tensor_tensor(q0, q0, r4, ALU.mult), ["qi0", "R4"], ["qi0"])
        V(lambda e: e.tensor_tensor(q1, q1, q2, ALU.add), ["qi1", "qi2"], ["qi1"])
        V(lambda e: e.tensor_tensor(q1, q1, r4, ALU.mult), ["qi1", "R4"], ["qi1"])

    def s5_B_pe(k):
        st, sn = load_slab(SL_S5W + k)
        for half, bi in ((0, 4), (1, 5)):
            for s_ in range(4):
                sg = half * 4 + s_
                pairs = [(st[:, (sg * 4 + j) * 128:(sg * 4 + j + 1) * 128], u4[:, k, :, j]) for j in range(4)]
                mm_group(psb[bi][:, s_ * 128:(s_ + 1) * 128], pairs, [sn, f"gated{16 + k}"], [f"ps{bi}"])

    def s5_B_steps(k):
        cT = cosT[:, k * 4:(k + 1) * 4, :]
        sT = sinT[:, k * 4:(k + 1) * 4, :]
        Xr = v4(psb[4][:, :])
        Xi = v4(psb[5][:, :])
        Zr, Zi, Qr, Qi = (zq[:, i, :] for i in range(4))
        tmp = lt[:, 0, :]
        ks = slice(k * 4, (k + 1) * 4)
        q0, q1 = qinit[:, 0, ks], qinit[:, 1, ks]
        cr, ci = s5carry_r[:, ks], s5carry_i[:, ks]
        Rk = Rtab[:, k * 4:(k + 1) * 4, :].rearrange("p s c -> p (s c)")
        Sre = S_sb[:, k, 0:4, 1:129]
        Sim = S_sb[:, k, 4:8, 1:129]
        sname = f"S_sb{k}"
        ta, tb_ = lt[:, 2, :], lt[:, 3, :]
        TT = lambda o, a, b, op, r, w: (lambda: V(lambda e: e.tensor_tensor(o, a, b, op), r, w))
        return [
            TT(v4(Zr), Xr, cT, ALU.mult, ["ps4", "tab"], ["zq0"]),
            TT(v4(tmp), Xi, sT, ALU.mult, ["ps5", "tab"], ["lt0"]),
            TT(Zr, Zr, tmp, ALU.add, ["zq0", "lt0"], ["zq0"]),
            TT(v4(Zi), Xi, cT, ALU.mult, ["ps5", "tab"], ["zq1"]),
            TT(v4(tmp), Xr, sT, ALU.mult, ["ps4", "tab"], ["lt0"]),
            TT(Zi, Zi, tmp, ALU.subtract, ["zq1", "lt0"], ["zq1"]),
            TT(v4(Zr)[:, :, 0], v4(Zr)[:, :, 0], q0, ALU.add, ["zq0", "qi0"], ["zq0"]),
            TT(v4(Zi)[:, :, 0], v4(Zi)[:, :, 0], q1, ALU.add, ["zq1", "qi1"], ["zq1"]),
            lambda: V(lambda e: e.tensor_tensor_scan(Qr, Rk, Zr, 0.0, op0=ALU.mult, op1=ALU.add), ["zq0", "Rtab"], ["zq2"]),
            lambda: V(lambda e: e.tensor_tensor_scan(Qi, Rk, Zi, 0.0, op0=ALU.mult, op1=ALU.add), ["zq1", "Rtab"], ["zq3"]),
            lambda: V(lambda e: e.tensor_copy(S_sb[:, k, :, 0:1], S_sb[:, k, :, 128:129]), [sname], [sname]),
            TT(v4(ta), v4(Qr), cT, ALU.mult, ["zq2", "tab"], ["lt2"]),
            TT(v4(tb_), v4(Qi), sT, ALU.mult, ["zq3", "tab"], ["lt3"]),
            TT(Sre, v4(ta), v4(tb_), ALU.subtract, ["lt2", "lt3"], [sname]),
            TT(cr, v4(ta)[:, :, 127], v4(tb_)[:, :, 127], ALU.subtract, ["lt2", "lt3"], ["s5c"]),
            TT(v4(ta), v4(Qr), sT, ALU.mult, ["zq2", "tab"], ["lt2"]),
            TT(v4(tb_), v4(Qi), cT, ALU.mult, ["zq3", "tab"], ["lt3"]),
            TT(Sim, v4(ta), v4(tb_), ALU.add, ["lt2", "lt3"], [sname]),
            TT(ci, v4(ta)[:, :, 127], v4(tb_)[:, :, 127], ALU.add, ["lt2", "lt3"], ["s5c"]),
        ]

    def lru_steps(mt):
        xl = xlh[:, mt, :]
        xn = f"xl{mt}"
        (xc, xcn), (ra, ran), (i_, in_) = [(lt[:, r, :], f"lt{r}") for r in (4, 5, 1)]
        m_, mn = xc, xcn
        xb_ = xcb[:, 0, :]
        xbn = "xcb0"
        hb = hl[:, mt % 2, :]
        hn = f"hl{mt % 2}"
        bk = {}
        st = []
        st.append(lambda: A(lambda e: e.activation(xc, xl[:, 3:3 + TB], AF.Identity, scale=cp(C_LCW + 3 * 4 + mt), bias=cp(C_LCB + mt)),
                            [xn, "colp"], [xcn]))

        def tap(t_):
            return lambda: V(lambda e: e.scalar_tensor_tensor(xc, xl[:, t_:t_ + TB], cp(C_LCW + t_ * 4 + mt), xc, op0=ALU.mult, op1=ALU.add),
                             [xn, xcn], [xcn])
        for t_ in range(3):
            st.append(tap(t_))
        st.append(lambda: G(lambda e: e.tensor_copy(xl[:, 0:3], xl[:, TB:TB + 3]), [xn], [xn]))
        st.append(lambda: A(lambda e: e.copy(xb_, xc), [xcn], [xbn]))

        def gates():
            bk["b1"], bk["b2"] = nextbank(), nextbank()
            mm_group(psb[bk["b1"]][:, :], [(lruw[:, mt, :], xb_)], ["lruw", xbn], [f"ps{bk['b1']}"])
            mm_group(psb[bk["b2"]][:, :], [(lruw[:, 4 + mt, :], xb_)], ["lruw", xbn], [f"ps{bk['b2']}"])
        st.append(gates)
        st.append(lambda: A(lambda e: e.activation(ra, psb[bk["b1"]][:, :], AF.Tanh, scale=0.5, bias=hbias[:, mt:mt + 1]),
                            [f"ps{bk['b1']}", "hbias"], [ran]))
        st.append(lambda: A(lambda e: e.activation(i_, psb[bk["b2"]][:, :], AF.Tanh, scale=0.5, bias=hbias[:, 4 + mt:5 + mt]),
                            [f"ps{bk['b2']}", "hbias"], [in_]))
        st.append(lambda: V(lambda e: e.scalar_tensor_tensor(i_, i_, 1.0, xc, op0=ALU.add, op1=ALU.mult), [in_, xcn], [in_]))
        st.append(lambda: A(lambda e: e.activation(ra, ra, AF.Exp, scale=cnegh[:, mt:mt + 1], bias=cnegh[:, mt:mt + 1]), [ran, "cnegh"], [ran]))
        st.append(lambda: A(lambda e: e.activation(m_, ra, AF.Square), [ran, in_], [mn]))
        st.append(lambda: A(lambda e: e.activation(m_, m_, AF.Sqrt, scale=-1.0, bias=onec[:, :]), [mn, "onec"], [mn]))
        st.append(lambda: V(lambda e: e.scalar_tensor_tensor(i_, i_, 0.5, m_, op0=ALU.mult, op1=ALU.mult), [in_, mn], [in_]))
        st.append(lambda: V(lambda e: e.tensor_tensor_scan(hb, ra, i_, lrucarry[:, mt:mt + 1], op0=ALU.mult, op1=ALU.add),
                            [ran, in_, "lrucarry"], [hn]))
        st.append(lambda: V(lambda e: e.tensor_copy(lrucarry[:, mt:mt + 1], hb[:, TB - 1:TB]), [hn], ["lrucarry"]))
        prod = lambda: V(lambda e: e.tensor_tensor(ymix[:, 4 + mt, :], hb, gl[:, mt, :], ALU.mult), [hn, f"gl{mt}"], [f"ymix{4 + mt}"])
        return st, prod

    def zip_steps(a, b):
        for i in range(max(len(a), len(b))):
            if i < len(a):
                a[i]()
            if i < len(b):
                b[i]()

    def s5_D(k, stK, snK):
        st, sn = load_slab(SL_S5V + k)
        bi = nextbank()
        yv = psb[bi][:, :].rearrange("p (c i) -> p c i", i=4)
        for i in range(4):
            pairs = [(st[:, (sg * 4 + i) * 128:(sg * 4 + i + 1) * 128], S_sb[:, k, sg, 0:128]) for sg in range(8)]
            pairs += [(stK[:, (k * 4 + (i - j)) * 128:(k * 4 + (i - j) + 1) * 128], u4[:, k, :, j]) for j in range(i + 1)]
            mm_group(yv[:, :, i], pairs, [sn, snK, f"S_sb{k}", f"gated{16 + k}"], [f"ps{bi}"])
        A(lambda e: e.activation(ygelu[:, k, :], psb[bi][:, :], AF.Gelu), [f"ps{bi}"], [f"gated{k}"])

    def glu_tile(mt, stK, snK):
        bi = nextbank()
        pairs = [(stK[:, 2048 + kc * 512 + mt * 128: 2048 + kc * 512 + (mt + 1) * 128], ygelu[:, kc, :]) for kc in range(4)]
        mm_group(psb[bi][:, :], pairs, [snK] + [f"gated{k}" for k in range(4)], [f"ps{bi}"])
        gt = lt[:, 2 + mt % 2, :]
        gn = f"lt{2 + mt % 2}"
        A(lambda e: e.activation(gt, psb[bi][:, :], AF.Sigmoid, bias=cp(C_BGLU + mt)), [f"ps{bi}", "colp"], [gn])
        V(lambda e: e.tensor_tensor(ymix[:, mt, :], ygelu[:, mt, :], gt, ALU.mult), [f"gated{mt}", gn], [f"ymix{mt}"])

    def attn_head(hd):
        def sc(mc):
            bi = nextbank6()
            pairs = [(Kfm[:, hd * 2 + c2, mc * 128:(mc + 1) * 128], q_sb[:, hd * 2 + c2, :]) for c2 in range(2)]
            mm_group(psb[bi][:, :], pairs, ["Kfm", f"ymix{hd * 2}", f"ymix{hd * 2 + 1}"], [f"ps{bi}"])
            A(lambda e: e.activation(pT[:, hd, mc, :], psb[bi][:, :], AF.Exp), [f"ps{bi}"], [f"gated{2 * hd}", f"gated{2 * hd + 1}"])
        sc(0)
        sc(1)

    def attn_tail(hd):
        bd = nextbank6()
        mm_group(psb[bd][:, :], [(onesb[:, :], pT[:, hd, mc, :]) for mc in range(2)], ["onesb", f"gated{2 * hd}", f"gated{2 * hd + 1}"], [f"ps{bd}"])
        rd = rden[:, hd % 2, :]
        rn = f"hl{hd % 2}"
        V(lambda e: e.reciprocal(rd, psb[bd][:, :]), [f"ps{bd}"], [rn])

        def pv(j):
            bi = nextbank6()
            pairs = [(Vtok[:, mc, hd * 256 + j * 128: hd * 256 + (j + 1) * 128], pT[:, hd, mc, :]) for mc in range(2)]
            mm_group(psb[bi][:, :], pairs, ["Vtok", f"gated{2 * hd}", f"gated{2 * hd + 1}"], [f"ps{bi}"])
            V(lambda e: e.tensor_tensor(o_sb[:, hd * 2 + j, :], psb[bi][:, :], rd, ALU.mult), [f"ps{bi}", rn], [f"gated{8 + hd * 2 + j}"])
        pv(0)
        pv(1)

    def ffn_tile_mm(st, sn, sa, tt, isg, tb):
        vt = 2 * sa + tt
        ch = vt + 22 * isg
        col = isg * 256 + tt * 128
        bi = nextbank6()
        pairs = [(st[:, kc * 512 + col: kc * 512 + col + 128], hfm[:, kc, :]) for kc in range(8)]
        mm_group(psb[bi][:, :], pairs, [sn] + hnames, [f"ps{bi}"])
        ps = psb[bi]
        pn = f"ps{bi}"
        dst = (cv if isg == 0 else cg)[:, tt, :]
        dn = f"lt{2 + 2 * isg + tt}"
        hold = ffnhalo[:, tb % 2, ch, :]
        hnew = ffnhalo[:, (tb + 1) % 2, ch, :]
        wcol = lambda t_: cp(C_FCW + t_ * 44 + ch)
        A(lambda e: e.activation(dst, ps[:, :], AF.Identity, scale=wcol(2), bias=cp(C_FCB + ch)), [pn, "colp"], [dn])
        A(lambda e: e.copy(hnew, ps[:, TB - 2:TB]), [pn], [f"ffnhalo{(tb + 1) % 2}"])
        return ps, pn, dst, dn, wcol, hold, f"ffnhalo{tb % 2}"

    def ffn_tap(t_, ps, pn, dst, dn, wcol, hold, hn):
        sh = 2 - t_
        V(lambda e: e.scalar_tensor_tensor(dst[:, sh:TB], ps[:, 0:TB - sh], wcol(t_), dst[:, sh:TB], op0=ALU.mult, op1=ALU.add),
          [pn, dn, "colp"], [dn])

    def ffn_halo_prep(tb):
        hold = ffnhalo[:, tb % 2, :, :]
        hn = f"ffnhalo{tb % 2}"
        W0, W1 = cp(C_FCW, 44), cp(C_FCW + 44, 44)
        V(lambda e: e.tensor_tensor(hc[:, :, 1], hold[:, :, 1], W0, ALU.mult), [hn, "colp"], ["hc"])
        V(lambda e: e.tensor_tensor(hc[:, :, 0], hold[:, :, 0], W0, ALU.mult), [hn, "colp"], ["hc"])
        V(lambda e: e.tensor_tensor(hc2[:, :], hold[:, :, 1], W1, ALU.mult), [hn, "colp"], ["hc2"])
        V(lambda e: e.tensor_tensor(hc[:, :, 0], hc[:, :, 0], hc2[:, :], ALU.add), ["hc", "hc2"], ["hc"])

    def ffn_halo_add(ch, dst, dn):
        G(lambda e: e.tensor_tensor(dst[:, 0:2], dst[:, 0:2], hc[:, ch, :], ALU.add), ["hc", dn], [dn])

    def ffn_pair(st, sn, sa, tt, tb, prev_tail):
        vt = 2 * sa + tt
        tiles = [ffn_tile_mm(st, sn, sa, tt, 0, tb), ffn_tile_mm(st, sn, sa, tt, 1, tb)]
        if prev_tail is not None:
            prev_tail()
        for t_ in (1, 0):
            for tl in tiles:
                ffn_tap(t_, *tl)
        for isg, tl in enumerate(tiles):
            ffn_halo_add(vt + 22 * isg, tl[2], tl[3])

        def tail():
            A(lambda e: e.activation(cg[:, tt, :], cg[:, tt, :], AF.Gelu), [f"lt{4 + tt}"], [f"lt{4 + tt}"])
            G(lambda e: e.tensor_tensor(gated[:, vt, :], cg[:, tt, :], cv[:, tt, :], ALU.mult), [f"lt{4 + tt}", f"lt{2 + tt}"], [f"gated{vt}"])
        return tail

    def down_group(xb, nh, sg3, kc0, nk, a, bi):
        st, sn = down_group.cur

        def fn(e):
            ins = None
            for kk in range(nk):
                ins = e.matmul(psb[bi][:, :], gated[:, kc0 + kk, a * 128:(a + 1) * 128], st[:, kk * 512:(kk + 1) * 512],
                               start=(sg3 == 0 and kk == 0), stop=(sg3 == 2 and kk == nk - 1))
            return ins
        P.op("pe", fn, r=[sn] + [f"gated{kc0 + kk}" for kk in range(nk)], w=[f"ps{bi}"])

    def final_norm(xb, a):
        xa = xres[xb][:, a, :]
        sa_, ra_ = ss[:, 4 + a:5 + a], rstd[:, 4 + a:5 + a]
        xn = f"x{xb}a{a}"
        A(lambda e: e.activation(junk[:, :], xa, AF.Square, accum_out=sa_), [xn], ["junk", f"ssf{a}"])
        A(lambda e: e.activation(ra_, sa_, AF.Sqrt, scale=1.0 / D, bias=epsc[:, :]), [f"ssf{a}", "epsc"], [f"rstdf{a}"])
        V(lambda e: e.reciprocal(ra_, ra_), [f"rstdf{a}"], [f"rstdf{a}"])
        V(lambda e: e.scalar_tensor_tensor(xa, xa, ra_, gfin[:, :], op0=ALU.mult, op1=ALU.mult), [xn, f"rstdf{a}", "gfin"], [xn])

    x_t = x_d.rearrange("(b a p) d -> b p a d", a=4, p=128)
    out_t = out_d.rearrange("(b a p) d -> b p a d", a=4, p=128)

    def load_x(tb):
        xb = tb % 2
        P.op("sp", lambda e: e.dma_start(out=xres[xb][:, :, :], in_=x_t[tb]),
             w=[f"x{xb}a{a}" for a in range(4)], dsem=f"d_x{xb}")

    def store_out(tb):
        xb = tb % 2
        P.op("sp", lambda e: e.dma_start(out=out_t[tb], in_=xres[xb][:, :, :]),
             r=[f"x{xb}a{a}" for a in range(4)], dsem=f"d_o{xb}")

    def win_evac(grp, mt):
        def ev(bi):
            if grp == 0:
                A(lambda e: e.copy(u_sb[:, mt, :], psb[bi][:, :]), [f"ps{bi}"], [f"gated{16 + mt}"])
            elif grp == 1:
                A(lambda e: e.copy(xlh[:, mt, 3:3 + TB], psb[bi][:, :]), [f"ps{bi}"], [f"xl{mt}"])
            else:
                A(lambda e: e.activation(gl[:, mt, :], psb[bi][:, :], AF.Gelu), [f"ps{bi}"], [f"gl{mt}"])
        return ev

    def q_evac(half, mt):
        def ev(bi):
            A(lambda e: e.activation(q_sb[:, half * 4 + mt, :], psb[bi][:, :], AF.Copy, scale=0.0625),
              [f"ps{bi}"], [f"ymix{half * 4 + mt}"])
        return ev

    def finish_block(tb):
        if stage >= 6:
            for a in range(4):
                final_norm(tb % 2, a)

    def block_body(tb):
        xb = tb % 2
        if tb == 0:
            norm_stats(xres[xb], f"x{xb}a", 4)
        norm_transposes(4, C_G1, hfm, "hfm", TB)
        st0, sn0 = load_slab(SL_WIN)
        for mt in range(4):
            fm_proj(st0, sn0, mt, win_evac(0, mt))
        if tb > 0:
            finish_block(tb - 1)
        s5_qinit()
        wslabs = {}

        def win_tiles(grp, mts):
            if grp not in wslabs:
                wslabs[grp] = load_slab(SL_WIN + grp)
            st, sn = wslabs[grp]
            for mt in mts:
                fm_proj(st, sn, mt, win_evac(grp, mt))
        s5_B_pe(0)
        zip_steps(s5_B_steps(0), [])
        win_tiles(1, [0, 1])
        prods = {}
        for k in range(1, 4):
            s5_B_pe(k)
            lst, prods[k - 1] = lru_steps(k - 1)
            zip_steps(s5_B_steps(k), lst)
            if k == 1:
                win_tiles(1, [2, 3])
            elif k == 2:
                win_tiles(2, [0, 1])
                prods[0]()
                prods[1]()
            else:
                win_tiles(2, [2, 3])
        if stage < 2:
            return
        stK, snK = load_slab(SL_KG)
        lst, prods[3] = lru_steps(3)
        cuts = [0, 6, 11, 15, len(lst)]
        for k in range(4):
            s5_D(k, stK, snK)
            for f in lst[cuts[k]:cuts[k + 1]]:
                f()
        prods[2]()
        prods[3]()
        for mt in range(4):
            glu_tile(mt, stK, snK)
        if tb > 0:
            store_out(tb - 1)
        if tb + 1 < nblk:
            load_x(tb + 1)
        if stage < 3:
            return
        proj_tok(ymix, [f"ymix{i}" for i in range(8)], [SL_WOUT, SL_WOUT + 1], xb, nextbank6)
        if stage < 4:
            return
        rmsnorm_to_hT(xres[xb], f"x{xb}a", 4, C_G2, hfm, "hfm", TB)
        for half in range(2):
            st, sn = load_slab(SL_WQ + half)
            for mt in range(4):
                fm_proj(st, sn, mt, q_evac(half, mt), nextbank6)
        for hd in range(4):
            attn_head(hd)
            attn_tail(hd)
        proj_tok(o_sb, [f"gated{8 + i}" for i in range(8)], [SL_WO, SL_WO + 1], xb, nextbank6)
        if stage < 5:
            return
        rmsnorm_to_hT(xres[xb], f"x{xb}a", 4, C_G3, hfm, "hfm", TB)
        ffn_halo_prep(tb)
        ptail = None
        for sa in range(11):
            st, sn = load_slab(SL_UP + sa)
            for tt in range(2):
                ptail = ffn_pair(st, sn, sa, tt, tb, ptail)
        ptail()
        if tb + 1 < nblk and stage >= 6:
            norm_stats(xres[1 - xb], f"x{1 - xb}a", 4)
        for nh in range(2):
            kc0 = 0
            dbanks = [nextbank6() for _ in range(4)]
            for sg3 in range(3):
                nk = 8 if sg3 < 2 else 6
                down_group.cur = load_slab(SL_DN + nh * 3 + sg3)
                for a in range(4):
                    down_group(xb, nh, sg3, kc0, nk, a, dbanks[a])
                kc0 += nk
            for a in range(4):
                add_resid(xb, a, nh, dbanks[a])

    load_x(0)
    for tb in range(nblk):
        block_body(tb)
    finish_block(nblk - 1)
    store_out(nblk - 1)

    if dbg is not None:
        P.barrier()
        for i_, (ap_, a_, b_) in enumerate(dbg(locals())):
            P.op("pool", lambda e, i_=i_, ap_=ap_, a_=a_, b_=b_: e.dma_start(
                out=dbg_d[i_][:, 0:a_ * b_].rearrange("p (a b) -> p a b", a=a_), in_=ap_), dsem=f"d_dbg{i_}")

    P.barrier()
    with nc.Block() as block:
        P.emit(block)
    es.close()
    return nc


def _slab_kc(w, c0, ncols=512, kc0=0, nkc=8):
    out = np.zeros((128, 8, ncols), np.float32)
    K = w.shape[0]
    for kc in range(nkc):
        r0 = (kc0 + kc) * 128
        if r0 >= K:
            break
        out[:, kc, :] = w[r0:r0 + 128, c0:c0 + ncols]
    return out.reshape(128, 8 * ncols)


def host_layout(inp):
    f = lambda k: np.asarray(inp[k], np.float32)
    w_in, w_out = f("w_in")[0], f("w_out")[0]
    wq, wk, wv, wo = f("xa_w_q")[0], f("xa_w_k")[0], f("xa_w_v")[0], f("xa_w_o")[0]
    wup, wdn, glu = f("ffn_w_up")[0], f("ffn_w_down")[0], f("s5_w_glu")[0]
    slabs = {}
    for g in range(3):
        slabs[SL_WIN + g] = _slab_kc(w_in, g * 512)
    kg = np.zeros((128, 4096), np.float32)
    kg[:, 2048:] = _slab_kc(glu, 0, 512, 0, 4)[:, :2048]
    slabs[SL_KG] = kg
    for h in range(2):
        slabs[SL_WOUT + h] = _slab_kc(w_out, h * 512)
        slabs[SL_WQ + h] = _slab_kc(wq, h * 512)
        slabs[SL_WO + h] = _slab_kc(wo, h * 512)
        slabs[SL_WK + h] = _slab_kc(wk, h * 512)
        slabs[SL_WV + h] = _slab_kc(wv, h * 512)
    for sa in range(11):
        cols = np.concatenate([np.arange(sa * 256, sa * 256 + 256), 2816 + np.arange(sa * 256, sa * 256 + 256)])
        slabs[SL_UP + sa] = _slab_kc(wup[:, cols], 0)
    for nh in range(2):
        for g3 in range(3):
            slabs[SL_DN + nh * 3 + g3] = _slab_kc(wdn, nh * 512, 512, g3 * 8, 8 if g3 < 2 else 6)
    w32 = np.stack([slabs[s] for s in HOST_SLABS]).astype(np.float32)

    colp = np.zeros((128, NCOL), np.float32)

    def putcols(c0, vec):
        v = np.asarray(vec, np.float32).reshape(-1, 128).T
        colp[:, c0:c0 + v.shape[1]] = v
    putcols(C_G1, f("ln_mix_g")[0])
    putcols(C_G2, f("ln_xa_g")[0])
    putcols(C_G3, f("ln_ffn_g")[0])
    putcols(C_GMEM, f("mem_norm_g"))
    putcols(C_BGLU, f("s5_b_glu")[0])
    lcw = f("lru_conv_w")[0]
    for t in range(4):
        putcols(C_LCW + t * 4, lcw[t])
    putcols(C_LCB, f("lru_conv_b")[0])
    putcols(C_BA, f("lru_b_a")[0].reshape(-1))
    putcols(C_BX, f("lru_b_x")[0].reshape(-1))
    putcols(C_LAM, f("lru_lam")[0].reshape(-1))
    fcw = f("ffn_conv_w")[0]
    for t in range(3):
        putcols(C_FCW + t * 44, fcw[t])
    putcols(C_FCB, f("ffn_conv_b")[0])
    putcols(C_DS5, f("s5_d")[0].reshape(-1))
    gfin = np.ascontiguousarray(np.broadcast_to(f("final_norm_g")[None, :], (128, D)))

    def gq(arr):
        a = arr.reshape(4, 8, 4, 16)
        return np.ascontiguousarray(a.transpose(1, 3, 0, 2).reshape(128, 16))
    lre, lim = f("s5_lam_re")[0], f("s5_lam_im")[0]
    ldt = np.broadcast_to(f("s5_log_dt")[0][:, None], (32, 64))
    s5par = np.stack([gq(lre), gq(lim), gq(ldt)], axis=1).astype(np.float32)

    def cb(b):
        a = b.reshape(4, 8, 4, 16, 16)
        return np.ascontiguousarray(a.transpose(1, 3, 0, 2, 4).reshape(128, 16, 16))
    bcc = np.stack([cb(f("s5_b_re")[0]), cb(f("s5_b_im")[0]),
                    cb(np.ascontiguousarray(f("s5_c_re")[0].transpose(0, 2, 1))),
                    cb(np.ascontiguousarray(f("s5_c_im")[0].transpose(0, 2, 1)))], axis=1).astype(np.float32)
    mask8 = np.zeros((128, 2, 8), np.float32)
    for p_ in range(128):
        mask8[p_, 0, p_ // 16] = 1.0
        mask8[p_, 1, p_ // 16] = -1.0
    lruw = np.zeros((128, 8, 128), np.float32)
    wa, wx = f("lru_w_a")[0], f("lru_w_x")[0]
    for mt in range(4):
        for hh in range(2):
            lruw[hh * 64:(hh + 1) * 64, mt, hh * 64:(hh + 1) * 64] = wa[2 * mt + hh]
            lruw[hh * 64:(hh + 1) * 64, 4 + mt, hh * 64:(hh + 1) * 64] = wx[2 * mt + hh]
    shared = {"w32": w32, "colp": colp, "gfin": gfin, "s5par": s5par, "bcc": bcc, "mask8": mask8, "lruw": lruw,
              "ident": np.eye(128, dtype=np.float32).astype(ml_dtypes.bfloat16),
              "identf": np.eye(128, dtype=np.float32)}
    return shared


def kernel(**inputs):
    shared = host_layout(inputs)
    x = np.asarray(inputs["x"], np.float32)
    mem = np.asarray(inputs["mem"], np.float32)
    nc = build(SEQ // TB)
    in_maps = []
    for c in range(8):
        m = dict(shared)
        m["x"] = np.ascontiguousarray(x[c])
        m["mem"] = np.ascontiguousarray(mem[c])
        in_maps.append(m)
    res = run_bass_kernel_spmd(nc, in_maps, core_ids=list(range(8)))
    return np.stack([np.asarray(r["out"], np.float32) for r in res.results], axis=0)
```

```python
import math
from contextlib import ExitStack
import numpy as np
import ml_dtypes
import concourse.bass as bass
import concourse.mybir as mybir
from concourse.bass_utils import run_bass_kernel_spmd

F32 = mybir.dt.float32
BF16 = mybir.dt.bfloat16
ALU = mybir.AluOpType
AF = mybir.ActivationFunctionType

SEQ = 4096
TB = 512
D = 1024
NS = 5
PI = math.pi

SL_WIN = 0
SL_S5W = 3
SL_S5V = 7
SL_KG = 11
SL_WOUT = 12
SL_WQ = 14
SL_WO = 16
SL_UP = 18
SL_DN = 29
SL_WK = 35
SL_WV = 37
NSLAB = 39
HOST_SLABS = [0, 1, 2, 11, 12, 13, 14, 15, 16, 17] + list(range(18, 35)) + [35, 36, 37, 38]

C_G1, C_G2, C_G3, C_GMEM = 0, 8, 16, 24
C_BGLU = 32
C_LCW = 36
C_LCB = 52
C_BA = 56
C_BX = 60
C_LAM = 64
C_FCW = 68
C_FCB = 200
C_DS5 = 244
NCOL = 248


class Prog:
    ENG = ("pe", "act", "dve", "pool", "sp")

    def __init__(self, nc, es):
        self.nc = nc
        self.es = es
        self.q = {e: [] for e in self.ENG}
        self.cnt = {}
        self.sems = {}
        self.seen = {e: {} for e in self.ENG}
        self.lastw = {}
        self.readers = {}

    def sem(self, key):
        if key not in self.sems:
            self.sems[key] = self.es.enter_context(self.nc.semaphore("s_" + key))
            self.cnt[key] = 0
        return self.sems[key]

    def op(self, eng, fn, r=(), w=(), dsem=None):
        deps = {}

        def add(tok):
            if tok is None:
                return
            k, v = tok
            if deps.get(k, 0) < v:
                deps[k] = v
        for b in r:
            add(self.lastw.get(b))
        for b in w:
            add(self.lastw.get(b))
            for k, v in self.readers.get(b, {}).items():
                add((k, v))
        waits = []
        for k, v in deps.items():
            if eng == "pe" and k == "pe":
                continue
            if self.seen[eng].get(k, 0) >= v:
                continue
            self.seen[eng][k] = v
            waits.append((k, v))
        if dsem is None:
            key, inc = eng, 1
        else:
            key, inc = dsem, 16
        self.sem(key)
        self.cnt[key] += inc
        tok = (key, self.cnt[key])
        self.q[eng].append((waits, fn, key, inc))
        for b in r:
            self.readers.setdefault(b, {})[key] = tok[1]
        for b in w:
            self.lastw[b] = tok
            self.readers[b] = {}
        return tok

    def barrier(self, skip=None):
        for e in self.ENG:
            waits = []
            for k, v in self.cnt.items():
                if skip is not None and k.startswith(skip):
                    continue
                if v > 0 and self.seen[e].get(k, 0) < v:
                    self.seen[e][k] = v
                    waits.append((k, v))
            if waits:
                self.q[e].append((waits, None, None, 0))

    def emit(self, block):
        sems = self.sems

        def run(eng, lst):
            for waits, fn, key, inc in lst:
                for k, v in waits:
                    eng.wait_ge(sems[k], v)
                if fn is not None:
                    ins = fn(eng)
                    ins.then_inc(sems[key], inc)

        @block.tensor
        def _(e):
            run(e, self.q["pe"])

        @block.scalar
        def _(e):
            run(e, self.q["act"])

        @block.vector
        def _(e):
            run(e, self.q["dve"])

        @block.gpsimd
        def _(e):
            run(e, self.q["pool"])

        @block.sync
        def _(e):
            run(e, self.q["sp"])


def build(nblk=8, stage=6, dbg=None, pro_only=False):
    nc = bass.Bass("TRN2", target_bir_lowering=False)
    ntok = nblk * TB

    def din(name, shape, dt=F32):
        return nc.dram_tensor(name, list(shape), dt, kind="ExternalInput").ap()
    x_d = din("x", [ntok, D])
    mem_d = din("mem", [256, D])
    w32_d = din("w32", [len(HOST_SLABS), 128, 4096])
    colp_d = din("colp", [128, NCOL])
    gfin_d = din("gfin", [128, D])
    s5par_d = din("s5par", [128, 3, 16])
    bcc_d = din("bcc", [128, 4, 16, 16])
    mask8_d = din("mask8", [128, 2, 8])
    lruw_d = din("lruw", [128, 8, 128])
    ident_d = din("ident", [128, 128], BF16)
    identf_d = din("identf", [128, 128])
    out_d = nc.dram_tensor("out", [ntok, D], F32, kind="ExternalOutput").ap()
    wscr = nc.dram_tensor("wscr", [NSLAB, 128, 4096], BF16, kind="Internal").ap()
    dbg_d = None
    if dbg is not None:
        dbg_d = nc.dram_tensor("dbg", [16, 128, 4096], F32, kind="ExternalOutput").ap()

    es = ExitStack()
    P = Prog(nc, es)

    def sb(name, shape, dt=F32, stack=es):
        return stack.enter_context(nc.sbuf_tensor("sb_" + name, list(shape), dt))

    colp = sb("colp", [128, NCOL])
    gfin = sb("gfin", [128, D])
    ident = sb("ident", [128, 128], BF16)
    onesb = sb("onesb", [128, 128], BF16)
    lruw = sb("lruw", [128, 8, 128], BF16)
    cneg = sb("cneg", [128, 4])
    epsc = sb("epsc", [128, 1])
    hpic = sb("hpic", [128, 1])
    onec = sb("onec", [128, 1])
    hbias = sb("hbias", [128, 12])
    cnegh = sb("cnegh", [128, 4])
    cosT = sb("cosT", [128, 16, 128])
    sinT = sb("sinT", [128, 16, 128])
    Rtab = sb("Rtab", [128, 16, 128])
    R4 = sb("R4", [128, 16])
    E4r = sb("E4r", [128, 16])
    E4i = sb("E4i", [128, 16])
    Kfm = sb("Kfm", [128, 8, 256], BF16)
    Vtok = sb("Vtok", [128, 2, D], BF16)
    slots = [sb(f"slot{i}", [128, 4096], BF16) for i in range(NS)]
    xres = [sb("xres0", [128, 4, D]), None]
    hT = sb("hT", [128, 4, D], BF16)
    hfm = sb("hfm", [128, 8, TB], BF16)
    ss = sb("ss", [128, 8])
    rstd = sb("rstd", [128, 8])
    junk = sb("junk", [128, D], BF16)
    s5carry_r = sb("s5cr", [128, 16])
    s5carry_i = sb("s5ci", [128, 16])
    lrucarry = sb("lrucarry", [128, 4])
    psb = [es.enter_context(nc.psum_tensor(f"pp{i}", [128, 512], F32)) for i in range(6)]
    psT = [es.enter_context(nc.psum_tensor(f"ppT{i}", [128, 1024], BF16)) for i in range(2)]

    cp = lambda c, n=1: colp[:, c:c + n]

    slab_state = {"n": 0}

    def load_slab(idx, eng="sp"):
        i = slab_state["n"] % NS
        slab_state["n"] += 1
        name = f"slot{i}"
        P.op(eng, lambda e, i=i, idx=idx: e.dma_start(out=slots[i][:, :], in_=wscr[idx]),
             r=[f"scr{idx}", f"scr{idx}g"], w=[name], dsem=f"d_slot{i}")
        return slots[i], name

    def mm_group(out_ap, pairs, r, w):
        n = len(pairs)

        def fn(e):
            ins = None
            for j, (l, rr) in enumerate(pairs):
                ins = e.matmul(out_ap, l, rr, start=(j == 0), stop=(j == n - 1))
            return ins
        P.op("pe", fn, r=r, w=w)

    def V(fn, r, w):
        P.op("dve", fn, r=r, w=w)

    def A(fn, r, w):
        P.op("act", fn, r=r, w=w)

    def G(fn, r, w):
        P.op("pool", fn, r=r, w=w)

    pst = ExitStack()
    par = sb("par", [128, 3, 16], stack=pst)
    identf = sb("identf", [128, 128], stack=pst)
    bcc = sb("bcc", [128, 4, 16, 16], stack=pst)
    mask8 = sb("mask8", [128, 2, 8], stack=pst)
    cs = sb("cs", [128, 4, 16, 16], stack=pst)
    t1 = sb("t1", [128, 16, 128], stack=pst)
    lruw32 = t1[:, 0:8, :]
    t2 = sb("t2", [128, 16, 128], stack=pst)
    NB = sb("NB", [128, 4, 2, 16, 128], BF16, stack=pst)
    Vm = sb("Vm", [128, 2, 4, 8, 128], BF16, stack=pst)
    Wst = sb("Wst", [128, 1, 8, 4, 128], BF16, stack=pst)
    Kst = hT[:, 2:4, :].rearrange("p a (b c) -> p (a b) c", c=128).rearrange("p (k t) c -> p k t c", k=4)
    sm = xres[0][:, 2:4, :].rearrange("p a (b c) -> p (a b) c", c=16)
    memx = xres[0]
    memfm = hfm

    def ld(dst, src, name, eng="sp"):
        P.op(eng, lambda e: e.dma_start(out=dst, in_=src), w=[name], dsem="d_" + name)
    ld(colp[:, :], colp_d, "colp")
    ld(gfin[:, :], gfin_d, "gfin")
    ld(ident[:, :], ident_d, "ident")
    ld(identf[:, :], identf_d, "identf")
    ld(par[:, :, :], s5par_d, "par")
    ld(bcc[:, :, :, :], bcc_d, "bcc")
    ld(mask8[:, :, :], mask8_d, "mask8")
    ld(lruw32, lruw_d, "t1")
    P.op("sp", lambda e: e.dma_start(out=memx[:, 0:2, :], in_=mem_d.rearrange("(a p) d -> p a d", p=128)),
         w=["x0a0", "x0a1"], dsem="d_x0")

    G(lambda e: e.memset(onesb[:, :], 1.0), [], ["onesb"])
    G(lambda e: e.memset(epsc[:, :], 1e-6), [], ["epsc"])
    G(lambda e: e.memset(hpic[:, :], PI / 2), [], ["hpic"])
    G(lambda e: e.memset(onec[:, :], 1.0), [], ["onec"])
    G(lambda e: e.memset(lrucarry[:, :], 0.0), [], ["lrucarry"])
    G(lambda e: e.memset(s5carry_r[:, :], 0.0), [], ["s5c"])
    G(lambda e: e.memset(s5carry_i[:, :], 0.0), [], ["s5c"])
    V(lambda e: e.tensor_copy(lruw[:, :, :], lruw32), ["t1"], ["lruw"])

    A(lambda e: e.activation(cneg[:, :], cp(C_LAM, 4), AF.Exp, scale=-1.0), ["colp"], ["cneg"])
    A(lambda e: e.activation(cneg[:, :], cneg[:, :], AF.Ln, bias=onec[:, :]), ["cneg", "onec"], ["cneg"])
    V(lambda e: e.tensor_scalar_mul(cneg[:, :], cneg[:, :], -8.0), ["cneg"], ["cneg"])
    V(lambda e: e.tensor_scalar_mul(cnegh[:, :], cneg[:, :], 0.5), ["cneg"], ["cnegh"])
    V(lambda e: e.tensor_scalar_mul(hbias[:, 0:4], cp(C_BA, 4), 0.5), ["colp"], ["hbias"])
    V(lambda e: e.tensor_scalar_mul(hbias[:, 4:8], cp(C_BX, 4), 0.5), ["colp"], ["hbias"])
    V(lambda e: e.tensor_scalar_mul(hbias[:, 8:12], cp(C_BGLU, 4), 0.5), ["colp"], ["hbias"])

    smn = {"i": 0}

    def S(name=None):
        i = smn["i"]
        smn["i"] += 1
        return sm[:, i, :], f"sm{i}"

    def vtt(o, a, b, op):
        (oa, on), (aa, an), (ba, bn) = o, a, b
        V(lambda e: e.tensor_tensor(oa, aa, ba, op), [an, bn], [on])

    def cmul(a_r, a_i, b_r, b_i):
        o_r, o_i, u1, u2 = S(), S(), S(), S()
        vtt(u1, a_r, b_r, ALU.mult)
        vtt(u2, a_i, b_i, ALU.mult)
        vtt(o_r, u1, u2, ALU.subtract)
        vtt(u1, a_r, b_i, ALU.mult)
        vtt(u2, a_i, b_r, ALU.mult)
        vtt(o_i, u1, u2, ALU.add)
        return o_r, o_i

    lre = (par[:, 0, :], "par")
    lim = (par[:, 1, :], "par")
    ldt = (par[:, 2, :], "par")
    dt_ = S()
    A(lambda e: e.activation(dt_[0], ldt[0], AF.Exp), ["par"], [dt_[1]])
    zr, zi = S(), S()
    vtt(zr, lre, dt_, ALU.mult)
    vtt(zi, lim, dt_, ALU.mult)
    mag = S()
    A(lambda e: e.activation(mag[0], zr[0], AF.Exp), [zr[1]], [mag[1]])

    sn0, cs0 = S(), S()
    A(lambda e: e.activation(sn0[0], zi[0], AF.Sin, scale=1.0 / 16), [zi[1]], [sn0[1]])
    A(lambda e: e.activation(cs0[0], zi[0], AF.Sin, scale=1.0 / 16, bias=hpic[:, :]), [zi[1], "hpic"], [cs0[1]])
    sn1, cs1 = sn0, cs0
    for _ in range(4):
        cs1, sn1 = cmul(cs1, sn1, cs1, sn1)
    L = [None] * 5
    L1r, L1i = S(), S()
    vtt(L1r, mag, cs1, ALU.mult)
    vtt(L1i, mag, sn1, ALU.mult)
    L[1] = (L1r, L1i)
    L[2] = cmul(L1r, L1i, L1r, L1i)
    L[3] = cmul(L[2][0], L[2][1], L1r, L1i)
    L[4] = cmul(L[2][0], L[2][1], L[2][0], L[2][1])
    one_, zero_ = S(), S()
    V(lambda e: e.memset(one_[0], 1.0), [], [one_[1]])
    V(lambda e: e.memset(zero_[0], 0.0), [], [zero_[1]])
    L[0] = (one_, zero_)
    am1 = S()
    V(lambda e: e.tensor_scalar_add(am1[0], L1r[0], -1.0), [L1r[1]], [am1[1]])
    nli = S()
    V(lambda e: e.tensor_scalar_mul(nli[0], lim[0], -1.0), ["par"], [nli[1]])
    num_r, num_i = cmul(am1, L1i, lre, nli)
    den, u3 = S(), S()
    vtt(den, lre, lre, ALU.mult)
    vtt(u3, lim, lim, ALU.mult)
    vtt(den, den, u3, ALU.add)
    kr, ki = S(), S()
    V(lambda e: e.reciprocal(den[0], den[0]), [den[1]], [den[1]])
    vtt(kr, num_r, den, ALU.mult)
    vtt(ki, num_i, den, ALU.mult)
    M = [cmul(L[3 - j][0], L[3 - j][1], kr, ki) for j in range(4)]
    r2 = S()
    vtt(r2, L[4][0], L[4][0], ALU.mult)
    vtt(u3, L[4][1], L[4][1], ALU.mult)
    vtt(r2, r2, u3, ALU.add)
    A(lambda e: e.activation(R4[:, :], r2[0], AF.Sqrt), [r2[1]], ["R4"])
    ir4 = S()
    V(lambda e: e.reciprocal(ir4[0], R4[:, :]), ["R4"], [ir4[1]])
    V(lambda e: e.tensor_tensor(E4r[:, :], L[4][0][0], ir4[0], ALU.mult), [L[4][0][1], ir4[1]], ["E4"])
    V(lambda e: e.tensor_tensor(E4i[:, :], L[4][1][0], ir4[0], ALU.mult), [L[4][1][1], ir4[1]], ["E4"])
    V(lambda e: e.memset(cosT[:, :, 0:1], 1.0), [], ["tab"])
    V(lambda e: e.memset(sinT[:, :, 0:1], 0.0), [], ["tab"])
    wr, wi = (E4r[:, :], "E4"), (E4i[:, :], "E4")
    n = 1
    while n < 128:
        wrb = wr[0].unsqueeze(2).to_broadcast([128, 16, n])
        wib = wi[0].unsqueeze(2).to_broadcast([128, 16, n])
        c0, s0 = cosT[:, :, 0:n], sinT[:, :, 0:n]
        c1, s1 = cosT[:, :, n:2 * n], sinT[:, :, n:2 * n]
        ta, tb_ = t1[:, :, 0:n], t2[:, :, 0:n]
        V(lambda e, c0=c0, wrb=wrb, ta=ta: e.tensor_tensor(ta, c0, wrb, ALU.mult), ["tab", wr[1]], ["t1"])
        V(lambda e, s0=s0, wib=wib, tb_=tb_: e.tensor_tensor(tb_, s0, wib, ALU.mult), ["tab", wi[1]], ["t2"])
        V(lambda e, c1=c1, ta=ta, tb_=tb_: e.tensor_tensor(c1, ta, tb_, ALU.subtract), ["t1", "t2"], ["tab"])
        V(lambda e, c0=c0, wib=wib, ta=ta: e.tensor_tensor(ta, c0, wib, ALU.mult), ["tab", wi[1]], ["t1"])
        V(lambda e, s0=s0, wrb=wrb, tb_=tb_: e.tensor_tensor(tb_, s0, wrb, ALU.mult), ["tab", wr[1]], ["t2"])
        V(lambda e, s1=s1, ta=ta, tb_=tb_: e.tensor_tensor(s1, ta, tb_, ALU.add), ["t1", "t2"], ["tab"])
        if n < 64:
            wr, wi = cmul(wr, wi, wr, wi)
        n *= 2
    V(lambda e: e.tensor_copy(Rtab[:, :, :], R4[:, :].unsqueeze(2).to_broadcast([128, 16, 128])), ["R4"], ["Rtab"])
    V(lambda e: e.memset(Rtab[:, :, 0:1], 0.0), ["Rtab"], ["Rtab"])

    def norm_stats(src, srcname, nsub):
        for a in range(nsub):
            if a % 2 == 0:
                A(lambda e, a=a: e.activation(junk[:, :], src[:, a, :], AF.Square, accum_out=ss[:, a:a + 1]),
                  [f"{srcname}{a}"], ["junk", f"ss{a}"])
            else:
                V(lambda e, a=a: e.scalar_tensor_tensor(hT[:, a, :], src[:, a, :], 1.0, src[:, a, :], op0=ALU.mult, op1=ALU.mult,
                                                        accum_out=ss[:, a:a + 1]), [f"{srcname}{a}"], [f"hT{a}", f"ss{a}"])
            A(lambda e, a=a: e.activation(rstd[:, a:a + 1], ss[:, a:a + 1], AF.Sqrt, scale=1.0 / D, bias=epsc[:, :]),
              [f"ss{a}", "epsc"], [f"rstd{a}"])
            V(lambda e, a=a: e.reciprocal(rstd[:, a:a + 1], rstd[:, a:a + 1]), [f"rstd{a}"], [f"rstd{a}"])
            if a % 2 == 0:
                A(lambda e, a=a: e.activation(hT[:, a, :], src[:, a, :], AF.Copy, scale=rstd[:, a:a + 1]),
                  [f"{srcname}{a}", f"rstd{a}"], [f"hT{a}"])
            else:
                V(lambda e, a=a: e.tensor_scalar_mul(hT[:, a, :], src[:, a, :], rstd[:, a:a + 1]),
                  [f"{srcname}{a}", f"rstd{a}"], [f"hT{a}"])

    def norm_transposes(nsub, col_g, dst_fm, dstname, ncols_tok):
        for kc in range(8):
            bi = kc % 2
            bank = psT[bi]

            def fn(e, kc=kc, bank=bank):
                ins = None
                for a in range(nsub):
                    ins = e.transpose(bank[:, a * 128:(a + 1) * 128], hT[:, a, kc * 128:(kc + 1) * 128], ident[:, :])
                return ins
            P.op("pe", fn, r=[f"hT{a}" for a in range(nsub)] + ["ident"], w=[f"psT{bi}"])
            if kc % 2 == 0:
                A(lambda e, kc=kc, bank=bank: e.activation(dst_fm[:, kc, 0:ncols_tok], bank[:, 0:ncols_tok], AF.Copy, scale=cp(col_g + kc)),
                  [f"psT{bi}", "colp"], [f"{dstname}{kc}"])
            else:
                V(lambda e, kc=kc, bank=bank: e.tensor_scalar_mul(dst_fm[:, kc, 0:ncols_tok], bank[:, 0:ncols_tok], cp(col_g + kc)),
                  [f"psT{bi}", "colp"], [f"{dstname}{kc}"])


    def rmsnorm_to_hT(src, srcname, nsub, col_g, dst_fm, dstname, ncols_tok):
        norm_stats(src, srcname, nsub)
        norm_transposes(nsub, col_g, dst_fm, dstname, ncols_tok)

    rmsnorm_to_hT(memx, "x0a", 2, C_GMEM, memfm, "hfm", 256)
    for half in range(2):
        P.op("pool", lambda e, half=half: e.dma_start(out=slots[half][:, :], in_=w32_d[HOST_SLABS.index(SL_WK + half)]),
             w=[f"slot{half}"], dsem=f"d_slot{half}")
        for mt in range(4):
            bi = 2 + mt % 2
            pairs = [(slots[half][:, kc * 512 + mt * 128: kc * 512 + (mt + 1) * 128], memfm[:, kc, 0:256]) for kc in range(8)]
            mm_group(psb[bi][:, 0:256], pairs, [f"slot{half}"] + [f"hfm{kc}" for kc in range(8)], [f"ps{bi}"])
            A(lambda e, half=half, mt=mt, bi=bi: e.copy(Kfm[:, half * 4 + mt, :], psb[bi][:, 0:256]), [f"ps{bi}"], ["Kfm"])
    for half in range(2):
        P.op("pool", lambda e, half=half: e.dma_start(out=slots[2 + half][:, :], in_=w32_d[HOST_SLABS.index(SL_WV + half)]),
             w=[f"slot{2 + half}"], dsem=f"d_slot{2 + half}")
        for mc in range(2):
            bi = 4 + mc
            pairs = [(memfm[:, kc, mc * 128:(mc + 1) * 128], slots[2 + half][:, kc * 512:(kc + 1) * 512]) for kc in range(8)]
            mm_group(psb[bi][:, :], pairs, [f"slot{2 + half}"] + [f"hfm{kc}" for kc in range(8)], [f"ps{bi}"])
            V(lambda e, half=half, mc=mc, bi=bi: e.tensor_copy(Vtok[:, mc, half * 512:(half + 1) * 512], psb[bi][:, :]),
              [f"ps{bi}"], ["Vtok"])

    bre, bim, cre, cim = (bcc[:, i_, :, :] for i_ in range(4))
    c1, c2, c3, c4 = (cs[:, i_, :, :] for i_ in range(4))
    mpos = mask8[:, 0, :].unsqueeze(1).unsqueeze(3)
    mneg = mask8[:, 1, :].unsqueeze(1).unsqueeze(3)

    def bc16(ap):
        return ap.unsqueeze(2).to_broadcast([128, 16, 16])
    for j in range(4):
        mr, mi = bc16(M[j][0][0]), bc16(M[j][1][0])
        mrn, min_ = M[j][0][1], M[j][1][1]
        V(lambda e, mr=mr: e.tensor_tensor(c1, bre, mr, ALU.mult), ["bcc", mrn], ["c1"])
        V(lambda e, mi=mi: e.tensor_tensor(c2, bim, mi, ALU.mult), ["bcc", min_], ["c2"])
        V(lambda e, mi=mi: e.tensor_tensor(c3, bre, mi, ALU.mult), ["bcc", min_], ["c3"])
        V(lambda e, mr=mr: e.tensor_tensor(c4, bim, mr, ALU.mult), ["bcc", mrn], ["c4"])
        V(lambda e: e.tensor_tensor(c1, c1, c2, ALU.subtract), ["c1", "c2"], ["c1"])
        V(lambda e: e.tensor_tensor(c3, c3, c4, ALU.add), ["c3", "c4"], ["c3"])
        for ri, cc, cn in ((0, c1, "c1"), (1, c3, "c3")):
            o = NB[:, j, ri, :, :].rearrange("p a (g h) -> p a g h", g=8)
            V(lambda e, o=o, cc=cc: e.tensor_tensor(o, cc.unsqueeze(2).to_broadcast([128, 16, 8, 16]),
                                                    mpos.to_broadcast([128, 16, 8, 16]), ALU.mult), [cn, "mask8"], [f"NB{j}"])
    tix = {"i": 0}

    def w_section(k):
        for sg in range(8):
            s_, ri = sg % 4, sg // 4
            bank = psT[tix["i"] % 2]
            bname = f"psT{tix['i'] % 2}"
            tix["i"] += 1

            def fn(e, s_=s_, ri=ri, bank=bank):
                ins = None
                for j in range(4):
                    ins = e.transpose(bank[:, j * 128:(j + 1) * 128], NB[:, j, ri, k * 4 + s_, :], ident[:, :])
                return ins
            P.op("pe", fn, r=[f"NB{j}" for j in range(4)] + ["ident"], w=[bname])
            dst = Wst[:, 0, sg, :, :]
            src = bank[:, 0:512].rearrange("p (j c) -> p j c", j=4)
            A(lambda e, dst=dst, src=src: e.copy(dst, src), [bname], ["Wst"])
        P.op("sp", lambda e: e.dma_start(out=wscr[SL_S5W + k], in_=Wst[:, 0, :, :, :].rearrange("p a b c -> p (a b c)")),
             r=["Wst"], w=[f"scr{SL_S5W + k}"], dsem=f"d_w{k}")
    def gen_V(m):
        lr, li = bc16(L[m][0][0]), bc16(L[m][1][0])
        lrn, lin = L[m][0][1], L[m][1][1]
        vb = Vm[:, m % 2, :, :, :]
        on = f"Vm{m % 2}"
        V(lambda e: e.tensor_tensor(c1, cre, lr, ALU.mult), ["bcc", lrn], ["c1"])
        V(lambda e: e.tensor_tensor(c2, cim, li, ALU.mult), ["bcc", lin], ["c2"])
        V(lambda e: e.tensor_tensor(c3, cre, li, ALU.mult), ["bcc", lin], ["c3"])
        V(lambda e: e.tensor_tensor(c4, cim, lr, ALU.mult), ["bcc", lrn], ["c4"])
        V(lambda e: e.tensor_tensor(c1, c1, c2, ALU.subtract), ["c1", "c2"], ["c1"])
        V(lambda e: e.tensor_tensor(c3, c3, c4, ALU.add), ["c3", "c4"], ["c3"])
        for k in range(4):
            for half, cc, cn, mk in ((0, c1, "c1", mpos), (1, c3, "c3", mneg)):
                o = vb[:, k, half * 4:(half + 1) * 4, :].rearrange("p s (g h) -> p s g h", g=8)
                V(lambda e, o=o, cc=cc, mk=mk, k=k: e.tensor_tensor(o, cc[:, k * 4:(k + 1) * 4, :].unsqueeze(2).to_broadcast([128, 4, 8, 16]),
                                                                   mk.to_broadcast([128, 4, 8, 16]), ALU.mult), [cn, "mask8"], [on])
        if m >= 1:
            for k in range(4):
                dst = wscr[SL_S5V + k].rearrange("p (s m c) -> p s m c", s=8, m=4)[:, :, m - 1, :]
                P.op("sp", lambda e, k=k, dst=dst: e.dma_start(out=dst, in_=vb[:, k, :, :]),
                     r=[on], w=[f"scr{SL_S5V + k}"], dsem=f"d_v{k}_{m}")
        if m <= 3:
            for k in range(4):
                pairs = [(NB[:, 3, sg // 4, k * 4 + sg % 4, :], vb[:, k, sg, :]) for sg in range(8)]
                mm_group(psb[k][:, m * 128:(m + 1) * 128], pairs, ["NB3", on], [f"ps{k}"])
    for m in range(5):
        gen_V(m)
        if m < 4:
            w_section(m)
    for k in range(4):
        V(lambda e, k=k: e.scalar_tensor_tensor(Kst[:, k, 0, :], identf[:, :], cp(C_DS5 + k), psb[k][:, 0:128],
                                                op0=ALU.mult, op1=ALU.add), [f"ps{k}", "identf", "colp"], ["Kst"])
        V(lambda e, k=k: e.tensor_copy(Kst[:, k, 1:4, :], psb[k][:, 128:512].rearrange("p (j c) -> p j c", j=3)),
          [f"ps{k}"], ["Kst"])
    P.op("sp", lambda e: e.dma_start(out=wscr[SL_KG][:, 0:2048], in_=hT[:, 2:4, :].rearrange("p a b -> p (a b)")),
         r=["Kst"], w=[f"scr{SL_KG}"], dsem="d_kst")

    for hi, sl in enumerate(HOST_SLABS):
        if sl >= SL_WK:
            continue
        if sl == SL_KG:
            P.op("pool", lambda e, hi=hi, sl=sl: e.dma_start(out=wscr[sl][:, 2048:4096], in_=w32_d[hi][:, 2048:4096]),
                 w=[f"scr{sl}g"], dsem=f"d_cast{hi}")
        else:
            P.op("pool", lambda e, hi=hi, sl=sl: e.dma_start(out=wscr[sl], in_=w32_d[hi]),
                 w=[f"scr{sl}"], dsem=f"d_cast{hi}")

    if dbg == "pro":
        pass
    P.barrier(skip="d_cast")
    pst.close()

    xres[1] = sb("xres1", [128, 4, D])
    xlh = sb("xlh", [128, 4, 3 + TB])
    ffnhalo = sb("ffnhalo", [128, 2, 44, 2])
    G(lambda e: e.memset(xlh[:, :, 0:3], 0.0), [], [f"xl{i}" for i in range(4)])
    G(lambda e: e.memset(ffnhalo[:, :, :, :], 0.0), [], ["ffnhalo0", "ffnhalo1"])
    gl = sb("gl", [128, 4, TB], BF16)
    zq = sb("zq", [128, 4, 512])
    S_sb = sb("S_sb", [128, 4, 8, 130], BF16)
    ymix = sb("ymix", [128, 8, TB], BF16)
    ltb = sb("lt", [128, 6, TB + 4])
    lt = ltb[:, :, 0:TB]
    xcb = sb("xcb", [128, 2, TB], BF16)
    hlb = sb("hl", [128, 2, TB + 4])
    hl = hlb[:, :, 0:TB]
    gated = sb("gated", [128, 22, TB], BF16)
    q_sb = ymix
    pT = gated[:, 0:8, :].rearrange("p (h m) t -> p h m t", h=4)
    o_sb = gated[:, 8:16, :]
    u_sb = gated[:, 16:20, :]
    ygelu = gated[:, 0:4, :]
    rden = hl
    zbuf = ltb[:, 0:2, 0:TB + 2]
    cv = lt[:, 2:4, :]
    cg = lt[:, 4:6, :]
    gg = cg
    qinit = sb("qinit", [128, 3, 16])
    hc = sb("hc", [128, 44, 2])
    hc2 = sb("hc2", [128, 44])
    G(lambda e: e.memset(S_sb[:, :, :, :], 0.0), [], ["S_sb0", "S_sb1", "S_sb2", "S_sb3"])

    rr = {"ps": 0, "ps6": 0}

    def nextbank():
        b = rr["ps"] % 4
        rr["ps"] += 1
        return b

    def nextbank6():
        b = rr["ps6"] % 6
        rr["ps6"] += 1
        return b

    hnames = [f"hfm{kc}" for kc in range(8)]

    def add_resid(xb, a, nh, bi):
        xs = xres[xb][:, a, nh * 512:(nh + 1) * 512]
        V(lambda e: e.tensor_tensor(xs, xs, psb[bi][:, :], ALU.add), [f"ps{bi}", f"x{xb}a{a}"], [f"x{xb}a{a}"])

    def proj_tok(lhs_tile, lhs_names, slab_ids, xb, bankfn):
        for nh in range(2):
            st, sn = load_slab(slab_ids[nh])
            for a in range(4):
                bi = bankfn()
                pairs = [(lhs_tile[:, kc, a * 128:(a + 1) * 128], st[:, kc * 512:(kc + 1) * 512]) for kc in range(8)]
                mm_group(psb[bi][:, :], pairs, [sn] + lhs_names, [f"ps{bi}"])
                add_resid(xb, a, nh, bi)

    def fm_proj(st, sn, mt, evac, bankfn=None):
        bi = (bankfn or nextbank)()
        pairs = [(st[:, kc * 512 + mt * 128: kc * 512 + (mt + 1) * 128], hfm[:, kc, :]) for kc in range(8)]
        mm_group(psb[bi][:, :], pairs, [sn] + hnames, [f"ps{bi}"])
        evac(bi)

    u4 = u_sb[:, :, :].rearrange("p k (c j) -> p k c j", j=4)

    def v4(ap):
        return ap.rearrange("p (s c) -> p s c", s=4)

    def s5_qinit():
        q0, q1, q2 = qinit[:, 0, :], qinit[:, 1, :], qinit[:, 2, :]
        cr, ci = s5carry_r[:, :], s5carry_i[:, :]
        er, ei, r4 = E4r[:, :], E4i[:, :], R4[:, :]
        V(lambda e: e.tensor_tensor(q0, cr, er, ALU.mult), ["s5c", "E4"], ["qi0"])
        V(lambda e: e.tensor_tensor(q2, ci, ei, ALU.mult), ["s5c", "E4"], ["qi2"])
        V(lambda e: e.tensor_tensor(q1, cr, ei, ALU.mult), ["s5c", "E4"], ["qi1"])
        V(lambda e: e.tensor_tensor(q0, q0, q2, ALU.subtract), ["qi0", "qi2"], ["qi0"])
        V(lambda e: e.tensor_tensor(q2, ci, er, ALU.mult), ["s5c", "E4", "qi0"], ["qi2"])
        V(lambda e: e.tensor_tensor(q0, q0, r4, ALU.mult), ["qi0", "R4"], ["qi0"])
        V(lambda e: e.tensor_tensor(q1, q1, q2, ALU.add), ["qi1", "qi2"], ["qi1"])
        V(lambda e: e.tensor_tensor(q1, q1, r4, ALU.mult), ["qi1", "R4"], ["qi1"])

    def s5_B_pe(k):
        st, sn = load_slab(SL_S5W + k)
        for half, bi in ((0, 4), (1, 5)):
            for s_ in range(4):
                sg = half * 4 + s_
                pairs = [(st[:, (sg * 4 + j) * 128:(sg * 4 + j + 1) * 128], u4[:, k, :, j]) for j in range(4)]
                mm_group(psb[bi][:, s_ * 128:(s_ + 1) * 128], pairs, [sn, f"gated{16 + k}"], [f"ps{bi}"])

    def s5_B_steps(k):
        cT = cosT[:, k * 4:(k + 1) * 4, :]
        sT = sinT[:, k * 4:(k + 1) * 4, :]
        Xr = v4(psb[4][:, :])
        Xi = v4(psb[5][:, :])
        Zr, Zi, Qr, Qi = (zq[:, i, :] for i in range(4))
        tmp = lt[:, 0, :]
        ks = slice(k * 4, (k + 1) * 4)
        q0, q1 = qinit[:, 0, ks], qinit[:, 1, ks]
        cr, ci = s5carry_r[:, ks], s5carry_i[:, ks]
        Rk = Rtab[:, k * 4:(k + 1) * 4, :].rearrange("p s c -> p (s c)")
        Sre = S_sb[:, k, 0:4, 1:129]
        Sim = S_sb[:, k, 4:8, 1:129]
        sname = f"S_sb{k}"
        ta, tb_ = lt[:, 2, :], lt[:, 3, :]
        TT = lambda o, a, b, op, r, w: (lambda: V(lambda e: e.tensor_tensor(o, a, b, op), r, w))
        return [
            TT(v4(Zr), Xr, cT, ALU.mult, ["ps4", "tab"], ["zq0"]),
            TT(v4(tmp), Xi, sT, ALU.mult, ["ps5", "tab"], ["lt0"]),
            TT(Zr, Zr, tmp, ALU.add, ["zq0", "lt0"], ["zq0"]),
            TT(v4(Zi), Xi, cT, ALU.mult, ["ps5", "tab"], ["zq1"]),
            TT(v4(tmp), Xr, sT, ALU.mult, ["ps4", "tab"], ["lt0"]),
            TT(Zi, Zi, tmp, ALU.subtract, ["zq1", "lt0"], ["zq1"]),
            TT(v4(Zr)[:, :, 0], v4(Zr)[:, :, 0], q0, ALU.add, ["zq0", "qi0"], ["zq0"]),
            TT(v4(Zi)[:, :, 0], v4(Zi)[:, :, 0], q1, ALU.add, ["zq1", "qi1"], ["zq1"]),
            lambda: V(lambda e: e.tensor_tensor_scan(Qr, Rk, Zr, 0.0, op0=ALU.mult, op1=ALU.add), ["zq0", "Rtab"], ["zq2"]),
            lambda: V(lambda e: e.tensor_tensor_scan(Qi, Rk, Zi, 0.0, op0=ALU.mult, op1=ALU.add), ["zq1", "Rtab"], ["zq3"]),
            lambda: V(lambda e: e.tensor_copy(S_sb[:, k, :, 0:1], S_sb[:, k, :, 128:129]), [sname], [sname]),
            TT(v4(ta), v4(Qr), cT, ALU.mult, ["zq2", "tab"], ["lt2"]),
            TT(v4(tb_), v4(Qi), sT, ALU.mult, ["zq3", "tab"], ["lt3"]),
            TT(Sre, v4(ta), v4(tb_), ALU.subtract, ["lt2", "lt3"], [sname]),
            TT(cr, v4(ta)[:, :, 127], v4(tb_)[:, :, 127], ALU.subtract, ["lt2", "lt3"], ["s5c"]),
            TT(v4(ta), v4(Qr), sT, ALU.mult, ["zq2", "tab"], ["lt2"]),
            TT(v4(tb_), v4(Qi), cT, ALU.mult, ["zq3", "tab"], ["lt3"]),
            TT(Sim, v4(ta), v4(tb_), ALU.add, ["lt2", "lt3"], [sname]),
            TT(ci, v4(ta)[:, :, 127], v4(tb_)[:, :, 127], ALU.add, ["lt2", "lt3"], ["s5c"]),
        ]

    def lru_steps(mt):
        xl = xlh[:, mt, :]
        xn = f"xl{mt}"
        (xc, xcn), (ra, ran), (i_, in_) = [(lt[:, r, :], f"lt{r}") for r in (4, 5, 1)]
        m_, mn = xc, xcn
        xb_ = xcb[:, 0, :]
        xbn = "xcb0"
        hb = hl[:, mt % 2, :]
        hn = f"hl{mt % 2}"
        bk = {}
        st = []
        st.append(lambda: A(lambda e: e.activation(xc, xl[:, 3:3 + TB], AF.Identity, scale=cp(C_LCW + 3 * 4 + mt), bias=cp(C_LCB + mt)),
                            [xn, "colp"], [xcn]))

        def tap(t_):
            return lambda: V(lambda e: e.scalar_tensor_tensor(xc, xl[:, t_:t_ + TB], cp(C_LCW + t_ * 4 + mt), xc, op0=ALU.mult, op1=ALU.add),
                             [xn, xcn], [xcn])
        for t_ in range(3):
            st.append(tap(t_))
        st.append(lambda: G(lambda e: e.tensor_copy(xl[:, 0:3], xl[:, TB:TB + 3]), [xn], [xn]))
        st.append(lambda: A(lambda e: e.copy(xb_, xc), [xcn], [xbn]))

        def gates():
            bk["b1"], bk["b2"] = nextbank(), nextbank()
            mm_group(psb[bk["b1"]][:, :], [(lruw[:, mt, :], xb_)], ["lruw", xbn], [f"ps{bk['b1']}"])
            mm_group(psb[bk["b2"]][:, :], [(lruw[:, 4 + mt, :], xb_)], ["lruw", xbn], [f"ps{bk['b2']}"])
        st.append(gates)
        st.append(lambda: A(lambda e: e.activation(ra, psb[bk["b1"]][:, :], AF.Tanh, scale=0.5, bias=hbias[:, mt:mt + 1]),
                            [f"ps{bk['b1']}", "hbias"], [ran]))
        st.append(lambda: A(lambda e: e.activation(i_, psb[bk["b2"]][:, :], AF.Tanh, scale=0.5, bias=hbias[:, 4 + mt:5 + mt]),
                            [f"ps{bk['b2']}", "hbias"], [in_]))
        st.append(lambda: V(lambda e: e.scalar_tensor_tensor(i_, i_, 1.0, xc, op0=ALU.add, op1=ALU.mult), [in_, xcn], [in_]))
        st.append(lambda: A(lambda e: e.activation(ra, ra, AF.Exp, scale=cnegh[:, mt:mt + 1], bias=cnegh[:, mt:mt + 1]), [ran, "cnegh"], [ran]))
        st.append(lambda: A(lambda e: e.activation(m_, ra, AF.Square), [ran, in_], [mn]))
        st.append(lambda: A(lambda e: e.activation(m_, m_, AF.Sqrt, scale=-1.0, bias=onec[:, :]), [mn, "onec"], [mn]))
        st.append(lambda: V(lambda e: e.scalar_tensor_tensor(i_, i_, 0.5, m_, op0=ALU.mult, op1=ALU.mult), [in_, mn], [in_]))
        st.append(lambda: V(lambda e: e.tensor_tensor_scan(hb, ra, i_, lrucarry[:, mt:mt + 1], op0=ALU.mult, op1=ALU.add),
                            [ran, in_, "lrucarry"], [hn]))
        st.append(lambda: V(lambda e: e.tensor_copy(lrucarry[:, mt:mt + 1], hb[:, TB - 1:TB]), [hn], ["lrucarry"]))
        prod = lambda: V(lambda e: e.tensor_tensor(ymix[:, 4 + mt, :], hb, gl[:, mt, :], ALU.mult), [hn, f"gl{mt}"], [f"ymix{4 + mt}"])
        return st, prod

    def zip_steps(a, b):
        for i in range(max(len(a), len(b))):
            if i < len(a):
                a[i]()
            if i < len(b):
                b[i]()

    def s5_D(k, stK, snK):
        st, sn = load_slab(SL_S5V + k)
        bi = nextbank()
        yv = psb[bi][:, :].rearrange("p (c i) -> p c i", i=4)
        for i in range(4):
            pairs = [(st[:, (sg * 4 + i) * 128:(sg * 4 + i + 1) * 128], S_sb[:, k, sg, 0:128]) for sg in range(8)]
            pairs += [(stK[:, (k * 4 + (i - j)) * 128:(k * 4 + (i - j) + 1) * 128], u4[:, k, :, j]) for j in range(i + 1)]
            mm_group(yv[:, :, i], pairs, [sn, snK, f"S_sb{k}", f"gated{16 + k}"], [f"ps{bi}"])
        A(lambda e: e.activation(ygelu[:, k, :], psb[bi][:, :], AF.Gelu), [f"ps{bi}"], [f"gated{k}"])

    def glu_tile(mt, stK, snK):
        bi = nextbank()
        pairs = [(stK[:, 2048 + kc * 512 + mt * 128: 2048 + kc * 512 + (mt + 1) * 128], ygelu[:, kc, :]) for kc in range(4)]
        mm_group(psb[bi][:, :], pairs, [snK] + [f"gated{k}" for k in range(4)], [f"ps{bi}"])
        gt = lt[:, 2 + mt % 2, :]
        gn = f"lt{2 + mt % 2}"
        A(lambda e: e.activation(gt, psb[bi][:, :], AF.Sigmoid, bias=cp(C_BGLU + mt)), [f"ps{bi}", "colp"], [gn])
        V(lambda e: e.tensor_tensor(ymix[:, mt, :], ygelu[:, mt, :], gt, ALU.mult), [f"gated{mt}", gn], [f"ymix{mt}"])

    def attn_head(hd):
        def sc(mc):
            bi = nextbank6()
            pairs = [(Kfm[:, hd * 2 + c2, mc * 128:(mc + 1) * 128], q_sb[:, hd * 2 + c2, :]) for c2 in range(2)]
            mm_group(psb[bi][:, :], pairs, ["Kfm", f"ymix{hd * 2}", f"ymix{hd * 2 + 1}"], [f"ps{bi}"])
            def expfn(e):
                return e.activation(pT[:, hd, mc, :], psb[bi][:, :], AF.Exp)
            A(expfn, [f"ps{bi}"], [f"gated{2 * hd}", f"gated{2 * hd + 1}"])
        sc(0)
        sc(1)

    def attn_tail(hd):
        bd = nextbank6()
        mm_group(psb[bd][:, :], [(onesb[:, :], pT[:, hd, mc, :]) for mc in range(2)], ["onesb", f"gated{2 * hd}", f"gated{2 * hd + 1}"], [f"ps{bd}"])
        rd = rden[:, hd % 2, :]
        rn = f"hl{hd % 2}"
        A(lambda e: e.activation(rd, psb[bd][:, :], AF.Ln), [f"ps{bd}"], [rn])
        A(lambda e: e.activation(rd, rd, AF.Exp, scale=-1.0), [rn], [rn])

        def pv(j):
            bi = nextbank6()
            pairs = [(Vtok[:, mc, hd * 256 + j * 128: hd * 256 + (j + 1) * 128], pT[:, hd, mc, :]) for mc in range(2)]
            mm_group(psb[bi][:, :], pairs, ["Vtok", f"gated{2 * hd}", f"gated{2 * hd + 1}"], [f"ps{bi}"])
            V(lambda e: e.tensor_tensor(o_sb[:, hd * 2 + j, :], psb[bi][:, :], rd, ALU.mult), [f"ps{bi}", rn], [f"gated{8 + hd * 2 + j}"])
        pv(0)
        pv(1)

    def ffn_tile_mm(st, sn, sa, tt, isg, tb):
        vt = 2 * sa + tt
        ch = vt + 22 * isg
        col = isg * 256 + tt * 128
        bi = nextbank6()
        pairs = [(st[:, kc * 512 + col: kc * 512 + col + 128], hfm[:, kc, :]) for kc in range(8)]
        mm_group(psb[bi][:, :], pairs, [sn] + hnames, [f"ps{bi}"])
        ps = psb[bi]
        pn = f"ps{bi}"
        dst = (cv if isg == 0 else cg)[:, tt, :]
        dn = f"lt{2 + 2 * isg + tt}"
        hold = ffnhalo[:, tb % 2, ch, :]
        hnew = ffnhalo[:, (tb + 1) % 2, ch, :]
        wcol = lambda t_: cp(C_FCW + t_ * 44 + ch)
        A(lambda e: e.activation(dst, ps[:, :], AF.Identity, scale=wcol(2), bias=cp(C_FCB + ch)), [pn, "colp"], [dn])
        A(lambda e: e.copy(hnew, ps[:, TB - 2:TB]), [pn], [f"ffnhalo{(tb + 1) % 2}"])
        return ps, pn, dst, dn, wcol, hold, f"ffnhalo{tb % 2}"

    def ffn_tap(t_, ps, pn, dst, dn, wcol, hold, hn):
        sh = 2 - t_
        V(lambda e: e.scalar_tensor_tensor(dst[:, sh:TB], ps[:, 0:TB - sh], wcol(t_), dst[:, sh:TB], op0=ALU.mult, op1=ALU.add),
          [pn, dn, "colp"], [dn])

    def ffn_halo_prep(tb):
        hold = ffnhalo[:, tb % 2, :, :]
        hn = f"ffnhalo{tb % 2}"
        W0, W1 = cp(C_FCW, 44), cp(C_FCW + 44, 44)
        V(lambda e: e.tensor_tensor(hc[:, :, 1], hold[:, :, 1], W0, ALU.mult), [hn, "colp"], ["hc"])
        V(lambda e: e.tensor_tensor(hc[:, :, 0], hold[:, :, 0], W0, ALU.mult), [hn, "colp"], ["hc"])
        V(lambda e: e.tensor_tensor(hc2[:, :], hold[:, :, 1], W1, ALU.mult), [hn, "colp"], ["hc2"])
        V(lambda e: e.tensor_tensor(hc[:, :, 0], hc[:, :, 0], hc2[:, :], ALU.add), ["hc", "hc2"], ["hc"])

    def ffn_halo_add(ch, dst, dn):
        G(lambda e: e.tensor_tensor(dst[:, 0:2], dst[:, 0:2], hc[:, ch, :], ALU.add), ["hc", dn], [dn])

    def ffn_pair(st, sn, sa, tt, tb, prev_tail):
        vt = 2 * sa + tt
        tiles = [ffn_tile_mm(st, sn, sa, tt, 0, tb), ffn_tile_mm(st, sn, sa, tt, 1, tb)]
        if prev_tail is not None:
            prev_tail()
        for t_ in (1, 0):
            for tl in tiles:
                ffn_tap(t_, *tl)
        for isg, tl in enumerate(tiles):
            ffn_halo_add(vt + 22 * isg, tl[2], tl[3])

        def tail():
            A(lambda e: e.activation(cg[:, tt, :], cg[:, tt, :], AF.Gelu), [f"lt{4 + tt}"], [f"lt{4 + tt}"])
            G(lambda e: e.tensor_tensor(gated[:, vt, :], cg[:, tt, :], cv[:, tt, :], ALU.mult), [f"lt{4 + tt}", f"lt{2 + tt}"], [f"gated{vt}"])
        return tail

    def down_group(xb, nh, sg3, kc0, nk, a, bi):
        st, sn = down_group.cur

        def fn(e):
            ins = None
            for kk in range(nk):
                ins = e.matmul(psb[bi][:, :], gated[:, kc0 + kk, a * 128:(a + 1) * 128], st[:, kk * 512:(kk + 1) * 512],
                               start=(sg3 == 0 and kk == 0), stop=(sg3 == 2 and kk == nk - 1))
            return ins
        P.op("pe", fn, r=[sn] + [f"gated{kc0 + kk}" for kk in range(nk)], w=[f"ps{bi}"])

    def final_norm(xb, a):
        xa = xres[xb][:, a, :]
        sa_, ra_ = ss[:, 4 + a:5 + a], rstd[:, 4 + a:5 + a]
        xn = f"x{xb}a{a}"
        A(lambda e: e.activation(junk[:, :], xa, AF.Square, accum_out=sa_), [xn], ["junk", f"ssf{a}"])
        A(lambda e: e.activation(ra_, sa_, AF.Sqrt, scale=1.0 / D, bias=epsc[:, :]), [f"ssf{a}", "epsc"], [f"rstdf{a}"])
        V(lambda e: e.reciprocal(ra_, ra_), [f"rstdf{a}"], [f"rstdf{a}"])
        V(lambda e: e.scalar_tensor_tensor(xa, xa, ra_, gfin[:, :], op0=ALU.mult, op1=ALU.mult), [xn, f"rstdf{a}", "gfin"], [xn])

    x_t = x_d.rearrange("(b a p) d -> b p a d", a=4, p=128)
    out_t = out_d.rearrange("(b a p) d -> b p a d", a=4, p=128)

    def load_x(tb):
        xb = tb % 2
        P.op("sp", lambda e: e.dma_start(out=xres[xb][:, :, :], in_=x_t[tb]),
             w=[f"x{xb}a{a}" for a in range(4)], dsem=f"d_x{xb}")

    def store_out(tb):
        xb = tb % 2
        P.op("sp", lambda e: e.dma_start(out=out_t[tb], in_=xres[xb][:, :, :]),
             r=[f"x{xb}a{a}" for a in range(4)], dsem=f"d_o{xb}")

    def win_evac(grp, mt):
        def ev(bi):
            if grp == 0:
                A(lambda e: e.copy(u_sb[:, mt, :], psb[bi][:, :]), [f"ps{bi}"], [f"gated{16 + mt}"])
            elif grp == 1:
                A(lambda e: e.copy(xlh[:, mt, 3:3 + TB], psb[bi][:, :]), [f"ps{bi}"], [f"xl{mt}"])
            else:
                A(lambda e: e.activation(gl[:, mt, :], psb[bi][:, :], AF.Gelu), [f"ps{bi}"], [f"gl{mt}"])
        return ev

    def q_evac(half, mt):
        def ev(bi):
            A(lambda e: e.activation(q_sb[:, half * 4 + mt, :], psb[bi][:, :], AF.Copy, scale=0.0625),
              [f"ps{bi}"], [f"ymix{half * 4 + mt}"])
        return ev

    def finish_block(tb):
        if stage >= 6:
            for a in range(4):
                final_norm(tb % 2, a)

    def block_body(tb):
        xb = tb % 2
        if tb == 0:
            norm_stats(xres[xb], f"x{xb}a", 4)
        norm_transposes(4, C_G1, hfm, "hfm", TB)
        st0, sn0 = load_slab(SL_WIN)
        for mt in range(4):
            fm_proj(st0, sn0, mt, win_evac(0, mt))
        if tb > 0:
            finish_block(tb - 1)
        s5_qinit()
        wslabs = {}

        def win_tiles(grp, mts):
            if grp not in wslabs:
                wslabs[grp] = load_slab(SL_WIN + grp)
            st, sn = wslabs[grp]
            for mt in mts:
                fm_proj(st, sn, mt, win_evac(grp, mt))
        s5_B_pe(0)
        zip_steps(s5_B_steps(0), [])
        win_tiles(1, [0, 1])
        prods = {}
        for k in range(1, 4):
            s5_B_pe(k)
            lst, prods[k - 1] = lru_steps(k - 1)
            zip_steps(s5_B_steps(k), lst)
            if k == 1:
                win_tiles(1, [2, 3])
            elif k == 2:
                win_tiles(2, [0, 1])
                prods[0]()
                prods[1]()
            else:
                win_tiles(2, [2, 3])
        if stage < 2:
            return
        stK, snK = load_slab(SL_KG)
        lst, prods[3] = lru_steps(3)
        cuts = [0, 6, 11, 15, len(lst)]
        for k in range(4):
            s5_D(k, stK, snK)
            for f in lst[cuts[k]:cuts[k + 1]]:
                f()
        prods[2]()
        prods[3]()
        for mt in range(4):
            glu_tile(mt, stK, snK)
        if tb > 0:
            store_out(tb - 1)
        if tb + 1 < nblk:
            load_x(tb + 1)
        if stage < 3:
            return
        proj_tok(ymix, [f"ymix{i}" for i in range(8)], [SL_WOUT, SL_WOUT + 1], xb, nextbank6)
        if stage < 4:
            return
        rmsnorm_to_hT(xres[xb], f"x{xb}a", 4, C_G2, hfm, "hfm", TB)
        for half in range(2):
            st, sn = load_slab(SL_WQ + half)
            for mt in range(4):
                fm_proj(st, sn, mt, q_evac(half, mt), nextbank6)
        for hd in range(4):
            attn_head(hd)
            attn_tail(hd)
        proj_tok(o_sb, [f"gated{8 + i}" for i in range(8)], [SL_WO, SL_WO + 1], xb, nextbank6)
        if stage < 5:
            return
        rmsnorm_to_hT(xres[xb], f"x{xb}a", 4, C_G3, hfm, "hfm", TB)
        ffn_halo_prep(tb)
        ptail = None
        for sa in range(11):
            st, sn = load_slab(SL_UP + sa)
            for tt in range(2):
                ptail = ffn_pair(st, sn, sa, tt, tb, ptail)
        ptail()
        if tb + 1 < nblk and stage >= 6:
            norm_stats(xres[1 - xb], f"x{1 - xb}a", 4)
        for nh in range(2):
            kc0 = 0
            dbanks = [nextbank6() for _ in range(4)]
            for sg3 in range(3):
                nk = 8 if sg3 < 2 else 6
                down_group.cur = load_slab(SL_DN + nh * 3 + sg3)
                for a in range(4):
                    down_group(xb, nh, sg3, kc0, nk, a, dbanks[a])
                kc0 += nk
            for a in range(4):
                add_resid(xb, a, nh, dbanks[a])

    load_x(0)
    for tb in range(nblk):
        block_body(tb)
    finish_block(nblk - 1)
    store_out(nblk - 1)

    if dbg is not None:
        P.barrier()
        for i_, (ap_, a_, b_) in enumerate(dbg(locals())):
            P.op("pool", lambda e, i_=i_, ap_=ap_, a_=a_, b_=b_: e.dma_start(
                out=dbg_d[i_][:, 0:a_ * b_].rearrange("p (a b) -> p a b", a=a_), in_=ap_), dsem=f"d_dbg{i_}")

    P.barrier()
    with nc.Block() as block:
        P.emit(block)
    es.close()
    return nc


def _slab_kc(w, c0, ncols=512, kc0=0, nkc=8):
    out = np.zeros((128, 8, ncols), np.float32)
    K = w.shape[0]
    for kc in range(nkc):
        r0 = (kc0 + kc) * 128
        if r0 >= K:
            break
        out[:, kc, :] = w[r0:r0 + 128, c0:c0 + ncols]
    return out.reshape(128, 8 * ncols)


def host_layout(inp):
    f = lambda k: np.asarray(inp[k], np.float32)
    w_in, w_out = f("w_in")[0], f("w_out")[0]
    wq, wk, wv, wo = f("xa_w_q")[0], f("xa_w_k")[0], f("xa_w_v")[0], f("xa_w_o")[0]
    wup, wdn, glu = f("ffn_w_up")[0], f("ffn_w_down")[0], f("s5_w_glu")[0]
    slabs = {}
    for g in range(3):
        slabs[SL_WIN + g] = _slab_kc(w_in, g * 512)
    kg = np.zeros((128, 4096), np.float32)
    kg[:, 2048:] = _slab_kc(glu, 0, 512, 0, 4)[:, :2048]
    slabs[SL_KG] = kg
    for h in range(2):
        slabs[SL_WOUT + h] = _slab_kc(w_out, h * 512)
        slabs[SL_WQ + h] = _slab_kc(wq, h * 512)
        slabs[SL_WO + h] = _slab_kc(wo, h * 512)
        slabs[SL_WK + h] = _slab_kc(wk, h * 512)
        slabs[SL_WV + h] = _slab_kc(wv, h * 512)
    for sa in range(11):
        cols = np.concatenate([np.arange(sa * 256, sa * 256 + 256), 2816 + np.arange(sa * 256, sa * 256 + 256)])
        slabs[SL_UP + sa] = _slab_kc(wup[:, cols], 0)
    for nh in range(2):
        for g3 in range(3):
            slabs[SL_DN + nh * 3 + g3] = _slab_kc(wdn, nh * 512, 512, g3 * 8, 8 if g3 < 2 else 6)
    w32 = np.stack([slabs[s] for s in HOST_SLABS]).astype(np.float32)

    colp = np.zeros((128, NCOL), np.float32)

    def putcols(c0, vec):
        v = np.asarray(vec, np.float32).reshape(-1, 128).T
        colp[:, c0:c0 + v.shape[1]] = v
    putcols(C_G1, f("ln_mix_g")[0])
    putcols(C_G2, f("ln_xa_g")[0])
    putcols(C_G3, f("ln_ffn_g")[0])
    putcols(C_GMEM, f("mem_norm_g"))
    putcols(C_BGLU, f("s5_b_glu")[0])
    lcw = f("lru_conv_w")[0]
    for t in range(4):
        putcols(C_LCW + t * 4, lcw[t])
    putcols(C_LCB, f("lru_conv_b")[0])
    putcols(C_BA, f("lru_b_a")[0].reshape(-1))
    putcols(C_BX, f("lru_b_x")[0].reshape(-1))
    putcols(C_LAM, f("lru_lam")[0].reshape(-1))
    fcw = f("ffn_conv_w")[0]
    for t in range(3):
        putcols(C_FCW + t * 44, fcw[t])
    putcols(C_FCB, f("ffn_conv_b")[0])
    putcols(C_DS5, f("s5_d")[0].reshape(-1))
    gfin = np.ascontiguousarray(np.broadcast_to(f("final_norm_g")[None, :], (128, D)))

    def gq(arr):
        a = arr.reshape(4, 8, 4, 16)
        return np.ascontiguousarray(a.transpose(1, 3, 0, 2).reshape(128, 16))
    lre, lim = f("s5_lam_re")[0], f("s5_lam_im")[0]
    ldt = np.broadcast_to(f("s5_log_dt")[0][:, None], (32, 64))
    s5par = np.stack([gq(lre), gq(lim), gq(ldt)], axis=1).astype(np.float32)

    def cb(b):
        a = b.reshape(4, 8, 4, 16, 16)
        return np.ascontiguousarray(a.transpose(1, 3, 0, 2, 4).reshape(128, 16, 16))
    bcc = np.stack([cb(f("s5_b_re")[0]), cb(f("s5_b_im")[0]),
                    cb(np.ascontiguousarray(f("s5_c_re")[0].transpose(0, 2, 1))),
                    cb(np.ascontiguousarray(f("s5_c_im")[0].transpose(0, 2, 1)))], axis=1).astype(np.float32)
    mask8 = np.zeros((128, 2, 8), np.float32)
    for p_ in range(128):
        mask8[p_, 0, p_ // 16] = 1.0
        mask8[p_, 1, p_ // 16] = -1.0
    lruw = np.zeros((128, 8, 128), np.float32)
    wa, wx = f("lru_w_a")[0], f("lru_w_x")[0]
    for mt in range(4):
        for hh in range(2):
            lruw[hh * 64:(hh + 1) * 64, mt, hh * 64:(hh + 1) * 64] = wa[2 * mt + hh]
            lruw[hh * 64:(hh + 1) * 64, 4 + mt, hh * 64:(hh + 1) * 64] = wx[2 * mt + hh]
    shared = {"w32": w32, "colp": colp, "gfin": gfin, "s5par": s5par, "bcc": bcc, "mask8": mask8, "lruw": lruw,
              "ident": np.eye(128, dtype=np.float32).astype(ml_dtypes.bfloat16),
              "identf": np.eye(128, dtype=np.float32)}
    return shared


def kernel(**inputs):
    shared = host_layout(inputs)
    x = np.asarray(inputs["x"], np.float32)
    mem = np.asarray(inputs["mem"], np.float32)
    nc = build(SEQ // TB)
    in_maps = []
    for c in range(8):
        m = dict(shared)
        m["x"] = np.ascontiguousarray(x[c])
        m["mem"] = np.ascontiguousarray(mem[c])
        in_maps.append(m)
    res = run_bass_kernel_spmd(nc, in_maps, core_ids=list(range(8)))
    return np.stack([np.asarray(r["out"], np.float32) for r in res.results], axis=0)
```

```python
import math
from contextlib import ExitStack
import numpy as np
import ml_dtypes
import concourse.bass as bass
import concourse.mybir as mybir
from concourse.bass_utils import run_bass_kernel_spmd

F32 = mybir.dt.float32
BF16 = mybir.dt.bfloat16
ALU = mybir.AluOpType
AF = mybir.ActivationFunctionType

SEQ = 4096
TB = 512
D = 1024
NS = 5
PI = math.pi

SL_WIN = 0
SL_S5W = 3
SL_S5V = 7
SL_KG = 11
SL_WOUT = 12
SL_WQ = 14
SL_WO = 16
SL_UP = 18
SL_DN = 29
SL_WK = 35
SL_WV = 37
NSLAB = 39
HOST_SLABS = [0, 1, 2, 11, 12, 13, 14, 15, 16, 17] + list(range(18, 35)) + [35, 36, 37, 38]

C_G1, C_G2, C_G3, C_GMEM = 0, 8, 16, 24
C_BGLU = 32
C_LCW = 36
C_LCB = 52
C_BA = 56
C_BX = 60
C_LAM = 64
C_FCW = 68
C_FCB = 200
C_DS5 = 244
NCOL = 248


class Prog:
    ENG = ("pe", "act", "dve", "pool", "sp")

    def __init__(self, nc, es):
        self.nc = nc
        self.es = es
        self.q = {e: [] for e in self.ENG}
        self.cnt = {}
        self.sems = {}
        self.seen = {e: {} for e in self.ENG}
        self.lastw = {}
        self.readers = {}

    def sem(self, key):
        if key not in self.sems:
            self.sems[key] = self.es.enter_context(self.nc.semaphore("s_" + key))
            self.cnt[key] = 0
        return self.sems[key]

    def op(self, eng, fn, r=(), w=(), dsem=None):
        deps = {}

        def add(tok):
            if tok is None:
                return
            k, v = tok
            if deps.get(k, 0) < v:
                deps[k] = v
        for b in r:
            add(self.lastw.get(b))
        for b in w:
            add(self.lastw.get(b))
            for k, v in self.readers.get(b, {}).items():
                add((k, v))
        waits = []
        for k, v in deps.items():
            if eng == "pe" and k == "pe":
                continue
            if self.seen[eng].get(k, 0) >= v:
                continue
            self.seen[eng][k] = v
            waits.append((k, v))
        if dsem is None:
            key, inc = eng, 1
        else:
            key, inc = dsem, 16
        self.sem(key)
        self.cnt[key] += inc
        tok = (key, self.cnt[key])
        self.q[eng].append((waits, fn, key, inc))
        for b in r:
            self.readers.setdefault(b, {})[key] = tok[1]
        for b in w:
            self.lastw[b] = tok
            self.readers[b] = {}
        return tok

    def barrier(self, skip=None):
        for e in self.ENG:
            waits = []
            for k, v in self.cnt.items():
                if skip is not None and k.startswith(skip):
                    continue
                if v > 0 and self.seen[e].get(k, 0) < v:
                    self.seen[e][k] = v
                    waits.append((k, v))
            if waits:
                self.q[e].append((waits, None, None, 0))

    def emit(self, block):
        sems = self.sems

        def run(eng, lst):
            for waits, fn, key, inc in lst:
                for k, v in waits:
                    eng.wait_ge(sems[k], v)
                if fn is not None:
                    ins = fn(eng)
                    ins.then_inc(sems[key], inc)

        @block.tensor
        def _(e):
            run(e, self.q["pe"])

        @block.scalar
        def _(e):
            run(e, self.q["act"])

        @block.vector
        def _(e):
            run(e, self.q["dve"])

        @block.gpsimd
        def _(e):
            run(e, self.q["pool"])

        @block.sync
        def _(e):
            run(e, self.q["sp"])


def build(nblk=8, stage=6, dbg=None, pro_only=False):
    nc = bass.Bass("TRN2", target_bir_lowering=False)
    ntok = nblk * TB

    def din(name, shape, dt=F32):
        return nc.dram_tensor(name, list(shape), dt, kind="ExternalInput").ap()
    x_d = din("x", [ntok, D])
    mem_d = din("mem", [256, D])
    w32_d = din("w32", [len(HOST_SLABS), 128, 4096])
    colp_d = din("colp", [128, NCOL])
    gfin_d = din("gfin", [128, D])
    s5par_d = din("s5par", [128, 3, 16])
    bcc_d = din("bcc", [128, 4, 16, 16])
    mask8_d = din("mask8", [128, 2, 8])
    lruw_d = din("lruw", [128, 8, 128])
    ident_d = din("ident", [128, 128], BF16)
    identf_d = din("identf", [128, 128])
    out_d = nc.dram_tensor("out", [ntok, D], F32, kind="ExternalOutput").ap()
    wscr = nc.dram_tensor("wscr", [NSLAB, 128, 4096], BF16, kind="Internal").ap()
    dbg_d = None
    if dbg is not None:
        dbg_d = nc.dram_tensor("dbg", [16, 128, 4096], F32, kind="ExternalOutput").ap()

    es = ExitStack()
    P = Prog(nc, es)

    def sb(name, shape, dt=F32, stack=es):
        return stack.enter_context(nc.sbuf_tensor("sb_" + name, list(shape), dt))

    colp = sb("colp", [128, NCOL])
    gfin = sb("gfin", [128, D])
    ident = sb("ident", [128, 128], BF16)
    onesb = sb("onesb", [128, 128], BF16)
    lruw = sb("lruw", [128, 8, 128], BF16)
    cneg = sb("cneg", [128, 4])
    epsc = sb("epsc", [128, 1])
    hpic = sb("hpic", [128, 1])
    onec = sb("onec", [128, 1])
    hbias = sb("hbias", [128, 12])
    cnegh = sb("cnegh", [128, 4])
    cosT = sb("cosT", [128, 16, 128])
    sinT = sb("sinT", [128, 16, 128])
    Rtab = sb("Rtab", [128, 16, 128])
    R4 = sb("R4", [128, 16])
    E4r = sb("E4r", [128, 16])
    E4i = sb("E4i", [128, 16])
    Kfm = sb("Kfm", [128, 8, 256], BF16)
    Vtok = sb("Vtok", [128, 2, D], BF16)
    slots = [sb(f"slot{i}", [128, 4096], BF16) for i in range(NS)]
    xres = [sb("xres0", [128, 4, D]), None]
    hT = sb("hT", [128, 4, D], BF16)
    hfm = sb("hfm", [128, 8, TB], BF16)
    ss = sb("ss", [128, 8])
    rstd = sb("rstd", [128, 8])
    junk = sb("junk", [128, D], BF16)
    s5carry_r = sb("s5cr", [128, 16])
    s5carry_i = sb("s5ci", [128, 16])
    lrucarry = sb("lrucarry", [128, 4])
    psb = [es.enter_context(nc.psum_tensor(f"pp{i}", [128, 512], F32)) for i in range(6)]
    psT = [es.enter_context(nc.psum_tensor(f"ppT{i}", [128, 1024], BF16)) for i in range(2)]

    cp = lambda c, n=1: colp[:, c:c + n]

    slab_state = {"n": 0}

    def load_slab(idx, eng="sp"):
        i = slab_state["n"] % NS
        slab_state["n"] += 1
        name = f"slot{i}"
        P.op(eng, lambda e, i=i, idx=idx: e.dma_start(out=slots[i][:, :], in_=wscr[idx]),
             r=[f"scr{idx}", f"scr{idx}g"], w=[name], dsem=f"d_slot{i}")
        return slots[i], name

    def mm_group(out_ap, pairs, r, w):
        n = len(pairs)

        def fn(e):
            ins = None
            for j, (l, rr) in enumerate(pairs):
                ins = e.matmul(out_ap, l, rr, start=(j == 0), stop=(j == n - 1))
            return ins
        P.op("pe", fn, r=r, w=w)

    def V(fn, r, w):
        P.op("dve", fn, r=r, w=w)

    def A(fn, r, w):
        P.op("act", fn, r=r, w=w)

    def G(fn, r, w):
        P.op("pool", fn, r=r, w=w)

    pst = ExitStack()
    par = sb("par", [128, 3, 16], stack=pst)
    identf = sb("identf", [128, 128], stack=pst)
    bcc = sb("bcc", [128, 4, 16, 16], stack=pst)
    mask8 = sb("mask8", [128, 2, 8], stack=pst)
    cs = sb("cs", [128, 4, 16, 16], stack=pst)
    t1 = sb("t1", [128, 16, 128], stack=pst)
    lruw32 = t1[:, 0:8, :]
    t2 = sb("t2", [128, 16, 128], stack=pst)
    NB = sb("NB", [128, 4, 2, 16, 128], BF16, stack=pst)
    Vm = sb("Vm", [128, 2, 4, 8, 128], BF16, stack=pst)
    Wst = sb("Wst", [128, 1, 8, 4, 128], BF16, stack=pst)
    Kst = hT[:, 2:4, :].rearrange("p a (b c) -> p (a b) c", c=128).rearrange("p (k t) c -> p k t c", k=4)
    sm = xres[0][:, 2:4, :].rearrange("p a (b c) -> p (a b) c", c=16)
    memx = xres[0]
    memfm = hfm

    def ld(dst, src, name, eng="sp"):
        P.op(eng, lambda e: e.dma_start(out=dst, in_=src), w=[name], dsem="d_" + name)
    ld(colp[:, :], colp_d, "colp")
    ld(gfin[:, :], gfin_d, "gfin")
    ld(ident[:, :], ident_d, "ident")
    ld(identf[:, :], identf_d, "identf")
    ld(par[:, :, :], s5par_d, "par")
    ld(bcc[:, :, :, :], bcc_d, "bcc")
    ld(mask8[:, :, :], mask8_d, "mask8")
    ld(lruw32, lruw_d, "t1")
    P.op("sp", lambda e: e.dma_start(out=memx[:, 0:2, :], in_=mem_d.rearrange("(a p) d -> p a d", p=128)),
         w=["x0a0", "x0a1"], dsem="d_x0")

    G(lambda e: e.memset(onesb[:, :], 1.0), [], ["onesb"])
    G(lambda e: e.memset(epsc[:, :], 1e-6), [], ["epsc"])
    G(lambda e: e.memset(hpic[:, :], PI / 2), [], ["hpic"])
    G(lambda e: e.memset(onec[:, :], 1.0), [], ["onec"])
    G(lambda e: e.memset(lrucarry[:, :], 0.0), [], ["lrucarry"])
    G(lambda e: e.memset(s5carry_r[:, :], 0.0), [], ["s5c"])
    G(lambda e: e.memset(s5carry_i[:, :], 0.0), [], ["s5c"])
    V(lambda e: e.tensor_copy(lruw[:, :, :], lruw32), ["t1"], ["lruw"])

    A(lambda e: e.activation(cneg[:, :], cp(C_LAM, 4), AF.Exp, scale=-1.0), ["colp"], ["cneg"])
    A(lambda e: e.activation(cneg[:, :], cneg[:, :], AF.Ln, bias=onec[:, :]), ["cneg", "onec"], ["cneg"])
    V(lambda e: e.tensor_scalar_mul(cneg[:, :], cneg[:, :], -8.0), ["cneg"], ["cneg"])
    V(lambda e: e.tensor_scalar_mul(cnegh[:, :], cneg[:, :], 0.5), ["cneg"], ["cnegh"])
    V(lambda e: e.tensor_scalar_mul(hbias[:, 0:4], cp(C_BA, 4), 0.5), ["colp"], ["hbias"])
    V(lambda e: e.tensor_scalar_mul(hbias[:, 4:8], cp(C_BX, 4), 0.5), ["colp"], ["hbias"])
    V(lambda e: e.tensor_scalar_mul(hbias[:, 8:12], cp(C_BGLU, 4), 0.5), ["colp"], ["hbias"])

    smn = {"i": 0}

    def S(name=None):
        i = smn["i"]
        smn["i"] += 1
        return sm[:, i, :], f"sm{i}"

    def vtt(o, a, b, op):
        (oa, on), (aa, an), (ba, bn) = o, a, b
        V(lambda e: e.tensor_tensor(oa, aa, ba, op), [an, bn], [on])

    def cmul(a_r, a_i, b_r, b_i):
        o_r, o_i, u1, u2 = S(), S(), S(), S()
        vtt(u1, a_r, b_r, ALU.mult)
        vtt(u2, a_i, b_i, ALU.mult)
        vtt(o_r, u1, u2, ALU.subtract)
        vtt(u1, a_r, b_i, ALU.mult)
        vtt(u2, a_i, b_r, ALU.mult)
        vtt(o_i, u1, u2, ALU.add)
        return o_r, o_i

    lre = (par[:, 0, :], "par")
    lim = (par[:, 1, :], "par")
    ldt = (par[:, 2, :], "par")
    dt_ = S()
    A(lambda e: e.activation(dt_[0], ldt[0], AF.Exp), ["par"], [dt_[1]])
    zr, zi = S(), S()
    vtt(zr, lre, dt_, ALU.mult)
    vtt(zi, lim, dt_, ALU.mult)
    mag = S()
    A(lambda e: e.activation(mag[0], zr[0], AF.Exp), [zr[1]], [mag[1]])

    sn0, cs0 = S(), S()
    A(lambda e: e.activation(sn0[0], zi[0], AF.Sin, scale=1.0 / 16), [zi[1]], [sn0[1]])
    A(lambda e: e.activation(cs0[0], zi[0], AF.Sin, scale=1.0 / 16, bias=hpic[:, :]), [zi[1], "hpic"], [cs0[1]])
    sn1, cs1 = sn0, cs0
    for _ in range(4):
        cs1, sn1 = cmul(cs1, sn1, cs1, sn1)
    L = [None] * 5
    L1r, L1i = S(), S()
    vtt(L1r, mag, cs1, ALU.mult)
    vtt(L1i, mag, sn1, ALU.mult)
    L[1] = (L1r, L1i)
    L[2] = cmul(L1r, L1i, L1r, L1i)
    L[3] = cmul(L[2][0], L[2][1], L1r, L1i)
    L[4] = cmul(L[2][0], L[2][1], L[2][0], L[2][1])
    one_, zero_ = S(), S()
    V(lambda e: e.memset(one_[0], 1.0), [], [one_[1]])
    V(lambda e: e.memset(zero_[0], 0.0), [], [zero_[1]])
    L[0] = (one_, zero_)
    am1 = S()
    V(lambda e: e.tensor_scalar_add(am1[0], L1r[0], -1.0), [L1r[1]], [am1[1]])
    nli = S()
    V(lambda e: e.tensor_scalar_mul(nli[0], lim[0], -1.0), ["par"], [nli[1]])
    num_r, num_i = cmul(am1, L1i, lre, nli)
    den, u3 = S(), S()
    vtt(den, lre, lre, ALU.mult)
    vtt(u3, lim, lim, ALU.mult)
    vtt(den, den, u3, ALU.add)
    kr, ki = S(), S()
    V(lambda e: e.reciprocal(den[0], den[0]), [den[1]], [den[1]])
    vtt(kr, num_r, den, ALU.mult)
    vtt(ki, num_i, den, ALU.mult)
    M = [cmul(L[3 - j][0], L[3 - j][1], kr, ki) for j in range(4)]
    r2 = S()
    vtt(r2, L[4][0], L[4][0], ALU.mult)
    vtt(u3, L[4][1], L[4][1], ALU.mult)
    vtt(r2, r2, u3, ALU.add)
    A(lambda e: e.activation(R4[:, :], r2[0], AF.Sqrt), [r2[1]], ["R4"])
    ir4 = S()
    V(lambda e: e.reciprocal(ir4[0], R4[:, :]), ["R4"], [ir4[1]])
    V(lambda e: e.tensor_tensor(E4r[:, :], L[4][0][0], ir4[0], ALU.mult), [L[4][0][1], ir4[1]], ["E4"])
    V(lambda e: e.tensor_tensor(E4i[:, :], L[4][1][0], ir4[0], ALU.mult), [L[4][1][1], ir4[1]], ["E4"])
    V(lambda e: e.memset(cosT[:, :, 0:1], 1.0), [], ["tab"])
    V(lambda e: e.memset(sinT[:, :, 0:1], 0.0), [], ["tab"])
    wr, wi = (E4r[:, :], "E4"), (E4i[:, :], "E4")
    n = 1
    while n < 128:
        wrb = wr[0].unsqueeze(2).to_broadcast([128, 16, n])
        wib = wi[0].unsqueeze(2).to_broadcast([128, 16, n])
        c0, s0 = cosT[:, :, 0:n], sinT[:, :, 0:n]
        c1, s1 = cosT[:, :, n:2 * n], sinT[:, :, n:2 * n]
        ta, tb_ = t1[:, :, 0:n], t2[:, :, 0:n]
        V(lambda e, c0=c0, wrb=wrb, ta=ta: e.tensor_tensor(ta, c0, wrb, ALU.mult), ["tab", wr[1]], ["t1"])
        V(lambda e, s0=s0, wib=wib, tb_=tb_: e.tensor_tensor(tb_, s0, wib, ALU.mult), ["tab", wi[1]], ["t2"])
        V(lambda e, c1=c1, ta=ta, tb_=tb_: e.tensor_tensor(c1, ta, tb_, ALU.subtract), ["t1", "t2"], ["tab"])
        V(lambda e, c0=c0, wib=wib, ta=ta: e.tensor_tensor(ta, c0, wib, ALU.mult), ["tab", wi[1]], ["t1"])
        V(lambda e, s0=s0, wrb=wrb, tb_=tb_: e.tensor_tensor(tb_, s0, wrb, ALU.mult), ["tab", wr[1]], ["t2"])
        V(lambda e, s1=s1, ta=ta, tb_=tb_: e.tensor_tensor(s1, ta, tb_, ALU.add), ["t1", "t2"], ["tab"])
        if n < 64:
            wr, wi = cmul(wr, wi, wr, wi)
        n *= 2
    V(lambda e: e.tensor_copy(Rtab[:, :, :], R4[:, :].unsqueeze(2).to_broadcast([128, 16, 128])), ["R4"], ["Rtab"])
    V(lambda e: e.memset(Rtab[:, :, 0:1], 0.0), ["Rtab"], ["Rtab"])

    def norm_stats(src, srcname, nsub):
        for a in range(nsub):
            if a % 2 == 0:
                A(lambda e, a=a: e.activation(junk[:, :], src[:, a, :], AF.Square, accum_out=ss[:, a:a + 1]),
                  [f"{srcname}{a}"], ["junk", f"ss{a}"])
            else:
                V(lambda e, a=a: e.scalar_tensor_tensor(hT[:, a, :], src[:, a, :], 1.0, src[:, a, :], op0=ALU.mult, op1=ALU.mult,
                                                        accum_out=ss[:, a:a + 1]), [f"{srcname}{a}"], [f"hT{a}", f"ss{a}"])
            A(lambda e, a=a: e.activation(rstd[:, a:a + 1], ss[:, a:a + 1], AF.Sqrt, scale=1.0 / D, bias=epsc[:, :]),
              [f"ss{a}", "epsc"], [f"rstd{a}"])
            V(lambda e, a=a: e.reciprocal(rstd[:, a:a + 1], rstd[:, a:a + 1]), [f"rstd{a}"], [f"rstd{a}"])
            if a % 2 == 0:
                A(lambda e, a=a: e.activation(hT[:, a, :], src[:, a, :], AF.Copy, scale=rstd[:, a:a + 1]),
                  [f"{srcname}{a}", f"rstd{a}"], [f"hT{a}"])
            else:
                V(lambda e, a=a: e.tensor_scalar_mul(hT[:, a, :], src[:, a, :], rstd[:, a:a + 1]),
                  [f"{srcname}{a}", f"rstd{a}"], [f"hT{a}"])

    def norm_transposes(nsub, col_g, dst_fm, dstname, ncols_tok):
        for kc in range(8):
            bi = kc % 2
            bank = psT[bi]

            def fn(e, kc=kc, bank=bank):
                ins = None
                for a in range(nsub):
                    ins = e.transpose(bank[:, a * 128:(a + 1) * 128], hT[:, a, kc * 128:(kc + 1) * 128], ident[:, :])
                return ins
            P.op("pe", fn, r=[f"hT{a}" for a in range(nsub)] + ["ident"], w=[f"psT{bi}"])
            if kc % 2 == 0:
                A(lambda e, kc=kc, bank=bank: e.activation(dst_fm[:, kc, 0:ncols_tok], bank[:, 0:ncols_tok], AF.Copy, scale=cp(col_g + kc)),
                  [f"psT{bi}", "colp"], [f"{dstname}{kc}"])
            else:
                V(lambda e, kc=kc, bank=bank: e.tensor_scalar_mul(dst_fm[:, kc, 0:ncols_tok], bank[:, 0:ncols_tok], cp(col_g + kc)),
                  [f"psT{bi}", "colp"], [f"{dstname}{kc}"])


    def rmsnorm_to_hT(src, srcname, nsub, col_g, dst_fm, dstname, ncols_tok):
        norm_stats(src, srcname, nsub)
        norm_transposes(nsub, col_g, dst_fm, dstname, ncols_tok)

    rmsnorm_to_hT(memx, "x0a", 2, C_GMEM, memfm, "hfm", 256)
    for half in range(2):
        P.op("pool", lambda e, half=half: e.dma_start(out=slots[half][:, :], in_=w32_d[HOST_SLABS.index(SL_WK + half)]),
             w=[f"slot{half}"], dsem=f"d_slot{half}")
        for mt in range(4):
            bi = 2 + mt % 2
            pairs = [(slots[half][:, kc * 512 + mt * 128: kc * 512 + (mt + 1) * 128], memfm[:, kc, 0:256]) for kc in range(8)]
            mm_group(psb[bi][:, 0:256], pairs, [f"slot{half}"] + [f"hfm{kc}" for kc in range(8)], [f"ps{bi}"])
            A(lambda e, half=half, mt=mt, bi=bi: e.copy(Kfm[:, half * 4 + mt, :], psb[bi][:, 0:256]), [f"ps{bi}"], ["Kfm"])
    for half in range(2):
        P.op("pool", lambda e, half=half: e.dma_start(out=slots[2 + half][:, :], in_=w32_d[HOST_SLABS.index(SL_WV + half)]),
             w=[f"slot{2 + half}"], dsem=f"d_slot{2 + half}")
        for mc in range(2):
            bi = 4 + mc
            pairs = [(memfm[:, kc, mc * 128:(mc + 1) * 128], slots[2 + half][:, kc * 512:(kc + 1) * 512]) for kc in range(8)]
            mm_group(psb[bi][:, :], pairs, [f"slot{2 + half}"] + [f"hfm{kc}" for kc in range(8)], [f"ps{bi}"])
            V(lambda e, half=half, mc=mc, bi=bi: e.tensor_copy(Vtok[:, mc, half * 512:(half + 1) * 512], psb[bi][:, :]),
              [f"ps{bi}"], ["Vtok"])

    bre, bim, cre, cim = (bcc[:, i_, :, :] for i_ in range(4))
    c1, c2, c3, c4 = (cs[:, i_, :, :] for i_ in range(4))
    mpos = mask8[:, 0, :].unsqueeze(1).unsqueeze(3)
    mneg = mask8[:, 1, :].unsqueeze(1).unsqueeze(3)

    def bc16(ap):
        return ap.unsqueeze(2).to_broadcast([128, 16, 16])
    for j in range(4):
        mr, mi = bc16(M[j][0][0]), bc16(M[j][1][0])
        mrn, min_ = M[j][0][1], M[j][1][1]
        V(lambda e, mr=mr: e.tensor_tensor(c1, bre, mr, ALU.mult), ["bcc", mrn], ["c1"])
        V(lambda e, mi=mi: e.tensor_tensor(c2, bim, mi, ALU.mult), ["bcc", min_], ["c2"])
        V(lambda e, mi=mi: e.tensor_tensor(c3, bre, mi, ALU.mult), ["bcc", min_], ["c3"])
        V(lambda e, mr=mr: e.tensor_tensor(c4, bim, mr, ALU.mult), ["bcc", mrn], ["c4"])
        V(lambda e: e.tensor_tensor(c1, c1, c2, ALU.subtract), ["c1", "c2"], ["c1"])
        V(lambda e: e.tensor_tensor(c3, c3, c4, ALU.add), ["c3", "c4"], ["c3"])
        for ri, cc, cn in ((0, c1, "c1"), (1, c3, "c3")):
            o = NB[:, j, ri, :, :].rearrange("p a (g h) -> p a g h", g=8)
            V(lambda e, o=o, cc=cc: e.tensor_tensor(o, cc.unsqueeze(2).to_broadcast([128, 16, 8, 16]),
                                                    mpos.to_broadcast([128, 16, 8, 16]), ALU.mult), [cn, "mask8"], [f"NB{j}"])
    tix = {"i": 0}

    def w_section(k):
        for sg in range(8):
            s_, ri = sg % 4, sg // 4
            bank = psT[tix["i"] % 2]
            bname = f"psT{tix['i'] % 2}"
            tix["i"] += 1

            def fn(e, s_=s_, ri=ri, bank=bank):
                ins = None
                for j in range(4):
                    ins = e.transpose(bank[:, j * 128:(j + 1) * 128], NB[:, j, ri, k * 4 + s_, :], ident[:, :])
                return ins
            P.op("pe", fn, r=[f"NB{j}" for j in range(4)] + ["ident"], w=[bname])
            dst = Wst[:, 0, sg, :, :]
            src = bank[:, 0:512].rearrange("p (j c) -> p j c", j=4)
            A(lambda e, dst=dst, src=src: e.copy(dst, src), [bname], ["Wst"])
        P.op("sp", lambda e: e.dma_start(out=wscr[SL_S5W + k], in_=Wst[:, 0, :, :, :].rearrange("p a b c -> p (a b c)")),
             r=["Wst"], w=[f"scr{SL_S5W + k}"], dsem=f"d_w{k}")
    def gen_V(m):
        lr, li = bc16(L[m][0][0]), bc16(L[m][1][0])
        lrn, lin = L[m][0][1], L[m][1][1]
        vb = Vm[:, m % 2, :, :, :]
        on = f"Vm{m % 2}"
        V(lambda e: e.tensor_tensor(c1, cre, lr, ALU.mult), ["bcc", lrn], ["c1"])
        V(lambda e: e.tensor_tensor(c2, cim, li, ALU.mult), ["bcc", lin], ["c2"])
        V(lambda e: e.tensor_tensor(c3, cre, li, ALU.mult), ["bcc", lin], ["c3"])
        V(lambda e: e.tensor_tensor(c4, cim, lr, ALU.mult), ["bcc", lrn], ["c4"])
        V(lambda e: e.tensor_tensor(c1, c1, c2, ALU.subtract), ["c1", "c2"], ["c1"])
        V(lambda e: e.tensor_tensor(c3, c3, c4, ALU.add), ["c3", "c4"], ["c3"])
        for k in range(4):
            for half, cc, cn, mk in ((0, c1, "c1", mpos), (1, c3, "c3", mneg)):
                o = vb[:, k, half * 4:(half + 1) * 4, :].rearrange("p s (g h) -> p s g h", g=8)
                V(lambda e, o=o, cc=cc, mk=mk, k=k: e.tensor_tensor(o, cc[:, k * 4:(k + 1) * 4, :].unsqueeze(2).to_broadcast([128, 4, 8, 16]),
                                                                   mk.to_broadcast([128, 4, 8, 16]), ALU.mult), [cn, "mask8"], [on])
        if m >= 1:
            for k in range(4):
                dst = wscr[SL_S5V + k].rearrange("p (s m c) -> p s m c", s=8, m=4)[:, :, m - 1, :]
                P.op("sp", lambda e, k=k, dst=dst: e.dma_start(out=dst, in_=vb[:, k, :, :]),
                     r=[on], w=[f"scr{SL_S5V + k}"], dsem=f"d_v{k}_{m}")
        if m <= 3:
            for k in range(4):
                pairs = [(NB[:, 3, sg // 4, k * 4 + sg % 4, :], vb[:, k, sg, :]) for sg in range(8)]
                mm_group(psb[k][:, m * 128:(m + 1) * 128], pairs, ["NB3", on], [f"ps{k}"])
    for m in range(5):
        gen_V(m)
        if m < 4:
            w_section(m)
    for k in range(4):
        V(lambda e, k=k: e.scalar_tensor_tensor(Kst[:, k, 0, :], identf[:, :], cp(C_DS5 + k), psb[k][:, 0:128],
                                                op0=ALU.mult, op1=ALU.add), [f"ps{k}", "identf", "colp"], ["Kst"])
        V(lambda e, k=k: e.tensor_copy(Kst[:, k, 1:4, :], psb[k][:, 128:512].rearrange("p (j c) -> p j c", j=3)),
          [f"ps{k}"], ["Kst"])
    P.op("sp", lambda e: e.dma_start(out=wscr[SL_KG][:, 0:2048], in_=hT[:, 2:4, :].rearrange("p a b -> p (a b)")),
         r=["Kst"], w=[f"scr{SL_KG}"], dsem="d_kst")

    for hi, sl in enumerate(HOST_SLABS):
        if sl >= SL_WK:
            continue
        if sl == SL_KG:
            P.op("pool", lambda e, hi=hi, sl=sl: e.dma_start(out=wscr[sl][:, 2048:4096], in_=w32_d[hi][:, 2048:4096]),
                 w=[f"scr{sl}g"], dsem=f"d_cast{hi}")
        else:
            P.op("pool", lambda e, hi=hi, sl=sl: e.dma_start(out=wscr[sl], in_=w32_d[hi]),
                 w=[f"scr{sl}"], dsem=f"d_cast{hi}")

    if dbg == "pro":
        pass
    P.barrier(skip="d_cast")
    pst.close()

    xres[1] = sb("xres1", [128, 4, D])
    xlh = sb("xlh", [128, 4, 3 + TB])
    ffnhalo = sb("ffnhalo", [128, 2, 44, 2])
    G(lambda e: e.memset(xlh[:, :, 0:3], 0.0), [], [f"xl{i}" for i in range(4)])
    G(lambda e: e.memset(ffnhalo[:, :, :, :], 0.0), [], ["ffnhalo0", "ffnhalo1"])
    gl = sb("gl", [128, 4, TB], BF16)
    zq = sb("zq", [128, 4, 512])
    S_sb = sb("S_sb", [128, 4, 8, 130], BF16)
    ymix = sb("ymix", [128, 8, TB], BF16)
    ltb = sb("lt", [128, 6, TB + 4])
    lt = ltb[:, :, 0:TB]
    xcb = sb("xcb", [128, 2, TB], BF16)
    hlb = sb("hl", [128, 2, TB + 4])
    hl = hlb[:, :, 0:TB]
    gated = sb("gated", [128, 22, TB], BF16)
    q_sb = ymix
    pT = gated[:, 0:8, :].rearrange("p (h m) t -> p h m t", h=4)
    o_sb = gated[:, 8:16, :]
    u_sb = gated[:, 16:20, :]
    ygelu = gated[:, 0:4, :]
    rden = hl
    zbuf = ltb[:, 0:2, 0:TB + 2]
    cv = lt[:, 2:4, :]
    cg = lt[:, 4:6, :]
    gg = cg
    qinit = sb("qinit", [128, 3, 16])
    hc = sb("hc", [128, 44, 2])
    hc2 = sb("hc2", [128, 44])
    G(lambda e: e.memset(S_sb[:, :, :, :], 0.0), [], ["S_sb0", "S_sb1", "S_sb2", "S_sb3"])

    rr = {"ps": 0, "ps6": 0}

    def nextbank():
        b = rr["ps"] % 4
        rr["ps"] += 1
        return b

    def nextbank6():
        b = rr["ps6"] % 6
        rr["ps6"] += 1
        return b

    hnames = [f"hfm{kc}" for kc in range(8)]

    def add_resid(xb, a, nh, bi):
        xs = xres[xb][:, a, nh * 512:(nh + 1) * 512]
        V(lambda e: e.tensor_tensor(xs, xs, psb[bi][:, :], ALU.add), [f"ps{bi}", f"x{xb}a{a}"], [f"x{xb}a{a}"])

    def proj_tok(lhs_tile, lhs_names, slab_ids, xb, bankfn):
        for nh in range(2):
            st, sn = load_slab(slab_ids[nh])
            for a in range(4):
                bi = bankfn()
                pairs = [(lhs_tile[:, kc, a * 128:(a + 1) * 128], st[:, kc * 512:(kc + 1) * 512]) for kc in range(8)]
                mm_group(psb[bi][:, :], pairs, [sn] + lhs_names, [f"ps{bi}"])
                add_resid(xb, a, nh, bi)

    def fm_proj(st, sn, mt, evac, bankfn=None):
        bi = (bankfn or nextbank)()
        pairs = [(st[:, kc * 512 + mt * 128: kc * 512 + (mt + 1) * 128], hfm[:, kc, :]) for kc in range(8)]
        mm_group(psb[bi][:, :], pairs, [sn] + hnames, [f"ps{bi}"])
        evac(bi)

    u4 = u_sb[:, :, :].rearrange("p k (c j) -> p k c j", j=4)

    def v4(ap):
        return ap.rearrange("p (s c) -> p s c", s=4)

    def s5_qinit():
        q0, q1, q2 = qinit[:, 0, :], qinit[:, 1, :], qinit[:, 2, :]
        cr, ci = s5carry_r[:, :], s5carry_i[:, :]
        er, ei, r4 = E4r[:, :], E4i[:, :], R4[:, :]
        V(lambda e: e.tensor_tensor(q0, cr, er, ALU.mult), ["s5c", "E4"], ["qi0"])
        V(lambda e: e.tensor_tensor(q2, ci, ei, ALU.mult), ["s5c", "E4"], ["qi2"])
        V(lambda e: e.tensor_tensor(q1, cr, ei, ALU.mult), ["s5c", "E4"], ["qi1"])
        V(lambda e: e.tensor_tensor(q0, q0, q2, ALU.subtract), ["qi0", "qi2"], ["qi0"])
        V(lambda e: e.tensor_tensor(q2, ci, er, ALU.mult), ["s5c", "E4", "qi0"], ["qi2"])
        V(lambda e: e.tensor_tensor(q0, q0, r4, ALU.mult), ["qi0", "R4"], ["qi0"])
        V(lambda e: e.tensor_tensor(q1, q1, q2, ALU.add), ["qi1", "qi2"], ["qi1"])
        V(lambda e: e.tensor_tensor(q1, q1, r4, ALU.mult), ["qi1", "R4"], ["qi1"])

    def s5_B_pe(k):
        st, sn = load_slab(SL_S5W + k)
        for half, bi in ((0, 4), (1, 5)):
            for s_ in range(4):
                sg = half * 4 + s_
                pairs = [(st[:, (sg * 4 + j) * 128:(sg * 4 + j + 1) * 128], u4[:, k, :, j]) for j in range(4)]
                mm_group(psb[bi][:, s_ * 128:(s_ + 1) * 128], pairs, [sn, f"gated{16 + k}"], [f"ps{bi}"])

    def s5_B_steps(k):
        cT = cosT[:, k * 4:(k + 1) * 4, :]
        sT = sinT[:, k * 4:(k + 1) * 4, :]
        Xr = v4(psb[4][:, :])
        Xi = v4(psb[5][:, :])
        Zr, Zi, Qr, Qi = (zq[:, i, :] for i in range(4))
        tmp = lt[:, 0, :]
        ks = slice(k * 4, (k + 1) * 4)
        q0, q1 = qinit[:, 0, ks], qinit[:, 1, ks]
        cr, ci = s5carry_r[:, ks], s5carry_i[:, ks]
        Rk = Rtab[:, k * 4:(k + 1) * 4, :].rearrange("p s c -> p (s c)")
        Sre = S_sb[:, k, 0:4, 1:129]
        Sim = S_sb[:, k, 4:8, 1:129]
        sname = f"S_sb{k}"
        ta, tb_ = lt[:, 2, :], lt[:, 3, :]
        TT = lambda o, a, b, op, r, w: (lambda: V(lambda e: e.tensor_tensor(o, a, b, op), r, w))
        return [
            TT(v4(Zr), Xr, cT, ALU.mult, ["ps4", "tab"], ["zq0"]),
            TT(v4(tmp), Xi, sT, ALU.mult, ["ps5", "tab"], ["lt0"]),
            TT(Zr, Zr, tmp, ALU.add, ["zq0", "lt0"], ["zq0"]),
            TT(v4(Zi), Xi, cT, ALU.mult, ["ps5", "tab"], ["zq1"]),
            TT(v4(tmp), Xr, sT, ALU.mult, ["ps4", "tab"], ["lt0"]),
            TT(Zi, Zi, tmp, ALU.subtract, ["zq1", "lt0"], ["zq1"]),
            TT(v4(Zr)[:, :, 0], v4(Zr)[:, :, 0], q0, ALU.add, ["zq0", "qi0"], ["zq0"]),
            TT(v4(Zi)[:, :, 0], v4(Zi)[:, :, 0], q1, ALU.add, ["zq1", "qi1"], ["zq1"]),
            lambda: V(lambda e: e.tensor_tensor_scan(Qr, Rk, Zr, 0.0, op0=ALU.mult, op1=ALU.add), ["zq0", "Rtab"], ["zq2"]),
            lambda: V(lambda e: e.tensor_tensor_scan(Qi, Rk, Zi, 0.0, op0=ALU.mult, op1=ALU.add), ["zq1", "Rtab"], ["zq3"]),
            lambda: V(lambda e: e.tensor_copy(S_sb[:, k, :, 0:1], S_sb[:, k, :, 128:129]), [sname], [sname]),
            TT(v4(ta), v4(Qr), cT, ALU.mult, ["zq2", "tab"], ["lt2"]),
            TT(v4(tb_), v4(Qi), sT, ALU.mult, ["zq3", "tab"], ["lt3"]),
            TT(Sre, v4(ta), v4(tb_), ALU.subtract, ["lt2", "lt3"], [sname]),
            TT(cr, v4(ta)[:, :, 127], v4(tb_)[:, :, 127], ALU.subtract, ["lt2", "lt3"], ["s5c"]),
            TT(v4(ta), v4(Qr), sT, ALU.mult, ["zq2", "tab"], ["lt2"]),
            TT(v4(tb_), v4(Qi), cT, ALU.mult, ["zq3", "tab"], ["lt3"]),
            TT(Sim, v4(ta), v4(tb_), ALU.add, ["lt2", "lt3"], [sname]),
            TT(ci, v4(ta)[:, :, 127], v4(tb_)[:, :, 127], ALU.add, ["lt2", "lt3"], ["s5c"]),
        ]

    def lru_steps(mt):
        xl = xlh[:, mt, :]
        xn = f"xl{mt}"
        (xc, xcn), (ra, ran), (i_, in_) = [(lt[:, r, :], f"lt{r}") for r in (4, 5, 1)]
        m_, mn = xc, xcn
        xb_ = xcb[:, 0, :]
        xbn = "xcb0"
        hb = hl[:, mt % 2, :]
        hn = f"hl{mt % 2}"
        bk = {}
        st = []
        st.append(lambda: A(lambda e: e.activation(xc, xl[:, 3:3 + TB], AF.Identity, scale=cp(C_LCW + 3 * 4 + mt), bias=cp(C_LCB + mt)),
                            [xn, "colp"], [xcn]))

        def tap(t_):
            return lambda: V(lambda e: e.scalar_tensor_tensor(xc, xl[:, t_:t_ + TB], cp(C_LCW + t_ * 4 + mt), xc, op0=ALU.mult, op1=ALU.add),
                             [xn, xcn], [xcn])
        for t_ in range(3):
            st.append(tap(t_))
        st.append(lambda: G(lambda e: e.tensor_copy(xl[:, 0:3], xl[:, TB:TB + 3]), [xn], [xn]))
        st.append(lambda: A(lambda e: e.copy(xb_, xc), [xcn], [xbn]))

        def gates():
            bk["b1"], bk["b2"] = nextbank(), nextbank()
            mm_group(psb[bk["b1"]][:, :], [(lruw[:, mt, :], xb_)], ["lruw", xbn], [f"ps{bk['b1']}"])
            mm_group(psb[bk["b2"]][:, :], [(lruw[:, 4 + mt, :], xb_)], ["lruw", xbn], [f"ps{bk['b2']}"])
        st.append(gates)
        st.append(lambda: A(lambda e: e.activation(ra, psb[bk["b1"]][:, :], AF.Tanh, scale=0.5, bias=hbias[:, mt:mt + 1]),
                            [f"ps{bk['b1']}", "hbias"], [ran]))
        st.append(lambda: A(lambda e: e.activation(i_, psb[bk["b2"]][:, :], AF.Tanh, scale=0.5, bias=hbias[:, 4 + mt:5 + mt]),
                            [f"ps{bk['b2']}", "hbias"], [in_]))
        st.append(lambda: V(lambda e: e.scalar_tensor_tensor(i_, i_, 1.0, xc, op0=ALU.add, op1=ALU.mult), [in_, xcn], [in_]))
        st.append(lambda: A(lambda e: e.activation(ra, ra, AF.Exp, scale=cnegh[:, mt:mt + 1], bias=cnegh[:, mt:mt + 1]), [ran, "cnegh"], [ran]))
        st.append(lambda: A(lambda e: e.activation(m_, ra, AF.Square), [ran, in_], [mn]))
        st.append(lambda: A(lambda e: e.activation(m_, m_, AF.Sqrt, scale=-1.0, bias=onec[:, :]), [mn, "onec"], [mn]))
        st.append(lambda: V(lambda e: e.scalar_tensor_tensor(i_, i_, 0.5, m_, op0=ALU.mult, op1=ALU.mult), [in_, mn], [in_]))
        st.append(lambda: V(lambda e: e.tensor_tensor_scan(hb, ra, i_, lrucarry[:, mt:mt + 1], op0=ALU.mult, op1=ALU.add),
                            [ran, in_, "lrucarry"], [hn]))
        st.append(lambda: V(lambda e: e.tensor_copy(lrucarry[:, mt:mt + 1], hb[:, TB - 1:TB]), [hn], ["lrucarry"]))
        prod = lambda: V(lambda e: e.tensor_tensor(ymix[:, 4 + mt, :], hb, gl[:, mt, :], ALU.mult), [hn, f"gl{mt}"], [f"ymix{4 + mt}"])
        return st, prod

    def zip_steps(a, b):
        for i in range(max(len(a), len(b))):
            if i < len(a):
                a[i]()
            if i < len(b):
                b[i]()

    def s5_D(k, stK, snK):
        st, sn = load_slab(SL_S5V + k)
        bi = nextbank()
        yv = psb[bi][:, :].rearrange("p (c i) -> p c i", i=4)
        for i in range(4):
            pairs = [(st[:, (sg * 4 + i) * 128:(sg * 4 + i + 1) * 128], S_sb[:, k, sg, 0:128]) for sg in range(8)]
            pairs += [(stK[:, (k * 4 + (i - j)) * 128:(k * 4 + (i - j) + 1) * 128], u4[:, k, :, j]) for j in range(i + 1)]
            mm_group(yv[:, :, i], pairs, [sn, snK, f"S_sb{k}", f"gated{16 + k}"], [f"ps{bi}"])
        A(lambda e: e.activation(ygelu[:, k, :], psb[bi][:, :], AF.Gelu), [f"ps{bi}"], [f"gated{k}"])

    def glu_tile(mt, stK, snK):
        bi = nextbank()
        pairs = [(stK[:, 2048 + kc * 512 + mt * 128: 2048 + kc * 512 + (mt + 1) * 128], ygelu[:, kc, :]) for kc in range(4)]
        mm_group(psb[bi][:, :], pairs, [snK] + [f"gated{k}" for k in range(4)], [f"ps{bi}"])
        gt = lt[:, 2 + mt % 2, :]
        gn = f"lt{2 + mt % 2}"
        A(lambda e: e.activation(gt, psb[bi][:, :], AF.Sigmoid, bias=cp(C_BGLU + mt)), [f"ps{bi}", "colp"], [gn])
        V(lambda e: e.tensor_tensor(ymix[:, mt, :], ygelu[:, mt, :], gt, ALU.mult), [f"gated{mt}", gn], [f"ymix{mt}"])

    def attn_head(hd):
        def sc(mc):
            bi = nextbank6()
            pairs = [(Kfm[:, hd * 2 + c2, mc * 128:(mc + 1) * 128], q_sb[:, hd * 2 + c2, :]) for c2 in range(2)]
            mm_group(psb[bi][:, :], pairs, ["Kfm", f"ymix{hd * 2}", f"ymix{hd * 2 + 1}"], [f"ps{bi}"])
            def expfn(e):
                return e.activation(pT[:, hd, mc, :], psb[bi][:, :], AF.Exp)
            A(expfn, [f"ps{bi}"], [f"gated{2 * hd}", f"gated{2 * hd + 1}"])
        sc(0)
        sc(1)

    def attn_tail(hd):
        bd = nextbank6()
        mm_group(psb[bd][:, :], [(onesb[:, :], pT[:, hd, mc, :]) for mc in range(2)], ["onesb", f"gated{2 * hd}", f"gated{2 * hd + 1}"], [f"ps{bd}"])
        rd = rden[:, hd % 2, :]
        rn = f"hl{hd % 2}"
        A(lambda e: e.activation(rd, psb[bd][:, :], AF.Ln), [f"ps{bd}"], [rn])
        A(lambda e: e.activation(rd, rd, AF.Exp, scale=-1.0), [rn], [rn])

        def pv(j):
            bi = nextbank6()
            pairs = [(Vtok[:, mc, hd * 256 + j * 128: hd * 256 + (j + 1) * 128], pT[:, hd, mc, :]) for mc in range(2)]
            mm_group(psb[bi][:, :], pairs, ["Vtok", f"gated{2 * hd}", f"gated{2 * hd + 1}"], [f"ps{bi}"])
            V(lambda e: e.tensor_tensor(o_sb[:, hd * 2 + j, :], psb[bi][:, :], rd, ALU.mult), [f"ps{bi}", rn], [f"gated{8 + hd * 2 + j}"])
        pv(0)
        pv(1)

    def ffn_tile_mm(st, sn, sa, tt, isg, tb):
        vt = 2 * sa + tt
        ch = vt + 22 * isg
        col = isg * 256 + tt * 128
        bi = nextbank6()
        pairs = [(st[:, kc * 512 + col: kc * 512 + col + 128], hfm[:, kc, :]) for kc in range(8)]
        mm_group(psb[bi][:, :], pairs, [sn] + hnames, [f"ps{bi}"])
        ps = psb[bi]
        pn = f"ps{bi}"
        dst = (cv if isg == 0 else cg)[:, tt, :]
        dn = f"lt{2 + 2 * isg + tt}"
        hold = ffnhalo[:, tb % 2, ch, :]
        hnew = ffnhalo[:, (tb + 1) % 2, ch, :]
        wcol = lambda t_: cp(C_FCW + t_ * 44 + ch)
        A(lambda e: e.activation(dst, ps[:, :], AF.Identity, scale=wcol(2), bias=cp(C_FCB + ch)), [pn, "colp"], [dn])
        A(lambda e: e.copy(hnew, ps[:, TB - 2:TB]), [pn], [f"ffnhalo{(tb + 1) % 2}"])
        return ps, pn, dst, dn, wcol, hold, f"ffnhalo{tb % 2}"

    def ffn_tap(t_, ps, pn, dst, dn, wcol, hold, hn):
        sh = 2 - t_
        V(lambda e: e.scalar_tensor_tensor(dst[:, sh:TB], ps[:, 0:TB - sh], wcol(t_), dst[:, sh:TB], op0=ALU.mult, op1=ALU.add),
          [pn, dn, "colp"], [dn])

    def ffn_halo_prep(tb):
        hold = ffnhalo[:, tb % 2, :, :]
        hn = f"ffnhalo{tb % 2}"
        W0, W1 = cp(C_FCW, 44), cp(C_FCW + 44, 44)
        V(lambda e: e.tensor_tensor(hc[:, :, 1], hold[:, :, 1], W0, ALU.mult), [hn, "colp"], ["hc"])
        V(lambda e: e.tensor_tensor(hc[:, :, 0], hold[:, :, 0], W0, ALU.mult), [hn, "colp"], ["hc"])
        V(lambda e: e.tensor_tensor(hc2[:, :], hold[:, :, 1], W1, ALU.mult), [hn, "colp"], ["hc2"])
        V(lambda e: e.tensor_tensor(hc[:, :, 0], hc[:, :, 0], hc2[:, :], ALU.add), ["hc", "hc2"], ["hc"])

    def ffn_halo_add(ch, dst, dn):
        G(lambda e: e.tensor_tensor(dst[:, 0:2], dst[:, 0:2], hc[:, ch, :], ALU.add), ["hc", dn], [dn])

    def ffn_pair(st, sn, sa, tt, tb, prev_tail):
        vt = 2 * sa + tt
        tiles = [ffn_tile_mm(st, sn, sa, tt, 0, tb), ffn_tile_mm(st, sn, sa, tt, 1, tb)]
        if prev_tail is not None:
            prev_tail()
        for t_ in (1, 0):
            for tl in tiles:
                ffn_tap(t_, *tl)
        for isg, tl in enumerate(tiles):
            ffn_halo_add(vt + 22 * isg, tl[2], tl[3])

        def tail():
            A(lambda e: e.activation(cg[:, tt, :], cg[:, tt, :], AF.Gelu), [f"lt{4 + tt}"], [f"lt{4 + tt}"])
            G(lambda e: e.tensor_tensor(gated[:, vt, :], cg[:, tt, :], cv[:, tt, :], ALU.mult), [f"lt{4 + tt}", f"lt{2 + tt}"], [f"gated{vt}"])
        return tail

    def down_group(xb, nh, sg3, kc0, nk, a, bi):
        st, sn = down_group.cur

        def fn(e):
            ins = None
            for kk in range(nk):
                ins = e.matmul(psb[bi][:, :], gated[:, kc0 + kk, a * 128:(a + 1) * 128], st[:, kk * 512:(kk + 1) * 512],
                               start=(sg3 == 0 and kk == 0), stop=(sg3 == 2 and kk == nk - 1))
            return ins
        P.op("pe", fn, r=[sn] + [f"gated{kc0 + kk}" for kk in range(nk)], w=[f"ps{bi}"])

    def final_norm(xb, a):
        xa = xres[xb][:, a, :]
        sa_, ra_ = ss[:, 4 + a:5 + a], rstd[:, 4 + a:5 + a]
        xn = f"x{xb}a{a}"
        A(lambda e: e.activation(junk[:, :], xa, AF.Square, accum_out=sa_), [xn], ["junk", f"ssf{a}"])
        A(lambda e: e.activation(ra_, sa_, AF.Sqrt, scale=1.0 / D, bias=epsc[:, :]), [f"ssf{a}", "epsc"], [f"rstdf{a}"])
        V(lambda e: e.reciprocal(ra_, ra_), [f"rstdf{a}"], [f"rstdf{a}"])
        V(lambda e: e.scalar_tensor_tensor(xa, xa, ra_, gfin[:, :], op0=ALU.mult, op1=ALU.mult), [xn, f"rstdf{a}", "gfin"], [xn])

    x_t = x_d.rearrange("(b a p) d -> b p a d", a=4, p=128)
    out_t = out_d.rearrange("(b a p) d -> b p a d", a=4, p=128)

    def load_x(tb):
        xb = tb % 2
        P.op("sp", lambda e: e.dma_start(out=xres[xb][:, :, :], in_=x_t[tb]),
             w=[f"x{xb}a{a}" for a in range(4)], dsem=f"d_x{xb}")

    def store_out(tb):
        xb = tb % 2
        P.op("sp", lambda e: e.dma_start(out=out_t[tb], in_=xres[xb][:, :, :]),
             r=[f"x{xb}a{a}" for a in range(4)], dsem=f"d_o{xb}")

    def win_evac(grp, mt):
        def ev(bi):
            if grp == 0:
                A(lambda e: e.copy(u_sb[:, mt, :], psb[bi][:, :]), [f"ps{bi}"], [f"gated{16 + mt}"])
            elif grp == 1:
                A(lambda e: e.copy(xlh[:, mt, 3:3 + TB], psb[bi][:, :]), [f"ps{bi}"], [f"xl{mt}"])
            else:
                A(lambda e: e.activation(gl[:, mt, :], psb[bi][:, :], AF.Gelu), [f"ps{bi}"], [f"gl{mt}"])
        return ev

    def q_evac(half, mt):
        def ev(bi):
            A(lambda e: e.activation(q_sb[:, half * 4 + mt, :], psb[bi][:, :], AF.Copy, scale=0.0625),
              [f"ps{bi}"], [f"ymix{half * 4 + mt}"])
        return ev

    def finish_block(tb):
        if stage >= 6:
            for a in range(4):
                final_norm(tb % 2, a)

    def block_body(tb):
        xb = tb % 2
        if tb == 0:
            norm_stats(xres[xb], f"x{xb}a", 4)
        norm_transposes(4, C_G1, hfm, "hfm", TB)
        st0, sn0 = load_slab(SL_WIN)
        for mt in range(4):
            fm_proj(st0, sn0, mt, win_evac(0, mt))
        if tb > 0:
            finish_block(tb - 1)
        s5_qinit()
        wslabs = {}

        def win_tiles(grp, mts):
            if grp not in wslabs:
                wslabs[grp] = load_slab(SL_WIN + grp)
            st, sn = wslabs[grp]
            for mt in mts:
                fm_proj(st, sn, mt, win_evac(grp, mt))
        s5_B_pe(0)
        zip_steps(s5_B_steps(0), [])
        win_tiles(1, [0, 1])
        prods = {}
        for k in range(1, 4):
            s5_B_pe(k)
            lst, prods[k - 1] = lru_steps(k - 1)
            zip_steps(s5_B_steps(k), lst)
            if k == 1:
                win_tiles(1, [2, 3])
            elif k == 2:
                win_tiles(2, [0, 1])
                prods[0]()
                prods[1]()
            else:
                win_tiles(2, [2, 3])
        if stage < 2:
            return
        stK, snK = load_slab(SL_KG)
        lst, prods[3] = lru_steps(3)
        cuts = [0, 6, 11, 15, len(lst)]
        for k in range(4):
            s5_D(k, stK, snK)
            for f in lst[cuts[k]:cuts[k + 1]]:
                f()
        prods[2]()
        prods[3]()
        for mt in range(4):
            glu_tile(mt, stK, snK)
        if tb > 0:
            store_out(tb - 1)
        if stage < 3:
            return
        proj_tok(ymix, [f"ymix{i}" for i in range(8)], [SL_WOUT, SL_WOUT + 1], xb, nextbank6)
        if stage < 4:
            return
        rmsnorm_to_hT(xres[xb], f"x{xb}a", 4, C_G2, hfm, "hfm", TB)
        for half in range(2):
            st, sn = load_slab(SL_WQ + half)
            for mt in range(4):
                fm_proj(st, sn, mt, q_evac(half, mt), nextbank6)
        attn_head(0)
        for hd in range(4):
            if hd + 1 < 4:
                attn_head(hd + 1)
            attn_tail(hd)
        proj_tok(o_sb, [f"gated{8 + i}" for i in range(8)], [SL_WO, SL_WO + 1], xb, nextbank6)
        if stage < 5:
            return
        rmsnorm_to_hT(xres[xb], f"x{xb}a", 4, C_G3, hfm, "hfm", TB)
        if tb + 1 < nblk:
            load_x(tb + 1)
        ffn_halo_prep(tb)
        ptail = None
        for sa in range(11):
            st, sn = load_slab(SL_UP + sa)
            for tt in range(2):
                ptail = ffn_pair(st, sn, sa, tt, tb, ptail)
        ptail()
        if tb + 1 < nblk and stage >= 6:
            norm_stats(xres[1 - xb], f"x{1 - xb}a", 4)
        for nh in range(2):
            kc0 = 0
            dbanks = [nextbank6() for _ in range(4)]
            for sg3 in range(3):
                nk = 8 if sg3 < 2 else 6
                down_group.cur = load_slab(SL_DN + nh * 3 + sg3)
                for a in range(4):
                    down_group(xb, nh, sg3, kc0, nk, a, dbanks[a])
                kc0 += nk
            for a in range(4):
                add_resid(xb, a, nh, dbanks[a])

    load_x(0)
    for tb in range(nblk):
        block_body(tb)
    finish_block(nblk - 1)
    store_out(nblk - 1)

    if dbg is not None:
        P.barrier()
        for i_, (ap_, a_, b_) in enumerate(dbg(locals())):
            P.op("pool", lambda e, i_=i_, ap_=ap_, a_=a_, b_=b_: e.dma_start(
                out=dbg_d[i_][:, 0:a_ * b_].rearrange("p (a b) -> p a b", a=a_), in_=ap_), dsem=f"d_dbg{i_}")

    P.barrier()
    with nc.Block() as block:
        P.emit(block)
    es.close()
    return nc


def _slab_kc(w, c0, ncols=512, kc0=0, nkc=8):
    out = np.zeros((128, 8, ncols), np.float32)
    K = w.shape[0]
    for kc in range(nkc):
        r0 = (kc0 + kc) * 128
        if r0 >= K:
            break
        out[:, kc, :] = w[r0:r0 + 128, c0:c0 + ncols]
    return out.reshape(128, 8 * ncols)


def host_layout(inp):
    f = lambda k: np.asarray(inp[k], np.float32)
    w_in, w_out = f("w_in")[0], f("w_out")[0]
    wq, wk, wv, wo = f("xa_w_q")[0], f("xa_w_k")[0], f("xa_w_v")[0], f("xa_w_o")[0]
    wup, wdn, glu = f("ffn_w_up")[0], f("ffn_w_down")[0], f("s5_w_glu")[0]
    slabs = {}
    for g in range(3):
        slabs[SL_WIN + g] = _slab_kc(w_in, g * 512)
    kg = np.zeros((128, 4096), np.float32)
    kg[:, 2048:] = _slab_kc(glu, 0, 512, 0, 4)[:, :2048]
    slabs[SL_KG] = kg
    for h in range(2):
        slabs[SL_WOUT + h] = _slab_kc(w_out, h * 512)
        slabs[SL_WQ + h] = _slab_kc(wq, h * 512)
        slabs[SL_WO + h] = _slab_kc(wo, h * 512)
        slabs[SL_WK + h] = _slab_kc(wk, h * 512)
        slabs[SL_WV + h] = _slab_kc(wv, h * 512)
    for sa in range(11):
        cols = np.concatenate([np.arange(sa * 256, sa * 256 + 256), 2816 + np.arange(sa * 256, sa * 256 + 256)])
        slabs[SL_UP + sa] = _slab_kc(wup[:, cols], 0)
    for nh in range(2):
        for g3 in range(3):
            slabs[SL_DN + nh * 3 + g3] = _slab_kc(wdn, nh * 512, 512, g3 * 8, 8 if g3 < 2 else 6)
    w32 = np.stack([slabs[s] for s in HOST_SLABS]).astype(np.float32)

    colp = np.zeros((128, NCOL), np.float32)

    def putcols(c0, vec):
        v = np.asarray(vec, np.float32).reshape(-1, 128).T
        colp[:, c0:c0 + v.shape[1]] = v
    putcols(C_G1, f("ln_mix_g")[0])
    putcols(C_G2, f("ln_xa_g")[0])
    putcols(C_G3, f("ln_ffn_g")[0])
    putcols(C_GMEM, f("mem_norm_g"))
    putcols(C_BGLU, f("s5_b_glu")[0])
    lcw = f("lru_conv_w")[0]
    for t in range(4):
        putcols(C_LCW + t * 4, lcw[t])
    putcols(C_LCB, f("lru_conv_b")[0])
    putcols(C_BA, f("lru_b_a")[0].reshape(-1))
    putcols(C_BX, f("lru_b_x")[0].reshape(-1))
    putcols(C_LAM, f("lru_lam")[0].reshape(-1))
    fcw = f("ffn_conv_w")[0]
    for t in range(3):
        putcols(C_FCW + t * 44, fcw[t])
    putcols(C_FCB, f("ffn_conv_b")[0])
    putcols(C_DS5, f("s5_d")[0].reshape(-1))
    gfin = np.ascontiguousarray(np.broadcast_to(f("final_norm_g")[None, :], (128, D)))

    def gq(arr):
        a = arr.reshape(4, 8, 4, 16)
        return np.ascontiguousarray(a.transpose(1, 3, 0, 2).reshape(128, 16))
    lre, lim = f("s5_lam_re")[0], f("s5_lam_im")[0]
    ldt = np.broadcast_to(f("s5_log_dt")[0][:, None], (32, 64))
    s5par = np.stack([gq(lre), gq(lim), gq(ldt)], axis=1).astype(np.float32)

    def cb(b):
        a = b.reshape(4, 8, 4, 16, 16)
        return np.ascontiguousarray(a.transpose(1, 3, 0, 2, 4).reshape(128, 16, 16))
    bcc = np.stack([cb(f("s5_b_re")[0]), cb(f("s5_b_im")[0]),
                    cb(np.ascontiguousarray(f("s5_c_re")[0].transpose(0, 2, 1))),
                    cb(np.ascontiguousarray(f("s5_c_im")[0].transpose(0, 2, 1)))], axis=1).astype(np.float32)
    mask8 = np.zeros((128, 2, 8), np.float32)
    for p_ in range(128):
        mask8[p_, 0, p_ // 16] = 1.0
        mask8[p_, 1, p_ // 16] = -1.0
    lruw = np.zeros((128, 8, 128), np.float32)
    wa, wx = f("lru_w_a")[0], f("lru_w_x")[0]
    for mt in range(4):
        for hh in range(2):
            lruw[hh * 64:(hh + 1) * 64, mt, hh * 64:(hh + 1) * 64] = wa[2 * mt + hh]
            lruw[hh * 64:(hh + 1) * 64, 4 + mt, hh * 64:(hh + 1) * 64] = wx[2 * mt + hh]
    shared = {"w32": w32, "colp": colp, "gfin": gfin, "s5par": s5par, "bcc": bcc, "mask8": mask8, "lruw": lruw,
              "ident": np.eye(128, dtype=np.float32).astype(ml_dtypes.bfloat16),
              "identf": np.eye(128, dtype=np.float32)}
    return shared


def kernel(**inputs):
    shared = host_layout(inputs)
    x = np.asarray(inputs["x"], np.float32)
    mem = np.asarray(inputs["mem"], np.float32)
    nc = build(SEQ // TB)
    in_maps = []
    for c in range(8):
        m = dict(shared)
        m["x"] = np.ascontiguousarray(x[c])
        m["mem"] = np.ascontiguousarray(mem[c])
        in_maps.append(m)
    res = run_bass_kernel_spmd(nc, in_maps, core_ids=list(range(8)))
    return np.stack([np.asarray(r["out"], np.float32) for r in res.results], axis=0)
```

```python
import math
from contextlib import ExitStack
import numpy as np
import ml_dtypes
import concourse.bass as bass
import concourse.mybir as mybir
from concourse.bass_utils import run_bass_kernel_spmd

F32 = mybir.dt.float32
BF16 = mybir.dt.bfloat16
ALU = mybir.AluOpType
AF = mybir.ActivationFunctionType

SEQ = 4096
TB = 512
D = 1024
NS = 5
PI = math.pi

SL_WIN = 0
SL_S5W = 3
SL_S5V = 7
SL_KG = 11
SL_WOUT = 12
SL_WQ = 14
SL_WO = 16
SL_UP = 18
SL_DN = 29
SL_WK = 35
SL_WV = 37
NSLAB = 39
HOST_SLABS = [0, 1, 2, 11, 12, 13, 14, 15, 16, 17] + list(range(18, 35)) + [35, 36, 37, 38]

C_G1, C_G2, C_G3, C_GMEM = 0, 8, 16, 24
C_BGLU = 32
C_LCW = 36
C_LCB = 52
C_BA = 56
C_BX = 60
C_LAM = 64
C_FCW = 68
C_FCB = 200
C_DS5 = 244
NCOL = 248


class Prog:
    ENG = ("pe", "act", "dve", "pool", "sp")

    def __init__(self, nc, es):
        self.nc = nc
        self.es = es
        self.q = {e: [] for e in self.ENG}
        self.cnt = {}
        self.sems = {}
        self.seen = {e: {} for e in self.ENG}
        self.lastw = {}
        self.readers = {}

    def sem(self, key):
        if key not in self.sems:
            self.sems[key] = self.es.enter_context(self.nc.semaphore("s_" + key))
            self.cnt[key] = 0
        return self.sems[key]

    def op(self, eng, fn, r=(), w=(), dsem=None):
        deps = {}

        def add(tok):
            if tok is None:
                return
            k, v = tok
            if deps.get(k, 0) < v:
                deps[k] = v
        for b in r:
            add(self.lastw.get(b))
        for b in w:
            add(self.lastw.get(b))
            for k, v in self.readers.get(b, {}).items():
                add((k, v))
        waits = []
        for k, v in deps.items():
            if eng == "pe" and k == "pe":
                continue
            if self.seen[eng].get(k, 0) >= v:
                continue
            self.seen[eng][k] = v
            waits.append((k, v))
        if dsem is None:
            key, inc = eng, 1
        else:
            key, inc = dsem, 16
        self.sem(key)
        self.cnt[key] += inc
        tok = (key, self.cnt[key])
        self.q[eng].append((waits, fn, key, inc))
        for b in r:
            self.readers.setdefault(b, {})[key] = tok[1]
        for b in w:
            self.lastw[b] = tok
            self.readers[b] = {}
        return tok

    def barrier(self, skip=None):
        for e in self.ENG:
            waits = []
            for k, v in self.cnt.items():
                if skip is not None and k.startswith(skip):
                    continue
                if v > 0 and self.seen[e].get(k, 0) < v:
                    self.seen[e][k] = v
                    waits.append((k, v))
            if waits:
                self.q[e].append((waits, None, None, 0))

    def emit(self, block):
        sems = self.sems

        def run(eng, lst):
            for waits, fn, key, inc in lst:
                for k, v in waits:
                    eng.wait_ge(sems[k], v)
                if fn is not None:
                    ins = fn(eng)
                    ins.then_inc(sems[key], inc)

        @block.tensor
        def _(e):
            run(e, self.q["pe"])

        @block.scalar
        def _(e):
            run(e, self.q["act"])

        @block.vector
        def _(e):
            run(e, self.q["dve"])

        @block.gpsimd
        def _(e):
            run(e, self.q["pool"])

        @block.sync
        def _(e):
            run(e, self.q["sp"])


def build(nblk=8, stage=6, dbg=None, pro_only=False):
    nc = bass.Bass("TRN2", target_bir_lowering=False)
    ntok = nblk * TB

    def din(name, shape, dt=F32):
        return nc.dram_tensor(name, list(shape), dt, kind="ExternalInput").ap()
    x_d = din("x", [ntok, D])
    mem_d = din("mem", [256, D])
    w32_d = din("w32", [len(HOST_SLABS), 128, 4096])
    colp_d = din("colp", [128, NCOL])
    gfin_d = din("gfin", [128, D])
    s5par_d = din("s5par", [128, 3, 16])
    bcc_d = din("bcc", [128, 4, 16, 16])
    mask8_d = din("mask8", [128, 2, 8])
    lruw_d = din("lruw", [128, 8, 128])
    ident_d = din("ident", [128, 128], BF16)
    identf_d = din("identf", [128, 128])
    out_d = nc.dram_tensor("out", [ntok, D], F32, kind="ExternalOutput").ap()
    wscr = nc.dram_tensor("wscr", [NSLAB, 128, 4096], BF16, kind="Internal").ap()
    dbg_d = None
    if dbg is not None:
        dbg_d = nc.dram_tensor("dbg", [16, 128, 4096], F32, kind="ExternalOutput").ap()

    es = ExitStack()
    P = Prog(nc, es)

    def sb(name, shape, dt=F32, stack=es):
        return stack.enter_context(nc.sbuf_tensor("sb_" + name, list(shape), dt))

    colp = sb("colp", [128, NCOL])
    gfin = sb("gfin", [128, D])
    ident = sb("ident", [128, 128], BF16)
    onesb = sb("onesb", [128, 128], BF16)
    lruw = sb("lruw", [128, 8, 128], BF16)
    lcd = sb("lcd", [128, 16, 128], BF16)
    cneg = sb("cneg", [128, 4])
    epsc = sb("epsc", [128, 1])
    hpic = sb("hpic", [128, 1])
    onec = sb("onec", [128, 1])
    hbias = sb("hbias", [128, 12])
    cnegh = sb("cnegh", [128, 4])
    cosT = sb("cosT", [128, 16, 128])
    sinT = sb("sinT", [128, 16, 128])
    Rtab = sb("Rtab", [128, 16, 128])
    R4 = sb("R4", [128, 16])
    E4r = sb("E4r", [128, 16])
    E4i = sb("E4i", [128, 16])
    Kfm = sb("Kfm", [128, 8, 256], BF16)
    Vtok = sb("Vtok", [128, 2, D], BF16)
    slots = [sb(f"slot{i}", [128, 4096], BF16) for i in range(NS)]
    xres = [sb("xres0", [128, 4, D]), None]
    hT = sb("hT", [128, 4, D], BF16)
    hfm = sb("hfm", [128, 8, TB], BF16)
    ss = sb("ss", [128, 8])
    rstd = sb("rstd", [128, 8])
    junk = sb("junk", [128, D], BF16)
    s5carry_r = sb("s5cr", [128, 16])
    s5carry_i = sb("s5ci", [128, 16])
    lrucarry = sb("lrucarry", [128, 4])
    psb = [es.enter_context(nc.psum_tensor(f"pp{i}", [128, 512], F32)) for i in range(6)]
    psT = [es.enter_context(nc.psum_tensor(f"ppT{i}", [128, 1024], BF16)) for i in range(2)]

    cp = lambda c, n=1: colp[:, c:c + n]

    slab_state = {"n": 0}

    def load_slab(idx, eng="sp"):
        i = slab_state["n"] % NS
        slab_state["n"] += 1
        name = f"slot{i}"
        P.op(eng, lambda e, i=i, idx=idx: e.dma_start(out=slots[i][:, :], in_=wscr[idx]),
             r=[f"scr{idx}", f"scr{idx}g"], w=[name], dsem=f"d_slot{i}")
        return slots[i], name

    def mm_group(out_ap, pairs, r, w):
        n = len(pairs)

        def fn(e):
            ins = None
            for j, (l, rr) in enumerate(pairs):
                ins = e.matmul(out_ap, l, rr, start=(j == 0), stop=(j == n - 1))
            return ins
        P.op("pe", fn, r=r, w=w)

    def V(fn, r, w):
        P.op("dve", fn, r=r, w=w)

    def A(fn, r, w):
        P.op("act", fn, r=r, w=w)

    def G(fn, r, w):
        P.op("pool", fn, r=r, w=w)

    pst = ExitStack()
    par = sb("par", [128, 3, 16], stack=pst)
    identf = sb("identf", [128, 128], stack=pst)
    bcc = sb("bcc", [128, 4, 16, 16], stack=pst)
    mask8 = sb("mask8", [128, 2, 8], stack=pst)
    cs = sb("cs", [128, 4, 16, 16], stack=pst)
    t1 = sb("t1", [128, 16, 128], stack=pst)
    lruw32 = t1[:, 0:8, :]
    t2 = sb("t2", [128, 16, 128], stack=pst)
    NB = sb("NB", [128, 4, 2, 16, 128], BF16, stack=pst)
    Vm = sb("Vm", [128, 2, 4, 8, 128], BF16, stack=pst)
    Wst = sb("Wst", [128, 1, 8, 4, 128], BF16, stack=pst)
    Kst = hT[:, 2:4, :].rearrange("p a (b c) -> p (a b) c", c=128).rearrange("p (k t) c -> p k t c", k=4)
    sm = xres[0][:, 2:4, :].rearrange("p a (b c) -> p (a b) c", c=16)
    memx = xres[0]
    memfm = hfm

    def ld(dst, src, name, eng="sp"):
        P.op(eng, lambda e: e.dma_start(out=dst, in_=src), w=[name], dsem="d_" + name)
    ld(colp[:, :], colp_d, "colp")
    ld(gfin[:, :], gfin_d, "gfin")
    ld(ident[:, :], ident_d, "ident")
    ld(identf[:, :], identf_d, "identf")
    ld(par[:, :, :], s5par_d, "par")
    ld(bcc[:, :, :, :], bcc_d, "bcc")
    ld(mask8[:, :, :], mask8_d, "mask8")
    ld(lruw32, lruw_d, "t1")
    P.op("sp", lambda e: e.dma_start(out=memx[:, 0:2, :], in_=mem_d.rearrange("(a p) d -> p a d", p=128)),
         w=["x0a0", "x0a1"], dsem="d_x0")

    G(lambda e: e.memset(onesb[:, :], 1.0), [], ["onesb"])
    G(lambda e: e.memset(epsc[:, :], 1e-6), [], ["epsc"])
    G(lambda e: e.memset(hpic[:, :], PI / 2), [], ["hpic"])
    G(lambda e: e.memset(onec[:, :], 1.0), [], ["onec"])
    G(lambda e: e.memset(lrucarry[:, :], 0.0), [], ["lrucarry"])
    G(lambda e: e.memset(s5carry_r[:, :], 0.0), [], ["s5c"])
    G(lambda e: e.memset(s5carry_i[:, :], 0.0), [], ["s5c"])
    V(lambda e: e.tensor_copy(lruw[:, :, :], lruw32), ["t1"], ["lruw"])
    for t_ in range(4):
        V(lambda e, t_=t_: e.tensor_tensor(lcd[:, t_ * 4:(t_ + 1) * 4, :], identf[:, :].unsqueeze(1).to_broadcast([128, 4, 128]),
                                            cp(C_LCW + t_ * 4, 4).unsqueeze(2).to_broadcast([128, 4, 128]), ALU.mult),
          ["identf", "colp"], ["lcd"])

    A(lambda e: e.activation(cneg[:, :], cp(C_LAM, 4), AF.Exp, scale=-1.0), ["colp"], ["cneg"])
    A(lambda e: e.activation(cneg[:, :], cneg[:, :], AF.Ln, bias=onec[:, :]), ["cneg", "onec"], ["cneg"])
    V(lambda e: e.tensor_scalar_mul(cneg[:, :], cneg[:, :], -8.0), ["cneg"], ["cneg"])
    V(lambda e: e.tensor_scalar_mul(cnegh[:, :], cneg[:, :], 0.5), ["cneg"], ["cnegh"])
    V(lambda e: e.tensor_scalar_mul(hbias[:, 0:4], cp(C_BA, 4), 0.5), ["colp"], ["hbias"])
    V(lambda e: e.tensor_scalar_mul(hbias[:, 4:8], cp(C_BX, 4), 0.5), ["colp"], ["hbias"])
    V(lambda e: e.tensor_scalar_mul(hbias[:, 8:12], cp(C_BGLU, 4), 0.5), ["colp"], ["hbias"])

    smn = {"i": 0}

    def S(name=None):
        i = smn["i"]
        smn["i"] += 1
        return sm[:, i, :], f"sm{i}"

    def vtt(o, a, b, op):
        (oa, on), (aa, an), (ba, bn) = o, a, b
        V(lambda e: e.tensor_tensor(oa, aa, ba, op), [an, bn], [on])

    def cmul(a_r, a_i, b_r, b_i):
        o_r, o_i, u1, u2 = S(), S(), S(), S()
        vtt(u1, a_r, b_r, ALU.mult)
        vtt(u2, a_i, b_i, ALU.mult)
        vtt(o_r, u1, u2, ALU.subtract)
        vtt(u1, a_r, b_i, ALU.mult)
        vtt(u2, a_i, b_r, ALU.mult)
        vtt(o_i, u1, u2, ALU.add)
        return o_r, o_i

    lre = (par[:, 0, :], "par")
    lim = (par[:, 1, :], "par")
    ldt = (par[:, 2, :], "par")
    dt_ = S()
    A(lambda e: e.activation(dt_[0], ldt[0], AF.Exp), ["par"], [dt_[1]])
    zr, zi = S(), S()
    vtt(zr, lre, dt_, ALU.mult)
    vtt(zi, lim, dt_, ALU.mult)
    mag = S()
    A(lambda e: e.activation(mag[0], zr[0], AF.Exp), [zr[1]], [mag[1]])

    sn0, cs0 = S(), S()
    A(lambda e: e.activation(sn0[0], zi[0], AF.Sin, scale=1.0 / 16), [zi[1]], [sn0[1]])
    A(lambda e: e.activation(cs0[0], zi[0], AF.Sin, scale=1.0 / 16, bias=hpic[:, :]), [zi[1], "hpic"], [cs0[1]])
    sn1, cs1 = sn0, cs0
    for _ in range(4):
        cs1, sn1 = cmul(cs1, sn1, cs1, sn1)
    L = [None] * 5
    L1r, L1i = S(), S()
    vtt(L1r, mag, cs1, ALU.mult)
    vtt(L1i, mag, sn1, ALU.mult)
    L[1] = (L1r, L1i)
    L[2] = cmul(L1r, L1i, L1r, L1i)
    L[3] = cmul(L[2][0], L[2][1], L1r, L1i)
    L[4] = cmul(L[2][0], L[2][1], L[2][0], L[2][1])
    one_, zero_ = S(), S()
    V(lambda e: e.memset(one_[0], 1.0), [], [one_[1]])
    V(lambda e: e.memset(zero_[0], 0.0), [], [zero_[1]])
    L[0] = (one_, zero_)
    am1 = S()
    V(lambda e: e.tensor_scalar_add(am1[0], L1r[0], -1.0), [L1r[1]], [am1[1]])
    nli = S()
    V(lambda e: e.tensor_scalar_mul(nli[0], lim[0], -1.0), ["par"], [nli[1]])
    num_r, num_i = cmul(am1, L1i, lre, nli)
    den, u3 = S(), S()
    vtt(den, lre, lre, ALU.mult)
    vtt(u3, lim, lim, ALU.mult)
    vtt(den, den, u3, ALU.add)
    kr, ki = S(), S()
    V(lambda e: e.reciprocal(den[0], den[0]), [den[1]], [den[1]])
    vtt(kr, num_r, den, ALU.mult)
    vtt(ki, num_i, den, ALU.mult)
    M = [cmul(L[3 - j][0], L[3 - j][1], kr, ki) for j in range(4)]
    r2 = S()
    vtt(r2, L[4][0], L[4][0], ALU.mult)
    vtt(u3, L[4][1], L[4][1], ALU.mult)
    vtt(r2, r2, u3, ALU.add)
    A(lambda e: e.activation(R4[:, :], r2[0], AF.Sqrt), [r2[1]], ["R4"])
    ir4 = S()
    V(lambda e: e.reciprocal(ir4[0], R4[:, :]), ["R4"], [ir4[1]])
    V(lambda e: e.tensor_tensor(E4r[:, :], L[4][0][0], ir4[0], ALU.mult), [L[4][0][1], ir4[1]], ["E4"])
    V(lambda e: e.tensor_tensor(E4i[:, :], L[4][1][0], ir4[0], ALU.mult), [L[4][1][1], ir4[1]], ["E4"])
    V(lambda e: e.memset(cosT[:, :, 0:1], 1.0), [], ["tab"])
    V(lambda e: e.memset(sinT[:, :, 0:1], 0.0), [], ["tab"])
    wr, wi = (E4r[:, :], "E4"), (E4i[:, :], "E4")
    n = 1
    while n < 128:
        wrb = wr[0].unsqueeze(2).to_broadcast([128, 16, n])
        wib = wi[0].unsqueeze(2).to_broadcast([128, 16, n])
        c0, s0 = cosT[:, :, 0:n], sinT[:, :, 0:n]
        c1, s1 = cosT[:, :, n:2 * n], sinT[:, :, n:2 * n]
        ta, tb_ = t1[:, :, 0:n], t2[:, :, 0:n]
        V(lambda e, c0=c0, wrb=wrb, ta=ta: e.tensor_tensor(ta, c0, wrb, ALU.mult), ["tab", wr[1]], ["t1"])
        V(lambda e, s0=s0, wib=wib, tb_=tb_: e.tensor_tensor(tb_, s0, wib, ALU.mult), ["tab", wi[1]], ["t2"])
        V(lambda e, c1=c1, ta=ta, tb_=tb_: e.tensor_tensor(c1, ta, tb_, ALU.subtract), ["t1", "t2"], ["tab"])
        V(lambda e, c0=c0, wib=wib, ta=ta: e.tensor_tensor(ta, c0, wib, ALU.mult), ["tab", wi[1]], ["t1"])
        V(lambda e, s0=s0, wrb=wrb, tb_=tb_: e.tensor_tensor(tb_, s0, wrb, ALU.mult), ["tab", wr[1]], ["t2"])
        V(lambda e, s1=s1, ta=ta, tb_=tb_: e.tensor_tensor(s1, ta, tb_, ALU.add), ["t1", "t2"], ["tab"])
        if n < 64:
            wr, wi = cmul(wr, wi, wr, wi)
        n *= 2
    V(lambda e: e.tensor_copy(Rtab[:, :, :], R4[:, :].unsqueeze(2).to_broadcast([128, 16, 128])), ["R4"], ["Rtab"])
    V(lambda e: e.memset(Rtab[:, :, 0:1], 0.0), ["Rtab"], ["Rtab"])

    def norm_stats(src, srcname, nsub):
        for a in range(nsub):
            if a % 2 == 0:
                A(lambda e, a=a: e.activation(junk[:, :], src[:, a, :], AF.Square, accum_out=ss[:, a:a + 1]),
                  [f"{srcname}{a}"], ["junk", f"ss{a}"])
            else:
                V(lambda e, a=a: e.scalar_tensor_tensor(hT[:, a, :], src[:, a, :], 1.0, src[:, a, :], op0=ALU.mult, op1=ALU.mult,
                                                        accum_out=ss[:, a:a + 1]), [f"{srcname}{a}"], [f"hT{a}", f"ss{a}"])
            A(lambda e, a=a: e.activation(rstd[:, a:a + 1], ss[:, a:a + 1], AF.Sqrt, scale=1.0 / D, bias=epsc[:, :]),
              [f"ss{a}", "epsc"], [f"rstd{a}"])
            V(lambda e, a=a: e.reciprocal(rstd[:, a:a + 1], rstd[:, a:a + 1]), [f"rstd{a}"], [f"rstd{a}"])
            if a % 2 == 0:
                A(lambda e, a=a: e.activation(hT[:, a, :], src[:, a, :], AF.Copy, scale=rstd[:, a:a + 1]),
                  [f"{srcname}{a}", f"rstd{a}"], [f"hT{a}"])
            else:
                V(lambda e, a=a: e.tensor_scalar_mul(hT[:, a, :], src[:, a, :], rstd[:, a:a + 1]),
                  [f"{srcname}{a}", f"rstd{a}"], [f"hT{a}"])

    def norm_transposes(nsub, col_g, dst_fm, dstname, ncols_tok):
        for kc in range(8):
            bi = kc % 2
            bank = psT[bi]

            def fn(e, kc=kc, bank=bank):
                ins = None
                for a in range(nsub):
                    ins = e.transpose(bank[:, a * 128:(a + 1) * 128], hT[:, a, kc * 128:(kc + 1) * 128], ident[:, :])
                return ins
            P.op("pe", fn, r=[f"hT{a}" for a in range(nsub)] + ["ident"], w=[f"psT{bi}"])
            if kc % 2 == 0:
                A(lambda e, kc=kc, bank=bank: e.activation(dst_fm[:, kc, 0:ncols_tok], bank[:, 0:ncols_tok], AF.Copy, scale=cp(col_g + kc)),
                  [f"psT{bi}", "colp"], [f"{dstname}{kc}"])
            else:
                V(lambda e, kc=kc, bank=bank: e.tensor_scalar_mul(dst_fm[:, kc, 0:ncols_tok], bank[:, 0:ncols_tok], cp(col_g + kc)),
                  [f"psT{bi}", "colp"], [f"{dstname}{kc}"])


    def rmsnorm_to_hT(src, srcname, nsub, col_g, dst_fm, dstname, ncols_tok):
        norm_stats(src, srcname, nsub)
        norm_transposes(nsub, col_g, dst_fm, dstname, ncols_tok)

    rmsnorm_to_hT(memx, "x0a", 2, C_GMEM, memfm, "hfm", 256)
    for half in range(2):
        P.op("pool", lambda e, half=half: e.dma_start(out=slots[half][:, :], in_=w32_d[HOST_SLABS.index(SL_WK + half)]),
             w=[f"slot{half}"], dsem=f"d_slot{half}")
        for mt in range(4):
            bi = 2 + mt % 2
            pairs = [(slots[half][:, kc * 512 + mt * 128: kc * 512 + (mt + 1) * 128], memfm[:, kc, 0:256]) for kc in range(8)]
            mm_group(psb[bi][:, 0:256], pairs, [f"slot{half}"] + [f"hfm{kc}" for kc in range(8)], [f"ps{bi}"])
            A(lambda e, half=half, mt=mt, bi=bi: e.copy(Kfm[:, half * 4 + mt, :], psb[bi][:, 0:256]), [f"ps{bi}"], ["Kfm"])
    for half in range(2):
        P.op("pool", lambda e, half=half: e.dma_start(out=slots[2 + half][:, :], in_=w32_d[HOST_SLABS.index(SL_WV + half)]),
             w=[f"slot{2 + half}"], dsem=f"d_slot{2 + half}")
        for mc in range(2):
            bi = 4 + mc
            pairs = [(memfm[:, kc, mc * 128:(mc + 1) * 128], slots[2 + half][:, kc * 512:(kc + 1) * 512]) for kc in range(8)]
            mm_group(psb[bi][:, :], pairs, [f"slot{2 + half}"] + [f"hfm{kc}" for kc in range(8)], [f"ps{bi}"])
            V(lambda e, half=half, mc=mc, bi=bi: e.tensor_copy(Vtok[:, mc, half * 512:(half + 1) * 512], psb[bi][:, :]),
              [f"ps{bi}"], ["Vtok"])

    bre, bim, cre, cim = (bcc[:, i_, :, :] for i_ in range(4))
    c1, c2, c3, c4 = (cs[:, i_, :, :] for i_ in range(4))
    mpos = mask8[:, 0, :].unsqueeze(1).unsqueeze(3)
    mneg = mask8[:, 1, :].unsqueeze(1).unsqueeze(3)

    def bc16(ap):
        return ap.unsqueeze(2).to_broadcast([128, 16, 16])
    for j in range(4):
        mr, mi = bc16(M[j][0][0]), bc16(M[j][1][0])
        mrn, min_ = M[j][0][1], M[j][1][1]
        V(lambda e, mr=mr: e.tensor_tensor(c1, bre, mr, ALU.mult), ["bcc", mrn], ["c1"])
        V(lambda e, mi=mi: e.tensor_tensor(c2, bim, mi, ALU.mult), ["bcc", min_], ["c2"])
        V(lambda e, mi=mi: e.tensor_tensor(c3, bre, mi, ALU.mult), ["bcc", min_], ["c3"])
        V(lambda e, mr=mr: e.tensor_tensor(c4, bim, mr, ALU.mult), ["bcc", mrn], ["c4"])
        V(lambda e: e.tensor_tensor(c1, c1, c2, ALU.subtract), ["c1", "c2"], ["c1"])
        V(lambda e: e.tensor_tensor(c3, c3, c4, ALU.add), ["c3", "c4"], ["c3"])
        for ri, cc, cn in ((0, c1, "c1"), (1, c3, "c3")):
            o = NB[:, j, ri, :, :].rearrange("p a (g h) -> p a g h", g=8)
            V(lambda e, o=o, cc=cc: e.tensor_tensor(o, cc.unsqueeze(2).to_broadcast([128, 16, 8, 16]),
                                                    mpos.to_broadcast([128, 16, 8, 16]), ALU.mult), [cn, "mask8"], [f"NB{j}"])
    tix = {"i": 0}

    def w_section(k):
        for sg in range(8):
            s_, ri = sg % 4, sg // 4
            bank = psT[tix["i"] % 2]
            bname = f"psT{tix['i'] % 2}"
            tix["i"] += 1

            def fn(e, s_=s_, ri=ri, bank=bank):
                ins = None
                for j in range(4):
                    ins = e.transpose(bank[:, j * 128:(j + 1) * 128], NB[:, j, ri, k * 4 + s_, :], ident[:, :])
                return ins
            P.op("pe", fn, r=[f"NB{j}" for j in range(4)] + ["ident"], w=[bname])
            dst = Wst[:, 0, sg, :, :]
            src = bank[:, 0:512].rearrange("p (j c) -> p j c", j=4)
            A(lambda e, dst=dst, src=src: e.copy(dst, src), [bname], ["Wst"])
        P.op("sp", lambda e: e.dma_start(out=wscr[SL_S5W + k], in_=Wst[:, 0, :, :, :].rearrange("p a b c -> p (a b c)")),
             r=["Wst"], w=[f"scr{SL_S5W + k}"], dsem=f"d_w{k}")
    def gen_V(m):
        lr, li = bc16(L[m][0][0]), bc16(L[m][1][0])
        lrn, lin = L[m][0][1], L[m][1][1]
        vb = Vm[:, m % 2, :, :, :]
        on = f"Vm{m % 2}"
        V(lambda e: e.tensor_tensor(c1, cre, lr, ALU.mult), ["bcc", lrn], ["c1"])
        V(lambda e: e.tensor_tensor(c2, cim, li, ALU.mult), ["bcc", lin], ["c2"])
        V(lambda e: e.tensor_tensor(c3, cre, li, ALU.mult), ["bcc", lin], ["c3"])
        V(lambda e: e.tensor_tensor(c4, cim, lr, ALU.mult), ["bcc", lrn], ["c4"])
        V(lambda e: e.tensor_tensor(c1, c1, c2, ALU.subtract), ["c1", "c2"], ["c1"])
        V(lambda e: e.tensor_tensor(c3, c3, c4, ALU.add), ["c3", "c4"], ["c3"])
        for k in range(4):
            for half, cc, cn, mk in ((0, c1, "c1", mpos), (1, c3, "c3", mneg)):
                o = vb[:, k, half * 4:(half + 1) * 4, :].rearrange("p s (g h) -> p s g h", g=8)
                V(lambda e, o=o, cc=cc, mk=mk, k=k: e.tensor_tensor(o, cc[:, k * 4:(k + 1) * 4, :].unsqueeze(2).to_broadcast([128, 4, 8, 16]),
                                                                   mk.to_broadcast([128, 4, 8, 16]), ALU.mult), [cn, "mask8"], [on])
        if m >= 1:
            for k in range(4):
                dst = wscr[SL_S5V + k].rearrange("p (s m c) -> p s m c", s=8, m=4)[:, :, m - 1, :]
                P.op("sp", lambda e, k=k, dst=dst: e.dma_start(out=dst, in_=vb[:, k, :, :]),
                     r=[on], w=[f"scr{SL_S5V + k}"], dsem=f"d_v{k}_{m}")
        if m <= 3:
            for k in range(4):
                pairs = [(NB[:, 3, sg // 4, k * 4 + sg % 4, :], vb[:, k, sg, :]) for sg in range(8)]
                mm_group(psb[k][:, m * 128:(m + 1) * 128], pairs, ["NB3", on], [f"ps{k}"])
    for m in range(5):
        gen_V(m)
        if m < 4:
            w_section(m)
    for k in range(4):
        V(lambda e, k=k: e.scalar_tensor_tensor(Kst[:, k, 0, :], identf[:, :], cp(C_DS5 + k), psb[k][:, 0:128],
                                                op0=ALU.mult, op1=ALU.add), [f"ps{k}", "identf", "colp"], ["Kst"])
        V(lambda e, k=k: e.tensor_copy(Kst[:, k, 1:4, :], psb[k][:, 128:512].rearrange("p (j c) -> p j c", j=3)),
          [f"ps{k}"], ["Kst"])
    P.op("sp", lambda e: e.dma_start(out=wscr[SL_KG][:, 0:2048], in_=hT[:, 2:4, :].rearrange("p a b -> p (a b)")),
         r=["Kst"], w=[f"scr{SL_KG}"], dsem="d_kst")

    for hi, sl in enumerate(HOST_SLABS):
        if sl >= SL_WK:
            continue
        if sl == SL_KG:
            P.op("pool", lambda e, hi=hi, sl=sl: e.dma_start(out=wscr[sl][:, 2048:4096], in_=w32_d[hi][:, 2048:4096]),
                 w=[f"scr{sl}g"], dsem=f"d_cast{hi}")
        else:
            P.op("pool", lambda e, hi=hi, sl=sl: e.dma_start(out=wscr[sl], in_=w32_d[hi]),
                 w=[f"scr{sl}"], dsem=f"d_cast{hi}")

    if dbg == "pro":
        pass
    P.barrier(skip="d_cast")
    pst.close()

    xres[1] = sb("xres1", [128, 4, D])
    xlh = sb("xlh", [128, 4, 3 + TB], BF16)
    ffnhalo = sb("ffnhalo", [128, 2, 44, 2])
    G(lambda e: e.memset(xlh[:, :, 0:3], 0.0), [], [f"xl{i}" for i in range(4)])
    G(lambda e: e.memset(ffnhalo[:, :, :, :], 0.0), [], ["ffnhalo0", "ffnhalo1"])
    gl = sb("gl", [128, 4, TB], BF16)
    zq = sb("zq", [128, 4, 512])
    S_sb = sb("S_sb", [128, 4, 8, 130], BF16)
    ymix = sb("ymix", [128, 8, TB], BF16)
    ltb = sb("lt", [128, 6, TB + 4])
    lt = ltb[:, :, 0:TB]
    xcb = sb("xcb", [128, 2, TB], BF16)
    hlb = sb("hl", [128, 2, TB + 4])
    hl = hlb[:, :, 0:TB]
    gated = sb("gated", [128, 22, TB], BF16)
    q_sb = ymix
    pT = gated[:, 0:8, :].rearrange("p (h m) t -> p h m t", h=4)
    o_sb = gated[:, 8:16, :]
    u_sb = gated[:, 16:20, :]
    ygelu = gated[:, 0:4, :]
    rden = hl
    zbuf = ltb[:, 0:2, 0:TB + 2]
    cv = lt[:, 2:4, :]
    cg = lt[:, 4:6, :]
    gg = cg
    qinit = sb("qinit", [128, 3, 16])
    hc = sb("hc", [128, 44, 2])
    hc2 = sb("hc2", [128, 44])
    G(lambda e: e.memset(S_sb[:, :, :, :], 0.0), [], ["S_sb0", "S_sb1", "S_sb2", "S_sb3"])

    rr = {"ps": 0, "ps6": 0}

    def nextbank():
        b = rr["ps"] % 4
        rr["ps"] += 1
        return b

    def nextbank6():
        b = rr["ps6"] % 6
        rr["ps6"] += 1
        return b

    hnames = [f"hfm{kc}" for kc in range(8)]

    def add_resid(xb, a, nh, bi):
        xs = xres[xb][:, a, nh * 512:(nh + 1) * 512]
        V(lambda e: e.tensor_tensor(xs, xs, psb[bi][:, :], ALU.add), [f"ps{bi}", f"x{xb}a{a}"], [f"x{xb}a{a}"])

    def proj_tok(lhs_tile, lhs_names, slab_ids, xb, bankfn):
        for nh in range(2):
            st, sn = load_slab(slab_ids[nh])
            for a in range(4):
                bi = bankfn()
                pairs = [(lhs_tile[:, kc, a * 128:(a + 1) * 128], st[:, kc * 512:(kc + 1) * 512]) for kc in range(8)]
                mm_group(psb[bi][:, :], pairs, [sn] + lhs_names, [f"ps{bi}"])
                add_resid(xb, a, nh, bi)

    def fm_proj(st, sn, mt, evac, bankfn=None):
        bi = (bankfn or nextbank)()
        pairs = [(st[:, kc * 512 + mt * 128: kc * 512 + (mt + 1) * 128], hfm[:, kc, :]) for kc in range(8)]
        mm_group(psb[bi][:, :], pairs, [sn] + hnames, [f"ps{bi}"])
        evac(bi)

    u4 = u_sb[:, :, :].rearrange("p k (c j) -> p k c j", j=4)

    def v4(ap):
        return ap.rearrange("p (s c) -> p s c", s=4)

    def s5_qinit():
        q0, q1, q2 = qinit[:, 0, :], qinit[:, 1, :], qinit[:, 2, :]
        cr, ci = s5carry_r[:, :], s5carry_i[:, :]
        er, ei, r4 = E4r[:, :], E4i[:, :], R4[:, :]
        V(lambda e: e.tensor_tensor(q0, cr, er, ALU.mult), ["s5c", "E4"], ["qi0"])
        V(lambda e: e.tensor_tensor(q2, ci, ei, ALU.mult), ["s5c", "E4"], ["qi2"])
        V(lambda e: e.tensor_tensor(q1, cr, ei, ALU.mult), ["s5c", "E4"], ["qi1"])
        V(lambda e: e.tensor_tensor(q0, q0, q2, ALU.subtract), ["qi0", "qi2"], ["qi0"])
        V(lambda e: e.tensor_tensor(q2, ci, er, ALU.mult), ["s5c", "E4", "qi0"], ["qi2"])
        V(lambda e: e.tensor_tensor(q0, q0, r4, ALU.mult), ["qi0", "R4"], ["qi0"])
        V(lambda e: e.tensor_tensor(q1, q1, q2, ALU.add), ["qi1", "qi2"], ["qi1"])
        V(lambda e: e.tensor_tensor(q1, q1, r4, ALU.mult), ["qi1", "R4"], ["qi1"])

    def s5_B_pe(k):
        st, sn = load_slab(SL_S5W + k)
        for half, bi in ((0, 4), (1, 5)):
            for s_ in range(4):
                sg = half * 4 + s_
                pairs = [(st[:, (sg * 4 + j) * 128:(sg * 4 + j + 1) * 128], u4[:, k, :, j]) for j in range(4)]
                mm_group(psb[bi][:, s_ * 128:(s_ + 1) * 128], pairs, [sn, f"gated{16 + k}"], [f"ps{bi}"])

    def s5_B_steps(k):
        cT = cosT[:, k * 4:(k + 1) * 4, :]
        sT = sinT[:, k * 4:(k + 1) * 4, :]
        Xr = v4(psb[4][:, :])
        Xi = v4(psb[5][:, :])
        Zr, Zi, Qr, Qi = (zq[:, i, :] for i in range(4))
        tmp = lt[:, 0, :]
        ks = slice(k * 4, (k + 1) * 4)
        q0, q1 = qinit[:, 0, ks], qinit[:, 1, ks]
        cr, ci = s5carry_r[:, ks], s5carry_i[:, ks]
        Rk = Rtab[:, k * 4:(k + 1) * 4, :].rearrange("p s c -> p (s c)")
        Sre = S_sb[:, k, 0:4, 1:129]
        Sim = S_sb[:, k, 4:8, 1:129]
        sname = f"S_sb{k}"
        ta, tb_ = lt[:, 2, :], lt[:, 3, :]
        TT = lambda o, a, b, op, r, w: (lambda: V(lambda e: e.tensor_tensor(o, a, b, op), r, w))
        return [
            TT(v4(Zr), Xr, cT, ALU.mult, ["ps4", "tab"], ["zq0"]),
            TT(v4(tmp), Xi, sT, ALU.mult, ["ps5", "tab"], ["lt0"]),
            TT(Zr, Zr, tmp, ALU.add, ["zq0", "lt0"], ["zq0"]),
            TT(v4(Zi), Xi, cT, ALU.mult, ["ps5", "tab"], ["zq1"]),
            TT(v4(tmp), Xr, sT, ALU.mult, ["ps4", "tab"], ["lt0"]),
            TT(Zi, Zi, tmp, ALU.subtract, ["zq1", "lt0"], ["zq1"]),
            TT(v4(Zr)[:, :, 0], v4(Zr)[:, :, 0], q0, ALU.add, ["zq0", "qi0"], ["zq0"]),
            TT(v4(Zi)[:, :, 0], v4(Zi)[:, :, 0], q1, ALU.add, ["zq1", "qi1"], ["zq1"]),
            lambda: V(lambda e: e.tensor_tensor_scan(Qr, Rk, Zr, 0.0, op0=ALU.mult, op1=ALU.add), ["zq0", "Rtab"], ["zq2"]),
            lambda: V(lambda e: e.tensor_tensor_scan(Qi, Rk, Zi, 0.0, op0=ALU.mult, op1=ALU.add), ["zq1", "Rtab"], ["zq3"]),
            lambda: V(lambda e: e.tensor_copy(S_sb[:, k, :, 0:1], S_sb[:, k, :, 128:129]), [sname], [sname]),
            TT(v4(ta), v4(Qr), cT, ALU.mult, ["zq2", "tab"], ["lt2"]),
            TT(v4(tb_), v4(Qi), sT, ALU.mult, ["zq3", "tab"], ["lt3"]),
            TT(Sre, v4(ta), v4(tb_), ALU.subtract, ["lt2", "lt3"], [sname]),
            TT(cr, v4(ta)[:, :, 127], v4(tb_)[:, :, 127], ALU.subtract, ["lt2", "lt3"], ["s5c"]),
            TT(v4(ta), v4(Qr), sT, ALU.mult, ["zq2", "tab"], ["lt2"]),
            TT(v4(tb_), v4(Qi), cT, ALU.mult, ["zq3", "tab"], ["lt3"]),
            TT(Sim, v4(ta), v4(tb_), ALU.add, ["lt2", "lt3"], [sname]),
            TT(ci, v4(ta)[:, :, 127], v4(tb_)[:, :, 127], ALU.add, ["lt2", "lt3"], ["s5c"]),
        ]

    def lru_steps(mt):
        xl = xlh[:, mt, :]
        xn = f"xl{mt}"
        (xc, xcn), (ra, ran), (i_, in_) = [(lt[:, r, :], f"lt{r}") for r in (4, 5, 1)]
        m_, mn = xc, xcn
        xb_ = xcb[:, 0, :]
        xbn = "xcb0"
        hb = hl[:, mt % 2, :]
        hn = f"hl{mt % 2}"
        bk = {}
        st = []
        def conv_mm():
            bk["bc"] = nextbank()
            mm_group(psb[bk["bc"]][:, :], [(lcd[:, t_ * 4 + mt, :], xl[:, t_:t_ + TB]) for t_ in range(4)], ["lcd", xn], [f"ps{bk['bc']}"])
        st.append(conv_mm)
        st.append(lambda: A(lambda e: e.activation(xc, psb[bk["bc"]][:, :], AF.Identity, bias=cp(C_LCB + mt)),
                            [f"ps{bk['bc']}", "colp"], [xcn]))
        st.append(lambda: G(lambda e: e.tensor_copy(xl[:, 0:3], xl[:, TB:TB + 3]), [xn], [xn]))
        st.append(lambda: A(lambda e: e.copy(xb_, xc), [xcn], [xbn]))

        def gates():
            bk["b1"], bk["b2"] = nextbank(), nextbank()
            mm_group(psb[bk["b1"]][:, :], [(lruw[:, mt, :], xb_)], ["lruw", xbn], [f"ps{bk['b1']}"])
            mm_group(psb[bk["b2"]][:, :], [(lruw[:, 4 + mt, :], xb_)], ["lruw", xbn], [f"ps{bk['b2']}"])
        st.append(gates)
        st.append(lambda: A(lambda e: e.activation(ra, psb[bk["b1"]][:, :], AF.Tanh, scale=0.5, bias=hbias[:, mt:mt + 1]),
                            [f"ps{bk['b1']}", "hbias"], [ran]))
        st.append(lambda: A(lambda e: e.activation(i_, psb[bk["b2"]][:, :], AF.Tanh, scale=0.5, bias=hbias[:, 4 + mt:5 + mt]),
                            [f"ps{bk['b2']}", "hbias"], [in_]))
        st.append(lambda: V(lambda e: e.scalar_tensor_tensor(i_, i_, 1.0, xc, op0=ALU.add, op1=ALU.mult), [in_, xcn], [in_]))
        st.append(lambda: A(lambda e: e.activation(ra, ra, AF.Exp, scale=cnegh[:, mt:mt + 1], bias=cnegh[:, mt:mt + 1]), [ran, "cnegh"], [ran]))
        st.append(lambda: A(lambda e: e.activation(m_, ra, AF.Square), [ran, in_], [mn]))
        st.append(lambda: A(lambda e: e.activation(m_, m_, AF.Sqrt, scale=-1.0, bias=onec[:, :]), [mn, "onec"], [mn]))
        st.append(lambda: V(lambda e: e.scalar_tensor_tensor(i_, i_, 0.5, m_, op0=ALU.mult, op1=ALU.mult), [in_, mn], [in_]))
        st.append(lambda: V(lambda e: e.tensor_tensor_scan(hb, ra, i_, lrucarry[:, mt:mt + 1], op0=ALU.mult, op1=ALU.add),
                            [ran, in_, "lrucarry"], [hn]))
        st.append(lambda: V(lambda e: e.tensor_copy(lrucarry[:, mt:mt + 1], hb[:, TB - 1:TB]), [hn], ["lrucarry"]))
        prod = lambda: V(lambda e: e.tensor_tensor(ymix[:, 4 + mt, :], hb, gl[:, mt, :], ALU.mult), [hn, f"gl{mt}"], [f"ymix{4 + mt}"])
        return st, prod

    def zip_steps(a, b):
        for i in range(max(len(a), len(b))):
            if i < len(a):
                a[i]()
            if i < len(b):
                b[i]()

    def s5_D(k, stK, snK):
        st, sn = load_slab(SL_S5V + k)
        bi = nextbank()
        yv = psb[bi][:, :].rearrange("p (c i) -> p c i", i=4)
        for i in range(4):
            pairs = [(st[:, (sg * 4 + i) * 128:(sg * 4 + i + 1) * 128], S_sb[:, k, sg, 0:128]) for sg in range(8)]
            pairs += [(stK[:, (k * 4 + (i - j)) * 128:(k * 4 + (i - j) + 1) * 128], u4[:, k, :, j]) for j in range(i + 1)]
            mm_group(yv[:, :, i], pairs, [sn, snK, f"S_sb{k}", f"gated{16 + k}"], [f"ps{bi}"])
        A(lambda e: e.activation(ygelu[:, k, :], psb[bi][:, :], AF.Gelu), [f"ps{bi}"], [f"gated{k}"])

    def glu_tile(mt, stK, snK):
        bi = nextbank()
        pairs = [(stK[:, 2048 + kc * 512 + mt * 128: 2048 + kc * 512 + (mt + 1) * 128], ygelu[:, kc, :]) for kc in range(4)]
        mm_group(psb[bi][:, :], pairs, [snK] + [f"gated{k}" for k in range(4)], [f"ps{bi}"])
        gt = lt[:, 2 + mt % 2, :]
        gn = f"lt{2 + mt % 2}"
        A(lambda e: e.activation(gt, psb[bi][:, :], AF.Sigmoid, bias=cp(C_BGLU + mt)), [f"ps{bi}", "colp"], [gn])
        V(lambda e: e.tensor_tensor(ymix[:, mt, :], ygelu[:, mt, :], gt, ALU.mult), [f"gated{mt}", gn], [f"ymix{mt}"])

    def attn_head(hd):
        def sc(mc):
            bi = nextbank6()
            pairs = [(Kfm[:, hd * 2 + c2, mc * 128:(mc + 1) * 128], q_sb[:, hd * 2 + c2, :]) for c2 in range(2)]
            mm_group(psb[bi][:, :], pairs, ["Kfm", f"ymix{hd * 2}", f"ymix{hd * 2 + 1}"], [f"ps{bi}"])
            def expfn(e):
                return e.activation(pT[:, hd, mc, :], psb[bi][:, :], AF.Exp)
            A(expfn, [f"ps{bi}"], [f"gated{2 * hd}", f"gated{2 * hd + 1}"])
        sc(0)
        sc(1)

    def attn_tail(hd):
        bd = nextbank6()
        mm_group(psb[bd][:, :], [(onesb[:, :], pT[:, hd, mc, :]) for mc in range(2)], ["onesb", f"gated{2 * hd}", f"gated{2 * hd + 1}"], [f"ps{bd}"])
        rd = rden[:, hd % 2, :]
        rn = f"hl{hd % 2}"
        A(lambda e: e.activation(rd, psb[bd][:, :], AF.Ln), [f"ps{bd}"], [rn])
        A(lambda e: e.activation(rd, rd, AF.Exp, scale=-1.0), [rn], [rn])

        def pv(j):
            bi = nextbank6()
            pairs = [(Vtok[:, mc, hd * 256 + j * 128: hd * 256 + (j + 1) * 128], pT[:, hd, mc, :]) for mc in range(2)]
            mm_group(psb[bi][:, :], pairs, ["Vtok", f"gated{2 * hd}", f"gated{2 * hd + 1}"], [f"ps{bi}"])
            V(lambda e: e.tensor_tensor(o_sb[:, hd * 2 + j, :], psb[bi][:, :], rd, ALU.mult), [f"ps{bi}", rn], [f"gated{8 + hd * 2 + j}"])
        pv(0)
        pv(1)

    def ffn_tile_mm(st, sn, sa, tt, isg, tb):
        vt = 2 * sa + tt
        ch = vt + 22 * isg
        col = isg * 256 + tt * 128
        bi = nextbank6()
        pairs = [(st[:, kc * 512 + col: kc * 512 + col + 128], hfm[:, kc, :]) for kc in range(8)]
        mm_group(psb[bi][:, :], pairs, [sn] + hnames, [f"ps{bi}"])
        ps = psb[bi]
        pn = f"ps{bi}"
        dst = (cv if isg == 0 else cg)[:, tt, :]
        dn = f"lt{2 + 2 * isg + tt}"
        hold = ffnhalo[:, tb % 2, ch, :]
        hnew = ffnhalo[:, (tb + 1) % 2, ch, :]
        wcol = lambda t_: cp(C_FCW + t_ * 44 + ch)
        A(lambda e: e.activation(dst, ps[:, :], AF.Identity, scale=wcol(2), bias=cp(C_FCB + ch)), [pn, "colp"], [dn])
        A(lambda e: e.copy(hnew, ps[:, TB - 2:TB]), [pn], [f"ffnhalo{(tb + 1) % 2}"])
        return ps, pn, dst, dn, wcol, hold, f"ffnhalo{tb % 2}"

    def ffn_tap(t_, ps, pn, dst, dn, wcol, hold, hn):
        sh = 2 - t_
        V(lambda e: e.scalar_tensor_tensor(dst[:, sh:TB], ps[:, 0:TB - sh], wcol(t_), dst[:, sh:TB], op0=ALU.mult, op1=ALU.add),
          [pn, dn, "colp"], [dn])

    def ffn_halo_prep(tb):
        hold = ffnhalo[:, tb % 2, :, :]
        hn = f"ffnhalo{tb % 2}"
        W0, W1 = cp(C_FCW, 44), cp(C_FCW + 44, 44)
        V(lambda e: e.tensor_tensor(hc[:, :, 1], hold[:, :, 1], W0, ALU.mult), [hn, "colp"], ["hc"])
        V(lambda e: e.tensor_tensor(hc[:, :, 0], hold[:, :, 0], W0, ALU.mult), [hn, "colp"], ["hc"])
        V(lambda e: e.tensor_tensor(hc2[:, :], hold[:, :, 1], W1, ALU.mult), [hn, "colp"], ["hc2"])
        V(lambda e: e.tensor_tensor(hc[:, :, 0], hc[:, :, 0], hc2[:, :], ALU.add), ["hc", "hc2"], ["hc"])

    def ffn_halo_add(ch, dst, dn):
        G(lambda e: e.tensor_tensor(dst[:, 0:2], dst[:, 0:2], hc[:, ch, :], ALU.add), ["hc", dn], [dn])

    def ffn_pair(st, sn, sa, tt, tb, prev_tail):
        vt = 2 * sa + tt
        tiles = [ffn_tile_mm(st, sn, sa, tt, 0, tb), ffn_tile_mm(st, sn, sa, tt, 1, tb)]
        if prev_tail is not None:
            prev_tail()
        for t_ in (1, 0):
            for tl in tiles:
                ffn_tap(t_, *tl)
        for isg, tl in enumerate(tiles):
            ffn_halo_add(vt + 22 * isg, tl[2], tl[3])

        def tail():
            A(lambda e: e.activation(cg[:, tt, :], cg[:, tt, :], AF.Gelu), [f"lt{4 + tt}"], [f"lt{4 + tt}"])
            G(lambda e: e.tensor_tensor(gated[:, vt, :], cg[:, tt, :], cv[:, tt, :], ALU.mult), [f"lt{4 + tt}", f"lt{2 + tt}"], [f"gated{vt}"])
        return tail

    def down_group(xb, nh, sg3, kc0, nk, a, bi):
        st, sn = down_group.cur

        def fn(e):
            ins = None
            for kk in range(nk):
                ins = e.matmul(psb[bi][:, :], gated[:, kc0 + kk, a * 128:(a + 1) * 128], st[:, kk * 512:(kk + 1) * 512],
                               start=(sg3 == 0 and kk == 0), stop=(sg3 == 2 and kk == nk - 1))
            return ins
        P.op("pe", fn, r=[sn] + [f"gated{kc0 + kk}" for kk in range(nk)], w=[f"ps{bi}"])

    def final_norm(xb, a):
        xa = xres[xb][:, a, :]
        sa_, ra_ = ss[:, 4 + a:5 + a], rstd[:, 4 + a:5 + a]
        xn = f"x{xb}a{a}"
        A(lambda e: e.activation(junk[:, :], xa, AF.Square, accum_out=sa_), [xn], ["junk", f"ssf{a}"])
        A(lambda e: e.activation(ra_, sa_, AF.Sqrt, scale=1.0 / D, bias=epsc[:, :]), [f"ssf{a}", "epsc"], [f"rstdf{a}"])
        V(lambda e: e.reciprocal(ra_, ra_), [f"rstdf{a}"], [f"rstdf{a}"])
        V(lambda e: e.scalar_tensor_tensor(xa, xa, ra_, gfin[:, :], op0=ALU.mult, op1=ALU.mult), [xn, f"rstdf{a}", "gfin"], [xn])

    x_t = x_d.rearrange("(b a p) d -> b p a d", a=4, p=128)
    out_t = out_d.rearrange("(b a p) d -> b p a d", a=4, p=128)

    def load_x(tb):
        xb = tb % 2
        P.op("sp", lambda e: e.dma_start(out=xres[xb][:, :, :], in_=x_t[tb]),
             w=[f"x{xb}a{a}" for a in range(4)], dsem=f"d_x{xb}")

    def store_out(tb):
        xb = tb % 2
        P.op("sp", lambda e: e.dma_start(out=out_t[tb], in_=xres[xb][:, :, :]),
             r=[f"x{xb}a{a}" for a in range(4)], dsem=f"d_o{xb}")

    def win_evac(grp, mt):
        def ev(bi):
            if grp == 0:
                A(lambda e: e.copy(u_sb[:, mt, :], psb[bi][:, :]), [f"ps{bi}"], [f"gated{16 + mt}"])
            elif grp == 1:
                A(lambda e: e.copy(xlh[:, mt, 3:3 + TB], psb[bi][:, :]), [f"ps{bi}"], [f"xl{mt}"])
            else:
                A(lambda e: e.activation(gl[:, mt, :], psb[bi][:, :], AF.Gelu), [f"ps{bi}"], [f"gl{mt}"])
        return ev

    def q_evac(half, mt):
        def ev(bi):
            A(lambda e: e.activation(q_sb[:, half * 4 + mt, :], psb[bi][:, :], AF.Copy, scale=0.0625),
              [f"ps{bi}"], [f"ymix{half * 4 + mt}"])
        return ev

    def finish_block(tb):
        if stage >= 6:
            for a in range(4):
                final_norm(tb % 2, a)

    def block_body(tb):
        xb = tb % 2
        if tb == 0:
            norm_stats(xres[xb], f"x{xb}a", 4)
        norm_transposes(4, C_G1, hfm, "hfm", TB)
        st0, sn0 = load_slab(SL_WIN)
        for mt in range(4):
            fm_proj(st0, sn0, mt, win_evac(0, mt))
        if tb > 0:
            finish_block(tb - 1)
        s5_qinit()
        wslabs = {}

        def win_tiles(grp, mts):
            if grp not in wslabs:
                wslabs[grp] = load_slab(SL_WIN + grp)
            st, sn = wslabs[grp]
            for mt in mts:
                fm_proj(st, sn, mt, win_evac(grp, mt))
        s5_B_pe(0)
        zip_steps(s5_B_steps(0), [])
        win_tiles(1, [0, 1])
        prods = {}
        for k in range(1, 4):
            s5_B_pe(k)
            lst, prods[k - 1] = lru_steps(k - 1)
            zip_steps(s5_B_steps(k), lst)
            if k == 1:
                win_tiles(1, [2, 3])
            elif k == 2:
                win_tiles(2, [0, 1])
                prods[0]()
                prods[1]()
            else:
                win_tiles(2, [2, 3])
        if stage < 2:
            return
        stK, snK = load_slab(SL_KG)
        lst, prods[3] = lru_steps(3)
        cuts = [0, 4, 9, 13, len(lst)]
        for k in range(4):
            s5_D(k, stK, snK)
            for f in lst[cuts[k]:cuts[k + 1]]:
                f()
        prods[2]()
        prods[3]()
        for mt in range(4):
            glu_tile(mt, stK, snK)
        if tb > 0:
            store_out(tb - 1)
        if stage < 3:
            return
        proj_tok(ymix, [f"ymix{i}" for i in range(8)], [SL_WOUT, SL_WOUT + 1], xb, nextbank6)
        if stage < 4:
            return
        rmsnorm_to_hT(xres[xb], f"x{xb}a", 4, C_G2, hfm, "hfm", TB)
        for half in range(2):
            st, sn = load_slab(SL_WQ + half)
            for mt in range(4):
                fm_proj(st, sn, mt, q_evac(half, mt), nextbank6)
        attn_head(0)
        for hd in range(4):
            if hd + 1 < 4:
                attn_head(hd + 1)
            attn_tail(hd)
        proj_tok(o_sb, [f"gated{8 + i}" for i in range(8)], [SL_WO, SL_WO + 1], xb, nextbank6)
        if stage < 5:
            return
        rmsnorm_to_hT(xres[xb], f"x{xb}a", 4, C_G3, hfm, "hfm", TB)
        if tb + 1 < nblk:
            load_x(tb + 1)
        ffn_halo_prep(tb)
        ptail = None
        for sa in range(11):
            st, sn = load_slab(SL_UP + sa)
            for tt in range(2):
                ptail = ffn_pair(st, sn, sa, tt, tb, ptail)
        ptail()
        if tb + 1 < nblk and stage >= 6:
            norm_stats(xres[1 - xb], f"x{1 - xb}a", 4)
        for nh in range(2):
            kc0 = 0
            dbanks = [nextbank6() for _ in range(4)]
            for sg3 in range(3):
                nk = 8 if sg3 < 2 else 6
                down_group.cur = load_slab(SL_DN + nh * 3 + sg3)
                for a in range(4):
                    down_group(xb, nh, sg3, kc0, nk, a, dbanks[a])
                kc0 += nk
            for a in range(4):
                add_resid(xb, a, nh, dbanks[a])

    load_x(0)
    for tb in range(nblk):
        block_body(tb)
    finish_block(nblk - 1)
    store_out(nblk - 1)

    if dbg is not None:
        P.barrier()
        for i_, (ap_, a_, b_) in enumerate(dbg(locals())):
            P.op("pool", lambda e, i_=i_, ap_=ap_, a_=a_, b_=b_: e.dma_start(
                out=dbg_d[i_][:, 0:a_ * b_].rearrange("p (a b) -> p a b", a=a_), in_=ap_), dsem=f"d_dbg{i_}")

    P.barrier()
    with nc.Block() as block:
        P.emit(block)
    es.close()
    return nc


def _slab_kc(w, c0, ncols=512, kc0=0, nkc=8):
    out = np.zeros((128, 8, ncols), np.float32)
    K = w.shape[0]
    for kc in range(nkc):
        r0 = (kc0 + kc) * 128
        if r0 >= K:
            break
        out[:, kc, :] = w[r0:r0 + 128, c0:c0 + ncols]
    return out.reshape(128, 8 * ncols)


def host_layout(inp):
    f = lambda k: np.asarray(inp[k], np.float32)
    w_in, w_out = f("w_in")[0], f("w_out")[0]
    wq, wk, wv, wo = f("xa_w_q")[0], f("xa_w_k")[0], f("xa_w_v")[0], f("xa_w_o")[0]
    wup, wdn, glu = f("ffn_w_up")[0], f("ffn_w_down")[0], f("s5_w_glu")[0]
    slabs = {}
    for g in range(3):
        slabs[SL_WIN + g] = _slab_kc(w_in, g * 512)
    kg = np.zeros((128, 4096), np.float32)
    kg[:, 2048:] = _slab_kc(glu, 0, 512, 0, 4)[:, :2048]
    slabs[SL_KG] = kg
    for h in range(2):
        slabs[SL_WOUT + h] = _slab_kc(w_out, h * 512)
        slabs[SL_WQ + h] = _slab_kc(wq, h * 512)
        slabs[SL_WO + h] = _slab_kc(wo, h * 512)
        slabs[SL_WK + h] = _slab_kc(wk, h * 512)
        slabs[SL_WV + h] = _slab_kc(wv, h * 512)
    for sa in range(11):
        cols = np.concatenate([np.arange(sa * 256, sa * 256 + 256), 2816 + np.arange(sa * 256, sa * 256 + 256)])
        slabs[SL_UP + sa] = _slab_kc(wup[:, cols], 0)
    for nh in range(2):
        for g3 in range(3):
            slabs[SL_DN + nh * 3 + g3] = _slab_kc(wdn, nh * 512, 512, g3 * 8, 8 if g3 < 2 else 6)
    w32 = np.stack([slabs[s] for s in HOST_SLABS]).astype(np.float32)

    colp = np.zeros((128, NCOL), np.float32)

    def putcols(c0, vec):
        v = np.asarray(vec, np.float32).reshape(-1, 128).T
        colp[:, c0:c0 + v.shape[1]] = v
    putcols(C_G1, f("ln_mix_g")[0])
    putcols(C_G2, f("ln_xa_g")[0])
    putcols(C_G3, f("ln_ffn_g")[0])
    putcols(C_GMEM, f("mem_norm_g"))
    putcols(C_BGLU, f("s5_b_glu")[0])
    lcw = f("lru_conv_w")[0]
    for t in range(4):
        putcols(C_LCW + t * 4, lcw[t])
    putcols(C_LCB, f("lru_conv_b")[0])
    putcols(C_BA, f("lru_b_a")[0].reshape(-1))
    putcols(C_BX, f("lru_b_x")[0].reshape(-1))
    putcols(C_LAM, f("lru_lam")[0].reshape(-1))
    fcw = f("ffn_conv_w")[0]
    for t in range(3):
        putcols(C_FCW + t * 44, fcw[t])
    putcols(C_FCB, f("ffn_conv_b")[0])
    putcols(C_DS5, f("s5_d")[0].reshape(-1))
    gfin = np.ascontiguousarray(np.broadcast_to(f("final_norm_g")[None, :], (128, D)))

    def gq(arr):
        a = arr.reshape(4, 8, 4, 16)
        return np.ascontiguousarray(a.transpose(1, 3, 0, 2).reshape(128, 16))
    lre, lim = f("s5_lam_re")[0], f("s5_lam_im")[0]
    ldt = np.broadcast_to(f("s5_log_dt")[0][:, None], (32, 64))
    s5par = np.stack([gq(lre), gq(lim), gq(ldt)], axis=1).astype(np.float32)

    def cb(b):
        a = b.reshape(4, 8, 4, 16, 16)
        return np.ascontiguousarray(a.transpose(1, 3, 0, 2, 4).reshape(128, 16, 16))
    bcc = np.stack([cb(f("s5_b_re")[0]), cb(f("s5_b_im")[0]),
                    cb(np.ascontiguousarray(f("s5_c_re")[0].transpose(0, 2, 1))),
                    cb(np.ascontiguousarray(f("s5_c_im")[0].transpose(0, 2, 1)))], axis=1).astype(np.float32)
    mask8 = np.zeros((128, 2, 8), np.float32)
    for p_ in range(128):
        mask8[p_, 0, p_ // 16] = 1.0
        mask8[p_, 1, p_ // 16] = -1.0
    lruw = np.zeros((128, 8, 128), np.float32)
    wa, wx = f("lru_w_a")[0], f("lru_w_x")[0]
    for mt in range(4):
        for hh in range(2):
            lruw[hh * 64:(hh + 1) * 64, mt, hh * 64:(hh + 1) * 64] = wa[2 * mt + hh]
            lruw[hh * 64:(hh + 1) * 64, 4 + mt, hh * 64:(hh + 1) * 64] = wx[2 * mt + hh]
    shared = {"w32": w32, "colp": colp, "gfin": gfin, "s5par": s5par, "bcc": bcc, "mask8": mask8, "lruw": lruw,
              "ident": np.eye(128, dtype=np.float32).astype(ml_dtypes.bfloat16),
              "identf": np.eye(128, dtype=np.float32)}
    return shared


def kernel(**inputs):
    shared = host_layout(inputs)
    x = np.asarray(inputs["x"], np.float32)
    mem = np.asarray(inputs["mem"], np.float32)
    nc = build(SEQ // TB)
    in_maps = []
    for c in range(8):
        m = dict(shared)
        m["x"] = np.ascontiguousarray(x[c])
        m["mem"] = np.ascontiguousarray(mem[c])
        in_maps.append(m)
    res = run_bass_kernel_spmd(nc, in_maps, core_ids=list(range(8)))
    return np.stack([np.asarray(r["out"], np.float32) for r in res.results], axis=0)
```

```python
import math
from contextlib import ExitStack
import numpy as np
import ml_dtypes
import concourse.bass as bass
import concourse.mybir as mybir
from concourse.bass_utils import run_bass_kernel_spmd

F32 = mybir.dt.float32
BF16 = mybir.dt.bfloat16
ALU = mybir.AluOpType
AF = mybir.ActivationFunctionType

SEQ = 4096
TB = 512
D = 1024
NS = 5
PI = math.pi

SL_WIN = 0
SL_S5W = 3
SL_S5V = 7
SL_KG = 11
SL_WOUT = 12
SL_WQ = 14
SL_WO = 16
SL_UP = 18
SL_DN = 29
SL_WK = 35
SL_WV = 37
NSLAB = 39
HOST_SLABS = [0, 1, 2, 11, 12, 13, 14, 15, 16, 17] + list(range(18, 35)) + [35, 36, 37, 38]

C_G1, C_G2, C_G3, C_GMEM = 0, 8, 16, 24
C_BGLU = 32
C_LCW = 36
C_LCB = 52
C_BA = 56
C_BX = 60
C_LAM = 64
C_FCW = 68
C_FCB = 200
C_DS5 = 244
NCOL = 248


class Prog:
    ENG = ("pe", "act", "dve", "pool", "sp")

    def __init__(self, nc, es):
        self.nc = nc
        self.es = es
        self.q = {e: [] for e in self.ENG}
        self.cnt = {}
        self.sems = {}
        self.seen = {e: {} for e in self.ENG}
        self.lastw = {}
        self.readers = {}

    def sem(self, key):
        if key not in self.sems:
            self.sems[key] = self.es.enter_context(self.nc.semaphore("s_" + key))
            self.cnt[key] = 0
        return self.sems[key]

    def op(self, eng, fn, r=(), w=(), dsem=None):
        deps = {}

        def add(tok):
            if tok is None:
                return
            k, v = tok
            if deps.get(k, 0) < v:
                deps[k] = v
        for b in r:
            add(self.lastw.get(b))
        for b in w:
            add(self.lastw.get(b))
            for k, v in self.readers.get(b, {}).items():
                add((k, v))
        waits = []
        for k, v in deps.items():
            if eng == "pe" and k == "pe":
                continue
            if self.seen[eng].get(k, 0) >= v:
                continue
            self.seen[eng][k] = v
            waits.append((k, v))
        if dsem is None:
            key, inc = eng, 1
        else:
            key, inc = dsem, 16
        self.sem(key)
        self.cnt[key] += inc
        tok = (key, self.cnt[key])
        self.q[eng].append((waits, fn, key, inc))
        for b in r:
            self.readers.setdefault(b, {})[key] = tok[1]
        for b in w:
            self.lastw[b] = tok
            self.readers[b] = {}
        return tok

    def barrier(self, skip=None):
        for e in self.ENG:
            waits = []
            for k, v in self.cnt.items():
                if skip is not None and k.startswith(skip):
                    continue
                if v > 0 and self.seen[e].get(k, 0) < v:
                    self.seen[e][k] = v
                    waits.append((k, v))
            if waits:
                self.q[e].append((waits, None, None, 0))

    def emit(self, block):
        sems = self.sems

        def run(eng, lst):
            for waits, fn, key, inc in lst:
                for k, v in waits:
                    eng.wait_ge(sems[k], v)
                if fn is not None:
                    ins = fn(eng)
                    ins.then_inc(sems[key], inc)

        @block.tensor
        def _(e):
            run(e, self.q["pe"])

        @block.scalar
        def _(e):
            run(e, self.q["act"])

        @block.vector
        def _(e):
            run(e, self.q["dve"])

        @block.gpsimd
        def _(e):
            run(e, self.q["pool"])

        @block.sync
        def _(e):
            run(e, self.q["sp"])


def build(nblk=8, stage=6, dbg=None, pro_only=False):
    nc = bass.Bass("TRN2", target_bir_lowering=False)
    ntok = nblk * TB

    def din(name, shape, dt=F32):
        return nc.dram_tensor(name, list(shape), dt, kind="ExternalInput").ap()
    x_d = din("x", [ntok, D])
    mem_d = din("mem", [256, D])
    w32_d = din("w32", [len(HOST_SLABS), 128, 4096])
    colp_d = din("colp", [128, NCOL])
    gfin_d = din("gfin", [128, D])
    s5par_d = din("s5par", [128, 3, 16])
    bcc_d = din("bcc", [128, 4, 16, 16])
    mask8_d = din("mask8", [128, 2, 8])
    lruw_d = din("lruw", [128, 8, 128])
    ident_d = din("ident", [128, 128], BF16)
    identf_d = din("identf", [128, 128])
    out_d = nc.dram_tensor("out", [ntok, D], F32, kind="ExternalOutput").ap()
    wscr = nc.dram_tensor("wscr", [NSLAB, 128, 4096], BF16, kind="Internal").ap()
    dbg_d = None
    if dbg is not None:
        dbg_d = nc.dram_tensor("dbg", [16, 128, 4096], F32, kind="ExternalOutput").ap()

    es = ExitStack()
    P = Prog(nc, es)

    def sb(name, shape, dt=F32, stack=es):
        return stack.enter_context(nc.sbuf_tensor("sb_" + name, list(shape), dt))

    colp = sb("colp", [128, NCOL])
    gfin = sb("gfin", [128, D])
    ident = sb("ident", [128, 128], BF16)
    onesb = sb("onesb", [128, 128], BF16)
    lruw = sb("lruw", [128, 8, 128], BF16)
    lcd = sb("lcd", [128, 16, 128], BF16)
    cneg = sb("cneg", [128, 4])
    epsc = sb("epsc", [128, 1])
    hpic = sb("hpic", [128, 1])
    onec = sb("onec", [128, 1])
    hbias = sb("hbias", [128, 12])
    cnegh = sb("cnegh", [128, 4])
    cosT = sb("cosT", [128, 16, 128])
    sinT = sb("sinT", [128, 16, 128])
    Rtab = sb("Rtab", [128, 16, 128])
    R4 = sb("R4", [128, 16])
    E4r = sb("E4r", [128, 16])
    E4i = sb("E4i", [128, 16])
    Kfm = sb("Kfm", [128, 8, 256], BF16)
    Vtok = sb("Vtok", [128, 2, D], BF16)
    slots = [sb(f"slot{i}", [128, 4096], BF16) for i in range(NS)]
    xres = [sb("xres0", [128, 4, D]), None]
    hT = sb("hT", [128, 4, D], BF16)
    hfm = sb("hfm", [128, 8, TB], BF16)
    ss = sb("ss", [128, 8])
    rstd = sb("rstd", [128, 8])
    junk = sb("junk", [128, D], BF16)
    s5carry_r = sb("s5cr", [128, 16])
    s5carry_i = sb("s5ci", [128, 16])
    lrucarry = sb("lrucarry", [128, 4])
    psb = [es.enter_context(nc.psum_tensor(f"pp{i}", [128, 512], F32)) for i in range(6)]
    psT = [es.enter_context(nc.psum_tensor(f"ppT{i}", [128, 1024], BF16)) for i in range(2)]

    cp = lambda c, n=1: colp[:, c:c + n]

    slab_state = {"n": 0}

    def load_slab(idx, eng="sp"):
        i = slab_state["n"] % NS
        slab_state["n"] += 1
        name = f"slot{i}"
        P.op(eng, lambda e, i=i, idx=idx: e.dma_start(out=slots[i][:, :], in_=wscr[idx]),
             r=[f"scr{idx}", f"scr{idx}g"], w=[name], dsem=f"d_slot{i}")
        return slots[i], name

    def mm_group(out_ap, pairs, r, w):
        n = len(pairs)

        def fn(e):
            ins = None
            for j, (l, rr) in enumerate(pairs):
                ins = e.matmul(out_ap, l, rr, start=(j == 0), stop=(j == n - 1))
            return ins
        P.op("pe", fn, r=r, w=w)

    def V(fn, r, w):
        P.op("dve", fn, r=r, w=w)

    def A(fn, r, w):
        P.op("act", fn, r=r, w=w)

    def G(fn, r, w):
        P.op("pool", fn, r=r, w=w)

    pst = ExitStack()
    par = sb("par", [128, 3, 16], stack=pst)
    identf = sb("identf", [128, 128], stack=pst)
    bcc = sb("bcc", [128, 4, 16, 16], stack=pst)
    mask8 = sb("mask8", [128, 2, 8], stack=pst)
    cs = sb("cs", [128, 4, 16, 16], stack=pst)
    t1 = sb("t1", [128, 16, 128], stack=pst)
    lruw32 = t1[:, 0:8, :]
    t2 = sb("t2", [128, 16, 128], stack=pst)
    NB = sb("NB", [128, 4, 2, 16, 128], BF16, stack=pst)
    Vm = sb("Vm", [128, 2, 4, 8, 128], BF16, stack=pst)
    Wst = sb("Wst", [128, 1, 8, 4, 128], BF16, stack=pst)
    Kst = hT[:, 2:4, :].rearrange("p a (b c) -> p (a b) c", c=128).rearrange("p (k t) c -> p k t c", k=4)
    sm = xres[0][:, 2:4, :].rearrange("p a (b c) -> p (a b) c", c=16)
    memx = xres[0]
    memfm = hfm

    def ld(dst, src, name, eng="sp"):
        P.op(eng, lambda e: e.dma_start(out=dst, in_=src), w=[name], dsem="d_" + name)
    ld(colp[:, :], colp_d, "colp")
    ld(gfin[:, :], gfin_d, "gfin")
    ld(ident[:, :], ident_d, "ident")
    ld(identf[:, :], identf_d, "identf")
    ld(par[:, :, :], s5par_d, "par")
    ld(bcc[:, :, :, :], bcc_d, "bcc")
    ld(mask8[:, :, :], mask8_d, "mask8")
    ld(lruw32, lruw_d, "t1")
    P.op("sp", lambda e: e.dma_start(out=memx[:, 0:2, :], in_=mem_d.rearrange("(a p) d -> p a d", p=128)),
         w=["x0a0", "x0a1"], dsem="d_x0")

    G(lambda e: e.memset(onesb[:, :], 1.0), [], ["onesb"])
    G(lambda e: e.memset(epsc[:, :], 1e-6), [], ["epsc"])
    G(lambda e: e.memset(hpic[:, :], PI / 2), [], ["hpic"])
    G(lambda e: e.memset(onec[:, :], 1.0), [], ["onec"])
    G(lambda e: e.memset(lrucarry[:, :], 0.0), [], ["lrucarry"])
    G(lambda e: e.memset(s5carry_r[:, :], 0.0), [], ["s5c"])
    G(lambda e: e.memset(s5carry_i[:, :], 0.0), [], ["s5c"])
    V(lambda e: e.tensor_copy(lruw[:, :, :], lruw32), ["t1"], ["lruw"])
    for t_ in range(4):
        V(lambda e, t_=t_: e.tensor_tensor(lcd[:, t_ * 4:(t_ + 1) * 4, :], identf[:, :].unsqueeze(1).to_broadcast([128, 4, 128]),
                                            cp(C_LCW + t_ * 4, 4).unsqueeze(2).to_broadcast([128, 4, 128]), ALU.mult),
          ["identf", "colp"], ["lcd"])

    A(lambda e: e.activation(cneg[:, :], cp(C_LAM, 4), AF.Exp, scale=-1.0), ["colp"], ["cneg"])
    A(lambda e: e.activation(cneg[:, :], cneg[:, :], AF.Ln, bias=onec[:, :]), ["cneg", "onec"], ["cneg"])
    V(lambda e: e.tensor_scalar_mul(cneg[:, :], cneg[:, :], -8.0), ["cneg"], ["cneg"])
    V(lambda e: e.tensor_scalar_mul(cnegh[:, :], cneg[:, :], 0.5), ["cneg"], ["cnegh"])
    V(lambda e: e.tensor_scalar_mul(hbias[:, 0:4], cp(C_BA, 4), 0.5), ["colp"], ["hbias"])
    V(lambda e: e.tensor_scalar_mul(hbias[:, 4:8], cp(C_BX, 4), 0.5), ["colp"], ["hbias"])
    V(lambda e: e.tensor_scalar_mul(hbias[:, 8:12], cp(C_BGLU, 4), 0.5), ["colp"], ["hbias"])

    smn = {"i": 0}

    def S(name=None):
        i = smn["i"]
        smn["i"] += 1
        return sm[:, i, :], f"sm{i}"

    def vtt(o, a, b, op):
        (oa, on), (aa, an), (ba, bn) = o, a, b
        V(lambda e: e.tensor_tensor(oa, aa, ba, op), [an, bn], [on])

    def cmul(a_r, a_i, b_r, b_i):
        o_r, o_i, u1, u2 = S(), S(), S(), S()
        vtt(u1, a_r, b_r, ALU.mult)
        vtt(u2, a_i, b_i, ALU.mult)
        vtt(o_r, u1, u2, ALU.subtract)
        vtt(u1, a_r, b_i, ALU.mult)
        vtt(u2, a_i, b_r, ALU.mult)
        vtt(o_i, u1, u2, ALU.add)
        return o_r, o_i

    lre = (par[:, 0, :], "par")
    lim = (par[:, 1, :], "par")
    ldt = (par[:, 2, :], "par")
    dt_ = S()
    A(lambda e: e.activation(dt_[0], ldt[0], AF.Exp), ["par"], [dt_[1]])
    zr, zi = S(), S()
    vtt(zr, lre, dt_, ALU.mult)
    vtt(zi, lim, dt_, ALU.mult)
    mag = S()
    A(lambda e: e.activation(mag[0], zr[0], AF.Exp), [zr[1]], [mag[1]])

    sn0, cs0 = S(), S()
    A(lambda e: e.activation(sn0[0], zi[0], AF.Sin, scale=1.0 / 16), [zi[1]], [sn0[1]])
    A(lambda e: e.activation(cs0[0], zi[0], AF.Sin, scale=1.0 / 16, bias=hpic[:, :]), [zi[1], "hpic"], [cs0[1]])
    sn1, cs1 = sn0, cs0
    for _ in range(4):
        cs1, sn1 = cmul(cs1, sn1, cs1, sn1)
    L = [None] * 5
    L1r, L1i = S(), S()
    vtt(L1r, mag, cs1, ALU.mult)
    vtt(L1i, mag, sn1, ALU.mult)
    L[1] = (L1r, L1i)
    L[2] = cmul(L1r, L1i, L1r, L1i)
    L[3] = cmul(L[2][0], L[2][1], L1r, L1i)
    L[4] = cmul(L[2][0], L[2][1], L[2][0], L[2][1])
    one_, zero_ = S(), S()
    V(lambda e: e.memset(one_[0], 1.0), [], [one_[1]])
    V(lambda e: e.memset(zero_[0], 0.0), [], [zero_[1]])
    L[0] = (one_, zero_)
    am1 = S()
    V(lambda e: e.tensor_scalar_add(am1[0], L1r[0], -1.0), [L1r[1]], [am1[1]])
    nli = S()
    V(lambda e: e.tensor_scalar_mul(nli[0], lim[0], -1.0), ["par"], [nli[1]])
    num_r, num_i = cmul(am1, L1i, lre, nli)
    den, u3 = S(), S()
    vtt(den, lre, lre, ALU.mult)
    vtt(u3, lim, lim, ALU.mult)
    vtt(den, den, u3, ALU.add)
    kr, ki = S(), S()
    V(lambda e: e.reciprocal(den[0], den[0]), [den[1]], [den[1]])
    vtt(kr, num_r, den, ALU.mult)
    vtt(ki, num_i, den, ALU.mult)
    M = [cmul(L[3 - j][0], L[3 - j][1], kr, ki) for j in range(4)]
    r2 = S()
    vtt(r2, L[4][0], L[4][0], ALU.mult)
    vtt(u3, L[4][1], L[4][1], ALU.mult)
    vtt(r2, r2, u3, ALU.add)
    A(lambda e: e.activation(R4[:, :], r2[0], AF.Sqrt), [r2[1]], ["R4"])
    ir4 = S()
    V(lambda e: e.reciprocal(ir4[0], R4[:, :]), ["R4"], [ir4[1]])
    V(lambda e: e.tensor_tensor(E4r[:, :], L[4][0][0], ir4[0], ALU.mult), [L[4][0][1], ir4[1]], ["E4"])
    V(lambda e: e.tensor_tensor(E4i[:, :], L[4][1][0], ir4[0], ALU.mult), [L[4][1][1], ir4[1]], ["E4"])
    V(lambda e: e.memset(cosT[:, :, 0:1], 1.0), [], ["tab"])
    V(lambda e: e.memset(sinT[:, :, 0:1], 0.0), [], ["tab"])
    wr, wi = (E4r[:, :], "E4"), (E4i[:, :], "E4")
    n = 1
    while n < 128:
        wrb = wr[0].unsqueeze(2).to_broadcast([128, 16, n])
        wib = wi[0].unsqueeze(2).to_broadcast([128, 16, n])
        c0, s0 = cosT[:, :, 0:n], sinT[:, :, 0:n]
        c1, s1 = cosT[:, :, n:2 * n], sinT[:, :, n:2 * n]
        ta, tb_ = t1[:, :, 0:n], t2[:, :, 0:n]
        V(lambda e, c0=c0, wrb=wrb, ta=ta: e.tensor_tensor(ta, c0, wrb, ALU.mult), ["tab", wr[1]], ["t1"])
        V(lambda e, s0=s0, wib=wib, tb_=tb_: e.tensor_tensor(tb_, s0, wib, ALU.mult), ["tab", wi[1]], ["t2"])
        V(lambda e, c1=c1, ta=ta, tb_=tb_: e.tensor_tensor(c1, ta, tb_, ALU.subtract), ["t1", "t2"], ["tab"])
        V(lambda e, c0=c0, wib=wib, ta=ta: e.tensor_tensor(ta, c0, wib, ALU.mult), ["tab", wi[1]], ["t1"])
        V(lambda e, s0=s0, wrb=wrb, tb_=tb_: e.tensor_tensor(tb_, s0, wrb, ALU.mult), ["tab", wr[1]], ["t2"])
        V(lambda e, s1=s1, ta=ta, tb_=tb_: e.tensor_tensor(s1, ta, tb_, ALU.add), ["t1", "t2"], ["tab"])
        if n < 64:
            wr, wi = cmul(wr, wi, wr, wi)
        n *= 2
    V(lambda e: e.tensor_copy(Rtab[:, :, :], R4[:, :].unsqueeze(2).to_broadcast([128, 16, 128])), ["R4"], ["Rtab"])
    V(lambda e: e.memset(Rtab[:, :, 0:1], 0.0), ["Rtab"], ["Rtab"])

    def norm_stats(src, srcname, nsub):
        for a in range(nsub):
            if a % 2 == 0:
                A(lambda e, a=a: e.activation(junk[:, :], src[:, a, :], AF.Square, accum_out=ss[:, a:a + 1]),
                  [f"{srcname}{a}"], ["junk", f"ss{a}"])
            else:
                V(lambda e, a=a: e.scalar_tensor_tensor(hT[:, a, :], src[:, a, :], 1.0, src[:, a, :], op0=ALU.mult, op1=ALU.mult,
                                                        accum_out=ss[:, a:a + 1]), [f"{srcname}{a}"], [f"hT{a}", f"ss{a}"])
            A(lambda e, a=a: e.activation(rstd[:, a:a + 1], ss[:, a:a + 1], AF.Sqrt, scale=1.0 / D, bias=epsc[:, :]),
              [f"ss{a}", "epsc"], [f"rstd{a}"])
            V(lambda e, a=a: e.reciprocal(rstd[:, a:a + 1], rstd[:, a:a + 1]), [f"rstd{a}"], [f"rstd{a}"])
            if a % 2 == 0:
                A(lambda e, a=a: e.activation(hT[:, a, :], src[:, a, :], AF.Copy, scale=rstd[:, a:a + 1]),
                  [f"{srcname}{a}", f"rstd{a}"], [f"hT{a}"])
            else:
                V(lambda e, a=a: e.tensor_scalar_mul(hT[:, a, :], src[:, a, :], rstd[:, a:a + 1]),
                  [f"{srcname}{a}", f"rstd{a}"], [f"hT{a}"])

    def norm_transposes(nsub, col_g, dst_fm, dstname, ncols_tok):
        for kc in range(8):
            bi = kc % 2
            bank = psT[bi]

            def fn(e, kc=kc, bank=bank):
                ins = None
                for a in range(nsub):
                    ins = e.transpose(bank[:, a * 128:(a + 1) * 128], hT[:, a, kc * 128:(kc + 1) * 128], ident[:, :])
                return ins
            P.op("pe", fn, r=[f"hT{a}" for a in range(nsub)] + ["ident"], w=[f"psT{bi}"])
            if kc % 2 == 0:
                A(lambda e, kc=kc, bank=bank: e.activation(dst_fm[:, kc, 0:ncols_tok], bank[:, 0:ncols_tok], AF.Copy, scale=cp(col_g + kc)),
                  [f"psT{bi}", "colp"], [f"{dstname}{kc}"])
            else:
                V(lambda e, kc=kc, bank=bank: e.tensor_scalar_mul(dst_fm[:, kc, 0:ncols_tok], bank[:, 0:ncols_tok], cp(col_g + kc)),
                  [f"psT{bi}", "colp"], [f"{dstname}{kc}"])


    def rmsnorm_to_hT(src, srcname, nsub, col_g, dst_fm, dstname, ncols_tok):
        norm_stats(src, srcname, nsub)
        norm_transposes(nsub, col_g, dst_fm, dstname, ncols_tok)

    rmsnorm_to_hT(memx, "x0a", 2, C_GMEM, memfm, "hfm", 256)
    for half in range(2):
        P.op("pool", lambda e, half=half: e.dma_start(out=slots[half][:, :], in_=w32_d[HOST_SLABS.index(SL_WK + half)]),
             w=[f"slot{half}"], dsem=f"d_slot{half}")
        for mt in range(4):
            bi = 2 + mt % 2
            pairs = [(slots[half][:, kc * 512 + mt * 128: kc * 512 + (mt + 1) * 128], memfm[:, kc, 0:256]) for kc in range(8)]
            mm_group(psb[bi][:, 0:256], pairs, [f"slot{half}"] + [f"hfm{kc}" for kc in range(8)], [f"ps{bi}"])
            A(lambda e, half=half, mt=mt, bi=bi: e.copy(Kfm[:, half * 4 + mt, :], psb[bi][:, 0:256]), [f"ps{bi}"], ["Kfm"])
    for half in range(2):
        P.op("pool", lambda e, half=half: e.dma_start(out=slots[2 + half][:, :], in_=w32_d[HOST_SLABS.index(SL_WV + half)]),
             w=[f"slot{2 + half}"], dsem=f"d_slot{2 + half}")
        for mc in range(2):
            bi = 4 + mc
            pairs = [(memfm[:, kc, mc * 128:(mc + 1) * 128], slots[2 + half][:, kc * 512:(kc + 1) * 512]) for kc in range(8)]
            mm_group(psb[bi][:, :], pairs, [f"slot{2 + half}"] + [f"hfm{kc}" for kc in range(8)], [f"ps{bi}"])
            V(lambda e, half=half, mc=mc, bi=bi: e.tensor_copy(Vtok[:, mc, half * 512:(half + 1) * 512], psb[bi][:, :]),
              [f"ps{bi}"], ["Vtok"])

    bre, bim, cre, cim = (bcc[:, i_, :, :] for i_ in range(4))
    c1, c2, c3, c4 = (cs[:, i_, :, :] for i_ in range(4))
    mpos = mask8[:, 0, :].unsqueeze(1).unsqueeze(3)
    mneg = mask8[:, 1, :].unsqueeze(1).unsqueeze(3)

    def bc16(ap):
        return ap.unsqueeze(2).to_broadcast([128, 16, 16])
    for j in range(4):
        mr, mi = bc16(M[j][0][0]), bc16(M[j][1][0])
        mrn, min_ = M[j][0][1], M[j][1][1]
        V(lambda e, mr=mr: e.tensor_tensor(c1, bre, mr, ALU.mult), ["bcc", mrn], ["c1"])
        V(lambda e, mi=mi: e.tensor_tensor(c2, bim, mi, ALU.mult), ["bcc", min_], ["c2"])
        V(lambda e, mi=mi: e.tensor_tensor(c3, bre, mi, ALU.mult), ["bcc", min_], ["c3"])
        V(lambda e, mr=mr: e.tensor_tensor(c4, bim, mr, ALU.mult), ["bcc", mrn], ["c4"])
        V(lambda e: e.tensor_tensor(c1, c1, c2, ALU.subtract), ["c1", "c2"], ["c1"])
        V(lambda e: e.tensor_tensor(c3, c3, c4, ALU.add), ["c3", "c4"], ["c3"])
        for ri, cc, cn in ((0, c1, "c1"), (1, c3, "c3")):
            o = NB[:, j, ri, :, :].rearrange("p a (g h) -> p a g h", g=8)
            V(lambda e, o=o, cc=cc: e.tensor_tensor(o, cc.unsqueeze(2).to_broadcast([128, 16, 8, 16]),
                                                    mpos.to_broadcast([128, 16, 8, 16]), ALU.mult), [cn, "mask8"], [f"NB{j}"])
    tix = {"i": 0}

    def w_section(k):
        for sg in range(8):
            s_, ri = sg % 4, sg // 4
            bank = psT[tix["i"] % 2]
            bname = f"psT{tix['i'] % 2}"
            tix["i"] += 1

            def fn(e, s_=s_, ri=ri, bank=bank):
                ins = None
                for j in range(4):
                    ins = e.transpose(bank[:, j * 128:(j + 1) * 128], NB[:, j, ri, k * 4 + s_, :], ident[:, :])
                return ins
            P.op("pe", fn, r=[f"NB{j}" for j in range(4)] + ["ident"], w=[bname])
            dst = Wst[:, 0, sg, :, :]
            src = bank[:, 0:512].rearrange("p (j c) -> p j c", j=4)
            A(lambda e, dst=dst, src=src: e.copy(dst, src), [bname], ["Wst"])
        P.op("sp", lambda e: e.dma_start(out=wscr[SL_S5W + k], in_=Wst[:, 0, :, :, :].rearrange("p a b c -> p (a b c)")),
             r=["Wst"], w=[f"scr{SL_S5W + k}"], dsem=f"d_w{k}")
    def gen_V(m):
        lr, li = bc16(L[m][0][0]), bc16(L[m][1][0])
        lrn, lin = L[m][0][1], L[m][1][1]
        vb = Vm[:, m % 2, :, :, :]
        on = f"Vm{m % 2}"
        V(lambda e: e.tensor_tensor(c1, cre, lr, ALU.mult), ["bcc", lrn], ["c1"])
        V(lambda e: e.tensor_tensor(c2, cim, li, ALU.mult), ["bcc", lin], ["c2"])
        V(lambda e: e.tensor_tensor(c3, cre, li, ALU.mult), ["bcc", lin], ["c3"])
        V(lambda e: e.tensor_tensor(c4, cim, lr, ALU.mult), ["bcc", lrn], ["c4"])
        V(lambda e: e.tensor_tensor(c1, c1, c2, ALU.subtract), ["c1", "c2"], ["c1"])
        V(lambda e: e.tensor_tensor(c3, c3, c4, ALU.add), ["c3", "c4"], ["c3"])
        for k in range(4):
            for half, cc, cn, mk in ((0, c1, "c1", mpos), (1, c3, "c3", mneg)):
                o = vb[:, k, half * 4:(half + 1) * 4, :].rearrange("p s (g h) -> p s g h", g=8)
                V(lambda e, o=o, cc=cc, mk=mk, k=k: e.tensor_tensor(o, cc[:, k * 4:(k + 1) * 4, :].unsqueeze(2).to_broadcast([128, 4, 8, 16]),
                                                                   mk.to_broadcast([128, 4, 8, 16]), ALU.mult), [cn, "mask8"], [on])
        if m >= 1:
            for k in range(4):
                dst = wscr[SL_S5V + k].rearrange("p (s m c) -> p s m c", s=8, m=4)[:, :, m - 1, :]
                P.op("sp", lambda e, k=k, dst=dst: e.dma_start(out=dst, in_=vb[:, k, :, :]),
                     r=[on], w=[f"scr{SL_S5V + k}"], dsem=f"d_v{k}_{m}")
        if m <= 3:
            for k in range(4):
                pairs = [(NB[:, 3, sg // 4, k * 4 + sg % 4, :], vb[:, k, sg, :]) for sg in range(8)]
                mm_group(psb[k][:, m * 128:(m + 1) * 128], pairs, ["NB3", on], [f"ps{k}"])
    for m in range(5):
        gen_V(m)
        if m < 4:
            w_section(m)
    for k in range(4):
        V(lambda e, k=k: e.scalar_tensor_tensor(Kst[:, k, 0, :], identf[:, :], cp(C_DS5 + k), psb[k][:, 0:128],
                                                op0=ALU.mult, op1=ALU.add), [f"ps{k}", "identf", "colp"], ["Kst"])
        V(lambda e, k=k: e.tensor_copy(Kst[:, k, 1:4, :], psb[k][:, 128:512].rearrange("p (j c) -> p j c", j=3)),
          [f"ps{k}"], ["Kst"])
    P.op("sp", lambda e: e.dma_start(out=wscr[SL_KG][:, 0:2048], in_=hT[:, 2:4, :].rearrange("p a b -> p (a b)")),
         r=["Kst"], w=[f"scr{SL_KG}"], dsem="d_kst")

    for hi, sl in enumerate(HOST_SLABS):
        if sl >= SL_WK:
            continue
        if sl == SL_KG:
            P.op("pool", lambda e, hi=hi, sl=sl: e.dma_start(out=wscr[sl][:, 2048:4096], in_=w32_d[hi][:, 2048:4096]),
                 w=[f"scr{sl}g"], dsem=f"d_cast{hi}")
        else:
            P.op("pool", lambda e, hi=hi, sl=sl: e.dma_start(out=wscr[sl], in_=w32_d[hi]),
                 w=[f"scr{sl}"], dsem=f"d_cast{hi}")

    if dbg == "pro":
        pass
    P.barrier(skip="d_cast")
    pst.close()

    xres[1] = sb("xres1", [128, 4, D])
    xlh = sb("xlh", [128, 4, 3 + TB], BF16)
    ffnhalo = sb("ffnhalo", [128, 2, 44, 2])
    G(lambda e: e.memset(xlh[:, :, 0:3], 0.0), [], [f"xl{i}" for i in range(4)])
    G(lambda e: e.memset(ffnhalo[:, :, :, :], 0.0), [], ["ffnhalo0", "ffnhalo1"])
    gl = sb("gl", [128, 4, TB], BF16)
    zq = sb("zq", [128, 4, 512])
    S_sb = sb("S_sb", [128, 4, 8, 130], BF16)
    ymix = sb("ymix", [128, 8, TB], BF16)
    ltb = sb("lt", [128, 6, TB + 4])
    lt = ltb[:, :, 0:TB]
    xcb = sb("xcb", [128, 2, TB], BF16)
    hlb = sb("hl", [128, 2, TB + 4])
    hl = hlb[:, :, 0:TB]
    gated = sb("gated", [128, 22, TB], BF16)
    q_sb = ymix
    pT = gated[:, 0:8, :].rearrange("p (h m) t -> p h m t", h=4)
    o_sb = gated[:, 8:16, :]
    u_sb = gated[:, 16:20, :]
    ygelu = gated[:, 0:4, :]
    rden = hl
    zbuf = ltb[:, 0:2, 0:TB + 2]
    cv = lt[:, 2:4, :]
    cg = lt[:, 4:6, :]
    gg = cg
    qinit = sb("qinit", [128, 3, 16])
    hc = sb("hc", [128, 44, 2])
    hc2 = sb("hc2", [128, 44])
    G(lambda e: e.memset(S_sb[:, :, :, :], 0.0), [], ["S_sb0", "S_sb1", "S_sb2", "S_sb3"])

    rr = {"ps": 0, "ps6": 0}

    def nextbank():
        b = rr["ps"] % 4
        rr["ps"] += 1
        return b

    def nextbank6():
        b = rr["ps6"] % 6
        rr["ps6"] += 1
        return b

    hnames = [f"hfm{kc}" for kc in range(8)]

    def add_resid(xb, a, nh, bi):
        xs = xres[xb][:, a, nh * 512:(nh + 1) * 512]
        V(lambda e: e.tensor_tensor(xs, xs, psb[bi][:, :], ALU.add), [f"ps{bi}", f"x{xb}a{a}"], [f"x{xb}a{a}"])

    def proj_tok(lhs_tile, lhs_names, slab_ids, xb, bankfn):
        for nh in range(2):
            st, sn = load_slab(slab_ids[nh])
            for a in range(4):
                bi = bankfn()
                pairs = [(lhs_tile[:, kc, a * 128:(a + 1) * 128], st[:, kc * 512:(kc + 1) * 512]) for kc in range(8)]
                mm_group(psb[bi][:, :], pairs, [sn] + lhs_names, [f"ps{bi}"])
                add_resid(xb, a, nh, bi)

    def fm_proj(st, sn, mt, evac, bankfn=None):
        bi = (bankfn or nextbank)()
        pairs = [(st[:, kc * 512 + mt * 128: kc * 512 + (mt + 1) * 128], hfm[:, kc, :]) for kc in range(8)]
        mm_group(psb[bi][:, :], pairs, [sn] + hnames, [f"ps{bi}"])
        evac(bi)

    u4 = u_sb[:, :, :].rearrange("p k (c j) -> p k c j", j=4)

    def v4(ap):
        return ap.rearrange("p (s c) -> p s c", s=4)

    def s5_qinit():
        q0, q1, q2 = qinit[:, 0, :], qinit[:, 1, :], qinit[:, 2, :]
        cr, ci = s5carry_r[:, :], s5carry_i[:, :]
        er, ei, r4 = E4r[:, :], E4i[:, :], R4[:, :]
        V(lambda e: e.tensor_tensor(q0, cr, er, ALU.mult), ["s5c", "E4"], ["qi0"])
        V(lambda e: e.tensor_tensor(q2, ci, ei, ALU.mult), ["s5c", "E4"], ["qi2"])
        V(lambda e: e.tensor_tensor(q1, cr, ei, ALU.mult), ["s5c", "E4"], ["qi1"])
        V(lambda e: e.tensor_tensor(q0, q0, q2, ALU.subtract), ["qi0", "qi2"], ["qi0"])
        V(lambda e: e.tensor_tensor(q2, ci, er, ALU.mult), ["s5c", "E4", "qi0"], ["qi2"])
        V(lambda e: e.tensor_tensor(q0, q0, r4, ALU.mult), ["qi0", "R4"], ["qi0"])
        V(lambda e: e.tensor_tensor(q1, q1, q2, ALU.add), ["qi1", "qi2"], ["qi1"])
        V(lambda e: e.tensor_tensor(q1, q1, r4, ALU.mult), ["qi1", "R4"], ["qi1"])

    def s5_B_pe(k):
        st, sn = load_slab(SL_S5W + k)
        for half, bi in ((0, 4), (1, 5)):
            for s_ in range(4):
                sg = half * 4 + s_
                pairs = [(st[:, (sg * 4 + j) * 128:(sg * 4 + j + 1) * 128], u4[:, k, :, j]) for j in range(4)]
                mm_group(psb[bi][:, s_ * 128:(s_ + 1) * 128], pairs, [sn, f"gated{16 + k}"], [f"ps{bi}"])

    def s5_B_steps(k):
        cT = cosT[:, k * 4:(k + 1) * 4, :]
        sT = sinT[:, k * 4:(k + 1) * 4, :]
        Xr = v4(psb[4][:, :])
        Xi = v4(psb[5][:, :])
        Zr, Zi, Qr, Qi = (zq[:, i, :] for i in range(4))
        tmp = lt[:, 0, :]
        ks = slice(k * 4, (k + 1) * 4)
        q0, q1 = qinit[:, 0, ks], qinit[:, 1, ks]
        cr, ci = s5carry_r[:, ks], s5carry_i[:, ks]
        Rk = Rtab[:, k * 4:(k + 1) * 4, :].rearrange("p s c -> p (s c)")
        Sre = S_sb[:, k, 0:4, 1:129]
        Sim = S_sb[:, k, 4:8, 1:129]
        sname = f"S_sb{k}"
        ta, tb_ = lt[:, 2, :], lt[:, 3, :]
        TT = lambda o, a, b, op, r, w: (lambda: V(lambda e: e.tensor_tensor(o, a, b, op), r, w))
        return [
            TT(v4(Zr), Xr, cT, ALU.mult, ["ps4", "tab"], ["zq0"]),
            TT(v4(tmp), Xi, sT, ALU.mult, ["ps5", "tab"], ["lt0"]),
            TT(Zr, Zr, tmp, ALU.add, ["zq0", "lt0"], ["zq0"]),
            TT(v4(Zi), Xi, cT, ALU.mult, ["ps5", "tab"], ["zq1"]),
            TT(v4(tmp), Xr, sT, ALU.mult, ["ps4", "tab"], ["lt0"]),
            TT(Zi, Zi, tmp, ALU.subtract, ["zq1", "lt0"], ["zq1"]),
            TT(v4(Zr)[:, :, 0], v4(Zr)[:, :, 0], q0, ALU.add, ["zq0", "qi0"], ["zq0"]),
            TT(v4(Zi)[:, :, 0], v4(Zi)[:, :, 0], q1, ALU.add, ["zq1", "qi1"], ["zq1"]),
            lambda: V(lambda e: e.tensor_tensor_scan(Qr, Rk, Zr, 0.0, op0=ALU.mult, op1=ALU.add), ["zq0", "Rtab"], ["zq2"]),
            lambda: V(lambda e: e.tensor_tensor_scan(Qi, Rk, Zi, 0.0, op0=ALU.mult, op1=ALU.add), ["zq1", "Rtab"], ["zq3"]),
            lambda: V(lambda e: e.tensor_copy(S_sb[:, k, :, 0:1], S_sb[:, k, :, 128:129]), [sname], [sname]),
            TT(v4(ta), v4(Qr), cT, ALU.mult, ["zq2", "tab"], ["lt2"]),
            TT(v4(tb_), v4(Qi), sT, ALU.mult, ["zq3", "tab"], ["lt3"]),
            TT(Sre, v4(ta), v4(tb_), ALU.subtract, ["lt2", "lt3"], [sname]),
            TT(cr, v4(ta)[:, :, 127], v4(tb_)[:, :, 127], ALU.subtract, ["lt2", "lt3"], ["s5c"]),
            TT(v4(ta), v4(Qr), sT, ALU.mult, ["zq2", "tab"], ["lt2"]),
            TT(v4(tb_), v4(Qi), cT, ALU.mult, ["zq3", "tab"], ["lt3"]),
            TT(Sim, v4(ta), v4(tb_), ALU.add, ["lt2", "lt3"], [sname]),
            TT(ci, v4(ta)[:, :, 127], v4(tb_)[:, :, 127], ALU.add, ["lt2", "lt3"], ["s5c"]),
        ]

    def lru_steps(mt):
        xl = xlh[:, mt, :]
        xn = f"xl{mt}"
        (xc, xcn), (ra, ran), (i_, in_) = [(lt[:, r, :], f"lt{r}") for r in (4, 5, 1)]
        m_, mn = xc, xcn
        xb_ = xcb[:, 0, :]
        xbn = "xcb0"
        hb = hl[:, mt % 2, :]
        hn = f"hl{mt % 2}"
        bk = {}
        st = []
        def conv_mm():
            bk["bc"] = nextbank()
            mm_group(psb[bk["bc"]][:, :], [(lcd[:, t_ * 4 + mt, :], xl[:, t_:t_ + TB]) for t_ in range(4)], ["lcd", xn], [f"ps{bk['bc']}"])
        st.append(conv_mm)
        st.append(lambda: A(lambda e: e.activation(xc, psb[bk["bc"]][:, :], AF.Identity, bias=cp(C_LCB + mt)),
                            [f"ps{bk['bc']}", "colp"], [xcn]))
        st.append(lambda: G(lambda e: e.tensor_copy(xl[:, 0:3], xl[:, TB:TB + 3]), [xn], [xn]))
        st.append(lambda: A(lambda e: e.copy(xb_, xc), [xcn], [xbn]))

        def gates():
            bk["b1"], bk["b2"] = nextbank(), nextbank()
            mm_group(psb[bk["b1"]][:, :], [(lruw[:, mt, :], xb_)], ["lruw", xbn], [f"ps{bk['b1']}"])
            mm_group(psb[bk["b2"]][:, :], [(lruw[:, 4 + mt, :], xb_)], ["lruw", xbn], [f"ps{bk['b2']}"])
        st.append(gates)
        st.append(lambda: A(lambda e: e.activation(ra, psb[bk["b1"]][:, :], AF.Tanh, scale=0.5, bias=hbias[:, mt:mt + 1]),
                            [f"ps{bk['b1']}", "hbias"], [ran]))
        st.append(lambda: A(lambda e: e.activation(i_, psb[bk["b2"]][:, :], AF.Tanh, scale=0.5, bias=hbias[:, 4 + mt:5 + mt]),
                            [f"ps{bk['b2']}", "hbias"], [in_]))
        st.append(lambda: V(lambda e: e.scalar_tensor_tensor(i_, i_, 1.0, xc, op0=ALU.add, op1=ALU.mult), [in_, xcn], [in_]))
        st.append(lambda: A(lambda e: e.activation(ra, ra, AF.Exp, scale=cnegh[:, mt:mt + 1], bias=cnegh[:, mt:mt + 1]), [ran, "cnegh"], [ran]))
        st.append(lambda: A(lambda e: e.activation(m_, ra, AF.Square), [ran, in_], [mn]))
        st.append(lambda: A(lambda e: e.activation(m_, m_, AF.Sqrt, scale=-1.0, bias=onec[:, :]), [mn, "onec"], [mn]))
        st.append(lambda: V(lambda e: e.scalar_tensor_tensor(i_, i_, 0.5, m_, op0=ALU.mult, op1=ALU.mult), [in_, mn], [in_]))
        st.append(lambda: V(lambda e: e.tensor_tensor_scan(hb, ra, i_, lrucarry[:, mt:mt + 1], op0=ALU.mult, op1=ALU.add),
                            [ran, in_, "lrucarry"], [hn]))
        st.append(lambda: V(lambda e: e.tensor_copy(lrucarry[:, mt:mt + 1], hb[:, TB - 1:TB]), [hn], ["lrucarry"]))
        prod = lambda: V(lambda e: e.tensor_tensor(ymix[:, 4 + mt, :], hb, gl[:, mt, :], ALU.mult), [hn, f"gl{mt}"], [f"ymix{4 + mt}"])
        return st, prod

    def zip_steps(a, b):
        for i in range(max(len(a), len(b))):
            if i < len(a):
                a[i]()
            if i < len(b):
                b[i]()

    def s5_D(k, stK, snK, pre=None):
        st, sn = pre if pre is not None else load_slab(SL_S5V + k)
        bi = nextbank()
        yv = psb[bi][:, :].rearrange("p (c i) -> p c i", i=4)
        for i in range(4):
            pairs = [(st[:, (sg * 4 + i) * 128:(sg * 4 + i + 1) * 128], S_sb[:, k, sg, 0:128]) for sg in range(8)]
            pairs += [(stK[:, (k * 4 + (i - j)) * 128:(k * 4 + (i - j) + 1) * 128], u4[:, k, :, j]) for j in range(i + 1)]
            mm_group(yv[:, :, i], pairs, [sn, snK, f"S_sb{k}", f"gated{16 + k}"], [f"ps{bi}"])
        A(lambda e: e.activation(ygelu[:, k, :], psb[bi][:, :], AF.Gelu), [f"ps{bi}"], [f"gated{k}"])

    def glu_tile(mt, stK, snK):
        bi = nextbank()
        pairs = [(stK[:, 2048 + kc * 512 + mt * 128: 2048 + kc * 512 + (mt + 1) * 128], ygelu[:, kc, :]) for kc in range(4)]
        mm_group(psb[bi][:, :], pairs, [snK] + [f"gated{k}" for k in range(4)], [f"ps{bi}"])
        gt = lt[:, 2 + mt % 2, :]
        gn = f"lt{2 + mt % 2}"
        A(lambda e: e.activation(gt, psb[bi][:, :], AF.Sigmoid, bias=cp(C_BGLU + mt)), [f"ps{bi}", "colp"], [gn])
        V(lambda e: e.tensor_tensor(ymix[:, mt, :], ygelu[:, mt, :], gt, ALU.mult), [f"gated{mt}", gn], [f"ymix{mt}"])

    def attn_head(hd):
        def sc(mc):
            bi = nextbank6()
            pairs = [(Kfm[:, hd * 2 + c2, mc * 128:(mc + 1) * 128], q_sb[:, hd * 2 + c2, :]) for c2 in range(2)]
            mm_group(psb[bi][:, :], pairs, ["Kfm", f"ymix{hd * 2}", f"ymix{hd * 2 + 1}"], [f"ps{bi}"])
            def expfn(e):
                return e.activation(pT[:, hd, mc, :], psb[bi][:, :], AF.Exp)
            A(expfn, [f"ps{bi}"], [f"gated{2 * hd}", f"gated{2 * hd + 1}"])
        sc(0)
        sc(1)

    def attn_tail(hd):
        bd = nextbank6()
        mm_group(psb[bd][:, :], [(onesb[:, :], pT[:, hd, mc, :]) for mc in range(2)], ["onesb", f"gated{2 * hd}", f"gated{2 * hd + 1}"], [f"ps{bd}"])
        rd = rden[:, hd % 2, :]
        rn = f"hl{hd % 2}"
        A(lambda e: e.activation(rd, psb[bd][:, :], AF.Ln), [f"ps{bd}"], [rn])
        A(lambda e: e.activation(rd, rd, AF.Exp, scale=-1.0), [rn], [rn])

        def pv(j):
            bi = nextbank6()
            pairs = [(Vtok[:, mc, hd * 256 + j * 128: hd * 256 + (j + 1) * 128], pT[:, hd, mc, :]) for mc in range(2)]
            mm_group(psb[bi][:, :], pairs, ["Vtok", f"gated{2 * hd}", f"gated{2 * hd + 1}"], [f"ps{bi}"])
            V(lambda e: e.tensor_tensor(o_sb[:, hd * 2 + j, :], psb[bi][:, :], rd, ALU.mult), [f"ps{bi}", rn], [f"gated{8 + hd * 2 + j}"])
        pv(0)
        pv(1)

    def ffn_tile_mm(st, sn, sa, tt, isg, tb):
        vt = 2 * sa + tt
        ch = vt + 22 * isg
        col = isg * 256 + tt * 128
        bi = nextbank6()
        pairs = [(st[:, kc * 512 + col: kc * 512 + col + 128], hfm[:, kc, :]) for kc in range(8)]
        mm_group(psb[bi][:, :], pairs, [sn] + hnames, [f"ps{bi}"])
        ps = psb[bi]
        pn = f"ps{bi}"
        dst = (cv if isg == 0 else cg)[:, tt, :]
        dn = f"lt{2 + 2 * isg + tt}"
        hold = ffnhalo[:, tb % 2, ch, :]
        hnew = ffnhalo[:, (tb + 1) % 2, ch, :]
        wcol = lambda t_: cp(C_FCW + t_ * 44 + ch)
        A(lambda e: e.activation(dst, ps[:, :], AF.Identity, scale=wcol(2), bias=cp(C_FCB + ch)), [pn, "colp"], [dn])
        A(lambda e: e.copy(hnew, ps[:, TB - 2:TB]), [pn], [f"ffnhalo{(tb + 1) % 2}"])
        return ps, pn, dst, dn, wcol, hold, f"ffnhalo{tb % 2}"

    def ffn_tap(t_, ps, pn, dst, dn, wcol, hold, hn):
        sh = 2 - t_
        V(lambda e: e.scalar_tensor_tensor(dst[:, sh:TB], ps[:, 0:TB - sh], wcol(t_), dst[:, sh:TB], op0=ALU.mult, op1=ALU.add),
          [pn, dn, "colp"], [dn])

    def ffn_halo_prep(tb):
        hold = ffnhalo[:, tb % 2, :, :]
        hn = f"ffnhalo{tb % 2}"
        W0, W1 = cp(C_FCW, 44), cp(C_FCW + 44, 44)
        V(lambda e: e.tensor_tensor(hc[:, :, 1], hold[:, :, 1], W0, ALU.mult), [hn, "colp"], ["hc"])
        V(lambda e: e.tensor_tensor(hc[:, :, 0], hold[:, :, 0], W0, ALU.mult), [hn, "colp"], ["hc"])
        V(lambda e: e.tensor_tensor(hc2[:, :], hold[:, :, 1], W1, ALU.mult), [hn, "colp"], ["hc2"])
        V(lambda e: e.tensor_tensor(hc[:, :, 0], hc[:, :, 0], hc2[:, :], ALU.add), ["hc", "hc2"], ["hc"])

    def ffn_halo_add(ch, dst, dn):
        G(lambda e: e.tensor_tensor(dst[:, 0:2], dst[:, 0:2], hc[:, ch, :], ALU.add), ["hc", dn], [dn])

    def ffn_pair(st, sn, sa, tt, tb, prev_tail):
        vt = 2 * sa + tt
        tiles = [ffn_tile_mm(st, sn, sa, tt, 0, tb), ffn_tile_mm(st, sn, sa, tt, 1, tb)]
        if prev_tail is not None:
            prev_tail()
        for t_ in (1, 0):
            for tl in tiles:
                ffn_tap(t_, *tl)
        for isg, tl in enumerate(tiles):
            ffn_halo_add(vt + 22 * isg, tl[2], tl[3])

        def tail():
            A(lambda e: e.activation(cg[:, tt, :], cg[:, tt, :], AF.Gelu), [f"lt{4 + tt}"], [f"lt{4 + tt}"])
            G(lambda e: e.tensor_tensor(gated[:, vt, :], cg[:, tt, :], cv[:, tt, :], ALU.mult), [f"lt{4 + tt}", f"lt{2 + tt}"], [f"gated{vt}"])
        return tail

    def down_group(xb, nh, sg3, kc0, nk, a, bi):
        st, sn = down_group.cur

        def fn(e):
            ins = None
            for kk in range(nk):
                ins = e.matmul(psb[bi][:, :], gated[:, kc0 + kk, a * 128:(a + 1) * 128], st[:, kk * 512:(kk + 1) * 512],
                               start=(sg3 == 0 and kk == 0), stop=(sg3 == 2 and kk == nk - 1))
            return ins
        P.op("pe", fn, r=[sn] + [f"gated{kc0 + kk}" for kk in range(nk)], w=[f"ps{bi}"])

    def final_norm(xb, a):
        xa = xres[xb][:, a, :]
        sa_, ra_ = ss[:, 4 + a:5 + a], rstd[:, 4 + a:5 + a]
        xn = f"x{xb}a{a}"
        A(lambda e: e.activation(junk[:, :], xa, AF.Square, accum_out=sa_), [xn], ["junk", f"ssf{a}"])
        A(lambda e: e.activation(ra_, sa_, AF.Sqrt, scale=1.0 / D, bias=epsc[:, :]), [f"ssf{a}", "epsc"], [f"rstdf{a}"])
        V(lambda e: e.reciprocal(ra_, ra_), [f"rstdf{a}"], [f"rstdf{a}"])
        V(lambda e: e.scalar_tensor_tensor(xa, xa, ra_, gfin[:, :], op0=ALU.mult, op1=ALU.mult), [xn, f"rstdf{a}", "gfin"], [xn])

    x_t = x_d.rearrange("(b a p) d -> b p a d", a=4, p=128)
    out_t = out_d.rearrange("(b a p) d -> b p a d", a=4, p=128)

    def load_x(tb):
        xb = tb % 2
        P.op("sp", lambda e: e.dma_start(out=xres[xb][:, :, :], in_=x_t[tb]),
             w=[f"x{xb}a{a}" for a in range(4)], dsem=f"d_x{xb}")

    def store_out(tb):
        xb = tb % 2
        P.op("sp", lambda e: e.dma_start(out=out_t[tb], in_=xres[xb][:, :, :]),
             r=[f"x{xb}a{a}" for a in range(4)], dsem=f"d_o{xb}")

    def win_evac(grp, mt):
        def ev(bi):
            if grp == 0:
                A(lambda e: e.copy(u_sb[:, mt, :], psb[bi][:, :]), [f"ps{bi}"], [f"gated{16 + mt}"])
            elif grp == 1:
                A(lambda e: e.copy(xlh[:, mt, 3:3 + TB], psb[bi][:, :]), [f"ps{bi}"], [f"xl{mt}"])
            else:
                A(lambda e: e.activation(gl[:, mt, :], psb[bi][:, :], AF.Gelu), [f"ps{bi}"], [f"gl{mt}"])
        return ev

    def q_evac(half, mt):
        def ev(bi):
            A(lambda e: e.activation(q_sb[:, half * 4 + mt, :], psb[bi][:, :], AF.Copy, scale=0.0625),
              [f"ps{bi}"], [f"ymix{half * 4 + mt}"])
        return ev

    def finish_block(tb):
        if stage >= 6:
            for a in range(4):
                final_norm(tb % 2, a)

    def block_body(tb):
        xb = tb % 2
        if tb == 0:
            norm_stats(xres[xb], f"x{xb}a", 4)
        norm_transposes(4, C_G1, hfm, "hfm", TB)
        st0, sn0 = load_slab(SL_WIN)
        for mt in range(4):
            fm_proj(st0, sn0, mt, win_evac(0, mt))
        s5_qinit()
        wslabs = {}

        def win_tiles(grp, mts):
            if grp not in wslabs:
                wslabs[grp] = load_slab(SL_WIN + grp)
            st, sn = wslabs[grp]
            for mt in mts:
                fm_proj(st, sn, mt, win_evac(grp, mt))
        s5_B_pe(0)
        zip_steps(s5_B_steps(0), [])
        win_tiles(1, [0, 1])
        prods = {}
        for k in range(1, 4):
            s5_B_pe(k)
            lst, prods[k - 1] = lru_steps(k - 1)
            zip_steps(s5_B_steps(k), lst)
            if k == 1:
                win_tiles(1, [2, 3])
            elif k == 2:
                win_tiles(2, [0, 1])
                prods[0]()
                prods[1]()
            else:
                win_tiles(2, [2, 3])
        if stage < 2:
            return
        v0pre = load_slab(SL_S5V)
        stK, snK = load_slab(SL_KG)
        lst, prods[3] = lru_steps(3)
        cuts = [0, 4, 9, 13, len(lst)]
        for k in range(4):
            s5_D(k, stK, snK, v0pre if k == 0 else None)
            for f in lst[cuts[k]:cuts[k + 1]]:
                f()
        prods[2]()
        prods[3]()
        for mt in range(4):
            glu_tile(mt, stK, snK)
        if stage < 3:
            return
        proj_tok(ymix, [f"ymix{i}" for i in range(8)], [SL_WOUT, SL_WOUT + 1], xb, nextbank6)
        if stage < 4:
            return
        rmsnorm_to_hT(xres[xb], f"x{xb}a", 4, C_G2, hfm, "hfm", TB)
        for half in range(2):
            st, sn = load_slab(SL_WQ + half)
            for mt in range(4):
                fm_proj(st, sn, mt, q_evac(half, mt), nextbank6)
        attn_head(0)
        for hd in range(4):
            if hd + 1 < 4:
                attn_head(hd + 1)
            attn_tail(hd)
        if tb > 0:
            finish_block(tb - 1)
        proj_tok(o_sb, [f"gated{8 + i}" for i in range(8)], [SL_WO, SL_WO + 1], xb, nextbank6)
        if tb > 0:
            store_out(tb - 1)
        if stage < 5:
            return
        rmsnorm_to_hT(xres[xb], f"x{xb}a", 4, C_G3, hfm, "hfm", TB)
        if tb + 1 < nblk:
            load_x(tb + 1)
        ffn_halo_prep(tb)
        ptail = None
        for sa in range(11):
            st, sn = load_slab(SL_UP + sa)
            for tt in range(2):
                ptail = ffn_pair(st, sn, sa, tt, tb, ptail)
        ptail()
        if tb + 1 < nblk and stage >= 6:
            norm_stats(xres[1 - xb], f"x{1 - xb}a", 4)
        for nh in range(2):
            kc0 = 0
            dbanks = [nextbank6() for _ in range(4)]
            for sg3 in range(3):
                nk = 8 if sg3 < 2 else 6
                down_group.cur = load_slab(SL_DN + nh * 3 + sg3)
                for a in range(4):
                    down_group(xb, nh, sg3, kc0, nk, a, dbanks[a])
                kc0 += nk
            for a in range(4):
                add_resid(xb, a, nh, dbanks[a])

    load_x(0)
    for tb in range(nblk):
        block_body(tb)
    finish_block(nblk - 1)
    store_out(nblk - 1)

    if dbg is not None:
        P.barrier()
        for i_, (ap_, a_, b_) in enumerate(dbg(locals())):
            P.op("pool", lambda e, i_=i_, ap_=ap_, a_=a_, b_=b_: e.dma_start(
                out=dbg_d[i_][:, 0:a_ * b_].rearrange("p (a b) -> p a b", a=a_), in_=ap_), dsem=f"d_dbg{i_}")

    P.barrier()
    with nc.Block() as block:
        P.emit(block)
    es.close()
    return nc


def _slab_kc(w, c0, ncols=512, kc0=0, nkc=8):
    out = np.zeros((128, 8, ncols), np.float32)
    K = w.shape[0]
    for kc in range(nkc):
        r0 = (kc0 + kc) * 128
        if r0 >= K:
            break
        out[:, kc, :] = w[r0:r0 + 128, c0:c0 + ncols]
    return out.reshape(128, 8 * ncols)


def host_layout(inp):
    f = lambda k: np.asarray(inp[k], np.float32)
    w_in, w_out = f("w_in")[0], f("w_out")[0]
    wq, wk, wv, wo = f("xa_w_q")[0], f("xa_w_k")[0], f("xa_w_v")[0], f("xa_w_o")[0]
    wup, wdn, glu = f("ffn_w_up")[0], f("ffn_w_down")[0], f("s5_w_glu")[0]
    slabs = {}
    for g in range(3):
        slabs[SL_WIN + g] = _slab_kc(w_in, g * 512)
    kg = np.zeros((128, 4096), np.float32)
    kg[:, 2048:] = _slab_kc(glu, 0, 512, 0, 4)[:, :2048]
    slabs[SL_KG] = kg
    for h in range(2):
        slabs[SL_WOUT + h] = _slab_kc(w_out, h * 512)
        slabs[SL_WQ + h] = _slab_kc(wq, h * 512)
        slabs[SL_WO + h] = _slab_kc(wo, h * 512)
        slabs[SL_WK + h] = _slab_kc(wk, h * 512)
        slabs[SL_WV + h] = _slab_kc(wv, h * 512)
    for sa in range(11):
        cols = np.concatenate([np.arange(sa * 256, sa * 256 + 256), 2816 + np.arange(sa * 256, sa * 256 + 256)])
        slabs[SL_UP + sa] = _slab_kc(wup[:, cols], 0)
    for nh in range(2):
        for g3 in range(3):
            slabs[SL_DN + nh * 3 + g3] = _slab_kc(wdn, nh * 512, 512, g3 * 8, 8 if g3 < 2 else 6)
    w32 = np.stack([slabs[s] for s in HOST_SLABS]).astype(np.float32)

    colp = np.zeros((128, NCOL), np.float32)

    def putcols(c0, vec):
        v = np.asarray(vec, np.float32).reshape(-1, 128).T
        colp[:, c0:c0 + v.shape[1]] = v
    putcols(C_G1, f("ln_mix_g")[0])
    putcols(C_G2, f("ln_xa_g")[0])
    putcols(C_G3, f("ln_ffn_g")[0])
    putcols(C_GMEM, f("mem_norm_g"))
    putcols(C_BGLU, f("s5_b_glu")[0])
    lcw = f("lru_conv_w")[0]
    for t in range(4):
        putcols(C_LCW + t * 4, lcw[t])
    putcols(C_LCB, f("lru_conv_b")[0])
    putcols(C_BA, f("lru_b_a")[0].reshape(-1))
    putcols(C_BX, f("lru_b_x")[0].reshape(-1))
    putcols(C_LAM, f("lru_lam")[0].reshape(-1))
    fcw = f("ffn_conv_w")[0]
    for t in range(3):
        putcols(C_FCW + t * 44, fcw[t])
    putcols(C_FCB, f("ffn_conv_b")[0])
    putcols(C_DS5, f("s5_d")[0].reshape(-1))
    gfin = np.ascontiguousarray(np.broadcast_to(f("final_norm_g")[None, :], (128, D)))

    def gq(arr):
        a = arr.reshape(4, 8, 4, 16)
        return np.ascontiguousarray(a.transpose(1, 3, 0, 2).reshape(128, 16))
    lre, lim = f("s5_lam_re")[0], f("s5_lam_im")[0]
    ldt = np.broadcast_to(f("s5_log_dt")[0][:, None], (32, 64))
    s5par = np.stack([gq(lre), gq(lim), gq(ldt)], axis=1).astype(np.float32)

    def cb(b):
        a = b.reshape(4, 8, 4, 16, 16)
        return np.ascontiguousarray(a.transpose(1, 3, 0, 2, 4).reshape(128, 16, 16))
    bcc = np.stack([cb(f("s5_b_re")[0]), cb(f("s5_b_im")[0]),
                    cb(np.ascontiguousarray(f("s5_c_re")[0].transpose(0, 2, 1))),
                    cb(np.ascontiguousarray(f("s5_c_im")[0].transpose(0, 2, 1)))], axis=1).astype(np.float32)
    mask8 = np.zeros((128, 2, 8), np.float32)
    for p_ in range(128):
        mask8[p_, 0, p_ // 16] = 1.0
        mask8[p_, 1, p_ // 16] = -1.0
    lruw = np.zeros((128, 8, 128), np.float32)
    wa, wx = f("lru_w_a")[0], f("lru_w_x")[0]
    for mt in range(4):
        for hh in range(2):
            lruw[hh * 64:(hh + 1) * 64, mt, hh * 64:(hh + 1) * 64] = wa[2 * mt + hh]
            lruw[hh * 64:(hh + 1) * 64, 4 + mt, hh * 64:(hh + 1) * 64] = wx[2 * mt + hh]
    shared = {"w32": w32, "colp": colp, "gfin": gfin, "s5par": s5par, "bcc": bcc, "mask8": mask8, "lruw": lruw,
              "ident": np.eye(128, dtype=np.float32).astype(ml_dtypes.bfloat16),
              "identf": np.eye(128, dtype=np.float32)}
    return shared


def kernel(**inputs):
    shared = host_layout(inputs)
    x = np.asarray(inputs["x"], np.float32)
    mem = np.asarray(inputs["mem"], np.float32)
    nc = build(SEQ // TB)
    in_maps = []
    for c in range(8):
        m = dict(shared)
        m["x"] = np.ascontiguousarray(x[c])
        m["mem"] = np.ascontiguousarray(mem[c])
        in_maps.append(m)
    res = run_bass_kernel_spmd(nc, in_maps, core_ids=list(range(8)))
    return np.stack([np.asarray(r["out"], np.float32) for r in res.results], axis=0)
```

```python
import math
from contextlib import ExitStack
import numpy as np
import ml_dtypes
import concourse.bass as bass
import concourse.mybir as mybir
from concourse.bass_utils import run_bass_kernel_spmd

F32 = mybir.dt.float32
BF16 = mybir.dt.bfloat16
ALU = mybir.AluOpType
AF = mybir.ActivationFunctionType

SEQ = 4096
TB = 512
D = 1024
NS = 5
PI = math.pi

SL_WIN = 0
SL_S5W = 3
SL_S5V = 7
SL_KG = 11
SL_WOUT = 12
SL_WQ = 14
SL_WO = 16
SL_UP = 18
SL_DN = 29
SL_WK = 35
SL_WV = 37
NSLAB = 39
HOST_SLABS = [0, 1, 2, 11, 12, 13, 14, 15, 16, 17] + list(range(18, 35)) + [35, 36, 37, 38]

C_G1, C_G2, C_G3, C_GMEM = 0, 8, 16, 24
C_BGLU = 32
C_LCW = 36
C_LCB = 52
C_BA = 56
C_BX = 60
C_LAM = 64
C_FCW = 68
C_FCB = 200
C_DS5 = 244
NCOL = 248


class Prog:
    ENG = ("pe", "act", "dve", "pool", "sp")

    def __init__(self, nc, es):
        self.nc = nc
        self.es = es
        self.q = {e: [] for e in self.ENG}
        self.cnt = {}
        self.sems = {}
        self.seen = {e: {} for e in self.ENG}
        self.lastw = {}
        self.readers = {}

    def sem(self, key):
        if key not in self.sems:
            self.sems[key] = self.es.enter_context(self.nc.semaphore("s_" + key))
            self.cnt[key] = 0
        return self.sems[key]

    def op(self, eng, fn, r=(), w=(), dsem=None):
        deps = {}

        def add(tok):
            if tok is None:
                return
            k, v = tok
            if deps.get(k, 0) < v:
                deps[k] = v
        for b in r:
            add(self.lastw.get(b))
        for b in w:
            add(self.lastw.get(b))
            for k, v in self.readers.get(b, {}).items():
                add((k, v))
        waits = []
        for k, v in deps.items():
            if eng == "pe" and k == "pe":
                continue
            if self.seen[eng].get(k, 0) >= v:
                continue
            self.seen[eng][k] = v
            waits.append((k, v))
        if dsem is None:
            key, inc = eng, 1
        else:
            key, inc = dsem, 16
        self.sem(key)
        self.cnt[key] += inc
        tok = (key, self.cnt[key])
        self.q[eng].append((waits, fn, key, inc))
        for b in r:
            self.readers.setdefault(b, {})[key] = tok[1]
        for b in w:
            self.lastw[b] = tok
            self.readers[b] = {}
        return tok

    def barrier(self, skip=None):
        for e in self.ENG:
            waits = []
            for k, v in self.cnt.items():
                if skip is not None and k.startswith(skip):
                    continue
                if v > 0 and self.seen[e].get(k, 0) < v:
                    self.seen[e][k] = v
                    waits.append((k, v))
            if waits:
                self.q[e].append((waits, None, None, 0))

    def emit(self, block):
        sems = self.sems

        def run(eng, lst):
            for waits, fn, key, inc in lst:
                for k, v in waits:
                    eng.wait_ge(sems[k], v)
                if fn is not None:
                    ins = fn(eng)
                    ins.then_inc(sems[key], inc)

        @block.tensor
        def _(e):
            run(e, self.q["pe"])

        @block.scalar
        def _(e):
            run(e, self.q["act"])

        @block.vector
        def _(e):
            run(e, self.q["dve"])

        @block.gpsimd
        def _(e):
            run(e, self.q["pool"])

        @block.sync
        def _(e):
            run(e, self.q["sp"])


def build(nblk=8, stage=6, dbg=None, pro_only=False):
    nc = bass.Bass("TRN2", target_bir_lowering=False)
    ntok = nblk * TB

    def din(name, shape, dt=F32):
        return nc.dram_tensor(name, list(shape), dt, kind="ExternalInput").ap()
    x_d = din("x", [ntok, D])
    mem_d = din("mem", [256, D])
    w32_d = din("w32", [len(HOST_SLABS), 128, 4096])
    colp_d = din("colp", [128, NCOL])
    gfin_d = din("gfin", [128, D])
    s5par_d = din("s5par", [128, 3, 16])
    bcc_d = din("bcc", [128, 4, 16, 16])
    mask8_d = din("mask8", [128, 2, 8])
    lruw_d = din("lruw", [128, 8, 128])
    ident_d = din("ident", [128, 128], BF16)
    identf_d = din("identf", [128, 128])
    out_d = nc.dram_tensor("out", [ntok, D], F32, kind="ExternalOutput").ap()
    wscr = nc.dram_tensor("wscr", [NSLAB, 128, 4096], BF16, kind="Internal").ap()
    dbg_d = None
    if dbg is not None:
        dbg_d = nc.dram_tensor("dbg", [16, 128, 4096], F32, kind="ExternalOutput").ap()

    es = ExitStack()
    P = Prog(nc, es)

    def sb(name, shape, dt=F32, stack=es):
        return stack.enter_context(nc.sbuf_tensor("sb_" + name, list(shape), dt))

    colp = sb("colp", [128, NCOL])
    gfin = sb("gfin", [128, D])
    ident = sb("ident", [128, 128], BF16)
    onesb = sb("onesb", [128, 128], BF16)
    lruw = sb("lruw", [128, 8, 128], BF16)
    lcd = sb("lcd", [128, 16, 128], BF16)
    cneg = sb("cneg", [128, 4])
    epsc = sb("epsc", [128, 1])
    hpic = sb("hpic", [128, 1])
    onec = sb("onec", [128, 1])
    hbias = sb("hbias", [128, 12])
    cnegh = sb("cnegh", [128, 4])
    cosT = sb("cosT", [128, 16, 128])
    sinT = sb("sinT", [128, 16, 128])
    Rtab = sb("Rtab", [128, 16, 128])
    R4 = sb("R4", [128, 16])
    E4r = sb("E4r", [128, 16])
    E4i = sb("E4i", [128, 16])
    Kfm = sb("Kfm", [128, 8, 256], BF16)
    Vtok = sb("Vtok", [128, 2, D], BF16)
    slots = [sb(f"slot{i}", [128, 4096], BF16) for i in range(NS)]
    xres = [sb("xres0", [128, 4, D]), None]
    hT = sb("hT", [128, 4, D], BF16)
    hfm = sb("hfm", [128, 8, TB], BF16)
    ss = sb("ss", [128, 8])
    rstd = sb("rstd", [128, 8])
    junk = sb("junk", [128, D], BF16)
    s5carry_r = sb("s5cr", [128, 16])
    s5carry_i = sb("s5ci", [128, 16])
    lrucarry = sb("lrucarry", [128, 4])
    psb = [es.enter_context(nc.psum_tensor(f"pp{i}", [128, 512], F32)) for i in range(6)]
    psT = [es.enter_context(nc.psum_tensor(f"ppT{i}", [128, 1024], BF16)) for i in range(2)]

    cp = lambda c, n=1: colp[:, c:c + n]

    slab_state = {"n": 0}

    def load_slab(idx, eng="sp"):
        i = slab_state["n"] % NS
        slab_state["n"] += 1
        name = f"slot{i}"
        P.op(eng, lambda e, i=i, idx=idx: e.dma_start(out=slots[i][:, :], in_=wscr[idx]),
             r=[f"scr{idx}", f"scr{idx}g"], w=[name], dsem=f"d_slot{i}")
        return slots[i], name

    def mm_group(out_ap, pairs, r, w):
        n = len(pairs)

        def fn(e):
            ins = None
            for j, (l, rr) in enumerate(pairs):
                ins = e.matmul(out_ap, l, rr, start=(j == 0), stop=(j == n - 1))
            return ins
        P.op("pe", fn, r=r, w=w)

    def V(fn, r, w):
        P.op("dve", fn, r=r, w=w)

    def A(fn, r, w):
        P.op("act", fn, r=r, w=w)

    def G(fn, r, w):
        P.op("pool", fn, r=r, w=w)

    pst = ExitStack()
    par = sb("par", [128, 3, 16], stack=pst)
    identf = sb("identf", [128, 128], stack=pst)
    bcc = sb("bcc", [128, 4, 16, 16], stack=pst)
    mask8 = sb("mask8", [128, 2, 8], stack=pst)
    cs = sb("cs", [128, 4, 16, 16], stack=pst)
    t1 = sb("t1", [128, 16, 128], stack=pst)
    lruw32 = t1[:, 0:8, :]
    t2 = sb("t2", [128, 16, 128], stack=pst)
    NB = sb("NB", [128, 4, 2, 16, 128], BF16, stack=pst)
    Vm = sb("Vm", [128, 2, 4, 8, 128], BF16, stack=pst)
    Wst = sb("Wst", [128, 1, 8, 4, 128], BF16, stack=pst)
    Kst = hT[:, 2:4, :].rearrange("p a (b c) -> p (a b) c", c=128).rearrange("p (k t) c -> p k t c", k=4)
    sm = xres[0][:, 2:4, :].rearrange("p a (b c) -> p (a b) c", c=16)
    memx = xres[0]
    memfm = hfm

    def ld(dst, src, name, eng="sp"):
        P.op(eng, lambda e: e.dma_start(out=dst, in_=src), w=[name], dsem="d_" + name)
    ld(colp[:, :], colp_d, "colp")
    ld(gfin[:, :], gfin_d, "gfin")
    ld(ident[:, :], ident_d, "ident")
    ld(identf[:, :], identf_d, "identf")
    ld(par[:, :, :], s5par_d, "par")
    ld(bcc[:, :, :, :], bcc_d, "bcc")
    ld(mask8[:, :, :], mask8_d, "mask8")
    ld(lruw32, lruw_d, "t1")
    P.op("sp", lambda e: e.dma_start(out=memx[:, 0:2, :], in_=mem_d.rearrange("(a p) d -> p a d", p=128)),
         w=["x0a0", "x0a1"], dsem="d_x0")

    G(lambda e: e.memset(onesb[:, :], 1.0), [], ["onesb"])
    G(lambda e: e.memset(epsc[:, :], 1e-6), [], ["epsc"])
    G(lambda e: e.memset(hpic[:, :], PI / 2), [], ["hpic"])
    G(lambda e: e.memset(onec[:, :], 1.0), [], ["onec"])
    G(lambda e: e.memset(lrucarry[:, :], 0.0), [], ["lrucarry"])
    G(lambda e: e.memset(s5carry_r[:, :], 0.0), [], ["s5c"])
    G(lambda e: e.memset(s5carry_i[:, :], 0.0), [], ["s5c"])
    V(lambda e: e.tensor_copy(lruw[:, :, :], lruw32), ["t1"], ["lruw"])
    for t_ in range(4):
        V(lambda e, t_=t_: e.tensor_tensor(lcd[:, t_ * 4:(t_ + 1) * 4, :], identf[:, :].unsqueeze(1).to_broadcast([128, 4, 128]),
                                            cp(C_LCW + t_ * 4, 4).unsqueeze(2).to_broadcast([128, 4, 128]), ALU.mult),
          ["identf", "colp"], ["lcd"])

    A(lambda e: e.activation(cneg[:, :], cp(C_LAM, 4), AF.Exp, scale=-1.0), ["colp"], ["cneg"])
    A(lambda e: e.activation(cneg[:, :], cneg[:, :], AF.Ln, bias=onec[:, :]), ["cneg", "onec"], ["cneg"])
    V(lambda e: e.tensor_scalar_mul(cneg[:, :], cneg[:, :], -8.0), ["cneg"], ["cneg"])
    V(lambda e: e.tensor_scalar_mul(cnegh[:, :], cneg[:, :], 0.5), ["cneg"], ["cnegh"])
    V(lambda e: e.tensor_scalar_mul(hbias[:, 0:4], cp(C_BA, 4), 0.5), ["colp"], ["hbias"])
    V(lambda e: e.tensor_scalar_mul(hbias[:, 4:8], cp(C_BX, 4), 0.5), ["colp"], ["hbias"])
    V(lambda e: e.tensor_scalar_mul(hbias[:, 8:12], cp(C_BGLU, 4), 0.5), ["colp"], ["hbias"])

    smn = {"i": 0}

    def S(name=None):
        i = smn["i"]
        smn["i"] += 1
        return sm[:, i, :], f"sm{i}"

    def vtt(o, a, b, op):
        (oa, on), (aa, an), (ba, bn) = o, a, b
        V(lambda e: e.tensor_tensor(oa, aa, ba, op), [an, bn], [on])

    def cmul(a_r, a_i, b_r, b_i):
        o_r, o_i, u1, u2 = S(), S(), S(), S()
        vtt(u1, a_r, b_r, ALU.mult)
        vtt(u2, a_i, b_i, ALU.mult)
        vtt(o_r, u1, u2, ALU.subtract)
        vtt(u1, a_r, b_i, ALU.mult)
        vtt(u2, a_i, b_r, ALU.mult)
        vtt(o_i, u1, u2, ALU.add)
        return o_r, o_i

    lre = (par[:, 0, :], "par")
    lim = (par[:, 1, :], "par")
    ldt = (par[:, 2, :], "par")
    dt_ = S()
    A(lambda e: e.activation(dt_[0], ldt[0], AF.Exp), ["par"], [dt_[1]])
    zr, zi = S(), S()
    vtt(zr, lre, dt_, ALU.mult)
    vtt(zi, lim, dt_, ALU.mult)
    mag = S()
    A(lambda e: e.activation(mag[0], zr[0], AF.Exp), [zr[1]], [mag[1]])

    sn0, cs0 = S(), S()
    A(lambda e: e.activation(sn0[0], zi[0], AF.Sin, scale=1.0 / 16), [zi[1]], [sn0[1]])
    A(lambda e: e.activation(cs0[0], zi[0], AF.Sin, scale=1.0 / 16, bias=hpic[:, :]), [zi[1], "hpic"], [cs0[1]])
    sn1, cs1 = sn0, cs0
    for _ in range(4):
        cs1, sn1 = cmul(cs1, sn1, cs1, sn1)
    L = [None] * 5
    L1r, L1i = S(), S()
    vtt(L1r, mag, cs1, ALU.mult)
    vtt(L1i, mag, sn1, ALU.mult)
    L[1] = (L1r, L1i)
    L[2] = cmul(L1r, L1i, L1r, L1i)
    L[3] = cmul(L[2][0], L[2][1], L1r, L1i)
    L[4] = cmul(L[2][0], L[2][1], L[2][0], L[2][1])
    one_, zero_ = S(), S()
    V(lambda e: e.memset(one_[0], 1.0), [], [one_[1]])
    V(lambda e: e.memset(zero_[0], 0.0), [], [zero_[1]])
    L[0] = (one_, zero_)
    am1 = S()
    V(lambda e: e.tensor_scalar_add(am1[0], L1r[0], -1.0), [L1r[1]], [am1[1]])
    nli = S()
    V(lambda e: e.tensor_scalar_mul(nli[0], lim[0], -1.0), ["par"], [nli[1]])
    num_r, num_i = cmul(am1, L1i, lre, nli)
    den, u3 = S(), S()
    vtt(den, lre, lre, ALU.mult)
    vtt(u3, lim, lim, ALU.mult)
    vtt(den, den, u3, ALU.add)
    kr, ki = S(), S()
    V(lambda e: e.reciprocal(den[0], den[0]), [den[1]], [den[1]])
    vtt(kr, num_r, den, ALU.mult)
    vtt(ki, num_i, den, ALU.mult)
    M = [cmul(L[3 - j][0], L[3 - j][1], kr, ki) for j in range(4)]
    r2 = S()
    vtt(r2, L[4][0], L[4][0], ALU.mult)
    vtt(u3, L[4][1], L[4][1], ALU.mult)
    vtt(r2, r2, u3, ALU.add)
    A(lambda e: e.activation(R4[:, :], r2[0], AF.Sqrt), [r2[1]], ["R4"])
    ir4 = S()
    V(lambda e: e.reciprocal(ir4[0], R4[:, :]), ["R4"], [ir4[1]])
    V(lambda e: e.tensor_tensor(E4r[:, :], L[4][0][0], ir4[0], ALU.mult), [L[4][0][1], ir4[1]], ["E4"])
    V(lambda e: e.tensor_tensor(E4i[:, :], L[4][1][0], ir4[0], ALU.mult), [L[4][1][1], ir4[1]], ["E4"])
    V(lambda e: e.memset(cosT[:, :, 0:1], 1.0), [], ["tab"])
    V(lambda e: e.memset(sinT[:, :, 0:1], 0.0), [], ["tab"])
    wr, wi = (E4r[:, :], "E4"), (E4i[:, :], "E4")
    n = 1
    while n < 128:
        wrb = wr[0].unsqueeze(2).to_broadcast([128, 16, n])
        wib = wi[0].unsqueeze(2).to_broadcast([128, 16, n])
        c0, s0 = cosT[:, :, 0:n], sinT[:, :, 0:n]
        c1, s1 = cosT[:, :, n:2 * n], sinT[:, :, n:2 * n]
        ta, tb_ = t1[:, :, 0:n], t2[:, :, 0:n]
        V(lambda e, c0=c0, wrb=wrb, ta=ta: e.tensor_tensor(ta, c0, wrb, ALU.mult), ["tab", wr[1]], ["t1"])
        V(lambda e, s0=s0, wib=wib, tb_=tb_: e.tensor_tensor(tb_, s0, wib, ALU.mult), ["tab", wi[1]], ["t2"])
        V(lambda e, c1=c1, ta=ta, tb_=tb_: e.tensor_tensor(c1, ta, tb_, ALU.subtract), ["t1", "t2"], ["tab"])
        V(lambda e, c0=c0, wib=wib, ta=ta: e.tensor_tensor(ta, c0, wib, ALU.mult), ["tab", wi[1]], ["t1"])
        V(lambda e, s0=s0, wrb=wrb, tb_=tb_: e.tensor_tensor(tb_, s0, wrb, ALU.mult), ["tab", wr[1]], ["t2"])
        V(lambda e, s1=s1, ta=ta, tb_=tb_: e.tensor_tensor(s1, ta, tb_, ALU.add), ["t1", "t2"], ["tab"])
        if n < 64:
            wr, wi = cmul(wr, wi, wr, wi)
        n *= 2
    V(lambda e: e.tensor_copy(Rtab[:, :, :], R4[:, :].unsqueeze(2).to_broadcast([128, 16, 128])), ["R4"], ["Rtab"])
    V(lambda e: e.memset(Rtab[:, :, 0:1], 0.0), ["Rtab"], ["Rtab"])

    def norm_stats(src, srcname, nsub):
        for a in range(nsub):
            if a % 2 == 0:
                A(lambda e, a=a: e.activation(junk[:, :], src[:, a, :], AF.Square, accum_out=ss[:, a:a + 1]),
                  [f"{srcname}{a}"], ["junk", f"ss{a}"])
            else:
                V(lambda e, a=a: e.scalar_tensor_tensor(hT[:, a, :], src[:, a, :], 1.0, src[:, a, :], op0=ALU.mult, op1=ALU.mult,
                                                        accum_out=ss[:, a:a + 1]), [f"{srcname}{a}"], [f"hT{a}", f"ss{a}"])
            A(lambda e, a=a: e.activation(rstd[:, a:a + 1], ss[:, a:a + 1], AF.Sqrt, scale=1.0 / D, bias=epsc[:, :]),
              [f"ss{a}", "epsc"], [f"rstd{a}"])
            V(lambda e, a=a: e.reciprocal(rstd[:, a:a + 1], rstd[:, a:a + 1]), [f"rstd{a}"], [f"rstd{a}"])
            if a % 2 == 0:
                A(lambda e, a=a: e.activation(hT[:, a, :], src[:, a, :], AF.Copy, scale=rstd[:, a:a + 1]),
                  [f"{srcname}{a}", f"rstd{a}"], [f"hT{a}"])
            else:
                V(lambda e, a=a: e.tensor_scalar_mul(hT[:, a, :], src[:, a, :], rstd[:, a:a + 1]),
                  [f"{srcname}{a}", f"rstd{a}"], [f"hT{a}"])

    def norm_transposes(nsub, col_g, dst_fm, dstname, ncols_tok):
        for kc in range(8):
            bi = kc % 2
            bank = psT[bi]

            def fn(e, kc=kc, bank=bank):
                ins = None
                for a in range(nsub):
                    ins = e.transpose(bank[:, a * 128:(a + 1) * 128], hT[:, a, kc * 128:(kc + 1) * 128], ident[:, :])
                return ins
            P.op("pe", fn, r=[f"hT{a}" for a in range(nsub)] + ["ident"], w=[f"psT{bi}"])
            if kc % 2 == 0:
                A(lambda e, kc=kc, bank=bank: e.activation(dst_fm[:, kc, 0:ncols_tok], bank[:, 0:ncols_tok], AF.Copy, scale=cp(col_g + kc)),
                  [f"psT{bi}", "colp"], [f"{dstname}{kc}"])
            else:
                V(lambda e, kc=kc, bank=bank: e.tensor_scalar_mul(dst_fm[:, kc, 0:ncols_tok], bank[:, 0:ncols_tok], cp(col_g + kc)),
                  [f"psT{bi}", "colp"], [f"{dstname}{kc}"])


    def rmsnorm_to_hT(src, srcname, nsub, col_g, dst_fm, dstname, ncols_tok):
        norm_stats(src, srcname, nsub)
        norm_transposes(nsub, col_g, dst_fm, dstname, ncols_tok)

    rmsnorm_to_hT(memx, "x0a", 2, C_GMEM, memfm, "hfm", 256)
    for half in range(2):
        P.op("pool", lambda e, half=half: e.dma_start(out=slots[half][:, :], in_=w32_d[HOST_SLABS.index(SL_WK + half)]),
             w=[f"slot{half}"], dsem=f"d_slot{half}")
        for mt in range(4):
            bi = 2 + mt % 2
            pairs = [(slots[half][:, kc * 512 + mt * 128: kc * 512 + (mt + 1) * 128], memfm[:, kc, 0:256]) for kc in range(8)]
            mm_group(psb[bi][:, 0:256], pairs, [f"slot{half}"] + [f"hfm{kc}" for kc in range(8)], [f"ps{bi}"])
            A(lambda e, half=half, mt=mt, bi=bi: e.copy(Kfm[:, half * 4 + mt, :], psb[bi][:, 0:256]), [f"ps{bi}"], ["Kfm"])
    for half in range(2):
        P.op("pool", lambda e, half=half: e.dma_start(out=slots[2 + half][:, :], in_=w32_d[HOST_SLABS.index(SL_WV + half)]),
             w=[f"slot{2 + half}"], dsem=f"d_slot{2 + half}")
        for mc in range(2):
            bi = 4 + mc
            pairs = [(memfm[:, kc, mc * 128:(mc + 1) * 128], slots[2 + half][:, kc * 512:(kc + 1) * 512]) for kc in range(8)]
            mm_group(psb[bi][:, :], pairs, [f"slot{2 + half}"] + [f"hfm{kc}" for kc in range(8)], [f"ps{bi}"])
            V(lambda e, half=half, mc=mc, bi=bi: e.tensor_copy(Vtok[:, mc, half * 512:(half + 1) * 512], psb[bi][:, :]),
              [f"ps{bi}"], ["Vtok"])

    bre, bim, cre, cim = (bcc[:, i_, :, :] for i_ in range(4))
    c1, c2, c3, c4 = (cs[:, i_, :, :] for i_ in range(4))
    mpos = mask8[:, 0, :].unsqueeze(1).unsqueeze(3)
    mneg = mask8[:, 1, :].unsqueeze(1).unsqueeze(3)

    def bc16(ap):
        return ap.unsqueeze(2).to_broadcast([128, 16, 16])
    for j in range(4):
        mr, mi = bc16(M[j][0][0]), bc16(M[j][1][0])
        mrn, min_ = M[j][0][1], M[j][1][1]
        V(lambda e, mr=mr: e.tensor_tensor(c1, bre, mr, ALU.mult), ["bcc", mrn], ["c1"])
        V(lambda e, mi=mi: e.tensor_tensor(c2, bim, mi, ALU.mult), ["bcc", min_], ["c2"])
        V(lambda e, mi=mi: e.tensor_tensor(c3, bre, mi, ALU.mult), ["bcc", min_], ["c3"])
        V(lambda e, mr=mr: e.tensor_tensor(c4, bim, mr, ALU.mult), ["bcc", mrn], ["c4"])
        V(lambda e: e.tensor_tensor(c1, c1, c2, ALU.subtract), ["c1", "c2"], ["c1"])
        V(lambda e: e.tensor_tensor(c3, c3, c4, ALU.add), ["c3", "c4"], ["c3"])
        for ri, cc, cn in ((0, c1, "c1"), (1, c3, "c3")):
            o = NB[:, j, ri, :, :].rearrange("p a (g h) -> p a g h", g=8)
            V(lambda e, o=o, cc=cc: e.tensor_tensor(o, cc.unsqueeze(2).to_broadcast([128, 16, 8, 16]),
                                                    mpos.to_broadcast([128, 16, 8, 16]), ALU.mult), [cn, "mask8"], [f"NB{j}"])
    tix = {"i": 0}

    def w_section(k):
        for sg in range(8):
            s_, ri = sg % 4, sg // 4
            bank = psT[tix["i"] % 2]
            bname = f"psT{tix['i'] % 2}"
            tix["i"] += 1

            def fn(e, s_=s_, ri=ri, bank=bank):
                ins = None
                for j in range(4):
                    ins = e.transpose(bank[:, j * 128:(j + 1) * 128], NB[:, j, ri, k * 4 + s_, :], ident[:, :])
                return ins
            P.op("pe", fn, r=[f"NB{j}" for j in range(4)] + ["ident"], w=[bname])
            dst = Wst[:, 0, sg, :, :]
            src = bank[:, 0:512].rearrange("p (j c) -> p j c", j=4)
            A(lambda e, dst=dst, src=src: e.copy(dst, src), [bname], ["Wst"])
        P.op("sp", lambda e: e.dma_start(out=wscr[SL_S5W + k], in_=Wst[:, 0, :, :, :].rearrange("p a b c -> p (a b c)")),
             r=["Wst"], w=[f"scr{SL_S5W + k}"], dsem=f"d_w{k}")
    def gen_V(m):
        lr, li = bc16(L[m][0][0]), bc16(L[m][1][0])
        lrn, lin = L[m][0][1], L[m][1][1]
        vb = Vm[:, m % 2, :, :, :]
        on = f"Vm{m % 2}"
        V(lambda e: e.tensor_tensor(c1, cre, lr, ALU.mult), ["bcc", lrn], ["c1"])
        V(lambda e: e.tensor_tensor(c2, cim, li, ALU.mult), ["bcc", lin], ["c2"])
        V(lambda e: e.tensor_tensor(c3, cre, li, ALU.mult), ["bcc", lin], ["c3"])
        V(lambda e: e.tensor_tensor(c4, cim, lr, ALU.mult), ["bcc", lrn], ["c4"])
        V(lambda e: e.tensor_tensor(c1, c1, c2, ALU.subtract), ["c1", "c2"], ["c1"])
        V(lambda e: e.tensor_tensor(c3, c3, c4, ALU.add), ["c3", "c4"], ["c3"])
        for k in range(4):
            for half, cc, cn, mk in ((0, c1, "c1", mpos), (1, c3, "c3", mneg)):
                o = vb[:, k, half * 4:(half + 1) * 4, :].rearrange("p s (g h) -> p s g h", g=8)
                V(lambda e, o=o, cc=cc, mk=mk, k=k: e.tensor_tensor(o, cc[:, k * 4:(k + 1) * 4, :].unsqueeze(2).to_broadcast([128, 4, 8, 16]),
                                                                   mk.to_broadcast([128, 4, 8, 16]), ALU.mult), [cn, "mask8"], [on])
        if m >= 1:
            for k in range(4):
                dst = wscr[SL_S5V + k].rearrange("p (s m c) -> p s m c", s=8, m=4)[:, :, m - 1, :]
                P.op("sp", lambda e, k=k, dst=dst: e.dma_start(out=dst, in_=vb[:, k, :, :]),
                     r=[on], w=[f"scr{SL_S5V + k}"], dsem=f"d_v{k}_{m}")
        if m <= 3:
            for k in range(4):
                pairs = [(NB[:, 3, sg // 4, k * 4 + sg % 4, :], vb[:, k, sg, :]) for sg in range(8)]
                mm_group(psb[k][:, m * 128:(m + 1) * 128], pairs, ["NB3", on], [f"ps{k}"])
    for m in range(5):
        gen_V(m)
        if m < 4:
            w_section(m)
    for k in range(4):
        V(lambda e, k=k: e.scalar_tensor_tensor(Kst[:, k, 0, :], identf[:, :], cp(C_DS5 + k), psb[k][:, 0:128],
                                                op0=ALU.mult, op1=ALU.add), [f"ps{k}", "identf", "colp"], ["Kst"])
        V(lambda e, k=k: e.tensor_copy(Kst[:, k, 1:4, :], psb[k][:, 128:512].rearrange("p (j c) -> p j c", j=3)),
          [f"ps{k}"], ["Kst"])
    P.op("sp", lambda e: e.dma_start(out=wscr[SL_KG][:, 0:2048], in_=hT[:, 2:4, :].rearrange("p a b -> p (a b)")),
         r=["Kst"], w=[f"scr{SL_KG}"], dsem="d_kst")

    for hi, sl in enumerate(HOST_SLABS):
        if sl >= SL_WK:
            continue
        if sl == SL_KG:
            P.op("pool", lambda e, hi=hi, sl=sl: e.dma_start(out=wscr[sl][:, 2048:4096], in_=w32_d[hi][:, 2048:4096]),
                 w=[f"scr{sl}g"], dsem=f"d_cast{hi}")
        else:
            P.op("pool", lambda e, hi=hi, sl=sl: e.dma_start(out=wscr[sl], in_=w32_d[hi]),
                 w=[f"scr{sl}"], dsem=f"d_cast{hi}")

    if dbg == "pro":
        pass
    P.barrier(skip="d_cast")
    pst.close()

    xres[1] = sb("xres1", [128, 4, D])
    xlh = sb("xlh", [128, 4, 3 + TB], BF16)
    ffnhalo = sb("ffnhalo", [128, 2, 44, 2])
    G(lambda e: e.memset(xlh[:, :, 0:3], 0.0), [], [f"xl{i}" for i in range(4)])
    G(lambda e: e.memset(ffnhalo[:, :, :, :], 0.0), [], ["ffnhalo0", "ffnhalo1"])
    gl = sb("gl", [128, 4, TB], BF16)
    zq = sb("zq", [128, 4, 512])
    S_sb = sb("S_sb", [128, 4, 8, 130], BF16)
    ymix = sb("ymix", [128, 8, TB], BF16)
    ltb = sb("lt", [128, 6, TB + 4])
    lt = ltb[:, :, 0:TB]
    xcb = sb("xcb", [128, 2, TB], BF16)
    hlb = sb("hl", [128, 2, TB + 4])
    hl = hlb[:, :, 0:TB]
    gated = sb("gated", [128, 22, TB], BF16)
    q_sb = ymix
    pT = gated[:, 0:8, :].rearrange("p (h m) t -> p h m t", h=4)
    o_sb = gated[:, 8:16, :]
    u_sb = gated[:, 16:20, :]
    ygelu = gated[:, 0:4, :]
    rden = hl
    zbuf = ltb[:, 0:2, 0:TB + 2]
    cv = lt[:, 2:4, :]
    cg = lt[:, 4:6, :]
    gg = cg
    qinit = sb("qinit", [128, 3, 16])
    hc = sb("hc", [128, 44, 2])
    hc2 = sb("hc2", [128, 44])
    G(lambda e: e.memset(S_sb[:, :, :, :], 0.0), [], ["S_sb0", "S_sb1", "S_sb2", "S_sb3"])

    rr = {"ps": 0, "ps6": 0}

    def nextbank():
        b = rr["ps"] % 4
        rr["ps"] += 1
        return b

    def nextbank6():
        b = rr["ps6"] % 6
        rr["ps6"] += 1
        return b

    hnames = [f"hfm{kc}" for kc in range(8)]

    def add_resid(xb, a, nh, bi):
        xs = xres[xb][:, a, nh * 512:(nh + 1) * 512]
        V(lambda e: e.tensor_tensor(xs, xs, psb[bi][:, :], ALU.add), [f"ps{bi}", f"x{xb}a{a}"], [f"x{xb}a{a}"])

    def proj_tok(lhs_tile, lhs_names, slab_ids, xb, bankfn):
        sts = [load_slab(slab_ids[0]), load_slab(slab_ids[1])]
        for a in range(4):
            for nh in range(2):
                st, sn = sts[nh]
                bi = bankfn()
                pairs = [(lhs_tile[:, kc, a * 128:(a + 1) * 128], st[:, kc * 512:(kc + 1) * 512]) for kc in range(8)]
                mm_group(psb[bi][:, :], pairs, [sn] + lhs_names, [f"ps{bi}"])
                add_resid(xb, a, nh, bi)

    def fm_proj(st, sn, mt, evac, bankfn=None):
        bi = (bankfn or nextbank)()
        pairs = [(st[:, kc * 512 + mt * 128: kc * 512 + (mt + 1) * 128], hfm[:, kc, :]) for kc in range(8)]
        mm_group(psb[bi][:, :], pairs, [sn] + hnames, [f"ps{bi}"])
        evac(bi)

    u4 = u_sb[:, :, :].rearrange("p k (c j) -> p k c j", j=4)

    def v4(ap):
        return ap.rearrange("p (s c) -> p s c", s=4)

    def s5_qinit():
        q0, q1, q2 = qinit[:, 0, :], qinit[:, 1, :], qinit[:, 2, :]
        cr, ci = s5carry_r[:, :], s5carry_i[:, :]
        er, ei, r4 = E4r[:, :], E4i[:, :], R4[:, :]
        V(lambda e: e.tensor_tensor(q0, cr, er, ALU.mult), ["s5c", "E4"], ["qi0"])
        V(lambda e: e.tensor_tensor(q2, ci, ei, ALU.mult), ["s5c", "E4"], ["qi2"])
        V(lambda e: e.tensor_tensor(q1, cr, ei, ALU.mult), ["s5c", "E4"], ["qi1"])
        V(lambda e: e.tensor_tensor(q0, q0, q2, ALU.subtract), ["qi0", "qi2"], ["qi0"])
        V(lambda e: e.tensor_tensor(q2, ci, er, ALU.mult), ["s5c", "E4", "qi0"], ["qi2"])
        V(lambda e: e.tensor_tensor(q0, q0, r4, ALU.mult), ["qi0", "R4"], ["qi0"])
        V(lambda e: e.tensor_tensor(q1, q1, q2, ALU.add), ["qi1", "qi2"], ["qi1"])
        V(lambda e: e.tensor_tensor(q1, q1, r4, ALU.mult), ["qi1", "R4"], ["qi1"])

    def s5_B_pe(k):
        st, sn = load_slab(SL_S5W + k)
        for half, bi in ((0, 4), (1, 5)):
            for s_ in range(4):
                sg = half * 4 + s_
                pairs = [(st[:, (sg * 4 + j) * 128:(sg * 4 + j + 1) * 128], u4[:, k, :, j]) for j in range(4)]
                mm_group(psb[bi][:, s_ * 128:(s_ + 1) * 128], pairs, [sn, f"gated{16 + k}"], [f"ps{bi}"])

    def s5_B_steps(k):
        cT = cosT[:, k * 4:(k + 1) * 4, :]
        sT = sinT[:, k * 4:(k + 1) * 4, :]
        Xr = v4(psb[4][:, :])
        Xi = v4(psb[5][:, :])
        Zr, Zi, Qr, Qi = (zq[:, i, :] for i in range(4))
        tmp = lt[:, 0, :]
        ks = slice(k * 4, (k + 1) * 4)
        q0, q1 = qinit[:, 0, ks], qinit[:, 1, ks]
        cr, ci = s5carry_r[:, ks], s5carry_i[:, ks]
        Rk = Rtab[:, k * 4:(k + 1) * 4, :].rearrange("p s c -> p (s c)")
        Sre = S_sb[:, k, 0:4, 1:129]
        Sim = S_sb[:, k, 4:8, 1:129]
        sname = f"S_sb{k}"
        ta, tb_ = lt[:, 2, :], lt[:, 3, :]
        TT = lambda o, a, b, op, r, w: (lambda: V(lambda e: e.tensor_tensor(o, a, b, op), r, w))
        return [
            TT(v4(Zr), Xr, cT, ALU.mult, ["ps4", "tab"], ["zq0"]),
            TT(v4(tmp), Xi, sT, ALU.mult, ["ps5", "tab"], ["lt0"]),
            TT(Zr, Zr, tmp, ALU.add, ["zq0", "lt0"], ["zq0"]),
            TT(v4(Zi), Xi, cT, ALU.mult, ["ps5", "tab"], ["zq1"]),
            TT(v4(tmp), Xr, sT, ALU.mult, ["ps4", "tab"], ["lt0"]),
            TT(Zi, Zi, tmp, ALU.subtract, ["zq1", "lt0"], ["zq1"]),
            TT(v4(Zr)[:, :, 0], v4(Zr)[:, :, 0], q0, ALU.add, ["zq0", "qi0"], ["zq0"]),
            TT(v4(Zi)[:, :, 0], v4(Zi)[:, :, 0], q1, ALU.add, ["zq1", "qi1"], ["zq1"]),
            lambda: V(lambda e: e.tensor_tensor_scan(Qr, Rk, Zr, 0.0, op0=ALU.mult, op1=ALU.add), ["zq0", "Rtab"], ["zq2"]),
            lambda: V(lambda e: e.tensor_tensor_scan(Qi, Rk, Zi, 0.0, op0=ALU.mult, op1=ALU.add), ["zq1", "Rtab"], ["zq3"]),
            lambda: V(lambda e: e.tensor_copy(S_sb[:, k, :, 0:1], S_sb[:, k, :, 128:129]), [sname], [sname]),
            TT(v4(ta), v4(Qr), cT, ALU.mult, ["zq2", "tab"], ["lt2"]),
            TT(v4(tb_), v4(Qi), sT, ALU.mult, ["zq3", "tab"], ["lt3"]),
            TT(Sre, v4(ta), v4(tb_), ALU.subtract, ["lt2", "lt3"], [sname]),
            TT(cr, v4(ta)[:, :, 127], v4(tb_)[:, :, 127], ALU.subtract, ["lt2", "lt3"], ["s5c"]),
            TT(v4(ta), v4(Qr), sT, ALU.mult, ["zq2", "tab"], ["lt2"]),
            TT(v4(tb_), v4(Qi), cT, ALU.mult, ["zq3", "tab"], ["lt3"]),
            TT(Sim, v4(ta), v4(tb_), ALU.add, ["lt2", "lt3"], [sname]),
            TT(ci, v4(ta)[:, :, 127], v4(tb_)[:, :, 127], ALU.add, ["lt2", "lt3"], ["s5c"]),
        ]

    def lru_steps(mt):
        xl = xlh[:, mt, :]
        xn = f"xl{mt}"
        (xc, xcn), (ra, ran), (i_, in_) = [(lt[:, r, :], f"lt{r}") for r in (4, 5, 1)]
        m_, mn = xc, xcn
        xb_ = xcb[:, 0, :]
        xbn = "xcb0"
        hb = hl[:, mt % 2, :]
        hn = f"hl{mt % 2}"
        bk = {}
        st = []
        def conv_mm():
            bk["bc"] = nextbank()
            mm_group(psb[bk["bc"]][:, :], [(lcd[:, t_ * 4 + mt, :], xl[:, t_:t_ + TB]) for t_ in range(4)], ["lcd", xn], [f"ps{bk['bc']}"])
        st.append(conv_mm)
        st.append(lambda: A(lambda e: e.activation(xc, psb[bk["bc"]][:, :], AF.Identity, bias=cp(C_LCB + mt)),
                            [f"ps{bk['bc']}", "colp"], [xcn]))
        st.append(lambda: G(lambda e: e.tensor_copy(xl[:, 0:3], xl[:, TB:TB + 3]), [xn], [xn]))
        st.append(lambda: A(lambda e: e.copy(xb_, xc), [xcn], [xbn]))

        def gates():
            bk["b1"], bk["b2"] = nextbank(), nextbank()
            mm_group(psb[bk["b1"]][:, :], [(lruw[:, mt, :], xb_)], ["lruw", xbn], [f"ps{bk['b1']}"])
            mm_group(psb[bk["b2"]][:, :], [(lruw[:, 4 + mt, :], xb_)], ["lruw", xbn], [f"ps{bk['b2']}"])
        st.append(gates)
        st.append(lambda: A(lambda e: e.activation(ra, psb[bk["b1"]][:, :], AF.Tanh, scale=0.5, bias=hbias[:, mt:mt + 1]),
                            [f"ps{bk['b1']}", "hbias"], [ran]))
        st.append(lambda: A(lambda e: e.activation(i_, psb[bk["b2"]][:, :], AF.Tanh, scale=0.5, bias=hbias[:, 4 + mt:5 + mt]),
                            [f"ps{bk['b2']}", "hbias"], [in_]))
        st.append(lambda: V(lambda e: e.scalar_tensor_tensor(i_, i_, 1.0, xc, op0=ALU.add, op1=ALU.mult), [in_, xcn], [in_]))
        st.append(lambda: A(lambda e: e.activation(ra, ra, AF.Exp, scale=cnegh[:, mt:mt + 1], bias=cnegh[:, mt:mt + 1]), [ran, "cnegh"], [ran]))
        st.append(lambda: A(lambda e: e.activation(m_, ra, AF.Square), [ran, in_], [mn]))
        st.append(lambda: A(lambda e: e.activation(m_, m_, AF.Sqrt, scale=-1.0, bias=onec[:, :]), [mn, "onec"], [mn]))
        st.append(lambda: V(lambda e: e.scalar_tensor_tensor(i_, i_, 0.5, m_, op0=ALU.mult, op1=ALU.mult), [in_, mn], [in_]))
        st.append(lambda: V(lambda e: e.tensor_tensor_scan(hb, ra, i_, lrucarry[:, mt:mt + 1], op0=ALU.mult, op1=ALU.add),
                            [ran, in_, "lrucarry"], [hn]))
        st.append(lambda: V(lambda e: e.tensor_copy(lrucarry[:, mt:mt + 1], hb[:, TB - 1:TB]), [hn], ["lrucarry"]))
        prod = lambda: V(lambda e: e.tensor_tensor(ymix[:, 4 + mt, :], hb, gl[:, mt, :], ALU.mult), [hn, f"gl{mt}"], [f"ymix{4 + mt}"])
        return st, prod

    def zip_steps(a, b):
        for i in range(max(len(a), len(b))):
            if i < len(a):
                a[i]()
            if i < len(b):
                b[i]()

    def s5_D(k, stK, snK, pre=None):
        st, sn = pre if pre is not None else load_slab(SL_S5V + k)
        bi = nextbank()
        yv = psb[bi][:, :].rearrange("p (c i) -> p c i", i=4)
        for i in range(4):
            pairs = [(st[:, (sg * 4 + i) * 128:(sg * 4 + i + 1) * 128], S_sb[:, k, sg, 0:128]) for sg in range(8)]
            pairs += [(stK[:, (k * 4 + (i - j)) * 128:(k * 4 + (i - j) + 1) * 128], u4[:, k, :, j]) for j in range(i + 1)]
            mm_group(yv[:, :, i], pairs, [sn, snK, f"S_sb{k}", f"gated{16 + k}"], [f"ps{bi}"])
        A(lambda e: e.activation(ygelu[:, k, :], psb[bi][:, :], AF.Gelu), [f"ps{bi}"], [f"gated{k}"])

    def glu_tile(mt, stK, snK):
        bi = nextbank()
        pairs = [(stK[:, 2048 + kc * 512 + mt * 128: 2048 + kc * 512 + (mt + 1) * 128], ygelu[:, kc, :]) for kc in range(4)]
        mm_group(psb[bi][:, :], pairs, [snK] + [f"gated{k}" for k in range(4)], [f"ps{bi}"])
        gt = lt[:, 2 + mt % 2, :]
        gn = f"lt{2 + mt % 2}"
        A(lambda e: e.activation(gt, psb[bi][:, :], AF.Sigmoid, bias=cp(C_BGLU + mt)), [f"ps{bi}", "colp"], [gn])
        V(lambda e: e.tensor_tensor(ymix[:, mt, :], ygelu[:, mt, :], gt, ALU.mult), [f"gated{mt}", gn], [f"ymix{mt}"])

    def attn_head(hd):
        def sc(mc):
            bi = nextbank6()
            pairs = [(Kfm[:, hd * 2 + c2, mc * 128:(mc + 1) * 128], q_sb[:, hd * 2 + c2, :]) for c2 in range(2)]
            mm_group(psb[bi][:, :], pairs, ["Kfm", f"ymix{hd * 2}", f"ymix{hd * 2 + 1}"], [f"ps{bi}"])
            def expfn(e):
                return e.activation(pT[:, hd, mc, :], psb[bi][:, :], AF.Exp)
            A(expfn, [f"ps{bi}"], [f"gated{2 * hd}", f"gated{2 * hd + 1}"])
        sc(0)
        sc(1)

    def attn_tail(hd):
        bd = nextbank6()
        mm_group(psb[bd][:, :], [(onesb[:, :], pT[:, hd, mc, :]) for mc in range(2)], ["onesb", f"gated{2 * hd}", f"gated{2 * hd + 1}"], [f"ps{bd}"])
        rd = rden[:, hd % 2, :]
        rn = f"hl{hd % 2}"
        A(lambda e: e.activation(rd, psb[bd][:, :], AF.Ln), [f"ps{bd}"], [rn])
        A(lambda e: e.activation(rd, rd, AF.Exp, scale=-1.0), [rn], [rn])

        def pv(j):
            bi = nextbank6()
            pairs = [(Vtok[:, mc, hd * 256 + j * 128: hd * 256 + (j + 1) * 128], pT[:, hd, mc, :]) for mc in range(2)]
            mm_group(psb[bi][:, :], pairs, ["Vtok", f"gated{2 * hd}", f"gated{2 * hd + 1}"], [f"ps{bi}"])
            V(lambda e: e.tensor_tensor(o_sb[:, hd * 2 + j, :], psb[bi][:, :], rd, ALU.mult), [f"ps{bi}", rn], [f"gated{8 + hd * 2 + j}"])
        pv(0)
        pv(1)

    def ffn_tile_mm(st, sn, sa, tt, isg, tb):
        vt = 2 * sa + tt
        ch = vt + 22 * isg
        col = isg * 256 + tt * 128
        bi = nextbank6()
        pairs = [(st[:, kc * 512 + col: kc * 512 + col + 128], hfm[:, kc, :]) for kc in range(8)]
        mm_group(psb[bi][:, :], pairs, [sn] + hnames, [f"ps{bi}"])
        ps = psb[bi]
        pn = f"ps{bi}"
        dst = (cv if isg == 0 else cg)[:, tt, :]
        dn = f"lt{2 + 2 * isg + tt}"
        hold = ffnhalo[:, tb % 2, ch, :]
        hnew = ffnhalo[:, (tb + 1) % 2, ch, :]
        wcol = lambda t_: cp(C_FCW + t_ * 44 + ch)
        A(lambda e: e.activation(dst, ps[:, :], AF.Identity, scale=wcol(2), bias=cp(C_FCB + ch)), [pn, "colp"], [dn])
        A(lambda e: e.copy(hnew, ps[:, TB - 2:TB]), [pn], [f"ffnhalo{(tb + 1) % 2}"])
        return ps, pn, dst, dn, wcol, hold, f"ffnhalo{tb % 2}"

    def ffn_tap(t_, ps, pn, dst, dn, wcol, hold, hn):
        sh = 2 - t_
        V(lambda e: e.scalar_tensor_tensor(dst[:, sh:TB], ps[:, 0:TB - sh], wcol(t_), dst[:, sh:TB], op0=ALU.mult, op1=ALU.add),
          [pn, dn, "colp"], [dn])

    def ffn_halo_prep(tb):
        hold = ffnhalo[:, tb % 2, :, :]
        hn = f"ffnhalo{tb % 2}"
        W0, W1 = cp(C_FCW, 44), cp(C_FCW + 44, 44)
        V(lambda e: e.tensor_tensor(hc[:, :, 1], hold[:, :, 1], W0, ALU.mult), [hn, "colp"], ["hc"])
        V(lambda e: e.tensor_tensor(hc[:, :, 0], hold[:, :, 0], W0, ALU.mult), [hn, "colp"], ["hc"])
        V(lambda e: e.tensor_tensor(hc2[:, :], hold[:, :, 1], W1, ALU.mult), [hn, "colp"], ["hc2"])
        V(lambda e: e.tensor_tensor(hc[:, :, 0], hc[:, :, 0], hc2[:, :], ALU.add), ["hc", "hc2"], ["hc"])

    def ffn_halo_add(ch, dst, dn):
        G(lambda e: e.tensor_tensor(dst[:, 0:2], dst[:, 0:2], hc[:, ch, :], ALU.add), ["hc", dn], [dn])

    def ffn_pair(st, sn, sa, tt, tb, prev_tail):
        vt = 2 * sa + tt
        tiles = [ffn_tile_mm(st, sn, sa, tt, 0, tb), ffn_tile_mm(st, sn, sa, tt, 1, tb)]
        if prev_tail is not None:
            prev_tail()
        for t_ in (1, 0):
            for tl in tiles:
                ffn_tap(t_, *tl)
        for isg, tl in enumerate(tiles):
            ffn_halo_add(vt + 22 * isg, tl[2], tl[3])

        def tail():
            A(lambda e: e.activation(cg[:, tt, :], cg[:, tt, :], AF.Gelu), [f"lt{4 + tt}"], [f"lt{4 + tt}"])
            G(lambda e: e.tensor_tensor(gated[:, vt, :], cg[:, tt, :], cv[:, tt, :], ALU.mult), [f"lt{4 + tt}", f"lt{2 + tt}"], [f"gated{vt}"])
        return tail

    def down_group(xb, nh, sg3, kc0, nk, a, bi):
        st, sn = down_group.cur

        def fn(e):
            ins = None
            for kk in range(nk):
                ins = e.matmul(psb[bi][:, :], gated[:, kc0 + kk, a * 128:(a + 1) * 128], st[:, kk * 512:(kk + 1) * 512],
                               start=(sg3 == 0 and kk == 0), stop=(sg3 == 2 and kk == nk - 1))
            return ins
        P.op("pe", fn, r=[sn] + [f"gated{kc0 + kk}" for kk in range(nk)], w=[f"ps{bi}"])

    def final_norm(xb, a):
        xa = xres[xb][:, a, :]
        sa_, ra_ = ss[:, 4 + a:5 + a], rstd[:, 4 + a:5 + a]
        xn = f"x{xb}a{a}"
        A(lambda e: e.activation(junk[:, :], xa, AF.Square, accum_out=sa_), [xn], ["junk", f"ssf{a}"])
        A(lambda e: e.activation(ra_, sa_, AF.Sqrt, scale=1.0 / D, bias=epsc[:, :]), [f"ssf{a}", "epsc"], [f"rstdf{a}"])
        V(lambda e: e.reciprocal(ra_, ra_), [f"rstdf{a}"], [f"rstdf{a}"])
        V(lambda e: e.scalar_tensor_tensor(xa, xa, ra_, gfin[:, :], op0=ALU.mult, op1=ALU.mult), [xn, f"rstdf{a}", "gfin"], [xn])

    x_t = x_d.rearrange("(b a p) d -> b p a d", a=4, p=128)
    out_t = out_d.rearrange("(b a p) d -> b p a d", a=4, p=128)

    def load_x(tb):
        xb = tb % 2
        P.op("sp", lambda e: e.dma_start(out=xres[xb][:, :, :], in_=x_t[tb]),
             w=[f"x{xb}a{a}" for a in range(4)], dsem=f"d_x{xb}")

    def store_out(tb):
        xb = tb % 2
        P.op("sp", lambda e: e.dma_start(out=out_t[tb], in_=xres[xb][:, :, :]),
             r=[f"x{xb}a{a}" for a in range(4)], dsem=f"d_o{xb}")

    def win_evac(grp, mt):
        def ev(bi):
            if grp == 0:
                A(lambda e: e.copy(u_sb[:, mt, :], psb[bi][:, :]), [f"ps{bi}"], [f"gated{16 + mt}"])
            elif grp == 1:
                A(lambda e: e.copy(xlh[:, mt, 3:3 + TB], psb[bi][:, :]), [f"ps{bi}"], [f"xl{mt}"])
            else:
                A(lambda e: e.activation(gl[:, mt, :], psb[bi][:, :], AF.Gelu), [f"ps{bi}"], [f"gl{mt}"])
        return ev

    def q_evac(half, mt):
        def ev(bi):
            if mt % 2 == 0:
                A(lambda e: e.activation(q_sb[:, half * 4 + mt, :], psb[bi][:, :], AF.Copy, scale=0.0625),
                  [f"ps{bi}"], [f"ymix{half * 4 + mt}"])
            else:
                V(lambda e: e.tensor_scalar_mul(q_sb[:, half * 4 + mt, :], psb[bi][:, :], 0.0625),
                  [f"ps{bi}"], [f"ymix{half * 4 + mt}"])
        return ev

    def finish_block(tb):
        if stage >= 6:
            for a in range(4):
                final_norm(tb % 2, a)

    def block_body(tb):
        xb = tb % 2
        if tb == 0:
            norm_stats(xres[xb], f"x{xb}a", 4)
        norm_transposes(4, C_G1, hfm, "hfm", TB)
        st0, sn0 = load_slab(SL_WIN)
        for mt in range(4):
            fm_proj(st0, sn0, mt, win_evac(0, mt))
        s5_qinit()
        wslabs = {}

        def win_tiles(grp, mts):
            if grp not in wslabs:
                wslabs[grp] = load_slab(SL_WIN + grp)
            st, sn = wslabs[grp]
            for mt in mts:
                fm_proj(st, sn, mt, win_evac(grp, mt))
        s5_B_pe(0)
        zip_steps(s5_B_steps(0), [])
        win_tiles(1, [0, 1])
        prods = {}
        for k in range(1, 4):
            s5_B_pe(k)
            lst, prods[k - 1] = lru_steps(k - 1)
            zip_steps(s5_B_steps(k), lst)
            if k == 1:
                win_tiles(1, [2, 3])
            elif k == 2:
                win_tiles(2, [0, 1])
                prods[0]()
                prods[1]()
            else:
                win_tiles(2, [2, 3])
        if stage < 2:
            return
        v0pre = load_slab(SL_S5V)
        stK, snK = load_slab(SL_KG)
        lst, prods[3] = lru_steps(3)
        cuts = [0, 4, 9, 13, len(lst)]
        for k in range(4):
            s5_D(k, stK, snK, v0pre if k == 0 else None)
            for f in lst[cuts[k]:cuts[k + 1]]:
                f()
        prods[2]()
        prods[3]()
        for mt in range(4):
            glu_tile(mt, stK, snK)
        if stage < 3:
            return
        proj_tok(ymix, [f"ymix{i}" for i in range(8)], [SL_WOUT, SL_WOUT + 1], xb, nextbank6)
        if stage < 4:
            return
        rmsnorm_to_hT(xres[xb], f"x{xb}a", 4, C_G2, hfm, "hfm", TB)
        for half in range(2):
            st, sn = load_slab(SL_WQ + half)
            for mt in range(4):
                fm_proj(st, sn, mt, q_evac(half, mt), nextbank6)
        attn_head(0)
        for hd in range(4):
            if hd + 1 < 4:
                attn_head(hd + 1)
            attn_tail(hd)
        if tb > 0:
            finish_block(tb - 1)
        proj_tok(o_sb, [f"gated{8 + i}" for i in range(8)], [SL_WO, SL_WO + 1], xb, nextbank6)
        if tb > 0:
            store_out(tb - 1)
        if stage < 5:
            return
        rmsnorm_to_hT(xres[xb], f"x{xb}a", 4, C_G3, hfm, "hfm", TB)
        if tb + 1 < nblk:
            load_x(tb + 1)
        ffn_halo_prep(tb)
        ptail = None
        for sa in range(11):
            st, sn = load_slab(SL_UP + sa)
            for tt in range(2):
                ptail = ffn_pair(st, sn, sa, tt, tb, ptail)
        ptail()
        if tb + 1 < nblk and stage >= 6:
            norm_stats(xres[1 - xb], f"x{1 - xb}a", 4)
        for nh in range(2):
            kc0 = 0
            dbanks = [nextbank6() for _ in range(4)]
            for sg3 in range(3):
                nk = 8 if sg3 < 2 else 6
                down_group.cur = load_slab(SL_DN + nh * 3 + sg3)
                for a in range(4):
                    down_group(xb, nh, sg3, kc0, nk, a, dbanks[a])
                kc0 += nk
            for a in range(4):
                add_resid(xb, a, nh, dbanks[a])

    load_x(0)
    for tb in range(nblk):
        block_body(tb)
    finish_block(nblk - 1)
    store_out(nblk - 1)

    if dbg is not None:
        P.barrier()
        for i_, (ap_, a_, b_) in enumerate(dbg(locals())):
            P.op("pool", lambda e, i_=i_, ap_=ap_, a_=a_, b_=b_: e.dma_start(
                out=dbg_d[i_][:, 0:a_ * b_].rearrange("p (a b) -> p a b", a=a_), in_=ap_), dsem=f"d_dbg{i_}")

    P.barrier()
    with nc.Block() as block:
        P.emit(block)
    es.close()
    return nc


def _slab_kc(w, c0, ncols=512, kc0=0, nkc=8):
    out = np.zeros((128, 8, ncols), np.float32)
    K = w.shape[0]
    for kc in range(nkc):
        r0 = (kc0 + kc) * 128
        if r0 >= K:
            break
        out[:, kc, :] = w[r0:r0 + 128, c0:c0 + ncols]
    return out.reshape(128, 8 * ncols)


def host_layout(inp):
    f = lambda k: np.asarray(inp[k], np.float32)
    w_in, w_out = f("w_in")[0], f("w_out")[0]
    wq, wk, wv, wo = f("xa_w_q")[0], f("xa_w_k")[0], f("xa_w_v")[0], f("xa_w_o")[0]
    wup, wdn, glu = f("ffn_w_up")[0], f("ffn_w_down")[0], f("s5_w_glu")[0]
    slabs = {}
    for g in range(3):
        slabs[SL_WIN + g] = _slab_kc(w_in, g * 512)
    kg = np.zeros((128, 4096), np.float32)
    kg[:, 2048:] = _slab_kc(glu, 0, 512, 0, 4)[:, :2048]
    slabs[SL_KG] = kg
    for h in range(2):
        slabs[SL_WOUT + h] = _slab_kc(w_out, h * 512)
        slabs[SL_WQ + h] = _slab_kc(wq, h * 512)
        slabs[SL_WO + h] = _slab_kc(wo, h * 512)
        slabs[SL_WK + h] = _slab_kc(wk, h * 512)
        slabs[SL_WV + h] = _slab_kc(wv, h * 512)
    for sa in range(11):
        cols = np.concatenate([np.arange(sa * 256, sa * 256 + 256), 2816 + np.arange(sa * 256, sa * 256 + 256)])
        slabs[SL_UP + sa] = _slab_kc(wup[:, cols], 0)
    for nh in range(2):
        for g3 in range(3):
            slabs[SL_DN + nh * 3 + g3] = _slab_kc(wdn, nh * 512, 512, g3 * 8, 8 if g3 < 2 else 6)
    w32 = np.stack([slabs[s] for s in HOST_SLABS]).astype(np.float32)

    colp = np.zeros((128, NCOL), np.float32)

    def putcols(c0, vec):
        v = np.asarray(vec, np.float32).reshape(-1, 128).T
        colp[:, c0:c0 + v.shape[1]] = v
    putcols(C_G1, f("ln_mix_g")[0])
    putcols(C_G2, f("ln_xa_g")[0])
    putcols(C_G3, f("ln_ffn_g")[0])
    putcols(C_GMEM, f("mem_norm_g"))
    putcols(C_BGLU, f("s5_b_glu")[0])
    lcw = f("lru_conv_w")[0]
    for t in range(4):
        putcols(C_LCW + t * 4, lcw[t])
    putcols(C_LCB, f("lru_conv_b")[0])
    putcols(C_BA, f("lru_b_a")[0].reshape(-1))
    putcols(C_BX, f("lru_b_x")[0].reshape(-1))
    putcols(C_LAM, f("lru_lam")[0].reshape(-1))
    fcw = f("ffn_conv_w")[0]
    for t in range(3):
        putcols(C_FCW + t * 44, fcw[t])
    putcols(C_FCB, f("ffn_conv_b")[0])
    putcols(C_DS5, f("s5_d")[0].reshape(-1))
    gfin = np.ascontiguousarray(np.broadcast_to(f("final_norm_g")[None, :], (128, D)))

    def gq(arr):
        a = arr.reshape(4, 8, 4, 16)
        return np.ascontiguousarray(a.transpose(1, 3, 0, 2).reshape(128, 16))
    lre, lim = f("s5_lam_re")[0], f("s5_lam_im")[0]
    ldt = np.broadcast_to(f("s5_log_dt")[0][:, None], (32, 64))
    s5par = np.stack([gq(lre), gq(lim), gq(ldt)], axis=1).astype(np.float32)

    def cb(b):
        a = b.reshape(4, 8, 4, 16, 16)
        return np.ascontiguousarray(a.transpose(1, 3, 0, 2, 4).reshape(128, 16, 16))
    bcc = np.stack([cb(f("s5_b_re")[0]), cb(f("s5_b_im")[0]),
                    cb(np.ascontiguousarray(f("s5_c_re")[0].transpose(0, 2, 1))),
                    cb(np.ascontiguousarray(f("s5_c_im")[0].transpose(0, 2, 1)))], axis=1).astype(np.float32)
    mask8 = np.zeros((128, 2, 8), np.float32)
    for p_ in range(128):
        mask8[p_, 0, p_ // 16] = 1.0
        mask8[p_, 1, p_ // 16] = -1.0
    lruw = np.zeros((128, 8, 128), np.float32)
    wa, wx = f("lru_w_a")[0], f("lru_w_x")[0]
    for mt in range(4):
        for hh in range(2):
            lruw[hh * 64:(hh + 1) * 64, mt, hh * 64:(hh + 1) * 64] = wa[2 * mt + hh]
            lruw[hh * 64:(hh + 1) * 64, 4 + mt, hh * 64:(hh + 1) * 64] = wx[2 * mt + hh]
    shared = {"w32": w32, "colp": colp, "gfin": gfin, "s5par": s5par, "bcc": bcc, "mask8": mask8, "lruw": lruw,
              "ident": np.eye(128, dtype=np.float32).astype(ml_dtypes.bfloat16),
              "identf": np.eye(128, dtype=np.float32)}
    return shared


def kernel(**inputs):
    shared = host_layout(inputs)
    x = np.asarray(inputs["x"], np.float32)
    mem = np.asarray(inputs["mem"], np.float32)
    nc = build(SEQ // TB)
    in_maps = []
    for c in range(8):
        m = dict(shared)
        m["x"] = np.ascontiguousarray(x[c])
        m["mem"] = np.ascontiguousarray(mem[c])
        in_maps.append(m)
    res = run_bass_kernel_spmd(nc, in_maps, core_ids=list(range(8)))
    return np.stack([np.asarray(r["out"], np.float32) for r in res.results], axis=0)
```
